# Optimizing a Trainium2 kernel written in Bass

```python
import math, functools
import jax, jax.numpy as jnp
from jax import lax
import numpy as np

D_MODEL = 1024
BATCH = 2
SEQ = 8192
DEPTH = 2
DEC_BATCH = 8
DEC_SEQ = 16
PAST_LEN = 1024

CHUNK = 64
N_BRANCH = 4
BRANCH_W = 512
S5_GROUP = 16
S5_GROUPS = BRANCH_W // S5_GROUP
S5_STATE = 64
DSA_HEADS = 8
DSA_KV_HEADS = 2
DSA_HEAD_DIM = 64
IDX_HEADS = 4
IDX_DIM = 64
DSA_TOPK_MAX = 256
Q_BLOCK = 128
REL_BUCKETS = 32
REL_MAX_DIST = 128
GDN_HEADS = 4
GDN_DK = 128
GDN_DV = 128
GDN_QK = GDN_HEADS * GDN_DK
GDN_VW = GDN_HEADS * GDN_DV
CONV_W = 4
GDN_CONV_CH = 2 * GDN_QK + GDN_VW
GLA_HEADS = 4
GLA_DK = 64
GLA_DV = 128
GLA_GATE_RANK = 16
GLA_TAU = 16.0
GLA_BLOCK = 16
LN_EPS = 1e-5
RMS_EPS = 1e-6
DN_ALPHA = (2 * DEPTH) ** 0.25
DN_BETA = (8 * DEPTH) ** -0.25

IN_LAYOUT = (
    ('a_u', BRANCH_W), ('a_gate', BRANCH_W),
    ('b_q', DSA_HEADS * DSA_HEAD_DIM), ('b_k', DSA_KV_HEADS * DSA_HEAD_DIM),
    ('b_v', DSA_KV_HEADS * DSA_HEAD_DIM), ('b_qi', IDX_HEADS * IDX_DIM), ('b_ki', IDX_DIM),
    ('b_wi', IDX_HEADS), ('b_gate', BRANCH_W),
    ('c_qkv', GDN_CONV_CH), ('c_beta', GDN_HEADS), ('c_a', GDN_HEADS), ('c_gate', BRANCH_W),
    ('d_q', GLA_HEADS * GLA_DK), ('d_k', GLA_HEADS * GLA_DK), ('d_v', GLA_HEADS * GLA_DV),
    ('d_g', GLA_GATE_RANK), ('d_gate', BRANCH_W),
    ('merge', N_BRANCH * D_MODEL),
)
IN_WIDTH = sum(w for _, w in IN_LAYOUT)

kernel_name = 'hybrid_streaming_encoder_step'


def split_cols(h):
    out = {}
    off = 0
    for name, width in IN_LAYOUT:
        out[name] = h[..., off:off + width]
        off += width
    return out


def layer_norm(x, g, b):
    xf = x.astype(jnp.float32)
    mu = jnp.mean(xf, axis=-1, keepdims=True)
    var = jnp.mean(jnp.square(xf - mu), axis=-1, keepdims=True)
    y = (xf - mu) * lax.rsqrt(var + LN_EPS) * g.astype(jnp.float32) + b.astype(jnp.float32)
    return y.astype(x.dtype)


def rms_norm(x, g):
    xf = x.astype(jnp.float32)
    return xf * lax.rsqrt(jnp.mean(xf * xf, axis=-1, keepdims=True) + RMS_EPS) * g.astype(jnp.float32)


def l2_normalize(x):
    xf = x.astype(jnp.float32)
    return xf * lax.rsqrt(jnp.sum(xf * xf, axis=-1, keepdims=True) + RMS_EPS)


def pad_time(a, blk):
    pad = (-a.shape[1]) % blk
    if pad:
        a = jnp.pad(a, [(0, 0), (0, pad)] + [(0, 0)] * (a.ndim - 2))
    return a


def to_blocks(a, blk):
    n, t = a.shape[:2]
    a = a.reshape((n, t // blk, blk) + a.shape[2:])
    return jnp.swapaxes(jnp.moveaxis(a, 1, 0), 2, 3)


def from_blocks(a, t):
    a = jnp.moveaxis(jnp.swapaxes(a, 2, 3), 0, 1)
    n, nb, blk = a.shape[:3]
    return a.reshape((n, nb * blk) + a.shape[3:])[:, :t]


def s5_branch(u, p, st):
    n, t, _ = u.shape
    f32 = jnp.float32
    uf = u.astype(f32).reshape(n, t, S5_GROUPS, S5_GROUP)
    lam_r = jnp.minimum(p['s5_a_re'].astype(f32), -1e-4)
    lam_i = p['s5_a_im'].astype(f32)
    dt = jnp.exp(p['s5_log_dt'].astype(f32))[:, None]
    mag = jnp.exp(lam_r * dt)
    ar = mag * jnp.cos(lam_i * dt)
    ai = mag * jnp.sin(lam_i * dt)
    den = lam_r * lam_r + lam_i * lam_i
    nr = ar - 1.0
    cr = (nr * lam_r + ai * lam_i) / den
    ci = (ai * lam_r - nr * lam_i) / den
    b_re = p['s5_b_re'].astype(f32)
    b_im = p['s5_b_im'].astype(f32)
    bb_r = cr[..., None] * b_re - ci[..., None] * b_im
    bb_i = cr[..., None] * b_im + ci[..., None] * b_re
    xr = jnp.einsum('ntgh,gph->ntgp', uf, bb_r)
    xi = jnp.einsum('ntgh,gph->ntgp', uf, bb_i)
    if st is not None:
        h0r = st['s5_re'].astype(f32)
        h0i = st['s5_im'].astype(f32)
        xr = xr.at[:, 0].add(ar * h0r - ai * h0i)
        xi = xi.at[:, 0].add(ar * h0i + ai * h0r)
    a_r = jnp.broadcast_to(ar, xr.shape)
    a_i = jnp.broadcast_to(ai, xi.shape)

    def combine(e1, e2):
        a1r, a1i, b1r, b1i = e1
        a2r, a2i, b2r, b2i = e2
        return (a2r * a1r - a2i * a1i, a2r * a1i + a2i * a1r,
                a2r * b1r - a2i * b1i + b2r, a2r * b1i + a2i * b1r + b2i)

    _, _, hr, hi = lax.associative_scan(combine, (a_r, a_i, xr, xi), axis=1)
    y = (jnp.einsum('ntgp,ghp->ntgh', hr, p['s5_c_re'].astype(f32))
         - jnp.einsum('ntgp,ghp->ntgh', hi, p['s5_c_im'].astype(f32)))
    y = y + p['s5_d'].astype(f32) * uf
    y = jax.nn.gelu(y.reshape(n, t, BRANCH_W)).astype(u.dtype)
    g = y @ p['s5_w_glu']
    y = g[..., :BRANCH_W] * jax.nn.sigmoid(g[..., BRANCH_W:])
    return y, hr[:, -1], hi[:, -1]


def t5_bucket(rel):
    nb = REL_BUCKETS // 2
    max_exact = nb // 2
    ret = jnp.where(rel > 0, nb, 0)
    dist = jnp.abs(rel)
    distf = jnp.maximum(dist, 1).astype(jnp.float32)
    large = max_exact + (jnp.log(distf / max_exact) / math.log(REL_MAX_DIST / max_exact)
                         * (nb - max_exact)).astype(jnp.int32)
    large = jnp.minimum(large, nb - 1)
    return ret + jnp.where(dist < max_exact, dist, large)


def dsa_attend(q, qi, wi, qpos, k, v, ki, kpos, rel_bias, topk):
    f32 = jnp.float32
    n, nq, nh, dh = q.shape
    rep = nh // DSA_KV_HEADS
    sc = jnp.einsum('nqhd,nld->nqhl', qi.astype(f32), ki.astype(f32)) * IDX_DIM ** -0.5
    sc = jnp.einsum('nqhl,nqh->nql', jax.nn.relu(sc), wi.astype(f32) * IDX_HEADS ** -0.5)
    allowed = kpos[None, :] < ((qpos // CHUNK + 1) * CHUNK)[:, None]
    sc = jnp.where(allowed[None], sc, -jnp.inf)
    top_s, top_i = lax.top_k(sc, topk)
    valid = jnp.isfinite(top_s)
    kg = jax.vmap(lambda kb, ib: kb[ib])(k, top_i)
    vg = jax.vmap(lambda vb, ib: vb[ib])(v, top_i)
    qg = q.reshape(n, nq, DSA_KV_HEADS, rep, dh)
    logits = jnp.einsum('nqgrd,nqkgd->nqgrk', qg, kg).astype(f32) * dh ** -0.5
    rel = kpos[top_i] - qpos[None, :, None]
    bias = rel_bias[t5_bucket(rel)].astype(f32)
    bias = bias.reshape(n, nq, topk, DSA_KV_HEADS, rep).transpose(0, 1, 3, 4, 2)
    logits = jnp.where(valid[:, :, None, None, :], logits + bias, -jnp.inf)
    prob = jax.nn.softmax(logits, axis=-1).astype(v.dtype)
    o = jnp.einsum('nqgrk,nqkgd->nqgrd', prob, vg)
    return o.reshape(n, nq, nh * dh)


def dsa_branch(h, rel_bias, st):
    n, t, _ = h['b_q'].shape
    q = h['b_q'].reshape(n, t, DSA_HEADS, DSA_HEAD_DIM)
    k = h['b_k'].reshape(n, t, DSA_KV_HEADS, DSA_HEAD_DIM)
    v = h['b_v'].reshape(n, t, DSA_KV_HEADS, DSA_HEAD_DIM)
    qi = h['b_qi'].reshape(n, t, IDX_HEADS, IDX_DIM)
    ki = h['b_ki']
    wi = h['b_wi']
    if st is None:
        k_all, v_all, ki_all = k, v, ki
        q0 = 0
    else:
        k_all = jnp.concatenate([st['k'].astype(k.dtype), k], axis=1)
        v_all = jnp.concatenate([st['v'].astype(v.dtype), v], axis=1)
        ki_all = jnp.concatenate([st['kidx'].astype(ki.dtype), ki], axis=1)
        q0 = st['k'].shape[1]
    n_keys = k_all.shape[1]
    topk = min(DSA_TOPK_MAX, n_keys // 4)
    kpos = jnp.arange(n_keys, dtype=jnp.int32)
    qpos = q0 + jnp.arange(t, dtype=jnp.int32)
    attend = functools.partial(dsa_attend, k=k_all, v=v_all, ki=ki_all, kpos=kpos,
                               rel_bias=rel_bias, topk=topk)
    if t % Q_BLOCK == 0:
        nb = t // Q_BLOCK
        blk = lambda a: jnp.moveaxis(a.reshape((n, nb, Q_BLOCK) + a.shape[2:]), 1, 0)
        o = lax.map(lambda xs: attend(*xs), (blk(q), blk(qi), blk(wi), qpos.reshape(nb, Q_BLOCK)))
        o = jnp.moveaxis(o, 0, 1).reshape(n, t, DSA_HEADS * DSA_HEAD_DIM)
    else:
        o = attend(q, qi, wi, qpos)
    return o, k, v, ki


def causal_conv(x, w, prev):
    n, t, c = x.shape
    if prev is None:
        prev = jnp.zeros((n, CONV_W - 1, c), x.dtype)
    xp = jnp.concatenate([prev.astype(x.dtype), x], axis=1)
    y = xp[:, 0:t] * w[0]
    for i in range(1, CONV_W):
        y = y + xp[:, i:i + t] * w[i]
    return y, xp[:, t:]


def gdn_block_step(s, xs):
    q, k, v, g, beta = xs
    c = q.shape[2]
    gc = jnp.cumsum(g, axis=-1)
    idx = jnp.arange(c)
    strict = idx[:, None] > idx[None, :]
    incl = idx[:, None] >= idx[None, :]
    diff = gc[..., :, None] - gc[..., None, :]
    dec_strict = jnp.where(strict, jnp.exp(jnp.where(strict, diff, 0.0)), 0.0)
    dec_incl = jnp.where(incl, jnp.exp(jnp.where(incl, diff, 0.0)), 0.0)
    a = beta[..., :, None] * jnp.einsum('nhid,nhjd->nhij', k, k) * dec_strict
    m = a + jnp.eye(c, dtype=a.dtype)
    rhs = jnp.concatenate([(beta * jnp.exp(gc))[..., None] * k, beta[..., None] * v], axis=-1)
    sol = lax.linalg.triangular_solve(m, rhs, left_side=True, lower=True, unit_diagonal=True)
    w_blk, u_blk = sol[..., :GDN_DK], sol[..., GDN_DK:]
    v_new = u_blk - jnp.einsum('nhik,nhkv->nhiv', w_blk, s)
    o = (jnp.exp(gc)[..., None] * jnp.einsum('nhik,nhkv->nhiv', q, s)
         + jnp.einsum('nhij,nhjv->nhiv', jnp.einsum('nhik,nhjk->nhij', q, k) * dec_incl, v_new))
    g_last = gc[..., -1:]
    s = (jnp.exp(g_last)[..., None] * s
         + jnp.einsum('nhjk,nhjv->nhkv', k * jnp.exp(g_last - gc)[..., None], v_new))
    return s, o


def gdn_branch(h, p, st):
    raw = h['c_qkv']
    n, t, _ = raw.shape
    f32 = jnp.float32
    conv_out, conv_state = causal_conv(raw, p['gdn_conv'], None if st is None else st['gdn_conv'])
    qkv = jax.nn.silu(conv_out.astype(f32))
    q = l2_normalize(qkv[..., :GDN_QK].reshape(n, t, GDN_HEADS, GDN_DK)) * GDN_DK ** -0.5
    k = l2_normalize(qkv[..., GDN_QK:2 * GDN_QK].reshape(n, t, GDN_HEADS, GDN_DK))
    v = qkv[..., 2 * GDN_QK:].reshape(n, t, GDN_HEADS, GDN_DV)
    beta = jax.nn.sigmoid(h['c_beta'].astype(f32))
    g = -jnp.exp(p['gdn_a_log'].astype(f32)) * jax.nn.softplus(
        h['c_a'].astype(f32) + p['gdn_dt_bias'].astype(f32))
    s0 = (jnp.zeros((n, GDN_HEADS, GDN_DK, GDN_DV), f32) if st is None
          else st['gdn'].astype(f32))
    xs = tuple(to_blocks(pad_time(a, CHUNK), CHUNK) for a in (q, k, v, g, beta))
    s_new, o = lax.scan(gdn_block_step, s0, xs)
    o = rms_norm(from_blocks(o, t), p['gdn_norm'])
    return o.reshape(n, t, BRANCH_W).astype(raw.dtype), s_new, conv_state


def gla_block_step(s, xs):
    q, k, v, lg = xs
    c = q.shape[2]
    bc = jnp.cumsum(lg, axis=2)
    idx = jnp.arange(c)
    incl = (idx[:, None] >= idx[None, :])[:, :, None]
    diff = bc[:, :, :, None, :] - bc[:, :, None, :, :]
    dec = jnp.where(incl, jnp.exp(jnp.where(incl, diff, 0.0)), 0.0)
    a = jnp.einsum('nhik,nhjk,nhijk->nhij', q, k, dec)
    o = (jnp.einsum('nhij,nhjv->nhiv', a, v)
         + jnp.einsum('nhik,nhkv->nhiv', q * jnp.exp(bc), s))
    b_last = bc[:, :, -1:, :]
    s = (jnp.exp(b_last[:, :, 0, :])[..., None] * s
         + jnp.einsum('nhjk,nhjv->nhkv', k * jnp.exp(b_last - bc), v))
    return s, o


def gla_branch(h, p, st):
    n, t, _ = h['d_q'].shape
    f32 = jnp.float32
    q = h['d_q'].astype(f32).reshape(n, t, GLA_HEADS, GLA_DK) * GLA_DK ** -0.5
    k = h['d_k'].astype(f32).reshape(n, t, GLA_HEADS, GLA_DK)
    v = h['d_v'].astype(f32).reshape(n, t, GLA_HEADS, GLA_DV)
    logit = h['d_g'].astype(f32) @ p['gla_w_g2'].astype(f32) + p['gla_b_g'].astype(f32)
    lg = (jax.nn.log_sigmoid(logit) / GLA_TAU).reshape(n, t, GLA_HEADS, GLA_DK)
    s0 = (jnp.zeros((n, GLA_HEADS, GLA_DK, GLA_DV), f32) if st is None
          else st['gla'].astype(f32))
    xs = tuple(to_blocks(pad_time(a, GLA_BLOCK), GLA_BLOCK) for a in (q, k, v, lg))
    s_new, o = lax.scan(gla_block_step, s0, xs)
    o = rms_norm(from_blocks(o, t), p['gla_norm'])
    return o.reshape(n, t, BRANCH_W).astype(h['d_q'].dtype), s_new


def trunk_layer(x, p, rel_bias, st):
    n, t, _ = x.shape
    h = split_cols(x @ p['w_in'])
    ya, s5_re, s5_im = s5_branch(h['a_u'], p, st)
    yb, k_new, v_new, ki_new = dsa_branch(h, rel_bias, st)
    yc, gdn_s, gdn_conv = gdn_branch(h, p, st)
    yd, gla_s = gla_branch(h, p, st)
    branches = jnp.stack([ya * jax.nn.silu(h['a_gate']), yb * jax.nn.silu(h['b_gate']),
                          yc * jax.nn.silu(h['c_gate']), yd * jax.nn.silu(h['d_gate'])], axis=2)
    proj = jnp.einsum('ntbw,bwd->ntbd', branches, p['w_branch'])
    gates = jax.nn.sigmoid(h['merge'].reshape(n, t, N_BRANCH, D_MODEL))
    mixed = jnp.sum(gates * proj, axis=2) @ p['w_out']
    y = layer_norm(DN_ALPHA * x + mixed, p['ln_g'], p['ln_b'])
    new = {'k': k_new, 'v': v_new, 'kidx': ki_new, 's5_re': s5_re, 's5_im': s5_im,
           'gdn': gdn_s, 'gdn_conv': gdn_conv, 'gla': gla_s}
    return y, new


def stack_layers(states, name):
    return jnp.stack([s[name] for s in states], axis=0)


def setup_inputs(seed: int = 0) -> dict:
    key = jax.random.key(seed)
    ks = jax.random.split(key, 32)
    f32 = jnp.float32

    def nrm(i, shape, scale):
        return scale * jax.random.normal(ks[i], shape, f32)

    def unif(i, shape, lo, hi):
        return jax.random.uniform(ks[i], shape, f32, lo, hi)

    gdn_dt = jnp.exp(unif(25, (DEPTH, GDN_HEADS), math.log(1e-3), math.log(1e-1)))
    return {
        'x_prompt': nrm(0, (BATCH, SEQ, D_MODEL), 1.0),
        'x_sample': nrm(1, (DEC_BATCH, DEC_SEQ, D_MODEL), 1.0),
        'cache_k': nrm(2, (DEPTH, DEC_BATCH, PAST_LEN, DSA_KV_HEADS, DSA_HEAD_DIM), 1.0),
        'cache_v': nrm(3, (DEPTH, DEC_BATCH, PAST_LEN, DSA_KV_HEADS, DSA_HEAD_DIM), 1.0),
        'cache_kidx': nrm(4, (DEPTH, DEC_BATCH, PAST_LEN, IDX_DIM), 1.0),
        'state_s5_re': nrm(5, (DEPTH, DEC_BATCH, S5_GROUPS, S5_STATE), 0.1),
        'state_s5_im': nrm(6, (DEPTH, DEC_BATCH, S5_GROUPS, S5_STATE), 0.1),
        'state_gdn': nrm(7, (DEPTH, DEC_BATCH, GDN_HEADS, GDN_DK, GDN_DV), 0.1),
        'state_gdn_conv': nrm(8, (DEPTH, DEC_BATCH, CONV_W - 1, GDN_CONV_CH), 1.0),
        'state_gla': nrm(9, (DEPTH, DEC_BATCH, GLA_HEADS, GLA_DK, GLA_DV), 0.1),
        'w_in': nrm(10, (DEPTH, D_MODEL, IN_WIDTH), D_MODEL ** -0.5),
        'w_branch': nrm(11, (DEPTH, N_BRANCH, BRANCH_W, D_MODEL), BRANCH_W ** -0.5),
        'w_out': nrm(12, (DEPTH, D_MODEL, D_MODEL), DN_BETA * D_MODEL ** -0.5),
        'ln_g': 1.0 + nrm(13, (DEPTH, D_MODEL), 0.02),
        'ln_b': nrm(14, (DEPTH, D_MODEL), 0.02),
        'rel_bias': nrm(15, (REL_BUCKETS, DSA_HEADS), 0.2),
        's5_a_re': -0.5 + nrm(16, (DEPTH, S5_GROUPS, S5_STATE), 0.01),
        's5_a_im': math.pi * jnp.arange(S5_STATE, dtype=f32) + nrm(17, (DEPTH, S5_GROUPS, S5_STATE), 0.01),
        's5_log_dt': unif(18, (DEPTH, S5_GROUPS), math.log(1e-3), math.log(1e-1)),
        's5_b_re': nrm(19, (DEPTH, S5_GROUPS, S5_STATE, S5_GROUP), (2 * S5_GROUP) ** -0.5),
        's5_b_im': nrm(20, (DEPTH, S5_GROUPS, S5_STATE, S5_GROUP), (2 * S5_GROUP) ** -0.5),
        's5_c_re': nrm(21, (DEPTH, S5_GROUPS, S5_GROUP, S5_STATE), (2 * S5_STATE) ** -0.5),
        's5_c_im': nrm(22, (DEPTH, S5_GROUPS, S5_GROUP, S5_STATE), (2 * S5_STATE) ** -0.5),
        's5_d': nrm(23, (DEPTH, S5_GROUPS, S5_GROUP), 1.0),
        's5_w_glu': nrm(24, (DEPTH, BRANCH_W, 2 * BRANCH_W), BRANCH_W ** -0.5),
        'gdn_conv': nrm(26, (DEPTH, CONV_W, GDN_CONV_CH), CONV_W ** -0.5),
        'gdn_a_log': jnp.log(unif(27, (DEPTH, GDN_HEADS), 1.0, 16.0)),
        'gdn_dt_bias': gdn_dt + jnp.log(-jnp.expm1(-gdn_dt)),
        'gdn_norm': 1.0 + nrm(28, (DEPTH, GDN_DV), 0.02),
        'gla_w_g2': nrm(29, (DEPTH, GLA_GATE_RANK, GLA_HEADS * GLA_DK), GLA_GATE_RANK ** -0.5),
        'gla_b_g': nrm(30, (DEPTH, GLA_HEADS * GLA_DK), 0.1),
        'gla_norm': 1.0 + nrm(31, (DEPTH, GLA_DV), 0.02),
    }


def reference(x_prompt, x_sample, cache_k, cache_v, cache_kidx, state_s5_re, state_s5_im,
              state_gdn, state_gdn_conv, state_gla, w_in, w_branch, w_out, ln_g, ln_b, rel_bias,
              s5_a_re, s5_a_im, s5_log_dt, s5_b_re, s5_b_im, s5_c_re, s5_c_im, s5_d, s5_w_glu,
              gdn_conv, gdn_a_log, gdn_dt_bias, gdn_norm, gla_w_g2, gla_b_g, gla_norm):
    yp = x_prompt
    ys = x_sample
    new_p = []
    new_s = []
    for l in range(DEPTH):
        p = {'w_in': w_in[l], 'w_branch': w_branch[l], 'w_out': w_out[l],
             'ln_g': ln_g[l], 'ln_b': ln_b[l],
             's5_a_re': s5_a_re[l], 's5_a_im': s5_a_im[l], 's5_log_dt': s5_log_dt[l],
             's5_b_re': s5_b_re[l], 's5_b_im': s5_b_im[l], 's5_c_re': s5_c_re[l],
             's5_c_im': s5_c_im[l], 's5_d': s5_d[l], 's5_w_glu': s5_w_glu[l],
             'gdn_conv': gdn_conv[l], 'gdn_a_log': gdn_a_log[l], 'gdn_dt_bias': gdn_dt_bias[l],
             'gdn_norm': gdn_norm[l], 'gla_w_g2': gla_w_g2[l], 'gla_b_g': gla_b_g[l],
             'gla_norm': gla_norm[l]}
        st = {'k': cache_k[l], 'v': cache_v[l], 'kidx': cache_kidx[l],
              's5_re': state_s5_re[l], 's5_im': state_s5_im[l], 'gdn': state_gdn[l],
              'gdn_conv': state_gdn_conv[l], 'gla': state_gla[l]}
        yp, stp = trunk_layer(yp, p, rel_bias, None)
        ys, sts = trunk_layer(ys, p, rel_bias, st)
        new_p.append(stp)
        new_s.append(sts)
    return (yp, ys,
            stack_layers(new_p, 'k'), stack_layers(new_p, 'v'), stack_layers(new_p, 'kidx'),
            stack_layers(new_p, 's5_re'), stack_layers(new_p, 's5_im'), stack_layers(new_p, 'gdn'),
            stack_layers(new_p, 'gdn_conv'), stack_layers(new_p, 'gla'),
            stack_layers(new_s, 'k'), stack_layers(new_s, 'v'), stack_layers(new_s, 'kidx'),
            stack_layers(new_s, 's5_re'), stack_layers(new_s, 's5_im'), stack_layers(new_s, 'gdn'),
            stack_layers(new_s, 'gdn_conv'), stack_layers(new_s, 'gla'))
```

```python
import math
from contextlib import ExitStack
import numpy as np
import concourse.bass as bass
import concourse.mybir as mybir
from concourse.bass_utils import run_bass_kernel_spmd

F32 = mybir.dt.float32
BF16 = mybir.dt.bfloat16
U8 = mybir.dt.uint8
ALU = mybir.AluOpType
AF = mybir.ActivationFunctionType
AX = mybir.AxisListType

D = 1024
SEQ = 8192
DEPTH = 2
DEC_SEQ = 16
PAST = 1024
NCORE = 8
TOPK_MAX = 256
LN_EPS = 1e-5
RMS_EPS = 1e-6
DN_ALPHA = (2 * DEPTH) ** 0.25

_LAY = (('a_u', 512), ('a_gate', 512), ('b_q', 512), ('b_k', 128), ('b_v', 128), ('b_qi', 256), ('b_ki', 64),
        ('b_wi', 4), ('b_gate', 512), ('c_qkv', 1536), ('c_beta', 4), ('c_a', 4), ('c_gate', 512),
        ('d_q', 256), ('d_k', 256), ('d_v', 512), ('d_g', 16), ('d_gate', 512), ('merge', 4096))
OFF = {}
_o = 0
for _n, _w in _LAY:
    OFF[_n] = _o
    _o += _w
IN_WIDTH = _o


def _chunks():
    ch = []
    r = lambda n, a, b: list(range(OFF[n] + a, OFF[n] + b))
    ch.append(('kvA', r('b_k', 0, 128)))
    ch.append(('kvB', r('b_v', 0, 128)))
    ch.append(('kvC', r('b_ki', 0, 64) + r('b_ki', 0, 64)))
    ch.append(('kvD', r('b_wi', 0, 4) + r('c_beta', 0, 4) + r('c_a', 0, 4)))
    for h in range(4):
        ch.append(('qq%d' % h, r('b_q', 64 * h, 64 * h + 64) + r('b_q', 64 * (h + 4), 64 * (h + 4) + 64)))
    for h in range(4):
        ch.append(('qi%d' % h, r('b_qi', 64 * h, 64 * h + 64) + r('b_qi', 64 * h, 64 * h + 64)))
    for c in range(4):
        ch.append(('au%d' % c, r('a_u', 128 * c, 128 * c + 128)))
    for c in range(12):
        ch.append(('cq%d' % c, r('c_qkv', 128 * c, 128 * c + 128)))
    for h in range(4):
        ch.append(('dq%d' % h, r('d_q', 64 * h, 64 * h + 64)))
    for h in range(4):
        ch.append(('dk%d' % h, r('d_k', 64 * h, 64 * h + 64)))
    for c in range(4):
        ch.append(('dv%d' % c, r('d_v', 128 * c, 128 * c + 128)))
    ch.append(('dg', r('d_g', 0, 16)))
    for c in range(4):
        ch.append(('ag%d' % c, r('a_gate', 128 * c, 128 * c + 128)))
    for h in range(8):
        ch.append(('bg%d' % h, r('b_gate', 64 * h, 64 * h + 64)))
    for c in range(4):
        ch.append(('cg%d' % c, r('c_gate', 128 * c, 128 * c + 128)))
    for c in range(4):
        ch.append(('dgt%d' % c, r('d_gate', 128 * c, 128 * c + 128)))
    for b in range(4):
        for c in range(8):
            ch.append(('mg%d_%d' % (b, c), r('merge', 1024 * b + 128 * c, 1024 * b + 128 * c + 128)))
    return ch


CHUNKS = _chunks()
CIDX = {n: i for i, (n, _) in enumerate(CHUNKS)}
NCH = len(CHUNKS)


def _t5_bucket(rel):
    nb = 16
    max_exact = 8
    ret = np.where(rel > 0, nb, 0)
    dist = np.abs(rel)
    distf = np.maximum(dist, 1).astype(np.float32)
    large = max_exact + (np.log(distf / max_exact) / math.log(128 / max_exact) * (nb - max_exact)).astype(np.int32)
    large = np.minimum(large, nb - 1)
    return ret + np.where(dist < max_exact, dist, large)


class Tk:
    __slots__ = ('h', 'w', 'r')

    def __init__(self, h=None):
        self.h = h
        self.w = None
        self.r = []

    def __getitem__(self, idx):
        return self.h[idx]


class Sched:
    def __init__(self, nc, stack):
        self.nc = nc
        self.stack = stack
        self.eng = {}
        for n, h in [('pe', nc.tensor), ('dve', nc.vector), ('act', nc.scalar), ('pool', nc.gpsimd), ('sp', nc.sync)]:
            sem = stack.enter_context(nc.semaphore('s_' + n))
            self.eng[n] = dict(h=h, sem=sem, cnt=0, known={})
        self.dma_sems = {}
        for q in ('sp', 'pool', 'act'):
            self.dma_sems[q] = [[stack.enter_context(nc.semaphore('d%s%d' % (q, i))), 0] for i in range(12)]
        self.dma_rr = {'sp': 0, 'pool': 0, 'act': 0}
        self.nid = 0
        self.n_ins = 0

    def sb(self, shape, dt=F32):
        self.nid += 1
        return Tk(self.stack.enter_context(self.nc.sbuf_tensor('t%d' % self.nid, shape, dt)))

    def ps(self, shape, dt=F32):
        self.nid += 1
        return Tk(self.stack.enter_context(self.nc.psum_tensor('p%d' % self.nid, shape, dt)))

    def _wait(self, e, sem, val):
        E = self.eng[e]
        k = id(sem)
        if E['known'].get(k, 0) >= val:
            return
        E['h'].wait_ge(sem, val)
        E['known'][k] = val

    def _deps(self, e, reads, writes):
        E = self.eng[e]
        own = E['sem']
        pe = (e == 'pe')
        for t in reads:
            if t.w is not None and not (pe and t.w[0] is own):
                self._wait(e, *t.w)
        for t in writes:
            if t.w is not None and not (pe and t.w[0] is own):
                self._wait(e, *t.w)
            for (s, v) in t.r:
                if not (pe and s is own):
                    self._wait(e, s, v)

    def _mark(self, tok, reads, writes):
        for t in writes:
            t.w = tok
            t.r = []
        for t in reads:
            if t not in writes:
                if len(t.r) > 6:
                    d = {}
                    for (s, v) in t.r:
                        d[id(s)] = (s, max(v, d.get(id(s), (s, 0))[1]))
                    t.r = list(d.values())
                t.r.append(tok)

    def op(self, e, fn, reads=(), writes=()):
        E = self.eng[e]
        self._deps(e, reads, writes)
        ins = fn(E['h'])
        E['cnt'] += 1
        ins.then_inc(E['sem'], 1)
        self._mark((E['sem'], E['cnt']), reads, writes)
        self.n_ins += 1
        return ins

    def dma(self, e, out, in_, reads=(), writes=(), **kw):
        E = self.eng[e]
        slots = self.dma_sems[e]
        slot = slots[self.dma_rr[e]]
        self.dma_rr[e] = (self.dma_rr[e] + 1) % len(slots)
        if slot[1] > 0:
            self._wait(e, slot[0], slot[1])
        self._deps(e, reads, writes)
        ins = E['h'].dma_start(out=out, in_=in_, **kw)
        slot[1] += 16
        ins.then_inc(slot[0], 16)
        self._mark((slot[0], slot[1]), reads, writes)
        self.n_ins += 1
        return ins

    def finish(self, tiles):
        for t in tiles:
            if t.w is not None:
                self._wait('sp', *t.w)


class Builder:
    def __init__(self, T):
        self.T = T
        self.G = min(128, T)
        self.nc = bass.Bass("TRN2", target_bir_lowering=False)

    def dram_in(self, name, shape, dt=F32):
        return self.nc.dram_tensor(name, list(shape), dt, kind="ExternalInput").ap()

    def dram_out(self, name, shape, dt=F32):
        return self.nc.dram_tensor(name, list(shape), dt, kind="ExternalOutput").ap()

    def build(self):
        nc = self.nc
        T = self.T
        I = {}
        O = {}
        I['xp'] = self.dram_in('xp', [T, D])
        I['xs'] = self.dram_in('xs', [DEC_SEQ, D])
        I['win'] = self.dram_in('win', [DEPTH, NCH, 128, 8, 128])
        I['wbr'] = self.dram_in('wbr', [DEPTH, 4, 512, D])
        I['wout'] = self.dram_in('wout', [DEPTH, D, D])
        I['lng'] = self.dram_in('lng', [DEPTH, D])
        I['lnb'] = self.dram_in('lnb', [DEPTH, D])
        I['cst'] = self.dram_in('cst', [128, 1024])
        I['glaw'] = self.dram_in('glaw', [DEPTH, 16, 256])
        I['glab'] = self.dram_in('glab', [DEPTH, 64, 4])
        I['glan'] = self.dram_in('glan', [DEPTH, 128, 1])
        I['sgla'] = self.dram_in('sgla', [DEPTH, 4, 64, 128])
        I['relb'] = self.dram_in('relb', [32, 8])
        I['oh'] = self.dram_in('oh', [32, 384])
        I['ck'] = self.dram_in('ck', [DEPTH, PAST, 128])
        I['cv'] = self.dram_in('cv', [DEPTH, PAST, 128])
        I['cki'] = self.dram_in('cki', [DEPTH, PAST, 128])
        I['gcw'] = self.dram_in('gcw', [DEPTH, 128, 12, 4])
        I['gpar'] = self.dram_in('gpar', [DEPTH, 128, 8])
        I['gnorm'] = self.dram_in('gnorm', [DEPTH, 128, 1])
        I['sgdn'] = self.dram_in('sgdn', [DEPTH, 4, 128, 128])
        I['sconv'] = self.dram_in('sconv', [DEPTH, 128, 12, 3])
        I['s5p'] = self.dram_in('s5p', [DEPTH, 128, 3, 16])
        I['s5b'] = self.dram_in('s5b', [DEPTH, 2, 2, 128, 4, 128])
        I['s5c'] = self.dram_in('s5c', [DEPTH, 2, 128, 16, 128])
        I['s5d'] = self.dram_in('s5d', [DEPTH, 128, 4])
        I['s5h0'] = self.dram_in('s5h0', [DEPTH, 128, 2, 16])
        I['wglu'] = self.dram_in('wglu', [DEPTH, 512, 1024])
        O['yp'] = self.dram_out('yp', [T, D])
        O['ys'] = self.dram_out('ys', [DEC_SEQ, D])
        for g, tt in (('p', T), ('s', DEC_SEQ)):
            O['k' + g] = self.dram_out('k' + g, [DEPTH, tt, 128])
            O['v' + g] = self.dram_out('v' + g, [DEPTH, tt, 128])
            O['ki' + g] = self.dram_out('ki' + g, [DEPTH, tt, 64])
            O['gla' + g] = self.dram_out('gla' + g, [DEPTH, 4, 64, 128])
            O['conv' + g] = self.dram_out('conv' + g, [DEPTH, 3, 1536])
            O['gdn' + g] = self.dram_out('ogdn' + g, [DEPTH, 4, 128, 128])
            O['s5' + g] = self.dram_out('os5' + g, [DEPTH, 128, 2, 16])
        self.I, self.O = I, O
        self.y0p = nc.dram_tensor('y0p', [T, D], F32).ap()
        self.fd = nc.dram_tensor('fd', [8, 384], F32).ap()
        self.cfd = nc.dram_tensor('cfd', [8, 1], F32).ap()
        self.y0s = nc.dram_tensor('y0s', [DEC_SEQ, D], F32).ap()
        with ExitStack() as st:
            S = Sched(nc, st)
            self.S = S
            self.dtk = {}
            self.setup()
            for l in range(DEPTH):
                self.layer_setup(l)
                self.run_seq(l, 'p')
                self.run_seq(l, 's')
            S.finish(list(self.dtk.values()))
        return nc

    def dk(self, name):
        if name not in self.dtk:
            self.dtk[name] = Tk(None)
        return self.dtk[name]

    def setup(self):
        S = self.S
        G = self.G
        self.cst = S.sb([128, 1024])
        S.dma('sp', self.cst[:], self.I['cst'][:, :], writes=[self.cst])
        self.ident = self.cst
        self.pbanks = [S.ps([128, 512]) for _ in range(6)]
        self.pacc = [S.ps([128, 512]) for _ in range(2)]
        self.pb_i = 0
        self.xT = S.sb([128, 8, G])
        self.wch = [S.sb([128, 8, 128]) for _ in range(2)]
        self.wch_i = 0
        self.brT = S.sb([128, 16, G])
        self.lng = S.sb([128, D])
        self.lnb = S.sb([128, D])
        self.wout = S.sb([128, 8, 128])
        self.wbr = S.sb([128, 4, 128])
        self.wbrB = S.sb([64, 8, 128])
        self.brB = S.sb([64, 8, G])
        self.ARENA = 10240
        self.arena = S.sb([128, self.ARENA])
        self.fence_t = S.sb([1, 4])
        self.phase_views = {}
        self.phase_off = {}
        self.cur_phase = None
        self.fence_tok = None
        self.scr_i = 0
        self.ws_i = 0
        self.small = [S.sb([128, 16]) for _ in range(8)]
        self.sm3 = S.sb([128, 3, 16])
        self.small_i = 0
        self.epsb = S.sb([128, 2])
        S.op('dve', lambda h: h.memset(self.epsb[:, 0:1], LN_EPS), writes=[self.epsb])
        S.op('dve', lambda h: h.memset(self.epsb[:, 1:2], RMS_EPS), writes=[self.epsb])
        self.LC = min(128, G)
        self.s5_cos = S.sb([128, 16, self.LC])
        self.s5_sin = S.sb([128, 16, self.LC])
        self.s5_cr = S.sb([128, 16, 128])
        self.s5_ci = S.sb([128, 16, 128])
        self.s5_b = S.sb([128, 4, 4, 128])
        self.s5_q = S.sb([128, 12, 16])
        self.s5_d = S.sb([128, 4])
        self.s5_carry = S.sb([128, 2, 16])
        self.wglu = S.sb([128, 4, 128])
        T = self.T
        self.NK = max(T, PAST + 128)
        self.NH = 4096 if T == 8192 else self.NK
        self.NB = self.NK // 128
        self.kT = S.sb([128, self.NK], BF16)
        self.kiT = S.sb([128, self.NH])
        self.vA = S.sb([128, self.NB, 130], BF16)
        self.bias8 = S.sb([128, 4, 512])
        self.qT = S.sb([128, 4, G], BF16)
        self.qiT = S.sb([128, 4, G])
        self.wiT = S.sb([128, 4])
        self.offrow = S.sb([33, 4 * G], BF16)
        self.kmax2 = S.sb([33, 2])
        self.cf = S.sb([33, 4])
        self.bs = S.sb([128, 16])
        self.Et = [S.sb([128, 4 * G], BF16) for _ in range(2)]
        self.Pt = [S.sb([128, 4 * G], BF16) for _ in range(2)]
        self.selT = [S.sb([128, G], BF16) for _ in range(2)]
        self.onesb = S.sb([33, 128], BF16)
        self.phase('setup')
        self.dsa_setup()
        self.gdn_S = S.sb([128, 4, 128])
        self.gdn_cw = S.sb([128, 12, 4])
        self.gdn_par = S.sb([128, 8])
        self.gdn_n = S.sb([128, 1])
        self.gdn_tail = S.sb([128, 12, 3])
        self.gla_S = S.sb([64, 4, 128])
        self.gla_w = S.sb([16, 256])
        self.gla_b = S.sb([64, 4])
        self.gla_n = S.sb([128, 1])
        self.resetm = S.sb([64, 4 * G])

    def pbank(self):
        t = self.pbanks[self.pb_i]
        self.pb_i = (self.pb_i + 1) % len(self.pbanks)
        return t

    def phase(self, name):
        S = self.S
        if name == self.cur_phase:
            return
        prev = list(self.phase_views.get(self.cur_phase, {}).values()) if self.cur_phase else []
        nxt = list(self.phase_views.get(name, {}).values())
        ft = self.fence_t
        S.op('dve', lambda h: h.memset(ft[0:1, 0:1], 0.0), reads=[], writes=[ft] + prev + nxt)
        self.fence_tok = ft.w
        self.cur_phase = name
        self.phase_views.setdefault(name, {})
        self.phase_off.setdefault(name, 0)
        self.scr_i = 0
        self.ws_i = 0

    def av(self, key, ncols):
        pv = self.phase_views[self.cur_phase]
        if key not in pv:
            off = self.phase_off[self.cur_phase]
            assert off + ncols <= self.ARENA, (self.cur_phase, key, off, ncols)
            t = Tk(self.arena.h[:, off:off + ncols])
            t.w = self.fence_tok
            pv[key] = t
            self.phase_off[self.cur_phase] = off + ncols
        return pv[key]

    def scratch(self):
        t = self.av(('scr', self.scr_i % 8), 512)
        self.scr_i += 1
        return t

    def wsp(self, reset=False):
        if reset:
            self.ws_i = 0
        t = self.av(('ws', self.ws_i), 512)
        self.ws_i += 1
        return t

    def sm(self):
        t = self.small[self.small_i]
        self.small_i = (self.small_i + 1) % len(self.small)
        return t

    def layer_setup(self, l):
        S = self.S
        I = self.I
        S.dma('sp', self.lng[:], I['lng'][l:l + 1, :].partition_broadcast(128), writes=[self.lng])
        S.dma('sp', self.lnb[:], I['lnb'][l:l + 1, :].partition_broadcast(128), writes=[self.lnb])
        S.dma('sp', self.gla_w[:], I['glaw'][l], writes=[self.gla_w])
        S.dma('sp', self.gla_b[:], I['glab'][l], writes=[self.gla_b])
        S.dma('sp', self.gla_n[:], I['glan'][l], writes=[self.gla_n])
        self.phase('setup')
        self.s5_setup(l)
        self.gdn_setup(l)

    def load_w(self, l, name):
        S = self.S
        w = self.wch[self.wch_i]
        self.wch_i = (self.wch_i + 1) % len(self.wch)
        S.dma('sp', w[:], self.I['win'][l, CIDX[name]], writes=[w])
        return w

    def proj_fm(self, l, name, ncols, gw):
        S = self.S
        w = self.load_w(l, name)
        pb = self.pbank()
        xT = self.xT
        for k in range(8):
            S.op('pe', lambda h, k=k: h.matmul(pb[0:ncols, 0:gw], lhsT=w[:, k, 0:ncols], rhs=xT[:, k, 0:gw],
                                               start=(k == 0), stop=(k == 7)), reads=[w, xT], writes=[pb])
        return pb

    def proj_tm(self, l, name, ncols, t0, tw):
        S = self.S
        w = self.load_w(l, name)
        pb = self.pbank()
        xT = self.xT
        for k in range(8):
            S.op('pe', lambda h, k=k: h.matmul(pb[0:tw, 0:ncols], lhsT=xT[:, k, t0:t0 + tw], rhs=w[:, k, 0:ncols],
                                               start=(k == 0), stop=(k == 7)), reads=[w, xT], writes=[pb])
        return pb

    def run_seq(self, l, grp):
        S = self.S
        I, O = self.I, self.O
        T = self.T if grp == 'p' else DEC_SEQ
        G = min(self.G, T)
        TP = min(128, T)
        C = min(64, T)
        nG = T // G
        if l == 0:
            xin = I['xp'] if grp == 'p' else I['xs']
            xin_tk = self.dk('in')
        else:
            xin = self.y0p if grp == 'p' else self.y0s
            xin_tk = self.dk('y0' + grp)
        yout = (O['yp'] if grp == 'p' else O['ys']) if l == DEPTH - 1 else (self.y0p if grp == 'p' else self.y0s)
        yout_tk = self.dk('yout' + grp) if l == DEPTH - 1 else self.dk('y0' + grp)

        if grp == 'p':
            S.op('dve', lambda h: h.memset(self.gla_S[:], 0.0), writes=[self.gla_S])
        else:
            S.dma('sp', self.gla_S[:], I['sgla'][l].rearrange("h k v -> k h v"), writes=[self.gla_S])
        self.s5_init(l, grp)
        self.gdn_init(l, grp)
        self.dsa_init(l, grp)
        S.op('dve', lambda h: h.memset(self.resetm[:], 1.0), writes=[self.resetm])
        rm3 = self.resetm[:, 0:4 * G].rearrange("p (a c) -> p a c", c=C)
        S.op('dve', lambda h: h.memset(rm3[:, :, 0:1], 0.0), writes=[self.resetm])

        for gi in range(nG):
            g0 = gi * G
            self.phase('kv')
            xts = []
            for ti in range(G // TP):
                xt = self.av('xtm', D)
                xts.append(xt)
                S.dma('sp', xt[0:TP, :], xin[g0 + ti * TP: g0 + (ti + 1) * TP, :], reads=[xin_tk], writes=[xt])
                for half in range(2):
                    pb = self.pbank()
                    for kk in range(4):
                        k = half * 4 + kk
                        S.op('pe', lambda h, k=k, kk=kk: h.transpose(pb[:, kk * 128: kk * 128 + TP], xt[0:TP, k * 128:(k + 1) * 128], self.ident[0:TP, 0:TP]),
                             reads=[xt, self.cst], writes=[pb])
                    dst = self.xT[:, half * 4:(half + 1) * 4, ti * TP:(ti + 1) * TP]
                    src = pb[:, :].rearrange("p (a c) -> p a c", c=128)[:, :, 0:TP]
                    S.op('act' if half else 'dve',
                         (lambda h: h.activation(out=dst, in_=src, func=AF.Copy)) if half else (lambda h: h.tensor_copy(out=dst, in_=src)),
                         reads=[pb], writes=[self.xT])
            self.branch_zero(G)
            self.phase('kv')
            self.kv_part(l, grp, g0, G, TP)
            self.phase('s5')
            self.s5_part(l, grp, g0, G, last=(gi == nG - 1))
            self.phase('gla')
            self.gla_part(l, grp, g0, G, C, last=(gi == nG - 1))
            self.phase('gdn')
            self.gdn_part(l, grp, g0, G, C, last=(gi == nG - 1))
            self.phase('dsa')
            self.dsa_part(l, grp, g0, G, TP)
            self.phase('out')
            self.out_stage(l, grp, g0, G, TP, xin, xin_tk, yout, yout_tk)

    def branch_zero(self, G):
        S = self.S
        S.op('dve', lambda h: h.memset(self.brT[:, :, 0:G], 0.0), writes=[self.brT])
        S.op('dve', lambda h: h.memset(self.brB[:, :, 0:G], 0.0), writes=[self.brB])

    def kv_part(self, l, grp, g0, G, TP):
        S = self.S
        O = self.O
        kbase = 0 if grp == 'p' else PAST
        for ti in range(G // TP):
            t0 = ti * TP
            tiles = []
            for nm, ncols, oname, key in (('kvA', 128, 'k', 'k_tm'), ('kvB', 128, 'v', 'v_tm'), ('kvC', 128, 'ki', 'c_tm')):
                pb = self.proj_tm(l, nm, ncols, t0, TP)
                sc = self.av(key, 128)
                S.op('act', lambda h: h.activation(out=sc[0:TP, 0:ncols], in_=pb[0:TP, 0:ncols], func=AF.Copy), reads=[pb], writes=[sc])
                oc = 64 if oname == 'ki' else 128
                S.dma('pool', O[oname + grp][l, g0 + t0: g0 + t0 + TP, :], sc[0:TP, 0:oc], reads=[sc], writes=[self.dk('o' + oname + grp)])
                tiles.append(sc)
            self.add_keys(tiles[0], tiles[1], tiles[2], TP, kbase + g0 + t0)
            pb = self.proj_tm(l, 'kvD', 12, t0, TP)
            S.op('dve', lambda h: h.tensor_copy(out=self.wiT[0:TP, 0:4], in_=pb[0:TP, 0:4]), reads=[pb], writes=[self.wiT])
        T = self.T if grp == 'p' else DEC_SEQ
        if g0 + G == T:
            for c in range(12):
                w = self.load_w(l, 'cq%d' % c)
                pb = self.pbank()
                for k in range(8):
                    S.op('pe', lambda h, k=k: h.matmul(pb[0:3, 0:128], lhsT=self.xT[:, k, G - 3:G], rhs=w[:, k, :],
                                                       start=(k == 0), stop=(k == 7)), reads=[w, self.xT], writes=[pb])
                sc = self.scratch()
                S.op('act', lambda h: h.activation(out=sc[0:3, 0:128], in_=pb[0:3, 0:128], func=AF.Copy), reads=[pb], writes=[sc])
                S.dma('pool', O['conv' + grp][l, :, c * 128:(c + 1) * 128], sc[0:3, 0:128], reads=[sc], writes=[self.dk('oconv' + grp)])


    def range_reduce(self, x, n):
        S = self.S
        TWO_PI = 2.0 * math.pi
        ki = self.s5_ki
        kf = self.s5_kf
        S.op('dve', lambda h: h.tensor_scalar(out=kf[:, 0:n], in0=x, scalar1=1.0 / TWO_PI, scalar2=None, op0=ALU.mult), reads=[self.s5_ang], writes=[self.s5_kfT])
        S.op('dve', lambda h: h.tensor_copy(out=ki[:, 0:n], in_=kf[:, 0:n]), reads=[self.s5_kfT], writes=[self.s5_kiT])
        S.op('dve', lambda h: h.tensor_copy(out=kf[:, 0:n], in_=ki[:, 0:n]), reads=[self.s5_kiT], writes=[self.s5_kfT])
        S.op('dve', lambda h: h.scalar_tensor_tensor(out=x, in0=kf[:, 0:n], scalar=-TWO_PI, in1=x, op0=ALU.mult, op1=ALU.add), reads=[self.s5_kfT, self.s5_ang], writes=[self.s5_ang])
        S.op('dve', lambda h: h.tensor_scalar(out=kf[:, 0:n], in0=x, scalar1=math.pi, scalar2=-TWO_PI, op0=ALU.is_gt, op1=ALU.mult), reads=[self.s5_ang], writes=[self.s5_kfT])
        S.op('dve', lambda h: h.tensor_tensor(out=x, in0=x, in1=kf[:, 0:n], op=ALU.add), reads=[self.s5_kfT, self.s5_ang], writes=[self.s5_ang])
        S.op('dve', lambda h: h.tensor_scalar(out=kf[:, 0:n], in0=x, scalar1=-math.pi, scalar2=TWO_PI, op0=ALU.is_lt, op1=ALU.mult), reads=[self.s5_ang], writes=[self.s5_kfT])
        S.op('dve', lambda h: h.tensor_tensor(out=x, in0=x, in1=kf[:, 0:n], op=ALU.add), reads=[self.s5_kfT, self.s5_ang], writes=[self.s5_ang])

    def s5_setup(self, l):
        S = self.S
        I = self.I
        LC = self.LC
        q = self.s5_q
        if not hasattr(self, 's5_ang'):
            self.s5_ang = S.sb([128, 256])
            self.s5_kfT = S.sb([128, 256])
            self.s5_kiT = S.sb([128, 256], mybir.dt.int32)
            self.s5_kf = self.s5_kfT
            self.s5_ki = self.s5_kiT
        ang = self.s5_ang
        raw = self.sm3
        S.dma('sp', raw[:, :, :], I['s5p'][l], writes=[raw])
        S.dma('sp', self.s5_d[:, :], I['s5d'][l], writes=[self.s5_d])
        S.dma('sp', self.s5_b[:, :, :, :], I['s5b'][l].rearrange("v r p c f -> p (v r) c f"), writes=[self.s5_b])
        Q = lambda i: q[:, i, :]
        rq = [q]
        S.op('dve', lambda h: h.tensor_scalar(out=Q(4), in0=raw[:, 0, :], scalar1=-1e-4, scalar2=None, op0=ALU.min), reads=[raw], writes=rq)
        S.op('dve', lambda h: h.tensor_copy(out=Q(5), in_=raw[:, 1, :]), reads=[raw], writes=rq)
        S.op('act', lambda h: h.activation(out=Q(6), in_=raw[:, 2, :], func=AF.Exp), reads=[raw], writes=rq)
        S.op('dve', lambda h: h.tensor_tensor(out=Q(7), in0=Q(4), in1=Q(6), op=ALU.mult), reads=rq, writes=rq)
        S.op('act', lambda h: h.activation(out=Q(0), in_=Q(7), func=AF.Exp), reads=rq, writes=rq)
        S.op('dve', lambda h: h.tensor_tensor(out=Q(1), in0=Q(5), in1=Q(6), op=ALU.mult), reads=rq, writes=rq)
        S.op('dve', lambda h: h.tensor_copy(out=ang[:, 0:16], in_=Q(1)), reads=rq, writes=[ang])
        self.range_reduce(ang[:, 0:16], 16)
        S.op('dve', lambda h: h.tensor_copy(out=Q(1), in_=ang[:, 0:16]), reads=[ang], writes=rq)
        S.op('act', lambda h: h.activation(out=Q(8), in_=ang[:, 0:16], func=AF.Sin), reads=[ang], writes=rq)
        S.op('dve', lambda h: h.tensor_scalar(out=ang[:, 0:16], in0=ang[:, 0:16], scalar1=math.pi / 2, scalar2=None, op0=ALU.add), reads=[ang], writes=[ang])
        self.range_reduce(ang[:, 0:16], 16)
        S.op('act', lambda h: h.activation(out=Q(9), in_=ang[:, 0:16], func=AF.Sin), reads=[ang], writes=rq)
        S.op('dve', lambda h: h.tensor_tensor(out=Q(9), in0=Q(9), in1=Q(0), op=ALU.mult), reads=rq, writes=rq)
        S.op('dve', lambda h: h.tensor_tensor(out=Q(8), in0=Q(8), in1=Q(0), op=ALU.mult), reads=rq, writes=rq)
        S.op('dve', lambda h: h.tensor_scalar(out=Q(9), in0=Q(9), scalar1=-1.0, scalar2=None, op0=ALU.add), reads=rq, writes=rq)
        S.op('dve', lambda h: h.tensor_tensor(out=Q(10), in0=Q(4), in1=Q(4), op=ALU.mult), reads=rq, writes=rq)
        S.op('dve', lambda h: h.tensor_tensor(out=Q(11), in0=Q(5), in1=Q(5), op=ALU.mult), reads=rq, writes=rq)
        S.op('dve', lambda h: h.tensor_tensor(out=Q(10), in0=Q(10), in1=Q(11), op=ALU.add), reads=rq, writes=rq)
        S.op('dve', lambda h: h.reciprocal(out=Q(10), in_=Q(10)), reads=rq, writes=rq)
        S.op('dve', lambda h: h.tensor_tensor(out=Q(2), in0=Q(9), in1=Q(4), op=ALU.mult), reads=rq, writes=rq)
        S.op('dve', lambda h: h.tensor_tensor(out=Q(11), in0=Q(8), in1=Q(5), op=ALU.mult), reads=rq, writes=rq)
        S.op('dve', lambda h: h.tensor_tensor(out=Q(2), in0=Q(2), in1=Q(11), op=ALU.add), reads=rq, writes=rq)
        S.op('dve', lambda h: h.tensor_tensor(out=Q(2), in0=Q(2), in1=Q(10), op=ALU.mult), reads=rq, writes=rq)
        S.op('dve', lambda h: h.tensor_tensor(out=Q(3), in0=Q(8), in1=Q(4), op=ALU.mult), reads=rq, writes=rq)
        S.op('dve', lambda h: h.tensor_tensor(out=Q(11), in0=Q(9), in1=Q(5), op=ALU.mult), reads=rq, writes=rq)
        S.op('dve', lambda h: h.tensor_tensor(out=Q(3), in0=Q(3), in1=Q(11), op=ALU.subtract), reads=rq, writes=rq)
        S.op('dve', lambda h: h.tensor_tensor(out=Q(3), in0=Q(3), in1=Q(10), op=ALU.mult), reads=rq, writes=rq)
        for st_ in range(16):
            for which, tab in ((0, self.s5_sin), (1, self.s5_cos)):
                S.op('dve', lambda h, st_=st_, which=which: h.tensor_scalar(out=ang[:, 0:LC], in0=self.cst[:, 512:512 + LC], scalar1=q[:, 1, st_:st_ + 1],
                                                                         scalar2=(math.pi / 2 if which else 0.0), op0=ALU.mult, op1=ALU.add), reads=[self.cst, q], writes=[ang])
                self.range_reduce(ang[:, 0:LC], LC)
                S.op('act', lambda h, st_=st_, tab=tab: h.activation(out=tab[:, st_, :], in_=ang[:, 0:LC], func=AF.Sin), reads=[ang], writes=[tab])
        for st_ in range(16):
            c_r = self.scratch()
            c_i = self.scratch()
            S.dma('sp', c_r[:, 0:128], I['s5c'][l, 0, :, st_, :], writes=[c_r])
            S.dma('sp', c_i[:, 0:128], I['s5c'][l, 1, :, st_, :], writes=[c_i])
            t1 = self.scratch()
            S.op('dve', lambda h, st_=st_: h.tensor_scalar(out=t1[:, 0:128], in0=c_i[:, 0:128], scalar1=q[:, 3, st_:st_ + 1], scalar2=None, op0=ALU.mult), reads=[c_i, q], writes=[t1])
            S.op('dve', lambda h, st_=st_: h.scalar_tensor_tensor(out=self.s5_cr[:, st_, :], in0=c_r[:, 0:128], scalar=q[:, 2, st_:st_ + 1], in1=t1[:, 0:128], op0=ALU.mult, op1=ALU.subtract),
                 reads=[c_r, q, t1], writes=[self.s5_cr])
            S.op('dve', lambda h, st_=st_: h.tensor_scalar(out=t1[:, 0:128], in0=c_i[:, 0:128], scalar1=q[:, 2, st_:st_ + 1], scalar2=-1.0, op0=ALU.mult, op1=ALU.mult), reads=[c_i, q], writes=[t1])
            S.op('dve', lambda h, st_=st_: h.tensor_scalar(out=c_r[:, 0:128], in0=c_r[:, 0:128], scalar1=q[:, 3, st_:st_ + 1], scalar2=None, op0=ALU.mult), reads=[c_r, q], writes=[c_r])
            S.op('dve', lambda h, st_=st_: h.tensor_tensor(out=self.s5_ci[:, st_, :], in0=t1[:, 0:128], in1=c_r[:, 0:128], op=ALU.subtract), reads=[t1, c_r], writes=[self.s5_ci])

    def s5_init(self, l, grp):
        S = self.S
        q = self.s5_q
        car = self.s5_carry
        if grp == 'p':
            S.op('dve', lambda h: h.memset(car[:, :, :], 0.0), writes=[car])
            return
        h0 = self.sm3
        S.dma('sp', h0[:, 0:2, :], self.I['s5h0'][l], writes=[h0])
        m = self.sm()
        n2 = self.sm()
        t = self.sm()
        S.op('dve', lambda h: h.tensor_tensor(out=m[:, 0:16], in0=q[:, 2, :], in1=q[:, 2, :], op=ALU.mult), reads=[q], writes=[m])
        S.op('dve', lambda h: h.tensor_tensor(out=n2[:, 0:16], in0=q[:, 3, :], in1=q[:, 3, :], op=ALU.mult), reads=[q], writes=[n2])
        S.op('dve', lambda h: h.tensor_tensor(out=m[:, 0:16], in0=m[:, 0:16], in1=n2[:, 0:16], op=ALU.add), reads=[m, n2], writes=[m])
        S.op('dve', lambda h: h.reciprocal(out=m[:, 0:16], in_=m[:, 0:16]), reads=[m], writes=[m])
        S.op('dve', lambda h: h.tensor_tensor(out=t[:, 0:16], in0=h0[:, 0, :], in1=q[:, 2, :], op=ALU.mult), reads=[h0, q], writes=[t])
        S.op('dve', lambda h: h.tensor_tensor(out=n2[:, 0:16], in0=h0[:, 1, :], in1=q[:, 3, :], op=ALU.mult), reads=[h0, q], writes=[n2])
        S.op('dve', lambda h: h.tensor_tensor(out=t[:, 0:16], in0=t[:, 0:16], in1=n2[:, 0:16], op=ALU.add), reads=[t, n2], writes=[t])
        S.op('dve', lambda h: h.tensor_tensor(out=car[:, 0, :], in0=t[:, 0:16], in1=m[:, 0:16], op=ALU.mult), reads=[t, m], writes=[car])
        S.op('dve', lambda h: h.tensor_tensor(out=t[:, 0:16], in0=h0[:, 1, :], in1=q[:, 2, :], op=ALU.mult), reads=[h0, q], writes=[t])
        S.op('dve', lambda h: h.tensor_tensor(out=n2[:, 0:16], in0=h0[:, 0, :], in1=q[:, 3, :], op=ALU.mult), reads=[h0, q], writes=[n2])
        S.op('dve', lambda h: h.tensor_tensor(out=t[:, 0:16], in0=t[:, 0:16], in1=n2[:, 0:16], op=ALU.subtract), reads=[t, n2], writes=[t])
        S.op('dve', lambda h: h.tensor_tensor(out=car[:, 1, :], in0=t[:, 0:16], in1=m[:, 0:16], op=ALU.mult), reads=[t, m], writes=[car])

    def s5_final(self, l, grp):
        S = self.S
        q = self.s5_q
        car = self.s5_carry
        o = self.sm3
        t = self.sm()
        S.op('dve', lambda h: h.tensor_tensor(out=o[:, 0, :], in0=car[:, 0, :], in1=q[:, 2, :], op=ALU.mult), reads=[car, q], writes=[o])
        S.op('dve', lambda h: h.tensor_tensor(out=t[:, 0:16], in0=car[:, 1, :], in1=q[:, 3, :], op=ALU.mult), reads=[car, q], writes=[t])
        S.op('dve', lambda h: h.tensor_tensor(out=o[:, 0, :], in0=o[:, 0, :], in1=t[:, 0:16], op=ALU.subtract), reads=[o, t], writes=[o])
        S.op('dve', lambda h: h.tensor_tensor(out=o[:, 1, :], in0=car[:, 0, :], in1=q[:, 3, :], op=ALU.mult), reads=[car, q], writes=[o])
        S.op('dve', lambda h: h.tensor_tensor(out=t[:, 0:16], in0=car[:, 1, :], in1=q[:, 2, :], op=ALU.mult), reads=[car, q], writes=[t])
        S.op('dve', lambda h: h.tensor_tensor(out=o[:, 1, :], in0=o[:, 1, :], in1=t[:, 0:16], op=ALU.add), reads=[o, t], writes=[o])
        S.dma('pool', self.O['s5' + grp][l], o[:, 0:2, :], reads=[o], writes=[self.dk('os5' + grp)])

    def s5_part(self, l, grp, g0, G, last):
        S = self.S
        I = self.I
        LC = min(self.LC, G)
        ug = self.av('ug', 4 * G)
        ug3 = ug.h[:, 0:4 * G].rearrange("p (a c) -> p a c", c=G)
        q = self.s5_q
        car = self.s5_carry
        for c in range(4):
            pb = self.proj_fm(l, 'au%d' % c, 128, G)
            S.op('act' if c % 2 else 'dve', (lambda h, c=c: h.activation(out=ug3[:, c, 0:G], in_=pb[:, 0:G], func=AF.Copy)) if c % 2 else (lambda h, c=c: h.tensor_copy(out=ug3[:, c, 0:G], in_=pb[:, 0:G])),
                 reads=[pb], writes=[ug])
        yT = self.wsp(True)
        y3 = yT[:, 0:4 * G].rearrange("p (a c) -> p a c", c=G)
        for s0 in range(0, G, LC):
            for c in range(4):
                py = self.pacc[c % 2]
                for s4 in range(4):
                    st_ = 4 * c + s4
                    brow = slice(32 * s4, 32 * s4 + 32) if s4 < 3 else slice(64, 128)
                    bvar = 0 if s4 < 3 else 2
                    pr = self.pbank()
                    pi = self.pbank()
                    S.op('pe', lambda h: h.matmul(pr[:, 0:LC], lhsT=self.s5_b[brow, bvar + 0, c, :], rhs=ug3[brow, c, s0:s0 + LC], start=True, stop=True),
                         reads=[self.s5_b, ug], writes=[pr])
                    S.op('pe', lambda h: h.matmul(pi[:, 0:LC], lhsT=self.s5_b[brow, bvar + 1, c, :], rhs=ug3[brow, c, s0:s0 + LC], start=True, stop=True),
                         reads=[self.s5_b, ug], writes=[pi])
                    cs = self.s5_cos[:, st_, 0:LC]
                    sn = self.s5_sin[:, st_, 0:LC]
                    a = self.scratch(); b = self.scratch(); xr = self.scratch(); xi = self.scratch()
                    rt = [self.s5_cos, self.s5_sin]
                    S.op('dve', lambda h: h.tensor_tensor(out=a[:, 0:LC], in0=pr[:, 0:LC], in1=cs, op=ALU.mult), reads=[pr] + rt, writes=[a])
                    S.op('dve', lambda h: h.tensor_tensor(out=b[:, 0:LC], in0=pi[:, 0:LC], in1=sn, op=ALU.mult), reads=[pi] + rt, writes=[b])
                    S.op('pool', lambda h: h.tensor_tensor(out=xr[:, 0:LC], in0=a[:, 0:LC], in1=b[:, 0:LC], op=ALU.add), reads=[a, b], writes=[xr])
                    a2 = self.scratch(); b2 = self.scratch()
                    S.op('dve', lambda h: h.tensor_tensor(out=a2[:, 0:LC], in0=pi[:, 0:LC], in1=cs, op=ALU.mult), reads=[pi] + rt, writes=[a2])
                    S.op('dve', lambda h: h.tensor_tensor(out=b2[:, 0:LC], in0=pr[:, 0:LC], in1=sn, op=ALU.mult), reads=[pr] + rt, writes=[b2])
                    S.op('pool', lambda h: h.tensor_tensor(out=xi[:, 0:LC], in0=a2[:, 0:LC], in1=b2[:, 0:LC], op=ALU.subtract), reads=[a2, b2], writes=[xi])
                    rho = q[:, 0, st_:st_ + 1].to_broadcast([128, LC])
                    S.op('dve', lambda h: h.tensor_tensor_scan(out=xr[:, 0:LC], data0=rho, data1=xr[:, 0:LC], initial=car[:, 0, st_:st_ + 1], op0=ALU.mult, op1=ALU.add),
                         reads=[q, xr, car], writes=[xr])
                    S.op('dve', lambda h: h.tensor_tensor_scan(out=xi[:, 0:LC], data0=rho, data1=xi[:, 0:LC], initial=car[:, 1, st_:st_ + 1], op0=ALU.mult, op1=ALU.add),
                         reads=[q, xi, car], writes=[xi])
                    S.op('pool', lambda h: h.tensor_tensor(out=a[:, 0:LC], in0=xr[:, 0:LC], in1=cs, op=ALU.mult), reads=[xr] + rt, writes=[a])
                    S.op('pool', lambda h: h.tensor_tensor(out=b[:, 0:LC], in0=xi[:, 0:LC], in1=sn, op=ALU.mult), reads=[xi] + rt, writes=[b])
                    S.op('dve', lambda h: h.tensor_tensor(out=a[:, 0:LC], in0=a[:, 0:LC], in1=b[:, 0:LC], op=ALU.subtract), reads=[a, b], writes=[a])
                    S.op('pool', lambda h: h.tensor_tensor(out=a2[:, 0:LC], in0=xr[:, 0:LC], in1=sn, op=ALU.mult), reads=[xr] + rt, writes=[a2])
                    S.op('pool', lambda h: h.tensor_tensor(out=b2[:, 0:LC], in0=xi[:, 0:LC], in1=cs, op=ALU.mult), reads=[xi] + rt, writes=[b2])
                    S.op('dve', lambda h: h.tensor_tensor(out=a2[:, 0:LC], in0=a2[:, 0:LC], in1=b2[:, 0:LC], op=ALU.add), reads=[a2, b2], writes=[a2])
                    S.op('dve', lambda h: h.tensor_copy(out=car[:, 0, st_:st_ + 1], in_=a[:, LC - 1:LC]), reads=[a], writes=[car])
                    S.op('dve', lambda h: h.tensor_copy(out=car[:, 1, st_:st_ + 1], in_=a2[:, LC - 1:LC]), reads=[a2], writes=[car])
                    S.op('pe', lambda h: h.matmul(py[:, 0:LC], lhsT=self.s5_cr[:, st_, :], rhs=a[:, 0:LC], start=(s4 == 0), stop=False), reads=[self.s5_cr, a], writes=[py])
                    S.op('pe', lambda h: h.matmul(py[:, 0:LC], lhsT=self.s5_ci[:, st_, :], rhs=a2[:, 0:LC], start=False, stop=(s4 == 3)), reads=[self.s5_ci, a2], writes=[py])
                S.op('dve', lambda h: h.scalar_tensor_tensor(out=y3[:, c, s0:s0 + LC], in0=ug3[:, c, s0:s0 + LC], scalar=self.s5_d[:, c:c + 1], in1=py[:, 0:LC], op0=ALU.mult, op1=ALU.add),
                     reads=[ug, self.s5_d, py], writes=[yT])
        if last:
            self.s5_final(l, grp)
        S.op('act', lambda h: h.activation(out=ug3[:, :, 0:G], in_=y3, func=AF.Gelu), reads=[yT], writes=[ug])
        for c in range(4):
            sgs = []
            for oc in (c + 4, c):
                S.dma('sp', self.wglu[:, :, :], I['wglu'][l, :, oc * 128:(oc + 1) * 128].rearrange("(a p) f -> p a f", p=128), writes=[self.wglu])
                pg = self.pbank()
                for kc in range(4):
                    S.op('pe', lambda h, kc=kc: h.matmul(pg[:, 0:G], lhsT=self.wglu[:, kc, :], rhs=ug3[:, kc, 0:G], start=(kc == 0), stop=(kc == 3)), reads=[self.wglu, ug], writes=[pg])
                sgs.append(pg)
            sg = self.scratch()
            S.op('act', lambda h: h.activation(out=sg[:, 0:G], in_=sgs[0][:, 0:G], func=AF.Sigmoid), reads=[sgs[0]], writes=[sg])
            va = self.scratch()
            S.op('dve', lambda h: h.tensor_tensor(out=va[:, 0:G], in0=sgs[1][:, 0:G], in1=sg[:, 0:G], op=ALU.mult), reads=[sgs[1], sg], writes=[va])
            pgt = self.proj_fm(l, 'ag%d' % c, 128, G)
            sg2 = self.scratch()
            S.op('act', lambda h: h.activation(out=sg2[:, 0:G], in_=pgt[:, 0:G], func=AF.Silu), reads=[pgt], writes=[sg2])
            S.op('dve', lambda h, c=c: h.tensor_tensor(out=self.brT[:, c, 0:G], in0=va[:, 0:G], in1=sg2[:, 0:G], op=ALU.mult), reads=[va, sg2], writes=[self.brT])


    def dsa_setup(self):
        S = self.S
        I = self.I
        cst = self.cst
        rb = self.sm()
        oh = self.av('oh', 384)
        S.dma('sp', rb[0:32, 0:8], I['relb'][:, :], writes=[rb])
        S.dma('sp', oh[0:32, 0:384], I['oh'][:, :], writes=[oh])
        pf = self.pbank()
        S.op('pe', lambda h: h.matmul(pf[0:8, 0:384], lhsT=rb[0:32, 0:8], rhs=oh[0:32, 0:384], start=True, stop=True), reads=[rb, oh], writes=[pf])
        f8 = self.av('f8', 384)
        cm = self.sm()
        S.op('dve', lambda h: h.tensor_copy(out=cm[0:8, 0:1], in_=pf[0:8, 382:383]), reads=[pf], writes=[cm])
        S.op('dve', lambda h: h.tensor_scalar(out=f8[0:8, 0:384], in0=pf[0:8, 0:384], scalar1=cm[0:8, 0:1], scalar2=8.0, op0=ALU.subtract, op1=ALU.mult), reads=[pf, cm], writes=[f8])
        S.op('dve', lambda h: h.tensor_reduce(out=cm[0:8, 1:2], in_=f8[0:8, 0:383], axis=AX.X, op=ALU.max, negate=True), reads=[f8], writes=[cm])
        fdk = self.dk('fd')
        S.dma('sp', self.fd[:, :], f8[0:8, 0:384], reads=[f8], writes=[fdk])
        S.dma('sp', self.cfd[:, :], cm[0:8, 1:2], reads=[cm], writes=[fdk])
        S.op('dve', lambda h: h.memset(self.cf[:, :], 0.0), writes=[self.cf])
        S.dma('sp', self.cf[0:1, 0:4], self.cfd[0:4, :].rearrange("a b -> b a"), reads=[fdk], writes=[self.cf])
        S.dma('sp', self.cf[32:33, 0:4], self.cfd[4:8, :].rearrange("a b -> b a"), reads=[fdk], writes=[self.cf])
        for lrow in range(128):
            for g in range(2):
                src = bass.AP(tensor=self.fd.tensor, offset=4 * g * 384 + 127 - lrow, ap=[[0, 1], [128, 2], [384, 4], [1, 128]])
                dst = self.bias8[lrow:lrow + 1, 2 * g:2 * g + 2, :].rearrange("p k (a c) -> p k a c", c=128)
                S.dma('sp' if g else 'pool', dst, src, reads=[fdk], writes=[self.bias8])
        S.op('dve', lambda h: h.memset(self.onesb[:, :], 1.0), writes=[self.onesb])

    def add_keys(self, k_tm, v_tm, c_tm, n, key0):
        S = self.S
        cst = self.cst
        blk = key0 // 128
        pk = self.pbank()
        S.op('pe', lambda h: h.transpose(pk[:, 0:n], k_tm[0:n, 0:128], self.ident[0:n, 0:n]), reads=[k_tm, cst], writes=[pk])
        S.op('act', lambda h: h.activation(out=self.kT[:, key0:key0 + n], in_=pk[:, 0:n], func=AF.Copy), reads=[pk], writes=[self.kT])
        sq = self.scratch()
        S.op('act', lambda h: h.activation(out=sq[:, 0:n], in_=pk[:, 0:n], func=AF.Square), reads=[pk], writes=[sq])
        pn = self.pbank()
        S.op('pe', lambda h: h.matmul(pn[0:33, 0:n], lhsT=cst[:, 896:929], rhs=sq[:, 0:n], start=True, stop=True), reads=[cst, sq], writes=[pn])
        S.op('dve', lambda h: h.tensor_reduce(out=self.kmax2[:, 1:2], in_=pn[0:33, 0:n], axis=AX.X, op=ALU.max), reads=[pn], writes=[self.kmax2])
        S.op('dve', lambda h: h.tensor_tensor(out=self.kmax2[:, 0:1], in0=self.kmax2[:, 0:1], in1=self.kmax2[:, 1:2], op=ALU.max), reads=[self.kmax2], writes=[self.kmax2])
        va = self.vA[0:n, blk, :].rearrange("p (g c) -> p g c", c=65)[:, :, 0:64]
        S.op('dve', lambda h: h.tensor_copy(out=va, in_=v_tm[0:n, 0:128].rearrange("p (g c) -> p g c", c=64)), reads=[v_tm], writes=[self.vA])
        pc = self.pbank()
        S.op('pe', lambda h: h.transpose(pc[:, 0:n], c_tm[0:n, 0:128], self.ident[0:n, 0:n]), reads=[c_tm, cst], writes=[pc])
        if key0 < self.NH:
            S.op('dve', lambda h: h.tensor_copy(out=self.kiT[0:64, key0:key0 + n], in_=pc[0:64, 0:n]), reads=[pc], writes=[self.kiT])
        else:
            S.op('dve', lambda h: h.tensor_copy(out=self.kiT[64:128, key0 - self.NH:key0 - self.NH + n], in_=pc[64:128, 0:n]), reads=[pc], writes=[self.kiT])

    def dsa_init(self, l, grp):
        S = self.S
        I = self.I
        S.op('dve', lambda h: h.memset(self.vA[:, :, :], 1.0), writes=[self.vA])
        S.op('dve', lambda h: h.memset(self.kmax2[:, :], 0.0), writes=[self.kmax2])
        if grp == 's':
            self.phase('kv')
            for b in range(PAST // 128):
                k_tm = self.av('k_tm', 128); v_tm = self.av('v_tm', 128); c_tm = self.av('c_tm', 128)
                S.dma('sp', k_tm[:, 0:128], I['ck'][l, b * 128:(b + 1) * 128, :], writes=[k_tm])
                S.dma('sp', v_tm[:, 0:128], I['cv'][l, b * 128:(b + 1) * 128, :], writes=[v_tm])
                S.dma('sp', c_tm[:, 0:128], I['cki'][l, b * 128:(b + 1) * 128, :], writes=[c_tm])
                self.add_keys(k_tm, v_tm, c_tm, 128, b * 128)

    def dsa_part(self, l, grp, g0, G, TP):
        S = self.S
        cst = self.cst
        kbase = 0 if grp == 'p' else PAST
        Qa = kbase + g0
        L = Qa + TP
        NKtot = (self.T if grp == 'p' else PAST + DEC_SEQ)
        KTOP = float(min(TOPK_MAX, NKtot // 4))
        W = 4 * TP
        sc = self.av('sc', 8192)
        junk = self.av('junk', 2048)
        junk8 = junk.h.bitcast(U8)
        offf_v = junk.h[0:33, 0:512]
        osb_v = junk.h[0:65, 512:1024]
        onn_v = junk.h[0:64, 1024:1536]
        qsq_v = junk.h[:, 1536:1536 + 4 * G].rearrange("p (a c) -> p a c", c=G)
        bs = self.bs
        for hh in range(4):
            pb = self.proj_fm(l, 'qq%d' % hh, 128, TP)
            S.op('act', lambda h, hh=hh: h.activation(out=self.qT[:, hh, 0:TP], in_=pb[:, 0:TP], func=AF.Copy), reads=[pb], writes=[self.qT])
            S.op('act', lambda h, hh=hh: h.activation(out=qsq_v[:, hh, 0:TP], in_=pb[:, 0:TP], func=AF.Square), reads=[pb], writes=[junk])
        for hh in range(4):
            pb = self.proj_fm(l, 'qi%d' % hh, 128, TP)
            S.op('dve', lambda h, hh=hh: h.tensor_copy(out=self.qiT[:, hh, 0:TP], in_=pb[:, 0:TP]), reads=[pb], writes=[self.qiT])
        pq = self.pbank()
        S.op('pe', lambda h: h.matmul(pq[0:33, 0:W], lhsT=cst[:, 896:929], rhs=qsq_v[:, :, 0:TP], start=True, stop=True), reads=[cst, junk], writes=[pq])
        S.op('act', lambda h: h.activation(out=offf_v[:, 0:W], in_=pq[0:33, 0:W], func=AF.Sqrt, scale=self.kmax2[:, 0:1]), reads=[pq, self.kmax2], writes=[junk])
        S.op('dve', lambda h: h.tensor_tensor(out=self.offrow[:, 0:W].rearrange("p (a c) -> p a c", c=TP), in0=self.cf[:, :].unsqueeze(2).to_broadcast([33, 4, TP]),
                                               in1=offf_v[:, 0:W].rearrange("p (a c) -> p a c", c=TP), op=ALU.subtract), reads=[self.cf, junk], writes=[self.offrow])
        for b0 in range(0, L, 512):
            bw = min(512, L - b0)
            if b0 < self.NH:
                rows = slice(0, 64); c0 = b0
            else:
                rows = slice(64, 128); c0 = b0 - self.NH
            for hh in range(4):
                ps = self.pbank()
                S.op('pe', lambda h, hh=hh: h.matmul(ps[0:TP, 0:bw], lhsT=self.qiT[rows, hh, 0:TP], rhs=self.kiT[rows, c0:c0 + bw], start=True, stop=True), reads=[self.qiT, self.kiT], writes=[ps])
                if hh == 0:
                    S.op('dve', lambda h: h.tensor_scalar(out=sc[0:TP, b0:b0 + bw], in0=ps[0:TP, 0:bw], scalar1=0.0, scalar2=self.wiT[0:TP, 0:1], op0=ALU.max, op1=ALU.mult), reads=[ps, self.wiT], writes=[sc])
                else:
                    tmp = junk.h[:, 512 * (hh % 2):512 * (hh % 2) + 512]
                    S.op('act', lambda h: h.activation(out=tmp[0:TP, 0:bw], in_=ps[0:TP, 0:bw], func=AF.Relu), reads=[ps], writes=[junk])
                    S.op('dve', lambda h, hh=hh: h.scalar_tensor_tensor(out=sc[0:TP, b0:b0 + bw], in0=tmp[0:TP, 0:bw], scalar=self.wiT[0:TP, hh:hh + 1], in1=sc[0:TP, b0:b0 + bw], op0=ALU.mult, op1=ALU.add),
                         reads=[junk, self.wiT, sc], writes=[sc])
        S.op('dve', lambda h: h.tensor_reduce(out=bs[0:TP, 0:1], in_=sc[0:TP, 0:L], axis=AX.X, op=ALU.max, apply_absolute_value=True), reads=[sc], writes=[bs])
        if grp == 'p' and TP == 128:
            S.op('dve', lambda h: h.memset(sc[0:64, L - 64:L], -1e30), writes=[sc])
        S.op('dve', lambda h: h.tensor_scalar(out=bs[0:TP, 2:3], in0=bs[0:TP, 0:1], scalar1=1.0, scalar2=None, op0=ALU.add), reads=[bs], writes=[bs])
        S.op('dve', lambda h: h.tensor_scalar(out=bs[0:TP, 1:2], in0=bs[0:TP, 2:3], scalar1=-1.0, scalar2=None, op0=ALU.mult), reads=[bs], writes=[bs])
        for it in range(30):
            S.op('dve', lambda h: h.tensor_scalar(out=bs[0:TP, 3:4], in0=bs[0:TP, 1:2], scalar1=bs[0:TP, 2:3], scalar2=0.5, op0=ALU.add, op1=ALU.mult), reads=[bs], writes=[bs])
            S.op('dve', lambda h: h.tensor_scalar(out=junk8[0:TP, 0:L], in0=sc[0:TP, 0:L], scalar1=bs[0:TP, 3:4], scalar2=None, op0=ALU.is_ge, op1=ALU.add, accum_out=bs[0:TP, 4:5]),
                 reads=[sc, bs], writes=[junk, bs])
            S.op('dve', lambda h: h.tensor_scalar(out=bs[0:TP, 5:6], in0=bs[0:TP, 4:5], scalar1=KTOP, scalar2=None, op0=ALU.is_ge), reads=[bs], writes=[bs])
            S.op('dve', lambda h: h.tensor_tensor(out=bs[0:TP, 6:7], in0=bs[0:TP, 3:4], in1=bs[0:TP, 1:2], op=ALU.subtract), reads=[bs], writes=[bs])
            S.op('dve', lambda h: h.tensor_tensor(out=bs[0:TP, 7:8], in0=bs[0:TP, 2:3], in1=bs[0:TP, 3:4], op=ALU.subtract), reads=[bs], writes=[bs])
            S.op('dve', lambda h: h.scalar_tensor_tensor(out=bs[0:TP, 1:2], in0=bs[0:TP, 6:7], scalar=bs[0:TP, 5:6], in1=bs[0:TP, 1:2], op0=ALU.mult, op1=ALU.add), reads=[bs], writes=[bs])
            S.op('dve', lambda h: h.scalar_tensor_tensor(out=bs[0:TP, 2:3], in0=bs[0:TP, 7:8], scalar=bs[0:TP, 5:6], in1=bs[0:TP, 3:4], op0=ALU.mult, op1=ALU.add), reads=[bs], writes=[bs])
        S.op('dve', lambda h: h.tensor_scalar(out=sc[0:TP, 0:L], in0=sc[0:TP, 0:L], scalar1=bs[0:TP, 1:2], scalar2=None, op0=ALU.is_ge), reads=[sc, bs], writes=[sc])
        nblk = (L + 127) // 128
        OT = self.pacc
        for kb in range(nblk):
            bw = min(128, L - kb * 128)
            k0 = kb * 128
            pst = self.pbank()
            S.op('pe', lambda h: h.transpose(pst[0:bw, 0:TP], sc[0:TP, k0:k0 + bw], self.ident[0:TP, 0:TP]), reads=[sc, cst], writes=[pst])
            sT = self.selT[kb % 2]
            S.op('act', lambda h: h.activation(out=sT[0:bw, 0:TP], in_=pst[0:bw, 0:TP], func=AF.Copy), reads=[pst], writes=[sT])
            kind = 0 if k0 == Qa else (1 if k0 == Qa - 128 else None)
            for g in range(2):
                ps = self.pbank()
                S.op('pe', lambda h, g=g: h.matmul(ps[0:bw, 0:W], lhsT=self.kT[64 * g:64 * g + 64, k0:k0 + bw], rhs=self.qT[64 * g:64 * g + 64, :, 0:TP], start=True, stop=False),
                     reads=[self.kT, self.qT], writes=[ps])
                S.op('pe', lambda h, g=g: h.matmul(ps[0:bw, 0:W], lhsT=self.onesb[32 * g:32 * g + 1, 0:bw], rhs=self.offrow[32 * g:32 * g + 1, 0:W], start=False, stop=True),
                     reads=[self.onesb, self.offrow], writes=[ps])
                Et = self.Et[g]
                Pt = self.Pt[g]
                if kind is not None:
                    tmp = junk.h[:, 1536:2048]
                    bt = self.bias8[0:bw, 2 * g + kind, :].rearrange("p (a c) -> p a c", c=128)[:, :, 0:TP]
                    S.op('dve', lambda h: h.tensor_tensor(out=tmp[0:bw, 0:W].rearrange("p (a c) -> p a c", c=TP), in0=ps[0:bw, 0:W].rearrange("p (a c) -> p a c", c=TP), in1=bt, op=ALU.add),
                         reads=[ps, self.bias8], writes=[junk])
                    S.op('act', lambda h: h.activation(out=Et[0:bw, 0:W], in_=tmp[0:bw, 0:W], func=AF.Exp, scale=0.125), reads=[junk], writes=[Et])
                else:
                    S.op('act', lambda h: h.activation(out=Et[0:bw, 0:W], in_=ps[0:bw, 0:W], func=AF.Exp, scale=0.125), reads=[ps], writes=[Et])
                S.op('dve', lambda h: h.tensor_tensor(out=Pt[0:bw, 0:W].rearrange("p (a c) -> p a c", c=TP), in0=Et[0:bw, 0:W].rearrange("p (a c) -> p a c", c=TP),
                                                       in1=sT[0:bw, 0:TP].unsqueeze(1).to_broadcast([bw, 4, TP]), op=ALU.mult), reads=[Et, sT], writes=[Pt])
                S.op('pe', lambda h, g=g: h.matmul(OT[g][0:65, 0:W], lhsT=self.vA[0:bw, kb, 65 * g:65 * g + 65], rhs=Pt[0:bw, 0:W], start=(kb == 0), stop=(kb == nblk - 1)),
                     reads=[self.vA, Pt], writes=[OT[g]])
        for g in range(2):
            osb = osb_v
            S.op('act', lambda h: h.activation(out=osb[0:65, 0:W], in_=OT[g][0:65, 0:W], func=AF.Copy), reads=[OT[g]], writes=[junk])
            S.op('dve', lambda h: h.reciprocal(out=osb[64:65, 0:W], in_=osb[64:65, 0:W]), reads=[junk], writes=[junk])
            pbc = self.pbank()
            S.op('pe', lambda h: h.matmul(pbc[0:64, 0:W], lhsT=cst[64:65, 256:320], rhs=osb[64:65, 0:W], start=True, stop=True), reads=[cst, junk], writes=[pbc])
            S.op('dve', lambda h: h.tensor_tensor(out=onn_v[:, 0:W], in0=osb[0:64, 0:W], in1=pbc[0:64, 0:W], op=ALU.mult), reads=[junk, pbc], writes=[junk])
            for hh in range(4):
                hd = 4 * g + hh
                pg = self.proj_fm(l, 'bg%d' % hd, 64, TP)
                sg = junk.h[:, 1536 + 128 * (hh % 2):1536 + 128 * (hh % 2) + 128]
                S.op('act', lambda h: h.activation(out=sg[0:64, 0:TP], in_=pg[0:64, 0:TP], func=AF.Silu), reads=[pg], writes=[junk])
                S.op('dve', lambda h, hh=hh, hd=hd: h.tensor_tensor(out=self.brB[:, hd, 0:TP], in0=onn_v[:, hh * TP:(hh + 1) * TP], in1=sg[0:64, 0:TP], op=ALU.mult), reads=[junk], writes=[self.brB])

    def gdn_setup(self, l):
        S = self.S
        I = self.I
        S.dma('sp', self.gdn_cw[:, :, :], I['gcw'][l], writes=[self.gdn_cw])
        S.dma('sp', self.gdn_par[:, :], I['gpar'][l], writes=[self.gdn_par])
        S.dma('sp', self.gdn_n[:, :], I['gnorm'][l], writes=[self.gdn_n])
        S.op('act', lambda h: h.activation(out=self.gdn_par[:, 0:4], in_=self.gdn_par[:, 0:4], func=AF.Exp), reads=[self.gdn_par], writes=[self.gdn_par])
        S.op('dve', lambda h: h.tensor_scalar(out=self.gdn_par[:, 0:4], in0=self.gdn_par[:, 0:4], scalar1=-1.0, scalar2=None, op0=ALU.mult), reads=[self.gdn_par], writes=[self.gdn_par])

    def gdn_init(self, l, grp):
        S = self.S
        if grp == 'p':
            S.op('dve', lambda h: h.memset(self.gdn_S[:, :, :], 0.0), writes=[self.gdn_S])
            S.op('dve', lambda h: h.memset(self.gdn_tail[:, :, :], 0.0), writes=[self.gdn_tail])
        else:
            S.dma('sp', self.gdn_S[:, :, :], self.I['sgdn'][l].rearrange("h k v -> k h v"), writes=[self.gdn_S])
            S.dma('sp', self.gdn_tail[:, :, :], self.I['sconv'][l], writes=[self.gdn_tail])

    def gdn_part(self, l, grp, g0, G, C, last):
        S = self.S
        cst = self.cst
        nC = G // C
        GE = G + 3
        xext = self.av('xext', 12 * GE)
        x3 = xext[:, 0:12 * GE].rearrange("p (a c) -> p a c", c=GE)
        qkv = self.av('qkv', 12 * G)
        q3 = qkv[:, 0:12 * G].rearrange("p (a c) -> p a c", c=G)
        oT = self.av('oT', 4 * G)
        o3 = oT[:, 0:4 * G].rearrange("p (a c) -> p a c", c=G)
        T1 = self.av('T1', 1024)
        T2 = self.av('T2', 1024)
        S.op('dve', lambda h: h.tensor_copy(out=x3[:, :, 0:3], in_=self.gdn_tail[:, :, :]), reads=[self.gdn_tail], writes=[xext])
        for c in range(12):
            pb = self.proj_fm(l, 'cq%d' % c, 128, G)
            S.op('act' if c % 2 else 'dve', (lambda h, c=c: h.activation(out=x3[:, c, 3:3 + G], in_=pb[:, 0:G], func=AF.Copy)) if c % 2 else (lambda h, c=c: h.tensor_copy(out=x3[:, c, 3:3 + G], in_=pb[:, 0:G])),
                 reads=[pb], writes=[xext])
        S.op('dve', lambda h: h.tensor_copy(out=self.gdn_tail[:, :, :], in_=x3[:, :, G:G + 3]), reads=[xext], writes=[self.gdn_tail])
        for c in range(12):
            S.op('dve', lambda h, c=c: h.tensor_scalar(out=q3[:, c, :], in0=x3[:, c, 0:G], scalar1=self.gdn_cw[:, c, 0:1], scalar2=None, op0=ALU.mult), reads=[xext, self.gdn_cw], writes=[qkv])
            for i in range(1, 4):
                S.op('dve', lambda h, c=c, i=i: h.scalar_tensor_tensor(out=q3[:, c, :], in0=x3[:, c, i:i + G], scalar=self.gdn_cw[:, c, i:i + 1], in1=q3[:, c, :], op0=ALU.mult, op1=ALU.add),
                     reads=[xext, self.gdn_cw, qkv], writes=[qkv])
        S.op('act', lambda h: h.activation(out=qkv[:, 0:12 * G], in_=qkv[:, 0:12 * G], func=AF.Silu), reads=[qkv], writes=[qkv])
        W8 = 8 * G
        S.op('act', lambda h: h.activation(out=T1[:, 0:W8], in_=qkv[:, 0:W8], func=AF.Square), reads=[qkv], writes=[T1])
        for b0 in range(0, W8, 512):
            bw = min(512, W8 - b0)
            pm = self.pbank()
            S.op('pe', lambda h: h.matmul(pm[:, 0:bw], lhsT=cst[:, 256:384], rhs=T1[:, b0:b0 + bw], start=True, stop=True), reads=[cst, T1], writes=[pm])
            S.op('act', lambda h: h.activation(out=T2[:, b0:b0 + bw], in_=pm[:, 0:bw], func=AF.Sqrt, bias=self.epsb[:, 1:2]), reads=[pm, self.epsb], writes=[T2])
        S.op('dve', lambda h: h.reciprocal(out=T2[:, 0:W8], in_=T2[:, 0:W8]), reads=[T2], writes=[T2])
        S.op('dve', lambda h: h.scalar_tensor_tensor(out=qkv[:, 0:4 * G], in0=qkv[:, 0:4 * G], scalar=float(128 ** -0.5), in1=T2[:, 0:4 * G], op0=ALU.mult, op1=ALU.mult), reads=[qkv, T2], writes=[qkv])
        S.op('dve', lambda h: h.tensor_tensor(out=qkv[:, 4 * G:8 * G], in0=qkv[:, 4 * G:8 * G], in1=T2[:, 4 * G:8 * G], op=ALU.mult), reads=[qkv, T2], writes=[qkv])
        wC = self.load_w(l, 'kvD')
        CC = 4 * C
        mU = cst[0:C, 128:128 + C].unsqueeze(1).to_broadcast([C, 4, C])
        mLs = cst[0:C, 768:768 + C].unsqueeze(1).to_broadcast([C, 4, C])
        K = {64: 5, 32: 4, 16: 3}[C]
        V = lambda key: self.av(key, 256)
        for ci in range(nC):
            c0 = ci * C
            pb = self.pbank()
            for k in range(8):
                S.op('pe', lambda h, k=k: h.matmul(pb[0:C, 0:12], lhsT=self.xT[:, k, c0:c0 + C], rhs=wC[:, k, 0:12], start=(k == 0), stop=(k == 7)), reads=[self.xT, wC], writes=[pb])
            bg = self.av('bg', 32)
            S.op('act', lambda h: h.activation(out=bg[0:C, 0:4], in_=pb[0:C, 4:8], func=AF.Sigmoid), reads=[pb], writes=[bg])
            S.op('dve', lambda h: h.tensor_tensor(out=bg[0:C, 4:8], in0=pb[0:C, 8:12], in1=self.gdn_par[0:C, 4:8], op=ALU.add), reads=[pb, self.gdn_par], writes=[bg])
            S.op('act', lambda h: h.activation(out=bg[0:C, 4:8], in_=bg[0:C, 4:8], func=AF.Exp), reads=[bg], writes=[bg])
            S.op('act', lambda h: h.activation(out=bg[0:C, 4:8], in_=bg[0:C, 4:8], func=AF.Ln, bias=1.0), reads=[bg], writes=[bg])
            S.op('dve', lambda h: h.tensor_tensor(out=bg[0:C, 8:12], in0=bg[0:C, 4:8], in1=self.gdn_par[0:C, 0:4], op=ALU.mult), reads=[bg, self.gdn_par], writes=[bg])
            pcol = self.pbank()
            S.op('pe', lambda h: h.matmul(pcol[0:C, 0:4], lhsT=cst[0:C, 128:128 + C], rhs=bg[0:C, 8:12], start=True, stop=True), reads=[cst, bg], writes=[pcol])
            R = V('R')
            R3 = R[0:C, 0:CC].rearrange("p (a c) -> p a c", c=C)
            for h4 in range(4):
                S.op('dve', lambda h, h4=h4: h.tensor_scalar(out=R3[:, h4, :], in0=cst[0:C, 128:128 + C], scalar1=bg[0:C, 8 + h4:9 + h4], scalar2=None, op0=ALU.mult), reads=[cst, bg], writes=[R])
            prow = self.pbank()
            S.op('pe', lambda h: h.matmul(prow[:, 0:CC], lhsT=cst[0:C, 256:384], rhs=R[0:C, 0:CC], start=True, stop=True), reads=[cst, R], writes=[prow])
            egrow = V('egrow')
            S.op('act', lambda h: h.activation(out=egrow[:, 0:CC], in_=prow[:, 0:CC], func=AF.Exp), reads=[prow], writes=[egrow])
            eg3 = egrow[:, 0:CC].rearrange("p (a c) -> p a c", c=C)
            S.op('dve', lambda h: h.tensor_copy(out=bg[0:C, 12:16], in_=pcol[0:C, 0:4]), reads=[pcol], writes=[bg])
            S.op('act', lambda h: h.activation(out=bg[0:C, 16:20], in_=pcol[0:C, 0:4], func=AF.Exp), reads=[pcol], writes=[bg])
            S.op('dve', lambda h: h.tensor_tensor(out=bg[0:C, 20:24], in0=bg[0:C, 16:20], in1=bg[0:C, 0:4], op=ALU.mult), reads=[bg], writes=[bg])
            Dm = V('Dm')
            D3 = Dm[0:C, 0:CC].rearrange("p (a c) -> p a c", c=C)
            p3 = prow[0:C, 0:CC].rearrange("p (a c) -> p a c", c=C)
            for h4 in range(4):
                S.op('dve', lambda h, h4=h4: h.tensor_scalar(out=D3[:, h4, :], in0=p3[:, h4, :], scalar1=bg[0:C, 12 + h4:13 + h4], scalar2=-1.0, op0=ALU.subtract, op1=ALU.mult), reads=[prow, bg], writes=[Dm])
            Elo = V('Elo')
            Eup = V('Eup')
            S.op('dve', lambda h: h.tensor_scalar(out=Elo[0:C, 0:CC], in0=Dm[0:C, 0:CC], scalar1=0.0, scalar2=None, op0=ALU.min), reads=[Dm], writes=[Elo])
            S.op('act', lambda h: h.activation(out=Elo[0:C, 0:CC], in_=Elo[0:C, 0:CC], func=AF.Exp), reads=[Elo], writes=[Elo])
            S.op('dve', lambda h: h.tensor_scalar(out=Eup[0:C, 0:CC], in0=Dm[0:C, 0:CC], scalar1=-1.0, scalar2=0.0, op0=ALU.mult, op1=ALU.min), reads=[Dm], writes=[Eup])
            S.op('act', lambda h: h.activation(out=Eup[0:C, 0:CC], in_=Eup[0:C, 0:CC], func=AF.Exp), reads=[Eup], writes=[Eup])
            El3 = Elo[0:C, 0:CC].rearrange("p (a c) -> p a c", c=C)
            Eu3 = Eup[0:C, 0:CC].rearrange("p (a c) -> p a c", c=C)
            S.op('dve', lambda h: h.tensor_copy(out=bg[0:C, 24:28], in_=Eu3[:, :, C - 1]), reads=[Eup], writes=[bg])
            S.op('dve', lambda h: h.tensor_tensor(out=El3, in0=El3, in1=mLs, op=ALU.mult), reads=[Elo, cst], writes=[Elo])
            S.op('dve', lambda h: h.tensor_tensor(out=Eu3, in0=Eu3, in1=mU, op=ALU.mult), reads=[Eup, cst], writes=[Eup])
            for h4 in range(4):
                S.op('dve', lambda h, h4=h4: h.tensor_scalar(out=El3[:, h4, :], in0=El3[:, h4, :], scalar1=bg[0:C, h4:h4 + 1], scalar2=None, op0=ALU.mult), reads=[Elo, bg], writes=[Elo])
            pkk = self.pbank()
            pmt = self.pbank()
            for h4 in range(4):
                S.op('pe', lambda h, h4=h4: h.matmul(pkk[0:C, h4 * C:(h4 + 1) * C], lhsT=q3[:, 4 + h4, c0:c0 + C], rhs=q3[:, 4 + h4, c0:c0 + C], start=True, stop=True), reads=[qkv], writes=[pkk])
                S.op('pe', lambda h, h4=h4: h.matmul(pmt[0:C, h4 * C:(h4 + 1) * C], lhsT=q3[:, 4 + h4, c0:c0 + C], rhs=q3[:, h4, c0:c0 + C], start=True, stop=True), reads=[qkv], writes=[pmt])
            P = [V('P0')]
            AT = V('AT')
            MT = V('MT')
            S.op('dve', lambda h: h.tensor_tensor(out=P[0][0:C, 0:CC], in0=pkk[0:C, 0:CC], in1=Elo[0:C, 0:CC], op=ALU.mult), reads=[pkk, Elo], writes=[P[0]])
            S.op('dve', lambda h: h.tensor_tensor(out=MT[0:C, 0:CC], in0=pmt[0:C, 0:CC], in1=Eup[0:C, 0:CC], op=ALU.mult), reads=[pmt, Eup], writes=[MT])
            pat = self.pbank()
            for h4 in range(4):
                S.op('pe', lambda h, h4=h4: h.transpose(pat[0:C, h4 * C:(h4 + 1) * C], P[0][0:C, h4 * C:(h4 + 1) * C], self.ident[0:C, 0:C]), reads=[P[0], cst], writes=[pat])
            S.op('act', lambda h: h.activation(out=AT[0:C, 0:CC], in_=pat[0:C, 0:CC], func=AF.Copy), reads=[pat], writes=[AT])
            X = T1
            X3 = X[0:C, 0:1024].rearrange("p (a c) -> p a c", c=256)
            Kdec = T2
            Kd3 = Kdec[0:C, 0:512].rearrange("p (a c) -> p a c", c=128)
            vn = T2
            vn3 = vn[0:C, 512:1024].rearrange("p (a c) -> p a c", c=128)
            pkt = self.pbank()
            pvt = self.pbank()
            for h4 in range(4):
                S.op('pe', lambda h, h4=h4: h.transpose(pkt[0:C, h4 * 128:(h4 + 1) * 128], q3[:, 4 + h4, c0:c0 + C], self.ident[:, 0:128]), reads=[qkv, cst], writes=[pkt])
                S.op('pe', lambda h, h4=h4: h.transpose(pvt[0:C, h4 * 128:(h4 + 1) * 128], q3[:, 8 + h4, c0:c0 + C], self.ident[:, 0:128]), reads=[qkv, cst], writes=[pvt])
            for h4 in range(4):
                S.op('dve', lambda h, h4=h4: h.tensor_scalar(out=X3[:, h4, 0:128], in0=pkt[0:C, h4 * 128:(h4 + 1) * 128], scalar1=bg[0:C, 20 + h4:21 + h4], scalar2=None, op0=ALU.mult), reads=[pkt, bg], writes=[X])
                S.op('dve', lambda h, h4=h4: h.tensor_scalar(out=X3[:, h4, 128:256], in0=pvt[0:C, h4 * 128:(h4 + 1) * 128], scalar1=bg[0:C, h4:h4 + 1], scalar2=None, op0=ALU.mult), reads=[pvt, bg], writes=[X])
                S.op('dve', lambda h, h4=h4: h.tensor_scalar(out=Kd3[:, h4, :], in0=pkt[0:C, h4 * 128:(h4 + 1) * 128], scalar1=bg[0:C, 24 + h4:25 + h4], scalar2=None, op0=ALU.mult), reads=[pkt, bg], writes=[Kdec])
            PTs = [AT]
            Ps = [P[0]]
            for k in range(1, K + 1):
                pk_ = self.pbank()
                pt_ = self.pbank()
                Pp, PTp = Ps[-1], PTs[-1]
                for h4 in range(4):
                    sl = slice(h4 * C, (h4 + 1) * C)
                    if k < K:
                        S.op('pe', lambda h, sl=sl: h.matmul(pk_[0:C, sl], lhsT=PTp[0:C, sl], rhs=Pp[0:C, sl], start=True, stop=True), reads=[PTp, Pp], writes=[pk_])
                    S.op('pe', lambda h, sl=sl: h.matmul(pt_[0:C, sl], lhsT=Pp[0:C, sl], rhs=PTp[0:C, sl], start=True, stop=True), reads=[PTp, Pp], writes=[pt_])
                nPT = V('PTk%d' % k)
                S.op('act', lambda h: h.activation(out=nPT[0:C, 0:CC], in_=pt_[0:C, 0:CC], func=AF.Copy), reads=[pt_], writes=[nPT])
                PTs.append(nPT)
                if k < K:
                    nP = V('Pk%d' % k)
                    S.op('dve', lambda h: h.tensor_copy(out=nP[0:C, 0:CC], in_=pk_[0:C, 0:CC]), reads=[pk_], writes=[nP])
                    Ps.append(nP)
            for k in range(K, -1, -1):
                PTk = PTs[k]
                for half in range(2):
                    px = self.pbank()
                    for hh in range(2):
                        h4 = 2 * half + hh
                        S.op('pe', lambda h, h4=h4, hh=hh: h.matmul(px[0:C, hh * 256:(hh + 1) * 256], lhsT=PTk[0:C, h4 * C:(h4 + 1) * C], rhs=X3[:, h4, :], start=True, stop=True), reads=[PTk, X], writes=[px])
                    xs = X[0:C, half * 512:(half + 1) * 512]
                    S.op('dve', lambda h: h.tensor_tensor(out=xs, in0=xs, in1=px[0:C, 0:512], op=(ALU.subtract if k == 0 else ALU.add)), reads=[X, px], writes=[X])
            pwt = self.pbank()
            for h4 in range(4):
                S.op('pe', lambda h, h4=h4: h.transpose(pwt[:, h4 * C:(h4 + 1) * C], X3[:, h4, 0:128], self.ident[0:C, 0:C]), reads=[X, cst], writes=[pwt])
            WT = Dm
            S.op('act', lambda h: h.activation(out=WT[:, 0:CC], in_=pwt[:, 0:CC], func=AF.Copy), reads=[pwt], writes=[WT])
            pws = self.pbank()
            for h4 in range(4):
                S.op('pe', lambda h, h4=h4: h.matmul(pws[0:C, h4 * 128:(h4 + 1) * 128], lhsT=WT[:, h4 * C:(h4 + 1) * C], rhs=self.gdn_S[:, h4, :], start=True, stop=True), reads=[WT, self.gdn_S], writes=[pws])
            S.op('dve', lambda h: h.tensor_tensor(out=vn3, in0=X3[:, :, 128:256], in1=pws[0:C, 0:512].rearrange("p (a c) -> p a c", c=128), op=ALU.subtract), reads=[X, pws], writes=[vn])
            QeT = R
            Qe3 = QeT[:, 0:CC].rearrange("p (a c) -> p a c", c=C)
            S.op('dve', lambda h: h.tensor_tensor(out=Qe3, in0=q3[:, 0:4, c0:c0 + C], in1=eg3, op=ALU.mult), reads=[qkv, egrow], writes=[QeT])
            po = self.pbank()
            for h4 in range(4):
                sl = slice(h4 * C, (h4 + 1) * C)
                S.op('pe', lambda h, h4=h4, sl=sl: h.matmul(po[:, sl], lhsT=self.gdn_S[:, h4, :], rhs=QeT[:, sl], start=True, stop=False), reads=[self.gdn_S, QeT], writes=[po])
                S.op('pe', lambda h, h4=h4, sl=sl: h.matmul(po[:, sl], lhsT=vn3[:, h4, :], rhs=MT[0:C, sl], start=False, stop=True), reads=[vn, MT], writes=[po])
            S.op('act', lambda h: h.activation(out=o3[:, :, c0:c0 + C], in_=po[:, 0:CC].rearrange("p (a c) -> p a c", c=C), func=AF.Copy), reads=[po], writes=[oT])
            pu = self.pbank()
            for h4 in range(4):
                S.op('pe', lambda h, h4=h4: h.matmul(pu[:, h4 * 128:(h4 + 1) * 128], lhsT=Kd3[:, h4, :], rhs=vn3[:, h4, :], start=True, stop=True), reads=[Kdec, vn], writes=[pu])
            for h4 in range(4):
                S.op('dve', lambda h, h4=h4: h.scalar_tensor_tensor(out=self.gdn_S[:, h4, :], in0=self.gdn_S[:, h4, :], scalar=eg3[:, h4, C - 1:C], in1=pu[:, h4 * 128:(h4 + 1) * 128], op0=ALU.mult, op1=ALU.add),
                     reads=[self.gdn_S, egrow, pu], writes=[self.gdn_S])
        if last:
            S.dma('pool', self.O['gdn' + grp][l].rearrange("h k v -> k h v"), self.gdn_S[:, :, :], reads=[self.gdn_S], writes=[self.dk('ogdn' + grp)])
        W4 = 4 * G
        S.op('act', lambda h: h.activation(out=T1[:, 0:W4], in_=oT[:, 0:W4], func=AF.Square), reads=[oT], writes=[T1])
        pm = self.pbank()
        S.op('pe', lambda h: h.matmul(pm[:, 0:W4], lhsT=cst[:, 384:512], rhs=T1[:, 0:W4], start=True, stop=True), reads=[cst, T1], writes=[pm])
        S.op('act', lambda h: h.activation(out=T2[:, 0:W4], in_=pm[:, 0:W4], func=AF.Sqrt, bias=self.epsb[:, 1:2]), reads=[pm, self.epsb], writes=[T2])
        S.op('dve', lambda h: h.reciprocal(out=T2[:, 0:W4], in_=T2[:, 0:W4]), reads=[T2], writes=[T2])
        S.op('dve', lambda h: h.scalar_tensor_tensor(out=oT[:, 0:W4], in0=oT[:, 0:W4], scalar=self.gdn_n[:, 0:1], in1=T2[:, 0:W4], op0=ALU.mult, op1=ALU.mult), reads=[oT, T2, self.gdn_n], writes=[oT])
        for c4 in range(4):
            pgt = self.proj_fm(l, 'cg%d' % c4, 128, G)
            sg = V('R') if c4 % 2 else V('Dm')
            S.op('act', lambda h: h.activation(out=sg[:, 0:G], in_=pgt[:, 0:G], func=AF.Silu), reads=[pgt], writes=[sg])
            S.op('dve', lambda h, c4=c4: h.tensor_tensor(out=self.brT[:, 8 + c4, 0:G], in0=o3[:, c4, :], in1=sg[:, 0:G], op=ALU.mult), reads=[oT, sg], writes=[self.brT])

    def gla_part(self, l, grp, g0, G, C, last):
        S = self.S
        cst = self.cst
        nC = G // C
        qT = self.wsp(True)
        kT = self.wsp()
        W4 = 4 * G
        q3 = qT[0:64, 0:W4].rearrange("p (a c) -> p a c", c=G)
        k3 = kT[0:64, 0:W4].rearrange("p (a c) -> p a c", c=G)
        for h4 in range(4):
            pb = self.proj_fm(l, 'dq%d' % h4, 64, G)
            S.op('act', lambda h, h4=h4: h.activation(out=q3[:, h4, :], in_=pb[0:64, 0:G], func=AF.Copy, scale=0.125), reads=[pb], writes=[qT])
            pb2 = self.proj_fm(l, 'dk%d' % h4, 64, G)
            S.op('dve', lambda h, h4=h4: h.tensor_copy(out=k3[:, h4, :], in_=pb2[0:64, 0:G]), reads=[pb2], writes=[kT])
        pg = self.proj_fm(l, 'dg', 16, G)
        dgT = self.sm()
        dg_sb = self.scratch()
        S.op('dve', lambda h: h.tensor_copy(out=dg_sb[0:16, 0:G], in_=pg[0:16, 0:G]), reads=[pg], writes=[dg_sb])
        spT = self.wsp()
        sp3 = spT[0:64, 0:W4].rearrange("p (a c) -> p a c", c=G)
        nb = self.sm()
        S.op('dve', lambda h: h.tensor_scalar(out=nb[0:64, 0:4], in0=self.gla_b[:, :], scalar1=-1.0, scalar2=None, op0=ALU.mult), reads=[self.gla_b], writes=[nb])
        for h4 in range(4):
            pl = self.pbank()
            S.op('pe', lambda h, h4=h4: h.matmul(pl[0:64, 0:G], lhsT=self.gla_w[:, h4 * 64:(h4 + 1) * 64], rhs=dg_sb[0:16, 0:G], start=True, stop=True),
                 reads=[self.gla_w, dg_sb], writes=[pl])
            S.op('act', lambda h, h4=h4: h.activation(out=sp3[:, h4, :], in_=pl[0:64, 0:G], func=AF.Exp, scale=-1.0, bias=nb[0:64, h4:h4 + 1]),
                 reads=[pl, nb], writes=[spT])
        S.op('act', lambda h: h.activation(out=spT[0:64, 0:W4], in_=spT[0:64, 0:W4], func=AF.Ln, bias=1.0), reads=[spT], writes=[spT])
        csT = self.wsp()
        S.op('dve', lambda h: h.tensor_tensor_scan(out=csT[0:64, 0:W4], data0=self.resetm[:, 0:W4], data1=spT[0:64, 0:W4], initial=0.0, op0=ALU.mult, op1=ALU.add),
             reads=[self.resetm, spT], writes=[csT])
        eq = self.wsp()
        ek = self.wsp()
        S.op('act', lambda h: h.activation(out=eq[0:64, 0:W4], in_=csT[0:64, 0:W4], func=AF.Exp, scale=-1.0 / 16.0), reads=[csT], writes=[eq])
        S.op('act', lambda h: h.activation(out=ek[0:64, 0:W4], in_=csT[0:64, 0:W4], func=AF.Exp, scale=1.0 / 16.0), reads=[csT], writes=[ek])
        S.op('dve', lambda h: h.tensor_tensor(out=qT[0:64, 0:W4], in0=qT[0:64, 0:W4], in1=eq[0:64, 0:W4], op=ALU.mult), reads=[qT, eq], writes=[qT])
        S.op('dve', lambda h: h.tensor_tensor(out=kT[0:64, 0:W4], in0=kT[0:64, 0:W4], in1=ek[0:64, 0:W4], op=ALU.mult), reads=[kT, ek], writes=[kT])
        eq3 = eq[0:64, 0:W4].rearrange("p (a c) -> p a c", c=G)
        oT = self.wsp()
        o3 = oT[:, 0:W4].rearrange("p (a c) -> p a c", c=G)
        for ci in range(nC):
            c0 = ci * C
            pv = self.pbank()
            for c4 in range(4):
                w = self.load_w(l, 'dv%d' % c4)
                for k in range(8):
                    S.op('pe', lambda h, k=k, c4=c4, w=w: h.matmul(pv[0:C, c4 * 128:(c4 + 1) * 128], lhsT=self.xT[:, k, c0:c0 + C], rhs=w[:, k, :],
                                                               start=(k == 0), stop=(k == 7)), reads=[w, self.xT], writes=[pv])
            vt = self.scratch()
            S.op('act', lambda h: h.activation(out=vt[0:C, 0:512], in_=pv[0:C, 0:512], func=AF.Copy), reads=[pv], writes=[vt])
            pk = self.pbank()
            for h4 in range(4):
                S.op('pe', lambda h, h4=h4: h.transpose(pk[0:C, h4 * 64:(h4 + 1) * 64], k3[:, h4, c0:c0 + C], self.ident[0:64, 0:64]),
                     reads=[kT, cst], writes=[pk])
            kt = self.scratch()
            S.op('dve', lambda h: h.tensor_copy(out=kt[0:C, 0:256], in_=pk[0:C, 0:256]), reads=[pk], writes=[kt])
            pa = self.pbank()
            for h4 in range(4):
                S.op('pe', lambda h, h4=h4: h.matmul(pa[0:C, h4 * C:(h4 + 1) * C], lhsT=k3[:, h4, c0:c0 + C], rhs=q3[:, h4, c0:c0 + C], start=True, stop=True),
                     reads=[kT, qT], writes=[pa])
            at = self.scratch()
            pa3 = pa[0:C, 0:4 * C].rearrange("p (a c) -> p a c", c=C)
            at3 = at[0:C, 0:4 * C].rearrange("p (a c) -> p a c", c=C)
            msk = cst[0:C, 128:128 + C].unsqueeze(1).to_broadcast([C, 4, C])
            S.op('dve', lambda h: h.tensor_tensor(out=at3, in0=pa3, in1=msk, op=ALU.mult), reads=[pa, cst], writes=[at])
            po = self.pbank()
            for h4 in range(4):
                S.op('pe', lambda h, h4=h4: h.matmul(po[:, h4 * C:(h4 + 1) * C], lhsT=vt[0:C, h4 * 128:(h4 + 1) * 128], rhs=at3[:, h4, :], start=True, stop=False),
                     reads=[vt, at], writes=[po])
                S.op('pe', lambda h, h4=h4: h.matmul(po[:, h4 * C:(h4 + 1) * C], lhsT=self.gla_S[:, h4, :], rhs=q3[:, h4, c0:c0 + C], start=False, stop=True),
                     reads=[self.gla_S, qT], writes=[po])
            S.op('act', lambda h: h.activation(out=o3[:, :, c0:c0 + C], in_=po[:, 0:4 * C].rearrange("p (a c) -> p a c", c=C), func=AF.Copy), reads=[po], writes=[oT])
            pu = self.pbank()
            for h4 in range(4):
                S.op('pe', lambda h, h4=h4: h.matmul(pu[0:64, h4 * 128:(h4 + 1) * 128], lhsT=kt[0:C, h4 * 64:(h4 + 1) * 64], rhs=vt[0:C, h4 * 128:(h4 + 1) * 128], start=True, stop=True),
                     reads=[kt, vt], writes=[pu])
            S.op('dve', lambda h: h.tensor_tensor(out=self.gla_S[:, :, :], in0=self.gla_S[:, :, :], in1=pu[0:64, 0:512].rearrange("p (a c) -> p a c", c=128), op=ALU.add),
                 reads=[self.gla_S, pu], writes=[self.gla_S])
            for h4 in range(4):
                S.op('dve', lambda h, h4=h4: h.tensor_scalar(out=self.gla_S[:, h4, :], in0=self.gla_S[:, h4, :], scalar1=eq3[:, h4, c0 + C - 1:c0 + C], scalar2=None, op0=ALU.mult),
                     reads=[self.gla_S, eq], writes=[self.gla_S])
        if last:
            S.dma('pool', self.O['gla' + grp][l].rearrange("h k v -> k h v"), self.gla_S[:, :, :], reads=[self.gla_S], writes=[self.dk('ogla' + grp)])
        sq = spT
        S.op('act', lambda h: h.activation(out=sq[:, 0:W4], in_=oT[:, 0:W4], func=AF.Square), reads=[oT], writes=[sq])
        rst = csT
        for b0 in range(0, W4, 512):
            bw = min(512, W4 - b0)
            pm = self.pbank()
            S.op('pe', lambda h: h.matmul(pm[:, 0:bw], lhsT=cst[:, 384:512], rhs=sq[:, b0:b0 + bw], start=True, stop=True), reads=[cst, sq], writes=[pm])
            S.op('act', lambda h: h.activation(out=rst[:, b0:b0 + bw], in_=pm[:, 0:bw], func=AF.Sqrt, bias=self.epsb[:, 1:2]), reads=[pm, self.epsb], writes=[rst])
        S.op('dve', lambda h: h.reciprocal(out=rst[:, 0:W4], in_=rst[:, 0:W4]), reads=[rst], writes=[rst])
        S.op('dve', lambda h: h.scalar_tensor_tensor(out=oT[:, 0:W4], in0=oT[:, 0:W4], scalar=self.gla_n[:, 0:1], in1=rst[:, 0:W4], op0=ALU.mult, op1=ALU.mult),
             reads=[oT, rst, self.gla_n], writes=[oT])
        for c4 in range(4):
            pgt = self.proj_fm(l, 'dgt%d' % c4, 128, G)
            sg = self.scratch()
            S.op('act', lambda h: h.activation(out=sg[:, 0:G], in_=pgt[:, 0:G], func=AF.Silu), reads=[pgt], writes=[sg])
            S.op('dve', lambda h, c4=c4: h.tensor_tensor(out=self.brT[:, 12 + c4, 0:G], in0=o3[:, c4, :], in1=sg[:, 0:G], op=ALU.mult), reads=[oT, sg], writes=[self.brT])

    def out_stage(self, l, grp, g0, G, TP, xin, xin_tk, yout, yout_tk):
        S = self.S
        I = self.I
        mix = self.av('mixT', 8 * G)
        if not hasattr(mix, '_v3'):
            pass
        mix3 = mix.h[:, 0:8 * G].rearrange("p (a c) -> p a c", c=G)
        for dc in range(8):
            for b in range(4):
                pp = self.pbank()
                if b == 1:
                    wb = self.wbrB
                    S.dma('sp', wb[:, :, :], I['wbr'][l, b, :, dc * 128:(dc + 1) * 128].rearrange("(a p) c -> p a c", p=64), writes=[wb])
                    for hh in range(8):
                        S.op('pe', lambda h, hh=hh: h.matmul(pp[:, 0:G], lhsT=wb[:, hh, :], rhs=self.brB[:, hh, 0:G], start=(hh == 0), stop=(hh == 7)),
                             reads=[wb, self.brB], writes=[pp])
                else:
                    wb = self.wbr
                    S.dma('sp', wb[:, :, :], I['wbr'][l, b, :, dc * 128:(dc + 1) * 128].rearrange("(a p) c -> p a c", p=128), writes=[wb])
                    for kc in range(4):
                        S.op('pe', lambda h, kc=kc: h.matmul(pp[:, 0:G], lhsT=wb[:, kc, :], rhs=self.brT[:, 4 * b + kc, 0:G], start=(kc == 0), stop=(kc == 3)),
                             reads=[wb, self.brT], writes=[pp])
                pm = self.proj_fm(l, 'mg%d_%d' % (b, dc), 128, G)
                sg = self.scratch()
                S.op('act', lambda h: h.activation(out=sg[:, 0:G], in_=pm[:, 0:G], func=AF.Sigmoid), reads=[pm], writes=[sg])
                if b == 0:
                    S.op('dve', lambda h, dc=dc: h.tensor_tensor(out=mix3[:, dc, 0:G], in0=pp[:, 0:G], in1=sg[:, 0:G], op=ALU.mult), reads=[pp, sg], writes=[mix])
                else:
                    tmp = self.scratch()
                    S.op('dve', lambda h: h.tensor_tensor(out=tmp[:, 0:G], in0=pp[:, 0:G], in1=sg[:, 0:G], op=ALU.mult), reads=[pp, sg], writes=[tmp])
                    S.op('dve', lambda h, dc=dc: h.tensor_tensor(out=mix3[:, dc, 0:G], in0=mix3[:, dc, 0:G], in1=tmp[:, 0:G], op=ALU.add), reads=[mix, tmp], writes=[mix])
        for ti in range(G // TP):
            t0 = ti * TP
            xt = self.av('xtm', D)
            S.dma('sp', xt[0:TP, :], xin[g0 + t0: g0 + t0 + TP, :], reads=[xin_tk], writes=[xt])
            z = self.av('zt', D)
            for qt in range(8):
                S.dma('sp', self.wout[:, :, :], I['wout'][l, :, qt * 128:(qt + 1) * 128].rearrange("(a p) c -> p a c", p=128), writes=[self.wout])
                pz = self.pbank()
                for k in range(8):
                    S.op('pe', lambda h, k=k: h.matmul(pz[0:TP, 0:128], lhsT=mix3[:, k, t0:t0 + TP], rhs=self.wout[:, k, :], start=(k == 0), stop=(k == 7)),
                         reads=[mix, self.wout], writes=[pz])
                S.op('dve', lambda h, qt=qt: h.scalar_tensor_tensor(out=z[0:TP, qt * 128:(qt + 1) * 128], in0=xt[0:TP, qt * 128:(qt + 1) * 128], scalar=float(DN_ALPHA),
                                                                     in1=pz[0:TP, 0:128], op0=ALU.mult, op1=ALU.add), reads=[xt, pz], writes=[z])
            st = self.sm()
            for half in range(2):
                S.op('dve', lambda h, half=half: h.bn_stats(out=st[0:TP, half * 6:(half + 1) * 6], in_=z[0:TP, half * 512:(half + 1) * 512]), reads=[z], writes=[st])
            mv = self.sm()
            S.op('dve', lambda h: h.bn_aggr(out=mv[0:TP, 0:2], in_=st[0:TP, 0:12]), reads=[st], writes=[mv])
            S.op('act', lambda h: h.activation(out=mv[0:TP, 2:3], in_=mv[0:TP, 1:2], func=AF.Sqrt, bias=self.epsb[0:TP, 0:1]), reads=[mv, self.epsb], writes=[mv])
            S.op('dve', lambda h: h.reciprocal(out=mv[0:TP, 3:4], in_=mv[0:TP, 2:3]), reads=[mv], writes=[mv])
            S.op('dve', lambda h: h.tensor_scalar(out=z[0:TP, :], in0=z[0:TP, :], scalar1=mv[0:TP, 0:1], scalar2=mv[0:TP, 3:4], op0=ALU.subtract, op1=ALU.mult),
                 reads=[z, mv], writes=[z])
            S.op('dve', lambda h: h.tensor_tensor(out=z[0:TP, :], in0=z[0:TP, :], in1=self.lng[0:TP, :], op=ALU.mult), reads=[z, self.lng], writes=[z])
            S.op('dve', lambda h: h.tensor_tensor(out=z[0:TP, :], in0=z[0:TP, :], in1=self.lnb[0:TP, :], op=ALU.add), reads=[z, self.lnb], writes=[z])
            S.dma('pool', yout[g0 + t0: g0 + t0 + TP, :], z[0:TP, :], reads=[z], writes=[yout_tk])


def _prep_consts():
    c = np.zeros((128, 1024), np.float32)
    c[:, 0:128] = np.eye(128, dtype=np.float32)
    p = np.arange(128)[:, None]
    f = np.arange(128)[None, :]
    c[:, 128:256] = (f >= p).astype(np.float32)
    c[:, 256:384] = 1.0
    c[:, 384:512] = 1.0 / 128.0
    c[:, 512:768] = np.arange(1, 257, dtype=np.float32)[None, :]
    c[:, 768:896] = (p > f).astype(np.float32)
    c[0:64, 896] = 1.0
    c[64:128, 928] = 1.0
    return c


_CACHE = {}


def kernel(**inp):
    T = inp['x_prompt'].shape[1]
    if T not in _CACHE:
        b = Builder(T)
        _CACHE[T] = b.build()
    nc = _CACHE[T]
    f = lambda a: np.ascontiguousarray(np.asarray(a, dtype=np.float32))
    w_in = f(inp['w_in'])
    win = np.zeros((DEPTH, NCH, 128, 8, 128), np.float32)
    for ci, (_, cols) in enumerate(CHUNKS):
        blk = w_in[:, :, cols]
        blk = blk.reshape(DEPTH, 8, 128, len(cols)).transpose(0, 2, 1, 3)
        win[:, ci, :, :, :len(cols)] = blk
    cst = _prep_consts()
    glab = f(inp['gla_b_g']).reshape(DEPTH, 4, 64).transpose(0, 2, 1).copy()
    glan = f(inp['gla_norm']).reshape(DEPTH, 128, 1)
    def st_layout(a):
        return np.ascontiguousarray(a.reshape(DEPTH, 16, 2, 64).transpose(0, 2, 3, 1)).reshape(DEPTH, 128, 16)
    are, aim = st_layout(f(inp['s5_a_re'])), st_layout(f(inp['s5_a_im']))
    ldt = np.ascontiguousarray(np.broadcast_to(f(inp['s5_log_dt']).reshape(DEPTH, 16, 2, 1), (DEPTH, 16, 2, 64)).transpose(0, 2, 3, 1)).reshape(DEPTH, 128, 16)
    s5p = np.ascontiguousarray(np.stack([are, aim, ldt], axis=2))
    bre, bim = f(inp['s5_b_re']), f(inp['s5_b_im'])
    cre, cim = f(inp['s5_c_re']), f(inp['s5_c_im'])
    s5b = np.zeros((DEPTH, 2, 2, 128, 4, 128), np.float32)
    s5c = np.zeros((DEPTH, 2, 128, 16, 128), np.float32)
    for c in range(4):
        for s4 in range(4):
            for g2 in range(2):
                g = 8 * c + 2 * s4 + g2
                for ri, (bb, cc) in enumerate(((bre, cre), (bim, cim))):
                    s5b[:, 0 if s4 < 3 else 1, ri, 32 * s4 + 16 * g2: 32 * s4 + 16 * g2 + 16, c, 64 * g2: 64 * g2 + 64] = bb[:, g].transpose(0, 2, 1)
                    s5c[:, ri, 64 * g2: 64 * g2 + 64, 4 * c + s4, (2 * s4 + g2) * 16:(2 * s4 + g2) * 16 + 16] = cc[:, g].transpose(0, 2, 1)
    s5d = np.ascontiguousarray(f(inp['s5_d']).reshape(DEPTH, 4, 128).transpose(0, 2, 1))
    h0r, h0i = st_layout(f(inp['state_s5_re']).transpose(1, 0, 2, 3).reshape(NCORE * DEPTH, 32, 64).reshape(NCORE, DEPTH, 32, 64)[0]) if False else (None, None)
    gcw = np.ascontiguousarray(f(inp['gdn_conv']).reshape(DEPTH, 4, 12, 128).transpose(0, 3, 2, 1))
    gpar = np.ascontiguousarray(np.broadcast_to(np.concatenate([f(inp['gdn_a_log']), f(inp['gdn_dt_bias'])], axis=1)[:, None, :], (DEPTH, 128, 8)))
    gnorm = f(inp['gdn_norm']).reshape(DEPTH, 128, 1)
    rr = 127 - np.arange(384)
    oh = (_t5_bucket(rr)[None, :] == np.arange(32)[:, None]).astype(np.float32)
    common = dict(win=win, relb=f(inp['rel_bias']), oh=oh, gcw=gcw, gpar=gpar, gnorm=gnorm, s5p=s5p, s5b=s5b, s5c=s5c, s5d=s5d, wglu=f(inp['s5_w_glu']), wbr=f(inp['w_branch']), wout=f(inp['w_out']), lng=f(inp['ln_g']), lnb=f(inp['ln_b']), cst=cst,
                  glaw=f(inp['gla_w_g2']), glab=glab, glan=glan)
    xp = f(inp['x_prompt'])
    xs = f(inp['x_sample'])
    in_maps = []
    for c in range(NCORE):
        m = dict(common)
        m['xp'] = xp[c % 2]
        m['xs'] = xs[c]
        m['sgla'] = f(inp['state_gla'])[:, c]
        m['ck'] = f(inp['cache_k'])[:, c].reshape(DEPTH, PAST, 128)
        m['cv'] = f(inp['cache_v'])[:, c].reshape(DEPTH, PAST, 128)
        cki = f(inp['cache_kidx'])[:, c]
        m['cki'] = np.ascontiguousarray(np.concatenate([cki, cki], axis=-1))
        m['sgdn'] = f(inp['state_gdn'])[:, c]
        m['sconv'] = np.ascontiguousarray(f(inp['state_gdn_conv'])[:, c].reshape(DEPTH, 3, 12, 128).transpose(0, 3, 2, 1))
        m['s5h0'] = np.ascontiguousarray(np.stack([st_layout(f(inp['state_s5_re'])[:, c]), st_layout(f(inp['state_s5_im'])[:, c])], axis=2))
        in_maps.append(m)
    res = run_bass_kernel_spmd(nc, in_maps, core_ids=list(range(NCORE))).results
    P = lambda name: np.stack([res[b][name] for b in range(2)], axis=0)
    Sm = lambda name: np.stack([res[b][name] for b in range(NCORE)], axis=0)
    yp = P('yp')
    ys = Sm('ys')

    def kvfix(a, n):
        return np.ascontiguousarray(a.transpose(1, 0, 2, 3)).reshape(DEPTH, n, a.shape[2], 2, 64)

    def st(a):
        return np.ascontiguousarray(np.moveaxis(a, 0, 1))
    zeros = lambda *s: np.zeros(s, np.float32)

    def s5fix(a, ri):
        x = a[:, :, :, ri, :].reshape(a.shape[0], DEPTH, 2, 64, 16).transpose(1, 0, 4, 2, 3)
        return np.ascontiguousarray(x).reshape(DEPTH, a.shape[0], 32, 64)
    outs = [yp, ys]
    for g, getter, n, tt in (('p', P, 2, T), ('s', Sm, NCORE, DEC_SEQ)):
        outs += [kvfix(getter('k' + g), n), kvfix(getter('v' + g), n), st(getter('ki' + g)),
                 s5fix(getter('os5' + g), 0), s5fix(getter('os5' + g), 1), st(getter('ogdn' + g)),
                 st(getter('conv' + g)), st(getter('gla' + g))]
    return tuple(outs)
```

```python
import math
from contextlib import ExitStack
import numpy as np
import concourse.bass as bass
import concourse.mybir as mybir
from concourse.bass_utils import run_bass_kernel_spmd

F32 = mybir.dt.float32
BF16 = mybir.dt.bfloat16
U8 = mybir.dt.uint8
ALU = mybir.AluOpType
AF = mybir.ActivationFunctionType
AX = mybir.AxisListType

D = 1024
SEQ = 8192
DEPTH = 2
DEC_SEQ = 16
PAST = 1024
NCORE = 8
TOPK_MAX = 256
LN_EPS = 1e-5
RMS_EPS = 1e-6
DN_ALPHA = (2 * DEPTH) ** 0.25

_LAY = (('a_u', 512), ('a_gate', 512), ('b_q', 512), ('b_k', 128), ('b_v', 128), ('b_qi', 256), ('b_ki', 64),
        ('b_wi', 4), ('b_gate', 512), ('c_qkv', 1536), ('c_beta', 4), ('c_a', 4), ('c_gate', 512),
        ('d_q', 256), ('d_k', 256), ('d_v', 512), ('d_g', 16), ('d_gate', 512), ('merge', 4096))
OFF = {}
_o = 0
for _n, _w in _LAY:
    OFF[_n] = _o
    _o += _w
IN_WIDTH = _o


def _chunks():
    ch = []
    r = lambda n, a, b: list(range(OFF[n] + a, OFF[n] + b))
    ch.append(('kvA', r('b_k', 0, 128)))
    ch.append(('kvB', r('b_v', 0, 128)))
    ch.append(('kvC', r('b_ki', 0, 64) + r('b_ki', 0, 64)))
    ch.append(('kvD', r('b_wi', 0, 4) + r('c_beta', 0, 4) + r('c_a', 0, 4)))
    for h in range(4):
        ch.append(('qq%d' % h, r('b_q', 64 * h, 64 * h + 64) + r('b_q', 64 * (h + 4), 64 * (h + 4) + 64)))
    for h in range(4):
        ch.append(('qi%d' % h, r('b_qi', 64 * h, 64 * h + 64) + r('b_qi', 64 * h, 64 * h + 64)))
    for c in range(4):
        ch.append(('au%d' % c, r('a_u', 128 * c, 128 * c + 128)))
    for c in range(12):
        ch.append(('cq%d' % c, r('c_qkv', 128 * c, 128 * c + 128)))
    for h in range(4):
        ch.append(('dq%d' % h, r('d_q', 64 * h, 64 * h + 64)))
    for h in range(4):
        ch.append(('dk%d' % h, r('d_k', 64 * h, 64 * h + 64)))
    for c in range(4):
        ch.append(('dv%d' % c, r('d_v', 128 * c, 128 * c + 128)))
    ch.append(('dg', r('d_g', 0, 16)))
    for c in range(4):
        ch.append(('ag%d' % c, r('a_gate', 128 * c, 128 * c + 128)))
    for h in range(8):
        ch.append(('bg%d' % h, r('b_gate', 64 * h, 64 * h + 64)))
    for c in range(4):
        ch.append(('cg%d' % c, r('c_gate', 128 * c, 128 * c + 128)))
    for c in range(4):
        ch.append(('dgt%d' % c, r('d_gate', 128 * c, 128 * c + 128)))
    for b in range(4):
        for c in range(8):
            ch.append(('mg%d_%d' % (b, c), r('merge', 1024 * b + 128 * c, 1024 * b + 128 * c + 128)))
    return ch


CHUNKS = _chunks()
CIDX = {n: i for i, (n, _) in enumerate(CHUNKS)}
NCH = len(CHUNKS)


def _t5_bucket(rel):
    nb = 16
    max_exact = 8
    ret = np.where(rel > 0, nb, 0)
    dist = np.abs(rel)
    distf = np.maximum(dist, 1).astype(np.float32)
    large = max_exact + (np.log(distf / max_exact) / math.log(128 / max_exact) * (nb - max_exact)).astype(np.int32)
    large = np.minimum(large, nb - 1)
    return ret + np.where(dist < max_exact, dist, large)


class Tk:
    __slots__ = ('h', 'w', 'r')

    def __init__(self, h=None):
        self.h = h
        self.w = None
        self.r = []

    def __getitem__(self, idx):
        return self.h[idx]


class Sched:
    def __init__(self, nc, stack):
        self.nc = nc
        self.stack = stack
        self.eng = {}
        for n, h in [('pe', nc.tensor), ('dve', nc.vector), ('act', nc.scalar), ('pool', nc.gpsimd), ('sp', nc.sync)]:
            sem = stack.enter_context(nc.semaphore('s_' + n))
            self.eng[n] = dict(h=h, sem=sem, cnt=0, known={})
        self.dma_sems = {}
        for q in ('sp', 'pool', 'act'):
            self.dma_sems[q] = [[stack.enter_context(nc.semaphore('d%s%d' % (q, i))), 0] for i in range(12)]
        self.dma_rr = {'sp': 0, 'pool': 0, 'act': 0}
        self.nid = 0
        self.n_ins = 0

    def sb(self, shape, dt=F32):
        self.nid += 1
        return Tk(self.stack.enter_context(self.nc.sbuf_tensor('t%d' % self.nid, shape, dt)))

    def ps(self, shape, dt=F32):
        self.nid += 1
        return Tk(self.stack.enter_context(self.nc.psum_tensor('p%d' % self.nid, shape, dt)))

    def _wait(self, e, sem, val):
        E = self.eng[e]
        k = id(sem)
        if E['known'].get(k, 0) >= val:
            return
        E['h'].wait_ge(sem, val)
        E['known'][k] = val

    def _deps(self, e, reads, writes):
        E = self.eng[e]
        own = E['sem']
        pe = (e == 'pe')
        for t in reads:
            if t.w is not None and not (pe and t.w[0] is own):
                self._wait(e, *t.w)
        for t in writes:
            if t.w is not None and not (pe and t.w[0] is own):
                self._wait(e, *t.w)
            for (s, v) in t.r:
                if not (pe and s is own):
                    self._wait(e, s, v)

    def _mark(self, tok, reads, writes):
        for t in writes:
            t.w = tok
            t.r = []
        for t in reads:
            if t not in writes:
                if len(t.r) > 6:
                    d = {}
                    for (s, v) in t.r:
                        d[id(s)] = (s, max(v, d.get(id(s), (s, 0))[1]))
                    t.r = list(d.values())
                t.r.append(tok)

    def op(self, e, fn, reads=(), writes=()):
        E = self.eng[e]
        self._deps(e, reads, writes)
        ins = fn(E['h'])
        E['cnt'] += 1
        ins.then_inc(E['sem'], 1)
        self._mark((E['sem'], E['cnt']), reads, writes)
        self.n_ins += 1
        return ins

    def dma(self, e, out, in_, reads=(), writes=(), **kw):
        E = self.eng[e]
        slots = self.dma_sems[e]
        slot = slots[self.dma_rr[e]]
        self.dma_rr[e] = (self.dma_rr[e] + 1) % len(slots)
        if slot[1] > 0:
            self._wait(e, slot[0], slot[1])
        self._deps(e, reads, writes)
        ins = E['h'].dma_start(out=out, in_=in_, **kw)
        slot[1] += 16
        ins.then_inc(slot[0], 16)
        self._mark((slot[0], slot[1]), reads, writes)
        self.n_ins += 1
        return ins

    def finish(self, tiles):
        for t in tiles:
            if t.w is not None:
                self._wait('sp', *t.w)


class Builder:
    def __init__(self, T):
        self.T = T
        self.G = min(128, T)
        self.nc = bass.Bass("TRN2", target_bir_lowering=False)

    def dram_in(self, name, shape, dt=F32):
        return self.nc.dram_tensor(name, list(shape), dt, kind="ExternalInput").ap()

    def dram_out(self, name, shape, dt=F32):
        return self.nc.dram_tensor(name, list(shape), dt, kind="ExternalOutput").ap()

    def build(self):
        nc = self.nc
        T = self.T
        I = {}
        O = {}
        I['xp'] = self.dram_in('xp', [T, D])
        I['xs'] = self.dram_in('xs', [DEC_SEQ, D])
        I['win'] = self.dram_in('win', [DEPTH, NCH, 128, 8, 128])
        I['wbr'] = self.dram_in('wbr', [DEPTH, 4, 512, D])
        I['wout'] = self.dram_in('wout', [DEPTH, D, D])
        I['lng'] = self.dram_in('lng', [DEPTH, D])
        I['lnb'] = self.dram_in('lnb', [DEPTH, D])
        I['cst'] = self.dram_in('cst', [128, 1024])
        I['glaw'] = self.dram_in('glaw', [DEPTH, 16, 256])
        I['glab'] = self.dram_in('glab', [DEPTH, 64, 4])
        I['glan'] = self.dram_in('glan', [DEPTH, 128, 1])
        I['sgla'] = self.dram_in('sgla', [DEPTH, 4, 64, 128])
        I['relb'] = self.dram_in('relb', [32, 8])
        I['oh'] = self.dram_in('oh', [32, 384])
        I['ck'] = self.dram_in('ck', [DEPTH, PAST, 128])
        I['cv'] = self.dram_in('cv', [DEPTH, PAST, 128])
        I['cki'] = self.dram_in('cki', [DEPTH, PAST, 128])
        I['gcw'] = self.dram_in('gcw', [DEPTH, 128, 12, 4])
        I['gpar'] = self.dram_in('gpar', [DEPTH, 128, 8])
        I['gnorm'] = self.dram_in('gnorm', [DEPTH, 128, 1])
        I['sgdn'] = self.dram_in('sgdn', [DEPTH, 4, 128, 128])
        I['sconv'] = self.dram_in('sconv', [DEPTH, 128, 12, 3])
        I['s5p'] = self.dram_in('s5p', [DEPTH, 128, 3, 16])
        I['s5b'] = self.dram_in('s5b', [DEPTH, 2, 2, 128, 4, 128])
        I['s5c'] = self.dram_in('s5c', [DEPTH, 2, 128, 16, 128])
        I['s5d'] = self.dram_in('s5d', [DEPTH, 128, 4])
        I['s5h0'] = self.dram_in('s5h0', [DEPTH, 128, 2, 16])
        I['wglu'] = self.dram_in('wglu', [DEPTH, 512, 1024])
        O['yp'] = self.dram_out('yp', [T, D])
        O['ys'] = self.dram_out('ys', [DEC_SEQ, D])
        for g, tt in (('p', T), ('s', DEC_SEQ)):
            O['k' + g] = self.dram_out('k' + g, [DEPTH, tt, 128])
            O['v' + g] = self.dram_out('v' + g, [DEPTH, tt, 128])
            O['ki' + g] = self.dram_out('ki' + g, [DEPTH, tt, 64])
            O['gla' + g] = self.dram_out('gla' + g, [DEPTH, 4, 64, 128])
            O['conv' + g] = self.dram_out('conv' + g, [DEPTH, 3, 1536])
            O['gdn' + g] = self.dram_out('ogdn' + g, [DEPTH, 4, 128, 128])
            O['s5' + g] = self.dram_out('os5' + g, [DEPTH, 128, 2, 16])
        self.I, self.O = I, O
        self.y0p = nc.dram_tensor('y0p', [T, D], F32).ap()
        self.fd = nc.dram_tensor('fd', [8, 384], F32).ap()
        self.winb = nc.dram_tensor('winb', [DEPTH, NCH, 128, 8 * 128], BF16).ap()
        self.wbrb = nc.dram_tensor('wbrb', [DEPTH, 4, 8, 128, 4 * 128], BF16).ap()
        self.woutb = nc.dram_tensor('woutb', [DEPTH, 8, 128, 8 * 128], BF16).ap()
        self.cfd = nc.dram_tensor('cfd', [8, 1], F32).ap()
        self.y0s = nc.dram_tensor('y0s', [DEC_SEQ, D], F32).ap()
        with ExitStack() as st:
            S = Sched(nc, st)
            self.S = S
            self.dtk = {}
            self.setup()
            self.precast()
            for l in range(DEPTH):
                self.layer_setup(l)
                self.run_seq(l, 'p')
                self.run_seq(l, 's')
            S.finish(list(self.dtk.values()))
        return nc

    def dk(self, name):
        if name not in self.dtk:
            self.dtk[name] = Tk(None)
        return self.dtk[name]

    def setup(self):
        S = self.S
        G = self.G
        self.cst = S.sb([128, 1024])
        S.dma('sp', self.cst[:], self.I['cst'][:, :], writes=[self.cst])
        self.ident = self.cst
        self.pbanks = [S.ps([128, 512]) for _ in range(6)]
        self.pacc = [S.ps([128, 512]) for _ in range(2)]
        self.pb_i = 0
        self.xT = S.sb([128, 8, G])
        self.xTb = S.sb([128, 8, G], BF16)
        self.wch = [S.sb([128, 8, 128]) for _ in range(2)]
        self.wch_i = 0
        self.wchb = [S.sb([128, 8, 128], BF16) for _ in range(2)]
        self.wchb_i = 0
        self.brT = S.sb([128, 16, G], BF16)
        self.lng = S.sb([128, D])
        self.lnb = S.sb([128, D])
        self.wout = S.sb([128, 8, 128], BF16)
        self.wbr = S.sb([128, 4, 128], BF16)
        self.wbrB = S.sb([64, 8, 128], BF16)
        self.brB = S.sb([64, 8, G], BF16)
        self.ARENA = 10240
        self.arena = S.sb([128, self.ARENA])
        self.fence_t = S.sb([1, 4])
        self.phase_views = {}
        self.phase_off = {}
        self.cur_phase = None
        self.fence_tok = None
        self.scr_i = 0
        self.ws_i = 0
        self.small = [S.sb([128, 16]) for _ in range(8)]
        self.sm3 = S.sb([128, 3, 16])
        self.small_i = 0
        self.epsb = S.sb([128, 2])
        S.op('dve', lambda h: h.memset(self.epsb[:, 0:1], LN_EPS), writes=[self.epsb])
        S.op('dve', lambda h: h.memset(self.epsb[:, 1:2], RMS_EPS), writes=[self.epsb])
        self.LC = min(128, G)
        self.s5_cos = S.sb([128, 16, self.LC])
        self.s5_sin = S.sb([128, 16, self.LC])
        self.s5_cr = S.sb([128, 16, 128])
        self.s5_ci = S.sb([128, 16, 128])
        self.s5_b = S.sb([128, 4, 4, 128])
        self.s5_q = S.sb([128, 12, 16])
        self.s5_d = S.sb([128, 4])
        self.s5_carry = S.sb([128, 2, 16])
        self.wglu = S.sb([128, 4, 128])
        T = self.T
        self.NK = max(T, PAST + 128)
        self.NH = 4096 if T == 8192 else self.NK
        self.NB = self.NK // 128
        self.kT = S.sb([128, self.NK], BF16)
        self.kiT = S.sb([128, self.NH])
        self.vA = S.sb([128, self.NB, 130], BF16)
        self.bias8 = S.sb([128, 4, 512])
        self.qT = S.sb([128, 4, G], BF16)
        self.qiT = S.sb([128, 4, G])
        self.wiT = S.sb([128, 4])
        self.offrow = S.sb([33, 4 * G], BF16)
        self.kmax2 = S.sb([33, 2])
        self.cf = S.sb([33, 4])
        self.bs = S.sb([128, 16])
        self.Et = [S.sb([128, 4 * G], BF16) for _ in range(2)]
        self.Pt = [S.sb([128, 4 * G], BF16) for _ in range(2)]
        self.selT = [S.sb([128, G], BF16) for _ in range(2)]
        self.onesb = S.sb([33, 128], BF16)
        self.phase('setup')
        self.dsa_setup()
        self.gdn_S = S.sb([128, 4, 128])
        self.gdn_cw = S.sb([128, 12, 4])
        self.gdn_par = S.sb([128, 8])
        self.gdn_n = S.sb([128, 1])
        self.gdn_tail = S.sb([128, 12, 3])
        self.gla_S = S.sb([64, 4, 128])
        self.gla_w = S.sb([16, 256])
        self.gla_b = S.sb([64, 4])
        self.gla_n = S.sb([128, 1])
        self.resetm = S.sb([64, 4 * G])

    def pbank(self):
        t = self.pbanks[self.pb_i]
        self.pb_i = (self.pb_i + 1) % len(self.pbanks)
        return t

    def phase(self, name):
        S = self.S
        if name == self.cur_phase:
            return
        prev = list(self.phase_views.get(self.cur_phase, {}).values()) if self.cur_phase else []
        nxt = list(self.phase_views.get(name, {}).values())
        ft = self.fence_t
        S.op('dve', lambda h: h.memset(ft[0:1, 0:1], 0.0), reads=[], writes=[ft] + prev + nxt)
        self.fence_tok = ft.w
        self.cur_phase = name
        self.phase_views.setdefault(name, {})
        self.phase_off.setdefault(name, 0)
        self.scr_i = 0
        self.ws_i = 0

    def av(self, key, ncols):
        pv = self.phase_views[self.cur_phase]
        if key not in pv:
            off = self.phase_off[self.cur_phase]
            assert off + ncols <= self.ARENA, (self.cur_phase, key, off, ncols)
            t = Tk(self.arena.h[:, off:off + ncols])
            t.w = self.fence_tok
            pv[key] = t
            self.phase_off[self.cur_phase] = off + ncols
        return pv[key]

    def scratch(self):
        t = self.av(('scr', self.scr_i % 8), 512)
        self.scr_i += 1
        return t

    def wsp(self, reset=False):
        if reset:
            self.ws_i = 0
        t = self.av(('ws', self.ws_i), 512)
        self.ws_i += 1
        return t

    def sm(self):
        t = self.small[self.small_i]
        self.small_i = (self.small_i + 1) % len(self.small)
        return t

    def layer_setup(self, l):
        S = self.S
        I = self.I
        S.dma('sp', self.lng[:], I['lng'][l:l + 1, :].partition_broadcast(128), writes=[self.lng])
        S.dma('sp', self.lnb[:], I['lnb'][l:l + 1, :].partition_broadcast(128), writes=[self.lnb])
        S.dma('sp', self.gla_w[:], I['glaw'][l], writes=[self.gla_w])
        S.dma('sp', self.gla_b[:], I['glab'][l], writes=[self.gla_b])
        S.dma('sp', self.gla_n[:], I['glan'][l], writes=[self.gla_n])
        self.phase('setup')
        self.s5_setup(l)
        self.gdn_setup(l)

    FP32_CHUNKS = ('kvC', 'kvD', 'qi0', 'qi1', 'qi2', 'qi3')

    def precast(self):
        S = self.S
        I = self.I
        st32 = self.wch[0]
        n = 0

        def piece(src_ap, rows, cols3, dst_ap, key):
            nonlocal n
            a, c = cols3
            w16 = self.wchb[n % len(self.wchb)]
            S.dma('sp', st32[0:rows, 0:a, 0:c], src_ap, writes=[st32])
            eng = 'act' if n % 2 else 'dve'
            if eng == 'act':
                S.op('act', lambda h: h.activation(out=w16[0:rows, 0:a, 0:c], in_=st32[0:rows, 0:a, 0:c], func=AF.Copy), reads=[st32], writes=[w16])
            else:
                S.op('dve', lambda h: h.tensor_copy(out=w16[0:rows, 0:a, 0:c], in_=st32[0:rows, 0:a, 0:c]), reads=[st32], writes=[w16])
            S.dma('sp', dst_ap, w16[0:rows, 0:a, 0:c], reads=[w16], writes=[self.dk(key)])
            n += 1
        for l in range(DEPTH):
            for name, _ in CHUNKS:
                if name in self.FP32_CHUNKS:
                    continue
                ci = CIDX[name]
                piece(I['win'][l, ci], 128, (8, 128), self.winb[l, ci].rearrange("p (a c) -> p a c", c=128), ('winb', l, ci))
            for b in range(4):
                for dc in range(8):
                    if b == 1:
                        piece(I['wbr'][l, b, :, dc * 128:(dc + 1) * 128].rearrange("(a p) c -> p a c", p=64), 64, (8, 128),
                              self.wbrb[l, b, dc].rearrange("(p h) f -> p (h f)", h=2).rearrange("p (a c) -> p a c", c=128), ('wbrb', l, b, dc))
                    else:
                        piece(I['wbr'][l, b, :, dc * 128:(dc + 1) * 128].rearrange("(a p) c -> p a c", p=128), 128, (4, 128),
                              self.wbrb[l, b, dc].rearrange("p (a c) -> p a c", c=128), ('wbrb', l, b, dc))
            for qt in range(8):
                piece(I['wout'][l, :, qt * 128:(qt + 1) * 128].rearrange("(a p) c -> p a c", p=128), 128, (8, 128),
                      self.woutb[l, qt].rearrange("p (a c) -> p a c", c=128), ('woutb', l, qt))

    def load_w(self, l, name):
        S = self.S
        w = self.wch[self.wch_i]
        self.wch_i = (self.wch_i + 1) % len(self.wch)
        S.dma('sp', w[:], self.I['win'][l, CIDX[name]], writes=[w])
        return w

    BF16_PREFIX = ('zzz',)

    def load_wx(self, l, name):
        S = self.S
        if name in self.FP32_CHUNKS:
            return self.load_w(l, name), self.xT
        w = self.wchb[self.wchb_i]
        self.wchb_i = (self.wchb_i + 1) % len(self.wchb)
        S.dma('sp', w[:], self.winb[l, CIDX[name]].rearrange("p (a c) -> p a c", c=128), reads=[self.dk(('winb', l, CIDX[name]))], writes=[w])
        return w, self.xTb

    def proj_fm(self, l, name, ncols, gw):
        S = self.S
        w, xT = self.load_wx(l, name)
        pb = self.pbank()
        for k in range(8):
            mm = 128 if xT is self.xTb else ncols
            S.op('pe', lambda h, k=k: h.matmul(pb[0:mm, 0:gw], lhsT=w[:, k, 0:mm], rhs=xT[:, k, 0:gw],
                                               start=(k == 0), stop=(k == 7)), reads=[w, xT], writes=[pb])
        return pb

    def proj_tm(self, l, name, ncols, t0, tw):
        S = self.S
        w, xT = self.load_wx(l, name)
        pb = self.pbank()
        for k in range(8):
            S.op('pe', lambda h, k=k: h.matmul(pb[0:tw, 0:ncols], lhsT=xT[:, k, t0:t0 + tw], rhs=w[:, k, 0:ncols],
                                               start=(k == 0), stop=(k == 7)), reads=[w, xT], writes=[pb])
        return pb

    def run_seq(self, l, grp):
        S = self.S
        I, O = self.I, self.O
        T = self.T if grp == 'p' else DEC_SEQ
        G = min(self.G, T)
        TP = min(128, T)
        C = min(64, T)
        nG = T // G
        if l == 0:
            xin = I['xp'] if grp == 'p' else I['xs']
            xin_tk = self.dk('in')
        else:
            xin = self.y0p if grp == 'p' else self.y0s
            xin_tk = self.dk('y0' + grp)
        yout = (O['yp'] if grp == 'p' else O['ys']) if l == DEPTH - 1 else (self.y0p if grp == 'p' else self.y0s)
        yout_tk = self.dk('yout' + grp) if l == DEPTH - 1 else self.dk('y0' + grp)

        if grp == 'p':
            S.op('dve', lambda h: h.memset(self.gla_S[:], 0.0), writes=[self.gla_S])
        else:
            S.dma('sp', self.gla_S[:], I['sgla'][l].rearrange("h k v -> k h v"), writes=[self.gla_S])
        self.s5_init(l, grp)
        self.gdn_init(l, grp)
        self.dsa_init(l, grp)
        S.op('dve', lambda h: h.memset(self.resetm[:], 1.0), writes=[self.resetm])
        rm3 = self.resetm[:, 0:4 * G].rearrange("p (a c) -> p a c", c=C)
        S.op('dve', lambda h: h.memset(rm3[:, :, 0:1], 0.0), writes=[self.resetm])

        for gi in range(nG):
            g0 = gi * G
            self.phase('kv')
            xts = []
            for ti in range(G // TP):
                xt = self.av('xtm', D)
                xts.append(xt)
                S.dma('sp', xt[0:TP, :], xin[g0 + ti * TP: g0 + (ti + 1) * TP, :], reads=[xin_tk], writes=[xt])
                for half in range(2):
                    pb = self.pbank()
                    for kk in range(4):
                        k = half * 4 + kk
                        S.op('pe', lambda h, k=k, kk=kk: h.transpose(pb[:, kk * 128: kk * 128 + TP], xt[0:TP, k * 128:(k + 1) * 128], self.ident[0:TP, 0:TP]),
                             reads=[xt, self.cst], writes=[pb])
                    dst = self.xT[:, half * 4:(half + 1) * 4, ti * TP:(ti + 1) * TP]
                    src = pb[:, :].rearrange("p (a c) -> p a c", c=128)[:, :, 0:TP]
                    dstb = self.xTb[:, half * 4:(half + 1) * 4, ti * TP:(ti + 1) * TP]
                    S.op('dve', lambda h: h.tensor_copy(out=dst, in_=src), reads=[pb], writes=[self.xT])
                    S.op('act', lambda h: h.activation(out=dstb, in_=dst, func=AF.Copy), reads=[self.xT], writes=[self.xTb])
            self.branch_zero(G)
            self.phase('kv')
            self.kv_part(l, grp, g0, G, TP)
            self.phase('s5')
            self.s5_part(l, grp, g0, G, last=(gi == nG - 1))
            self.phase('gla')
            self.gla_part(l, grp, g0, G, C, last=(gi == nG - 1))
            self.phase('gdn')
            self.gdn_part(l, grp, g0, G, C, last=(gi == nG - 1))
            self.phase('dsa')
            self.dsa_part(l, grp, g0, G, TP)
            self.phase('out')
            self.out_stage(l, grp, g0, G, TP, xin, xin_tk, yout, yout_tk)

    def branch_zero(self, G):
        S = self.S
        S.op('dve', lambda h: h.memset(self.brT[:, :, 0:G], 0.0), writes=[self.brT])
        S.op('dve', lambda h: h.memset(self.brB[:, :, 0:G], 0.0), writes=[self.brB])

    def kv_part(self, l, grp, g0, G, TP):
        S = self.S
        O = self.O
        kbase = 0 if grp == 'p' else PAST
        for ti in range(G // TP):
            t0 = ti * TP
            tiles = []
            for nm, ncols, oname, key in (('kvA', 128, 'k', 'k_tm'), ('kvB', 128, 'v', 'v_tm'), ('kvC', 128, 'ki', 'c_tm')):
                pb = self.proj_tm(l, nm, ncols, t0, TP)
                sc = self.av(key, 128)
                S.op('act', lambda h: h.activation(out=sc[0:TP, 0:ncols], in_=pb[0:TP, 0:ncols], func=AF.Copy), reads=[pb], writes=[sc])
                oc = 64 if oname == 'ki' else 128
                S.dma('pool', O[oname + grp][l, g0 + t0: g0 + t0 + TP, :], sc[0:TP, 0:oc], reads=[sc], writes=[self.dk('o' + oname + grp)])
                tiles.append(sc)
            self.add_keys(tiles[0], tiles[1], tiles[2], TP, kbase + g0 + t0)
            pb = self.proj_tm(l, 'kvD', 12, t0, TP)
            S.op('dve', lambda h: h.tensor_copy(out=self.wiT[0:TP, 0:4], in_=pb[0:TP, 0:4]), reads=[pb], writes=[self.wiT])
        T = self.T if grp == 'p' else DEC_SEQ
        if g0 + G == T:
            for c in range(12):
                w, xTw = self.load_wx(l, 'cq%d' % c)
                pb = self.pbank()
                for k in range(8):
                    S.op('pe', lambda h, k=k: h.matmul(pb[0:3, 0:128], lhsT=xTw[:, k, G - 3:G], rhs=w[:, k, :],
                                                       start=(k == 0), stop=(k == 7)), reads=[w, xTw], writes=[pb])
                sc = self.scratch()
                S.op('act', lambda h: h.activation(out=sc[0:3, 0:128], in_=pb[0:3, 0:128], func=AF.Copy), reads=[pb], writes=[sc])
                S.dma('pool', O['conv' + grp][l, :, c * 128:(c + 1) * 128], sc[0:3, 0:128], reads=[sc], writes=[self.dk('oconv' + grp)])


    def range_reduce(self, x, n):
        S = self.S
        TWO_PI = 2.0 * math.pi
        ki = self.s5_ki
        kf = self.s5_kf
        S.op('dve', lambda h: h.tensor_scalar(out=kf[:, 0:n], in0=x, scalar1=1.0 / TWO_PI, scalar2=None, op0=ALU.mult), reads=[self.s5_ang], writes=[self.s5_kfT])
        S.op('dve', lambda h: h.tensor_copy(out=ki[:, 0:n], in_=kf[:, 0:n]), reads=[self.s5_kfT], writes=[self.s5_kiT])
        S.op('dve', lambda h: h.tensor_copy(out=kf[:, 0:n], in_=ki[:, 0:n]), reads=[self.s5_kiT], writes=[self.s5_kfT])
        S.op('dve', lambda h: h.scalar_tensor_tensor(out=x, in0=kf[:, 0:n], scalar=-TWO_PI, in1=x, op0=ALU.mult, op1=ALU.add), reads=[self.s5_kfT, self.s5_ang], writes=[self.s5_ang])
        S.op('dve', lambda h: h.tensor_scalar(out=kf[:, 0:n], in0=x, scalar1=math.pi, scalar2=-TWO_PI, op0=ALU.is_gt, op1=ALU.mult), reads=[self.s5_ang], writes=[self.s5_kfT])
        S.op('dve', lambda h: h.tensor_tensor(out=x, in0=x, in1=kf[:, 0:n], op=ALU.add), reads=[self.s5_kfT, self.s5_ang], writes=[self.s5_ang])
        S.op('dve', lambda h: h.tensor_scalar(out=kf[:, 0:n], in0=x, scalar1=-math.pi, scalar2=TWO_PI, op0=ALU.is_lt, op1=ALU.mult), reads=[self.s5_ang], writes=[self.s5_kfT])
        S.op('dve', lambda h: h.tensor_tensor(out=x, in0=x, in1=kf[:, 0:n], op=ALU.add), reads=[self.s5_kfT, self.s5_ang], writes=[self.s5_ang])

    def s5_setup(self, l):
        S = self.S
        I = self.I
        LC = self.LC
        q = self.s5_q
        if not hasattr(self, 's5_ang'):
            self.s5_ang = S.sb([128, 256])
            self.s5_kfT = S.sb([128, 256])
            self.s5_kiT = S.sb([128, 256], mybir.dt.int32)
            self.s5_kf = self.s5_kfT
            self.s5_ki = self.s5_kiT
        ang = self.s5_ang
        raw = self.sm3
        S.dma('sp', raw[:, :, :], I['s5p'][l], writes=[raw])
        S.dma('sp', self.s5_d[:, :], I['s5d'][l], writes=[self.s5_d])
        S.dma('sp', self.s5_b[:, :, :, :], I['s5b'][l].rearrange("v r p c f -> p (v r) c f"), writes=[self.s5_b])
        Q = lambda i: q[:, i, :]
        rq = [q]
        S.op('dve', lambda h: h.tensor_scalar(out=Q(4), in0=raw[:, 0, :], scalar1=-1e-4, scalar2=None, op0=ALU.min), reads=[raw], writes=rq)
        S.op('dve', lambda h: h.tensor_copy(out=Q(5), in_=raw[:, 1, :]), reads=[raw], writes=rq)
        S.op('act', lambda h: h.activation(out=Q(6), in_=raw[:, 2, :], func=AF.Exp), reads=[raw], writes=rq)
        S.op('dve', lambda h: h.tensor_tensor(out=Q(7), in0=Q(4), in1=Q(6), op=ALU.mult), reads=rq, writes=rq)
        S.op('act', lambda h: h.activation(out=Q(0), in_=Q(7), func=AF.Exp), reads=rq, writes=rq)
        S.op('dve', lambda h: h.tensor_tensor(out=Q(1), in0=Q(5), in1=Q(6), op=ALU.mult), reads=rq, writes=rq)
        S.op('dve', lambda h: h.tensor_copy(out=ang[:, 0:16], in_=Q(1)), reads=rq, writes=[ang])
        self.range_reduce(ang[:, 0:16], 16)
        S.op('dve', lambda h: h.tensor_copy(out=Q(1), in_=ang[:, 0:16]), reads=[ang], writes=rq)
        S.op('act', lambda h: h.activation(out=Q(8), in_=ang[:, 0:16], func=AF.Sin), reads=[ang], writes=rq)
        S.op('dve', lambda h: h.tensor_scalar(out=ang[:, 0:16], in0=ang[:, 0:16], scalar1=math.pi / 2, scalar2=None, op0=ALU.add), reads=[ang], writes=[ang])
        self.range_reduce(ang[:, 0:16], 16)
        S.op('act', lambda h: h.activation(out=Q(9), in_=ang[:, 0:16], func=AF.Sin), reads=[ang], writes=rq)
        S.op('dve', lambda h: h.tensor_tensor(out=Q(9), in0=Q(9), in1=Q(0), op=ALU.mult), reads=rq, writes=rq)
        S.op('dve', lambda h: h.tensor_tensor(out=Q(8), in0=Q(8), in1=Q(0), op=ALU.mult), reads=rq, writes=rq)
        S.op('dve', lambda h: h.tensor_scalar(out=Q(9), in0=Q(9), scalar1=-1.0, scalar2=None, op0=ALU.add), reads=rq, writes=rq)
        S.op('dve', lambda h: h.tensor_tensor(out=Q(10), in0=Q(4), in1=Q(4), op=ALU.mult), reads=rq, writes=rq)
        S.op('dve', lambda h: h.tensor_tensor(out=Q(11), in0=Q(5), in1=Q(5), op=ALU.mult), reads=rq, writes=rq)
        S.op('dve', lambda h: h.tensor_tensor(out=Q(10), in0=Q(10), in1=Q(11), op=ALU.add), reads=rq, writes=rq)
        S.op('dve', lambda h: h.reciprocal(out=Q(10), in_=Q(10)), reads=rq, writes=rq)
        S.op('dve', lambda h: h.tensor_tensor(out=Q(2), in0=Q(9), in1=Q(4), op=ALU.mult), reads=rq, writes=rq)
        S.op('dve', lambda h: h.tensor_tensor(out=Q(11), in0=Q(8), in1=Q(5), op=ALU.mult), reads=rq, writes=rq)
        S.op('dve', lambda h: h.tensor_tensor(out=Q(2), in0=Q(2), in1=Q(11), op=ALU.add), reads=rq, writes=rq)
        S.op('dve', lambda h: h.tensor_tensor(out=Q(2), in0=Q(2), in1=Q(10), op=ALU.mult), reads=rq, writes=rq)
        S.op('dve', lambda h: h.tensor_tensor(out=Q(3), in0=Q(8), in1=Q(4), op=ALU.mult), reads=rq, writes=rq)
        S.op('dve', lambda h: h.tensor_tensor(out=Q(11), in0=Q(9), in1=Q(5), op=ALU.mult), reads=rq, writes=rq)
        S.op('dve', lambda h: h.tensor_tensor(out=Q(3), in0=Q(3), in1=Q(11), op=ALU.subtract), reads=rq, writes=rq)
        S.op('dve', lambda h: h.tensor_tensor(out=Q(3), in0=Q(3), in1=Q(10), op=ALU.mult), reads=rq, writes=rq)
        for st_ in range(16):
            for which, tab in ((0, self.s5_sin), (1, self.s5_cos)):
                S.op('dve', lambda h, st_=st_, which=which: h.tensor_scalar(out=ang[:, 0:LC], in0=self.cst[:, 512:512 + LC], scalar1=q[:, 1, st_:st_ + 1],
                                                                         scalar2=(math.pi / 2 if which else 0.0), op0=ALU.mult, op1=ALU.add), reads=[self.cst, q], writes=[ang])
                self.range_reduce(ang[:, 0:LC], LC)
                S.op('act', lambda h, st_=st_, tab=tab: h.activation(out=tab[:, st_, :], in_=ang[:, 0:LC], func=AF.Sin), reads=[ang], writes=[tab])
        for st_ in range(16):
            c_r = self.scratch()
            c_i = self.scratch()
            S.dma('sp', c_r[:, 0:128], I['s5c'][l, 0, :, st_, :], writes=[c_r])
            S.dma('sp', c_i[:, 0:128], I['s5c'][l, 1, :, st_, :], writes=[c_i])
            t1 = self.scratch()
            S.op('dve', lambda h, st_=st_: h.tensor_scalar(out=t1[:, 0:128], in0=c_i[:, 0:128], scalar1=q[:, 3, st_:st_ + 1], scalar2=None, op0=ALU.mult), reads=[c_i, q], writes=[t1])
            S.op('dve', lambda h, st_=st_: h.scalar_tensor_tensor(out=self.s5_cr[:, st_, :], in0=c_r[:, 0:128], scalar=q[:, 2, st_:st_ + 1], in1=t1[:, 0:128], op0=ALU.mult, op1=ALU.subtract),
                 reads=[c_r, q, t1], writes=[self.s5_cr])
            S.op('dve', lambda h, st_=st_: h.tensor_scalar(out=t1[:, 0:128], in0=c_i[:, 0:128], scalar1=q[:, 2, st_:st_ + 1], scalar2=-1.0, op0=ALU.mult, op1=ALU.mult), reads=[c_i, q], writes=[t1])
            S.op('dve', lambda h, st_=st_: h.tensor_scalar(out=c_r[:, 0:128], in0=c_r[:, 0:128], scalar1=q[:, 3, st_:st_ + 1], scalar2=None, op0=ALU.mult), reads=[c_r, q], writes=[c_r])
            S.op('dve', lambda h, st_=st_: h.tensor_tensor(out=self.s5_ci[:, st_, :], in0=t1[:, 0:128], in1=c_r[:, 0:128], op=ALU.subtract), reads=[t1, c_r], writes=[self.s5_ci])

    def s5_init(self, l, grp):
        S = self.S
        q = self.s5_q
        car = self.s5_carry
        if grp == 'p':
            S.op('dve', lambda h: h.memset(car[:, :, :], 0.0), writes=[car])
            return
        h0 = self.sm3
        S.dma('sp', h0[:, 0:2, :], self.I['s5h0'][l], writes=[h0])
        m = self.sm()
        n2 = self.sm()
        t = self.sm()
        S.op('dve', lambda h: h.tensor_tensor(out=m[:, 0:16], in0=q[:, 2, :], in1=q[:, 2, :], op=ALU.mult), reads=[q], writes=[m])
        S.op('dve', lambda h: h.tensor_tensor(out=n2[:, 0:16], in0=q[:, 3, :], in1=q[:, 3, :], op=ALU.mult), reads=[q], writes=[n2])
        S.op('dve', lambda h: h.tensor_tensor(out=m[:, 0:16], in0=m[:, 0:16], in1=n2[:, 0:16], op=ALU.add), reads=[m, n2], writes=[m])
        S.op('dve', lambda h: h.reciprocal(out=m[:, 0:16], in_=m[:, 0:16]), reads=[m], writes=[m])
        S.op('dve', lambda h: h.tensor_tensor(out=t[:, 0:16], in0=h0[:, 0, :], in1=q[:, 2, :], op=ALU.mult), reads=[h0, q], writes=[t])
        S.op('dve', lambda h: h.tensor_tensor(out=n2[:, 0:16], in0=h0[:, 1, :], in1=q[:, 3, :], op=ALU.mult), reads=[h0, q], writes=[n2])
        S.op('dve', lambda h: h.tensor_tensor(out=t[:, 0:16], in0=t[:, 0:16], in1=n2[:, 0:16], op=ALU.add), reads=[t, n2], writes=[t])
        S.op('dve', lambda h: h.tensor_tensor(out=car[:, 0, :], in0=t[:, 0:16], in1=m[:, 0:16], op=ALU.mult), reads=[t, m], writes=[car])
        S.op('dve', lambda h: h.tensor_tensor(out=t[:, 0:16], in0=h0[:, 1, :], in1=q[:, 2, :], op=ALU.mult), reads=[h0, q], writes=[t])
        S.op('dve', lambda h: h.tensor_tensor(out=n2[:, 0:16], in0=h0[:, 0, :], in1=q[:, 3, :], op=ALU.mult), reads=[h0, q], writes=[n2])
        S.op('dve', lambda h: h.tensor_tensor(out=t[:, 0:16], in0=t[:, 0:16], in1=n2[:, 0:16], op=ALU.subtract), reads=[t, n2], writes=[t])
        S.op('dve', lambda h: h.tensor_tensor(out=car[:, 1, :], in0=t[:, 0:16], in1=m[:, 0:16], op=ALU.mult), reads=[t, m], writes=[car])

    def s5_final(self, l, grp):
        S = self.S
        q = self.s5_q
        car = self.s5_carry
        o = self.sm3
        t = self.sm()
        S.op('dve', lambda h: h.tensor_tensor(out=o[:, 0, :], in0=car[:, 0, :], in1=q[:, 2, :], op=ALU.mult), reads=[car, q], writes=[o])
        S.op('dve', lambda h: h.tensor_tensor(out=t[:, 0:16], in0=car[:, 1, :], in1=q[:, 3, :], op=ALU.mult), reads=[car, q], writes=[t])
        S.op('dve', lambda h: h.tensor_tensor(out=o[:, 0, :], in0=o[:, 0, :], in1=t[:, 0:16], op=ALU.subtract), reads=[o, t], writes=[o])
        S.op('dve', lambda h: h.tensor_tensor(out=o[:, 1, :], in0=car[:, 0, :], in1=q[:, 3, :], op=ALU.mult), reads=[car, q], writes=[o])
        S.op('dve', lambda h: h.tensor_tensor(out=t[:, 0:16], in0=car[:, 1, :], in1=q[:, 2, :], op=ALU.mult), reads=[car, q], writes=[t])
        S.op('dve', lambda h: h.tensor_tensor(out=o[:, 1, :], in0=o[:, 1, :], in1=t[:, 0:16], op=ALU.add), reads=[o, t], writes=[o])
        S.dma('pool', self.O['s5' + grp][l], o[:, 0:2, :], reads=[o], writes=[self.dk('os5' + grp)])

    def s5_part(self, l, grp, g0, G, last):
        S = self.S
        I = self.I
        LC = min(self.LC, G)
        ug = self.av('ug', 4 * G)
        ug3 = ug.h[:, 0:4 * G].rearrange("p (a c) -> p a c", c=G)
        q = self.s5_q
        car = self.s5_carry
        for c in range(4):
            pb = self.proj_fm(l, 'au%d' % c, 128, G)
            S.op('act' if c % 2 else 'dve', (lambda h, c=c: h.activation(out=ug3[:, c, 0:G], in_=pb[:, 0:G], func=AF.Copy)) if c % 2 else (lambda h, c=c: h.tensor_copy(out=ug3[:, c, 0:G], in_=pb[:, 0:G])),
                 reads=[pb], writes=[ug])
        yT = self.wsp(True)
        y3 = yT[:, 0:4 * G].rearrange("p (a c) -> p a c", c=G)
        for s0 in range(0, G, LC):
            for c in range(4):
                py = self.pacc[c % 2]
                for s4 in range(4):
                    st_ = 4 * c + s4
                    brow = slice(32 * s4, 32 * s4 + 32) if s4 < 3 else slice(64, 128)
                    bvar = 0 if s4 < 3 else 2
                    pr = self.pbank()
                    pi = self.pbank()
                    S.op('pe', lambda h: h.matmul(pr[:, 0:LC], lhsT=self.s5_b[brow, bvar + 0, c, :], rhs=ug3[brow, c, s0:s0 + LC], start=True, stop=True),
                         reads=[self.s5_b, ug], writes=[pr])
                    S.op('pe', lambda h: h.matmul(pi[:, 0:LC], lhsT=self.s5_b[brow, bvar + 1, c, :], rhs=ug3[brow, c, s0:s0 + LC], start=True, stop=True),
                         reads=[self.s5_b, ug], writes=[pi])
                    cs = self.s5_cos[:, st_, 0:LC]
                    sn = self.s5_sin[:, st_, 0:LC]
                    a = self.scratch(); b = self.scratch(); xr = self.scratch(); xi = self.scratch()
                    rt = [self.s5_cos, self.s5_sin]
                    S.op('dve', lambda h: h.tensor_tensor(out=a[:, 0:LC], in0=pr[:, 0:LC], in1=cs, op=ALU.mult), reads=[pr] + rt, writes=[a])
                    S.op('dve', lambda h: h.tensor_tensor(out=b[:, 0:LC], in0=pi[:, 0:LC], in1=sn, op=ALU.mult), reads=[pi] + rt, writes=[b])
                    S.op('pool', lambda h: h.tensor_tensor(out=xr[:, 0:LC], in0=a[:, 0:LC], in1=b[:, 0:LC], op=ALU.add), reads=[a, b], writes=[xr])
                    a2 = self.scratch(); b2 = self.scratch()
                    S.op('dve', lambda h: h.tensor_tensor(out=a2[:, 0:LC], in0=pi[:, 0:LC], in1=cs, op=ALU.mult), reads=[pi] + rt, writes=[a2])
                    S.op('dve', lambda h: h.tensor_tensor(out=b2[:, 0:LC], in0=pr[:, 0:LC], in1=sn, op=ALU.mult), reads=[pr] + rt, writes=[b2])
                    S.op('pool', lambda h: h.tensor_tensor(out=xi[:, 0:LC], in0=a2[:, 0:LC], in1=b2[:, 0:LC], op=ALU.subtract), reads=[a2, b2], writes=[xi])
                    rho = q[:, 0, st_:st_ + 1].to_broadcast([128, LC])
                    S.op('dve', lambda h: h.tensor_tensor_scan(out=xr[:, 0:LC], data0=rho, data1=xr[:, 0:LC], initial=car[:, 0, st_:st_ + 1], op0=ALU.mult, op1=ALU.add),
                         reads=[q, xr, car], writes=[xr])
                    S.op('dve', lambda h: h.tensor_tensor_scan(out=xi[:, 0:LC], data0=rho, data1=xi[:, 0:LC], initial=car[:, 1, st_:st_ + 1], op0=ALU.mult, op1=ALU.add),
                         reads=[q, xi, car], writes=[xi])
                    S.op('pool', lambda h: h.tensor_tensor(out=a[:, 0:LC], in0=xr[:, 0:LC], in1=cs, op=ALU.mult), reads=[xr] + rt, writes=[a])
                    S.op('pool', lambda h: h.tensor_tensor(out=b[:, 0:LC], in0=xi[:, 0:LC], in1=sn, op=ALU.mult), reads=[xi] + rt, writes=[b])
                    S.op('dve', lambda h: h.tensor_tensor(out=a[:, 0:LC], in0=a[:, 0:LC], in1=b[:, 0:LC], op=ALU.subtract), reads=[a, b], writes=[a])
                    S.op('pool', lambda h: h.tensor_tensor(out=a2[:, 0:LC], in0=xr[:, 0:LC], in1=sn, op=ALU.mult), reads=[xr] + rt, writes=[a2])
                    S.op('pool', lambda h: h.tensor_tensor(out=b2[:, 0:LC], in0=xi[:, 0:LC], in1=cs, op=ALU.mult), reads=[xi] + rt, writes=[b2])
                    S.op('dve', lambda h: h.tensor_tensor(out=a2[:, 0:LC], in0=a2[:, 0:LC], in1=b2[:, 0:LC], op=ALU.add), reads=[a2, b2], writes=[a2])
                    S.op('dve', lambda h: h.tensor_copy(out=car[:, 0, st_:st_ + 1], in_=a[:, LC - 1:LC]), reads=[a], writes=[car])
                    S.op('dve', lambda h: h.tensor_copy(out=car[:, 1, st_:st_ + 1], in_=a2[:, LC - 1:LC]), reads=[a2], writes=[car])
                    S.op('pe', lambda h: h.matmul(py[:, 0:LC], lhsT=self.s5_cr[:, st_, :], rhs=a[:, 0:LC], start=(s4 == 0), stop=False), reads=[self.s5_cr, a], writes=[py])
                    S.op('pe', lambda h: h.matmul(py[:, 0:LC], lhsT=self.s5_ci[:, st_, :], rhs=a2[:, 0:LC], start=False, stop=(s4 == 3)), reads=[self.s5_ci, a2], writes=[py])
                S.op('dve', lambda h: h.scalar_tensor_tensor(out=y3[:, c, s0:s0 + LC], in0=ug3[:, c, s0:s0 + LC], scalar=self.s5_d[:, c:c + 1], in1=py[:, 0:LC], op0=ALU.mult, op1=ALU.add),
                     reads=[ug, self.s5_d, py], writes=[yT])
        if last:
            self.s5_final(l, grp)
        S.op('act', lambda h: h.activation(out=ug3[:, :, 0:G], in_=y3, func=AF.Gelu), reads=[yT], writes=[ug])
        for c in range(4):
            sgs = []
            for oc in (c + 4, c):
                S.dma('sp', self.wglu[:, :, :], I['wglu'][l, :, oc * 128:(oc + 1) * 128].rearrange("(a p) f -> p a f", p=128), writes=[self.wglu])
                pg = self.pbank()
                for kc in range(4):
                    S.op('pe', lambda h, kc=kc: h.matmul(pg[:, 0:G], lhsT=self.wglu[:, kc, :], rhs=ug3[:, kc, 0:G], start=(kc == 0), stop=(kc == 3)), reads=[self.wglu, ug], writes=[pg])
                sgs.append(pg)
            sg = self.scratch()
            S.op('act', lambda h: h.activation(out=sg[:, 0:G], in_=sgs[0][:, 0:G], func=AF.Sigmoid), reads=[sgs[0]], writes=[sg])
            va = self.scratch()
            S.op('dve', lambda h: h.tensor_tensor(out=va[:, 0:G], in0=sgs[1][:, 0:G], in1=sg[:, 0:G], op=ALU.mult), reads=[sgs[1], sg], writes=[va])
            pgt = self.proj_fm(l, 'ag%d' % c, 128, G)
            sg2 = self.scratch()
            S.op('act', lambda h: h.activation(out=sg2[:, 0:G], in_=pgt[:, 0:G], func=AF.Silu), reads=[pgt], writes=[sg2])
            S.op('dve', lambda h, c=c: h.tensor_tensor(out=self.brT[:, c, 0:G], in0=va[:, 0:G], in1=sg2[:, 0:G], op=ALU.mult), reads=[va, sg2], writes=[self.brT])


    def dsa_setup(self):
        S = self.S
        I = self.I
        cst = self.cst
        rb = self.sm()
        oh = self.av('oh', 384)
        S.dma('sp', rb[0:32, 0:8], I['relb'][:, :], writes=[rb])
        S.dma('sp', oh[0:32, 0:384], I['oh'][:, :], writes=[oh])
        pf = self.pbank()
        S.op('pe', lambda h: h.matmul(pf[0:8, 0:384], lhsT=rb[0:32, 0:8], rhs=oh[0:32, 0:384], start=True, stop=True), reads=[rb, oh], writes=[pf])
        f8 = self.av('f8', 384)
        cm = self.sm()
        S.op('dve', lambda h: h.tensor_copy(out=cm[0:8, 0:1], in_=pf[0:8, 382:383]), reads=[pf], writes=[cm])
        S.op('dve', lambda h: h.tensor_scalar(out=f8[0:8, 0:384], in0=pf[0:8, 0:384], scalar1=cm[0:8, 0:1], scalar2=8.0, op0=ALU.subtract, op1=ALU.mult), reads=[pf, cm], writes=[f8])
        S.op('dve', lambda h: h.tensor_reduce(out=cm[0:8, 1:2], in_=f8[0:8, 0:383], axis=AX.X, op=ALU.max, negate=True), reads=[f8], writes=[cm])
        fdk = self.dk('fd')
        S.dma('sp', self.fd[:, :], f8[0:8, 0:384], reads=[f8], writes=[fdk])
        S.dma('sp', self.cfd[:, :], cm[0:8, 1:2], reads=[cm], writes=[fdk])
        S.op('dve', lambda h: h.memset(self.cf[:, :], 0.0), writes=[self.cf])
        S.dma('sp', self.cf[0:1, 0:4], self.cfd[0:4, :].rearrange("a b -> b a"), reads=[fdk], writes=[self.cf])
        S.dma('sp', self.cf[32:33, 0:4], self.cfd[4:8, :].rearrange("a b -> b a"), reads=[fdk], writes=[self.cf])
        for lrow in range(128):
            for g in range(2):
                src = bass.AP(tensor=self.fd.tensor, offset=4 * g * 384 + 127 - lrow, ap=[[0, 1], [128, 2], [384, 4], [1, 128]])
                dst = self.bias8[lrow:lrow + 1, 2 * g:2 * g + 2, :].rearrange("p k (a c) -> p k a c", c=128)
                S.dma('sp' if g else 'pool', dst, src, reads=[fdk], writes=[self.bias8])
        S.op('dve', lambda h: h.memset(self.onesb[:, :], 1.0), writes=[self.onesb])

    def add_keys(self, k_tm, v_tm, c_tm, n, key0):
        S = self.S
        cst = self.cst
        blk = key0 // 128
        pk = self.pbank()
        S.op('pe', lambda h: h.transpose(pk[:, 0:n], k_tm[0:n, 0:128], self.ident[0:n, 0:n]), reads=[k_tm, cst], writes=[pk])
        S.op('act', lambda h: h.activation(out=self.kT[:, key0:key0 + n], in_=pk[:, 0:n], func=AF.Copy), reads=[pk], writes=[self.kT])
        sq = self.scratch()
        S.op('act', lambda h: h.activation(out=sq[:, 0:n], in_=pk[:, 0:n], func=AF.Square), reads=[pk], writes=[sq])
        pn = self.pbank()
        S.op('pe', lambda h: h.matmul(pn[0:33, 0:n], lhsT=cst[:, 896:929], rhs=sq[:, 0:n], start=True, stop=True), reads=[cst, sq], writes=[pn])
        S.op('dve', lambda h: h.tensor_reduce(out=self.kmax2[:, 1:2], in_=pn[0:33, 0:n], axis=AX.X, op=ALU.max), reads=[pn], writes=[self.kmax2])
        S.op('dve', lambda h: h.tensor_tensor(out=self.kmax2[:, 0:1], in0=self.kmax2[:, 0:1], in1=self.kmax2[:, 1:2], op=ALU.max), reads=[self.kmax2], writes=[self.kmax2])
        va = self.vA[0:n, blk, :].rearrange("p (g c) -> p g c", c=65)[:, :, 0:64]
        S.op('dve', lambda h: h.tensor_copy(out=va, in_=v_tm[0:n, 0:128].rearrange("p (g c) -> p g c", c=64)), reads=[v_tm], writes=[self.vA])
        pc = self.pbank()
        S.op('pe', lambda h: h.transpose(pc[:, 0:n], c_tm[0:n, 0:128], self.ident[0:n, 0:n]), reads=[c_tm, cst], writes=[pc])
        if key0 < self.NH:
            S.op('dve', lambda h: h.tensor_copy(out=self.kiT[0:64, key0:key0 + n], in_=pc[0:64, 0:n]), reads=[pc], writes=[self.kiT])
        else:
            S.op('dve', lambda h: h.tensor_copy(out=self.kiT[64:128, key0 - self.NH:key0 - self.NH + n], in_=pc[64:128, 0:n]), reads=[pc], writes=[self.kiT])

    def dsa_init(self, l, grp):
        S = self.S
        I = self.I
        S.op('dve', lambda h: h.memset(self.vA[:, :, :], 1.0), writes=[self.vA])
        S.op('dve', lambda h: h.memset(self.kmax2[:, :], 0.0), writes=[self.kmax2])
        if grp == 's':
            self.phase('kv')
            for b in range(PAST // 128):
                k_tm = self.av('k_tm', 128); v_tm = self.av('v_tm', 128); c_tm = self.av('c_tm', 128)
                S.dma('sp', k_tm[:, 0:128], I['ck'][l, b * 128:(b + 1) * 128, :], writes=[k_tm])
                S.dma('sp', v_tm[:, 0:128], I['cv'][l, b * 128:(b + 1) * 128, :], writes=[v_tm])
                S.dma('sp', c_tm[:, 0:128], I['cki'][l, b * 128:(b + 1) * 128, :], writes=[c_tm])
                self.add_keys(k_tm, v_tm, c_tm, 128, b * 128)

    def dsa_part(self, l, grp, g0, G, TP):
        S = self.S
        cst = self.cst
        kbase = 0 if grp == 'p' else PAST
        Qa = kbase + g0
        L = Qa + TP
        NKtot = (self.T if grp == 'p' else PAST + DEC_SEQ)
        KTOP = float(min(TOPK_MAX, NKtot // 4))
        W = 4 * TP
        sc = self.av('sc', 8192)
        junk = self.av('junk', 2048)
        junk8 = junk.h.bitcast(U8)
        offf_v = junk.h[0:33, 0:512]
        osb_v = junk.h[0:65, 512:1024]
        onn_v = junk.h[0:64, 1024:1536]
        qsq_v = junk.h[:, 1536:1536 + 4 * G].rearrange("p (a c) -> p a c", c=G)
        bs = self.bs
        for hh in range(4):
            pb = self.proj_fm(l, 'qq%d' % hh, 128, TP)
            S.op('act', lambda h, hh=hh: h.activation(out=self.qT[:, hh, 0:TP], in_=pb[:, 0:TP], func=AF.Copy), reads=[pb], writes=[self.qT])
            S.op('act', lambda h, hh=hh: h.activation(out=qsq_v[:, hh, 0:TP], in_=pb[:, 0:TP], func=AF.Square), reads=[pb], writes=[junk])
        for hh in range(4):
            pb = self.proj_fm(l, 'qi%d' % hh, 128, TP)
            S.op('dve', lambda h, hh=hh: h.tensor_copy(out=self.qiT[:, hh, 0:TP], in_=pb[:, 0:TP]), reads=[pb], writes=[self.qiT])
        pq = self.pbank()
        S.op('pe', lambda h: h.matmul(pq[0:33, 0:W], lhsT=cst[:, 896:929], rhs=qsq_v[:, :, 0:TP], start=True, stop=True), reads=[cst, junk], writes=[pq])
        S.op('act', lambda h: h.activation(out=offf_v[:, 0:W], in_=pq[0:33, 0:W], func=AF.Sqrt, scale=self.kmax2[:, 0:1]), reads=[pq, self.kmax2], writes=[junk])
        S.op('dve', lambda h: h.tensor_tensor(out=self.offrow[:, 0:W].rearrange("p (a c) -> p a c", c=TP), in0=self.cf[:, :].unsqueeze(2).to_broadcast([33, 4, TP]),
                                               in1=offf_v[:, 0:W].rearrange("p (a c) -> p a c", c=TP), op=ALU.subtract), reads=[self.cf, junk], writes=[self.offrow])
        for b0 in range(0, L, 512):
            bw = min(512, L - b0)
            if b0 < self.NH:
                rows = slice(0, 64); c0 = b0
            else:
                rows = slice(64, 128); c0 = b0 - self.NH
            for hh in range(4):
                ps = self.pbank()
                S.op('pe', lambda h, hh=hh: h.matmul(ps[0:TP, 0:bw], lhsT=self.qiT[rows, hh, 0:TP], rhs=self.kiT[rows, c0:c0 + bw], start=True, stop=True), reads=[self.qiT, self.kiT], writes=[ps])
                if hh == 0:
                    S.op('dve', lambda h: h.tensor_scalar(out=sc[0:TP, b0:b0 + bw], in0=ps[0:TP, 0:bw], scalar1=0.0, scalar2=self.wiT[0:TP, 0:1], op0=ALU.max, op1=ALU.mult), reads=[ps, self.wiT], writes=[sc])
                else:
                    tmp = junk.h[:, 512 * (hh % 2):512 * (hh % 2) + 512]
                    S.op('act', lambda h: h.activation(out=tmp[0:TP, 0:bw], in_=ps[0:TP, 0:bw], func=AF.Relu), reads=[ps], writes=[junk])
                    S.op('dve', lambda h, hh=hh: h.scalar_tensor_tensor(out=sc[0:TP, b0:b0 + bw], in0=tmp[0:TP, 0:bw], scalar=self.wiT[0:TP, hh:hh + 1], in1=sc[0:TP, b0:b0 + bw], op0=ALU.mult, op1=ALU.add),
                         reads=[junk, self.wiT, sc], writes=[sc])
        S.op('dve', lambda h: h.tensor_reduce(out=bs[0:TP, 0:1], in_=sc[0:TP, 0:L], axis=AX.X, op=ALU.max, apply_absolute_value=True), reads=[sc], writes=[bs])
        if grp == 'p' and TP == 128:
            S.op('dve', lambda h: h.memset(sc[0:64, L - 64:L], -1e30), writes=[sc])
        S.op('dve', lambda h: h.tensor_scalar(out=bs[0:TP, 2:3], in0=bs[0:TP, 0:1], scalar1=1.0, scalar2=None, op0=ALU.add), reads=[bs], writes=[bs])
        S.op('dve', lambda h: h.tensor_scalar(out=bs[0:TP, 1:2], in0=bs[0:TP, 2:3], scalar1=-1.0, scalar2=None, op0=ALU.mult), reads=[bs], writes=[bs])
        for it in range(24):
            S.op('dve', lambda h: h.tensor_scalar(out=bs[0:TP, 3:4], in0=bs[0:TP, 1:2], scalar1=bs[0:TP, 2:3], scalar2=0.5, op0=ALU.add, op1=ALU.mult), reads=[bs], writes=[bs])
            S.op('dve', lambda h: h.tensor_scalar(out=junk8[0:TP, 0:L], in0=sc[0:TP, 0:L], scalar1=bs[0:TP, 3:4], scalar2=None, op0=ALU.is_ge, op1=ALU.add, accum_out=bs[0:TP, 4:5]),
                 reads=[sc, bs], writes=[junk, bs])
            S.op('dve', lambda h: h.tensor_scalar(out=bs[0:TP, 5:6], in0=bs[0:TP, 4:5], scalar1=KTOP, scalar2=None, op0=ALU.is_ge), reads=[bs], writes=[bs])
            S.op('dve', lambda h: h.tensor_tensor(out=bs[0:TP, 6:7], in0=bs[0:TP, 3:4], in1=bs[0:TP, 1:2], op=ALU.subtract), reads=[bs], writes=[bs])
            S.op('dve', lambda h: h.tensor_tensor(out=bs[0:TP, 7:8], in0=bs[0:TP, 2:3], in1=bs[0:TP, 3:4], op=ALU.subtract), reads=[bs], writes=[bs])
            S.op('dve', lambda h: h.scalar_tensor_tensor(out=bs[0:TP, 1:2], in0=bs[0:TP, 6:7], scalar=bs[0:TP, 5:6], in1=bs[0:TP, 1:2], op0=ALU.mult, op1=ALU.add), reads=[bs], writes=[bs])
            S.op('dve', lambda h: h.scalar_tensor_tensor(out=bs[0:TP, 2:3], in0=bs[0:TP, 7:8], scalar=bs[0:TP, 5:6], in1=bs[0:TP, 3:4], op0=ALU.mult, op1=ALU.add), reads=[bs], writes=[bs])
        S.op('dve', lambda h: h.tensor_scalar(out=sc[0:TP, 0:L], in0=sc[0:TP, 0:L], scalar1=bs[0:TP, 1:2], scalar2=None, op0=ALU.is_ge), reads=[sc, bs], writes=[sc])
        nblk = (L + 127) // 128
        OT = self.pacc
        for kb in range(nblk):
            bw = min(128, L - kb * 128)
            k0 = kb * 128
            pst = self.pbank()
            S.op('pe', lambda h: h.transpose(pst[0:bw, 0:TP], sc[0:TP, k0:k0 + bw], self.ident[0:TP, 0:TP]), reads=[sc, cst], writes=[pst])
            sT = self.selT[kb % 2]
            S.op('act', lambda h: h.activation(out=sT[0:bw, 0:TP], in_=pst[0:bw, 0:TP], func=AF.Copy), reads=[pst], writes=[sT])
            kind = 0 if k0 == Qa else (1 if k0 == Qa - 128 else None)
            for g in range(2):
                ps = self.pbank()
                S.op('pe', lambda h, g=g: h.matmul(ps[0:bw, 0:W], lhsT=self.kT[64 * g:64 * g + 64, k0:k0 + bw], rhs=self.qT[64 * g:64 * g + 64, :, 0:TP], start=True, stop=False),
                     reads=[self.kT, self.qT], writes=[ps])
                S.op('pe', lambda h, g=g: h.matmul(ps[0:bw, 0:W], lhsT=self.onesb[32 * g:32 * g + 1, 0:bw], rhs=self.offrow[32 * g:32 * g + 1, 0:W], start=False, stop=True),
                     reads=[self.onesb, self.offrow], writes=[ps])
                Et = self.Et[g]
                Pt = self.Pt[g]
                if kind is not None:
                    tmp = junk.h[:, 1536:2048]
                    bt = self.bias8[0:bw, 2 * g + kind, :].rearrange("p (a c) -> p a c", c=128)[:, :, 0:TP]
                    S.op('dve', lambda h: h.tensor_tensor(out=tmp[0:bw, 0:W].rearrange("p (a c) -> p a c", c=TP), in0=ps[0:bw, 0:W].rearrange("p (a c) -> p a c", c=TP), in1=bt, op=ALU.add),
                         reads=[ps, self.bias8], writes=[junk])
                    S.op('act', lambda h: h.activation(out=Et[0:bw, 0:W], in_=tmp[0:bw, 0:W], func=AF.Exp, scale=0.125), reads=[junk], writes=[Et])
                else:
                    S.op('act', lambda h: h.activation(out=Et[0:bw, 0:W], in_=ps[0:bw, 0:W], func=AF.Exp, scale=0.125), reads=[ps], writes=[Et])
                S.op('dve', lambda h: h.tensor_tensor(out=Pt[0:bw, 0:W].rearrange("p (a c) -> p a c", c=TP), in0=Et[0:bw, 0:W].rearrange("p (a c) -> p a c", c=TP),
                                                       in1=sT[0:bw, 0:TP].unsqueeze(1).to_broadcast([bw, 4, TP]), op=ALU.mult), reads=[Et, sT], writes=[Pt])
                S.op('pe', lambda h, g=g: h.matmul(OT[g][0:65, 0:W], lhsT=self.vA[0:bw, kb, 65 * g:65 * g + 65], rhs=Pt[0:bw, 0:W], start=(kb == 0), stop=(kb == nblk - 1)),
                     reads=[self.vA, Pt], writes=[OT[g]])
        for g in range(2):
            osb = osb_v
            S.op('act', lambda h: h.activation(out=osb[0:65, 0:W], in_=OT[g][0:65, 0:W], func=AF.Copy), reads=[OT[g]], writes=[junk])
            S.op('dve', lambda h: h.reciprocal(out=osb[64:65, 0:W], in_=osb[64:65, 0:W]), reads=[junk], writes=[junk])
            pbc = self.pbank()
            S.op('pe', lambda h: h.matmul(pbc[0:64, 0:W], lhsT=cst[64:65, 256:320], rhs=osb[64:65, 0:W], start=True, stop=True), reads=[cst, junk], writes=[pbc])
            S.op('dve', lambda h: h.tensor_tensor(out=onn_v[:, 0:W], in0=osb[0:64, 0:W], in1=pbc[0:64, 0:W], op=ALU.mult), reads=[junk, pbc], writes=[junk])
            for hh in range(4):
                hd = 4 * g + hh
                pg = self.proj_fm(l, 'bg%d' % hd, 64, TP)
                sg = junk.h[:, 1536 + 128 * (hh % 2):1536 + 128 * (hh % 2) + 128]
                S.op('act', lambda h: h.activation(out=sg[0:64, 0:TP], in_=pg[0:64, 0:TP], func=AF.Silu), reads=[pg], writes=[junk])
                S.op('dve', lambda h, hh=hh, hd=hd: h.tensor_tensor(out=self.brB[:, hd, 0:TP], in0=onn_v[:, hh * TP:(hh + 1) * TP], in1=sg[0:64, 0:TP], op=ALU.mult), reads=[junk], writes=[self.brB])

    def gdn_setup(self, l):
        S = self.S
        I = self.I
        S.dma('sp', self.gdn_cw[:, :, :], I['gcw'][l], writes=[self.gdn_cw])
        S.dma('sp', self.gdn_par[:, :], I['gpar'][l], writes=[self.gdn_par])
        S.dma('sp', self.gdn_n[:, :], I['gnorm'][l], writes=[self.gdn_n])
        S.op('act', lambda h: h.activation(out=self.gdn_par[:, 0:4], in_=self.gdn_par[:, 0:4], func=AF.Exp), reads=[self.gdn_par], writes=[self.gdn_par])
        S.op('dve', lambda h: h.tensor_scalar(out=self.gdn_par[:, 0:4], in0=self.gdn_par[:, 0:4], scalar1=-1.0, scalar2=None, op0=ALU.mult), reads=[self.gdn_par], writes=[self.gdn_par])

    def gdn_init(self, l, grp):
        S = self.S
        if grp == 'p':
            S.op('dve', lambda h: h.memset(self.gdn_S[:, :, :], 0.0), writes=[self.gdn_S])
            S.op('dve', lambda h: h.memset(self.gdn_tail[:, :, :], 0.0), writes=[self.gdn_tail])
        else:
            S.dma('sp', self.gdn_S[:, :, :], self.I['sgdn'][l].rearrange("h k v -> k h v"), writes=[self.gdn_S])
            S.dma('sp', self.gdn_tail[:, :, :], self.I['sconv'][l], writes=[self.gdn_tail])

    def gdn_part(self, l, grp, g0, G, C, last):
        S = self.S
        cst = self.cst
        nC = G // C
        GE = G + 3
        xext = self.av('xext', 12 * GE)
        x3 = xext[:, 0:12 * GE].rearrange("p (a c) -> p a c", c=GE)
        qkv = self.av('qkv', 12 * G)
        q3 = qkv[:, 0:12 * G].rearrange("p (a c) -> p a c", c=G)
        oT = self.av('oT', 4 * G)
        o3 = oT[:, 0:4 * G].rearrange("p (a c) -> p a c", c=G)
        T1 = self.av('T1', 1024)
        T2 = self.av('T2', 1024)
        S.op('dve', lambda h: h.tensor_copy(out=x3[:, :, 0:3], in_=self.gdn_tail[:, :, :]), reads=[self.gdn_tail], writes=[xext])
        for c in range(12):
            pb = self.proj_fm(l, 'cq%d' % c, 128, G)
            S.op('act' if c % 2 else 'dve', (lambda h, c=c: h.activation(out=x3[:, c, 3:3 + G], in_=pb[:, 0:G], func=AF.Copy)) if c % 2 else (lambda h, c=c: h.tensor_copy(out=x3[:, c, 3:3 + G], in_=pb[:, 0:G])),
                 reads=[pb], writes=[xext])
        S.op('dve', lambda h: h.tensor_copy(out=self.gdn_tail[:, :, :], in_=x3[:, :, G:G + 3]), reads=[xext], writes=[self.gdn_tail])
        for c in range(12):
            S.op('dve', lambda h, c=c: h.tensor_scalar(out=q3[:, c, :], in0=x3[:, c, 0:G], scalar1=self.gdn_cw[:, c, 0:1], scalar2=None, op0=ALU.mult), reads=[xext, self.gdn_cw], writes=[qkv])
            for i in range(1, 4):
                S.op('dve', lambda h, c=c, i=i: h.scalar_tensor_tensor(out=q3[:, c, :], in0=x3[:, c, i:i + G], scalar=self.gdn_cw[:, c, i:i + 1], in1=q3[:, c, :], op0=ALU.mult, op1=ALU.add),
                     reads=[xext, self.gdn_cw, qkv], writes=[qkv])
        S.op('act', lambda h: h.activation(out=qkv[:, 0:12 * G], in_=qkv[:, 0:12 * G], func=AF.Silu), reads=[qkv], writes=[qkv])
        W8 = 8 * G
        S.op('act', lambda h: h.activation(out=T1[:, 0:W8], in_=qkv[:, 0:W8], func=AF.Square), reads=[qkv], writes=[T1])
        for b0 in range(0, W8, 512):
            bw = min(512, W8 - b0)
            pm = self.pbank()
            S.op('pe', lambda h: h.matmul(pm[:, 0:bw], lhsT=cst[:, 256:384], rhs=T1[:, b0:b0 + bw], start=True, stop=True), reads=[cst, T1], writes=[pm])
            S.op('act', lambda h: h.activation(out=T2[:, b0:b0 + bw], in_=pm[:, 0:bw], func=AF.Sqrt, bias=self.epsb[:, 1:2]), reads=[pm, self.epsb], writes=[T2])
        S.op('dve', lambda h: h.reciprocal(out=T2[:, 0:W8], in_=T2[:, 0:W8]), reads=[T2], writes=[T2])
        S.op('dve', lambda h: h.scalar_tensor_tensor(out=qkv[:, 0:4 * G], in0=qkv[:, 0:4 * G], scalar=float(128 ** -0.5), in1=T2[:, 0:4 * G], op0=ALU.mult, op1=ALU.mult), reads=[qkv, T2], writes=[qkv])
        S.op('dve', lambda h: h.tensor_tensor(out=qkv[:, 4 * G:8 * G], in0=qkv[:, 4 * G:8 * G], in1=T2[:, 4 * G:8 * G], op=ALU.mult), reads=[qkv, T2], writes=[qkv])
        wC = self.load_w(l, 'kvD')
        CC = 4 * C
        mU = cst[0:C, 128:128 + C].unsqueeze(1).to_broadcast([C, 4, C])
        mLs = cst[0:C, 768:768 + C].unsqueeze(1).to_broadcast([C, 4, C])
        K = {64: 5, 32: 4, 16: 3}[C]
        V = lambda key: self.av(key, 256)
        for ci in range(nC):
            c0 = ci * C
            pb = self.pbank()
            for k in range(8):
                S.op('pe', lambda h, k=k: h.matmul(pb[0:C, 0:12], lhsT=self.xT[:, k, c0:c0 + C], rhs=wC[:, k, 0:12], start=(k == 0), stop=(k == 7)), reads=[self.xT, wC], writes=[pb])
            bg = self.av('bg', 32)
            S.op('act', lambda h: h.activation(out=bg[0:C, 0:4], in_=pb[0:C, 4:8], func=AF.Sigmoid), reads=[pb], writes=[bg])
            S.op('dve', lambda h: h.tensor_tensor(out=bg[0:C, 4:8], in0=pb[0:C, 8:12], in1=self.gdn_par[0:C, 4:8], op=ALU.add), reads=[pb, self.gdn_par], writes=[bg])
            S.op('act', lambda h: h.activation(out=bg[0:C, 4:8], in_=bg[0:C, 4:8], func=AF.Exp), reads=[bg], writes=[bg])
            S.op('act', lambda h: h.activation(out=bg[0:C, 4:8], in_=bg[0:C, 4:8], func=AF.Ln, bias=1.0), reads=[bg], writes=[bg])
            S.op('dve', lambda h: h.tensor_tensor(out=bg[0:C, 8:12], in0=bg[0:C, 4:8], in1=self.gdn_par[0:C, 0:4], op=ALU.mult), reads=[bg, self.gdn_par], writes=[bg])
            pcol = self.pbank()
            S.op('pe', lambda h: h.matmul(pcol[0:C, 0:4], lhsT=cst[0:C, 128:128 + C], rhs=bg[0:C, 8:12], start=True, stop=True), reads=[cst, bg], writes=[pcol])
            R = V('R')
            R3 = R[0:C, 0:CC].rearrange("p (a c) -> p a c", c=C)
            for h4 in range(4):
                S.op('dve', lambda h, h4=h4: h.tensor_scalar(out=R3[:, h4, :], in0=cst[0:C, 128:128 + C], scalar1=bg[0:C, 8 + h4:9 + h4], scalar2=None, op0=ALU.mult), reads=[cst, bg], writes=[R])
            prow = self.pbank()
            S.op('pe', lambda h: h.matmul(prow[:, 0:CC], lhsT=cst[0:C, 256:384], rhs=R[0:C, 0:CC], start=True, stop=True), reads=[cst, R], writes=[prow])
            egrow = V('egrow')
            S.op('act', lambda h: h.activation(out=egrow[:, 0:CC], in_=prow[:, 0:CC], func=AF.Exp), reads=[prow], writes=[egrow])
            eg3 = egrow[:, 0:CC].rearrange("p (a c) -> p a c", c=C)
            S.op('dve', lambda h: h.tensor_copy(out=bg[0:C, 12:16], in_=pcol[0:C, 0:4]), reads=[pcol], writes=[bg])
            S.op('act', lambda h: h.activation(out=bg[0:C, 16:20], in_=pcol[0:C, 0:4], func=AF.Exp), reads=[pcol], writes=[bg])
            S.op('dve', lambda h: h.tensor_tensor(out=bg[0:C, 20:24], in0=bg[0:C, 16:20], in1=bg[0:C, 0:4], op=ALU.mult), reads=[bg], writes=[bg])
            Dm = V('Dm')
            D3 = Dm[0:C, 0:CC].rearrange("p (a c) -> p a c", c=C)
            p3 = prow[0:C, 0:CC].rearrange("p (a c) -> p a c", c=C)
            for h4 in range(4):
                S.op('dve', lambda h, h4=h4: h.tensor_scalar(out=D3[:, h4, :], in0=p3[:, h4, :], scalar1=bg[0:C, 12 + h4:13 + h4], scalar2=-1.0, op0=ALU.subtract, op1=ALU.mult), reads=[prow, bg], writes=[Dm])
            Elo = V('Elo')
            Eup = V('Eup')
            S.op('dve', lambda h: h.tensor_scalar(out=Elo[0:C, 0:CC], in0=Dm[0:C, 0:CC], scalar1=0.0, scalar2=None, op0=ALU.min), reads=[Dm], writes=[Elo])
            S.op('act', lambda h: h.activation(out=Elo[0:C, 0:CC], in_=Elo[0:C, 0:CC], func=AF.Exp), reads=[Elo], writes=[Elo])
            S.op('dve', lambda h: h.tensor_scalar(out=Eup[0:C, 0:CC], in0=Dm[0:C, 0:CC], scalar1=-1.0, scalar2=0.0, op0=ALU.mult, op1=ALU.min), reads=[Dm], writes=[Eup])
            S.op('act', lambda h: h.activation(out=Eup[0:C, 0:CC], in_=Eup[0:C, 0:CC], func=AF.Exp), reads=[Eup], writes=[Eup])
            El3 = Elo[0:C, 0:CC].rearrange("p (a c) -> p a c", c=C)
            Eu3 = Eup[0:C, 0:CC].rearrange("p (a c) -> p a c", c=C)
            S.op('dve', lambda h: h.tensor_copy(out=bg[0:C, 24:28], in_=Eu3[:, :, C - 1]), reads=[Eup], writes=[bg])
            S.op('dve', lambda h: h.tensor_tensor(out=El3, in0=El3, in1=mLs, op=ALU.mult), reads=[Elo, cst], writes=[Elo])
            S.op('dve', lambda h: h.tensor_tensor(out=Eu3, in0=Eu3, in1=mU, op=ALU.mult), reads=[Eup, cst], writes=[Eup])
            for h4 in range(4):
                S.op('dve', lambda h, h4=h4: h.tensor_scalar(out=El3[:, h4, :], in0=El3[:, h4, :], scalar1=bg[0:C, h4:h4 + 1], scalar2=None, op0=ALU.mult), reads=[Elo, bg], writes=[Elo])
            pkk = self.pbank()
            pmt = self.pbank()
            for h4 in range(4):
                S.op('pe', lambda h, h4=h4: h.matmul(pkk[0:C, h4 * C:(h4 + 1) * C], lhsT=q3[:, 4 + h4, c0:c0 + C], rhs=q3[:, 4 + h4, c0:c0 + C], start=True, stop=True), reads=[qkv], writes=[pkk])
                S.op('pe', lambda h, h4=h4: h.matmul(pmt[0:C, h4 * C:(h4 + 1) * C], lhsT=q3[:, 4 + h4, c0:c0 + C], rhs=q3[:, h4, c0:c0 + C], start=True, stop=True), reads=[qkv], writes=[pmt])
            P = [V('P0')]
            AT = V('AT')
            MT = V('MT')
            S.op('dve', lambda h: h.tensor_tensor(out=P[0][0:C, 0:CC], in0=pkk[0:C, 0:CC], in1=Elo[0:C, 0:CC], op=ALU.mult), reads=[pkk, Elo], writes=[P[0]])
            S.op('dve', lambda h: h.tensor_tensor(out=MT[0:C, 0:CC], in0=pmt[0:C, 0:CC], in1=Eup[0:C, 0:CC], op=ALU.mult), reads=[pmt, Eup], writes=[MT])
            pat = self.pbank()
            for h4 in range(4):
                S.op('pe', lambda h, h4=h4: h.transpose(pat[0:C, h4 * C:(h4 + 1) * C], P[0][0:C, h4 * C:(h4 + 1) * C], self.ident[0:C, 0:C]), reads=[P[0], cst], writes=[pat])
            S.op('act', lambda h: h.activation(out=AT[0:C, 0:CC], in_=pat[0:C, 0:CC], func=AF.Copy), reads=[pat], writes=[AT])
            X = T1
            X3 = X[0:C, 0:1024].rearrange("p (a c) -> p a c", c=256)
            Kdec = T2
            Kd3 = Kdec[0:C, 0:512].rearrange("p (a c) -> p a c", c=128)
            vn = T2
            vn3 = vn[0:C, 512:1024].rearrange("p (a c) -> p a c", c=128)
            pkt = self.pbank()
            pvt = self.pbank()
            for h4 in range(4):
                S.op('pe', lambda h, h4=h4: h.transpose(pkt[0:C, h4 * 128:(h4 + 1) * 128], q3[:, 4 + h4, c0:c0 + C], self.ident[:, 0:128]), reads=[qkv, cst], writes=[pkt])
                S.op('pe', lambda h, h4=h4: h.transpose(pvt[0:C, h4 * 128:(h4 + 1) * 128], q3[:, 8 + h4, c0:c0 + C], self.ident[:, 0:128]), reads=[qkv, cst], writes=[pvt])
            for h4 in range(4):
                S.op('dve', lambda h, h4=h4: h.tensor_scalar(out=X3[:, h4, 0:128], in0=pkt[0:C, h4 * 128:(h4 + 1) * 128], scalar1=bg[0:C, 20 + h4:21 + h4], scalar2=None, op0=ALU.mult), reads=[pkt, bg], writes=[X])
                S.op('dve', lambda h, h4=h4: h.tensor_scalar(out=X3[:, h4, 128:256], in0=pvt[0:C, h4 * 128:(h4 + 1) * 128], scalar1=bg[0:C, h4:h4 + 1], scalar2=None, op0=ALU.mult), reads=[pvt, bg], writes=[X])
                S.op('dve', lambda h, h4=h4: h.tensor_scalar(out=Kd3[:, h4, :], in0=pkt[0:C, h4 * 128:(h4 + 1) * 128], scalar1=bg[0:C, 24 + h4:25 + h4], scalar2=None, op0=ALU.mult), reads=[pkt, bg], writes=[Kdec])
            PTs = [AT]
            Ps = [P[0]]
            for k in range(1, K + 1):
                pk_ = self.pbank()
                pt_ = self.pbank()
                Pp, PTp = Ps[-1], PTs[-1]
                for h4 in range(4):
                    sl = slice(h4 * C, (h4 + 1) * C)
                    if k < K:
                        S.op('pe', lambda h, sl=sl: h.matmul(pk_[0:C, sl], lhsT=PTp[0:C, sl], rhs=Pp[0:C, sl], start=True, stop=True), reads=[PTp, Pp], writes=[pk_])
                    S.op('pe', lambda h, sl=sl: h.matmul(pt_[0:C, sl], lhsT=Pp[0:C, sl], rhs=PTp[0:C, sl], start=True, stop=True), reads=[PTp, Pp], writes=[pt_])
                nPT = V('PTk%d' % k)
                S.op('act', lambda h: h.activation(out=nPT[0:C, 0:CC], in_=pt_[0:C, 0:CC], func=AF.Copy), reads=[pt_], writes=[nPT])
                PTs.append(nPT)
                if k < K:
                    nP = V('Pk%d' % k)
                    S.op('dve', lambda h: h.tensor_copy(out=nP[0:C, 0:CC], in_=pk_[0:C, 0:CC]), reads=[pk_], writes=[nP])
                    Ps.append(nP)
            for k in range(K, -1, -1):
                PTk = PTs[k]
                for half in range(2):
                    px = self.pbank()
                    for hh in range(2):
                        h4 = 2 * half + hh
                        S.op('pe', lambda h, h4=h4, hh=hh: h.matmul(px[0:C, hh * 256:(hh + 1) * 256], lhsT=PTk[0:C, h4 * C:(h4 + 1) * C], rhs=X3[:, h4, :], start=True, stop=True), reads=[PTk, X], writes=[px])
                    xs = X[0:C, half * 512:(half + 1) * 512]
                    S.op('dve', lambda h: h.tensor_tensor(out=xs, in0=xs, in1=px[0:C, 0:512], op=(ALU.subtract if k == 0 else ALU.add)), reads=[X, px], writes=[X])
            pwt = self.pbank()
            for h4 in range(4):
                S.op('pe', lambda h, h4=h4: h.transpose(pwt[:, h4 * C:(h4 + 1) * C], X3[:, h4, 0:128], self.ident[0:C, 0:C]), reads=[X, cst], writes=[pwt])
            WT = Dm
            S.op('act', lambda h: h.activation(out=WT[:, 0:CC], in_=pwt[:, 0:CC], func=AF.Copy), reads=[pwt], writes=[WT])
            pws = self.pbank()
            for h4 in range(4):
                S.op('pe', lambda h, h4=h4: h.matmul(pws[0:C, h4 * 128:(h4 + 1) * 128], lhsT=WT[:, h4 * C:(h4 + 1) * C], rhs=self.gdn_S[:, h4, :], start=True, stop=True), reads=[WT, self.gdn_S], writes=[pws])
            S.op('dve', lambda h: h.tensor_tensor(out=vn3, in0=X3[:, :, 128:256], in1=pws[0:C, 0:512].rearrange("p (a c) -> p a c", c=128), op=ALU.subtract), reads=[X, pws], writes=[vn])
            QeT = R
            Qe3 = QeT[:, 0:CC].rearrange("p (a c) -> p a c", c=C)
            S.op('dve', lambda h: h.tensor_tensor(out=Qe3, in0=q3[:, 0:4, c0:c0 + C], in1=eg3, op=ALU.mult), reads=[qkv, egrow], writes=[QeT])
            po = self.pbank()
            for h4 in range(4):
                sl = slice(h4 * C, (h4 + 1) * C)
                S.op('pe', lambda h, h4=h4, sl=sl: h.matmul(po[:, sl], lhsT=self.gdn_S[:, h4, :], rhs=QeT[:, sl], start=True, stop=False), reads=[self.gdn_S, QeT], writes=[po])
                S.op('pe', lambda h, h4=h4, sl=sl: h.matmul(po[:, sl], lhsT=vn3[:, h4, :], rhs=MT[0:C, sl], start=False, stop=True), reads=[vn, MT], writes=[po])
            S.op('act', lambda h: h.activation(out=o3[:, :, c0:c0 + C], in_=po[:, 0:CC].rearrange("p (a c) -> p a c", c=C), func=AF.Copy), reads=[po], writes=[oT])
            pu = self.pbank()
            for h4 in range(4):
                S.op('pe', lambda h, h4=h4: h.matmul(pu[:, h4 * 128:(h4 + 1) * 128], lhsT=Kd3[:, h4, :], rhs=vn3[:, h4, :], start=True, stop=True), reads=[Kdec, vn], writes=[pu])
            for h4 in range(4):
                S.op('dve', lambda h, h4=h4: h.scalar_tensor_tensor(out=self.gdn_S[:, h4, :], in0=self.gdn_S[:, h4, :], scalar=eg3[:, h4, C - 1:C], in1=pu[:, h4 * 128:(h4 + 1) * 128], op0=ALU.mult, op1=ALU.add),
                     reads=[self.gdn_S, egrow, pu], writes=[self.gdn_S])
        if last:
            S.dma('pool', self.O['gdn' + grp][l].rearrange("h k v -> k h v"), self.gdn_S[:, :, :], reads=[self.gdn_S], writes=[self.dk('ogdn' + grp)])
        W4 = 4 * G
        S.op('act', lambda h: h.activation(out=T1[:, 0:W4], in_=oT[:, 0:W4], func=AF.Square), reads=[oT], writes=[T1])
        pm = self.pbank()
        S.op('pe', lambda h: h.matmul(pm[:, 0:W4], lhsT=cst[:, 384:512], rhs=T1[:, 0:W4], start=True, stop=True), reads=[cst, T1], writes=[pm])
        S.op('act', lambda h: h.activation(out=T2[:, 0:W4], in_=pm[:, 0:W4], func=AF.Sqrt, bias=self.epsb[:, 1:2]), reads=[pm, self.epsb], writes=[T2])
        S.op('dve', lambda h: h.reciprocal(out=T2[:, 0:W4], in_=T2[:, 0:W4]), reads=[T2], writes=[T2])
        S.op('dve', lambda h: h.scalar_tensor_tensor(out=oT[:, 0:W4], in0=oT[:, 0:W4], scalar=self.gdn_n[:, 0:1], in1=T2[:, 0:W4], op0=ALU.mult, op1=ALU.mult), reads=[oT, T2, self.gdn_n], writes=[oT])
        for c4 in range(4):
            pgt = self.proj_fm(l, 'cg%d' % c4, 128, G)
            sg = V('R') if c4 % 2 else V('Dm')
            S.op('act', lambda h: h.activation(out=sg[:, 0:G], in_=pgt[:, 0:G], func=AF.Silu), reads=[pgt], writes=[sg])
            S.op('dve', lambda h, c4=c4: h.tensor_tensor(out=self.brT[:, 8 + c4, 0:G], in0=o3[:, c4, :], in1=sg[:, 0:G], op=ALU.mult), reads=[oT, sg], writes=[self.brT])

    def gla_part(self, l, grp, g0, G, C, last):
        S = self.S
        cst = self.cst
        nC = G // C
        qT = self.wsp(True)
        kT = self.wsp()
        W4 = 4 * G
        q3 = qT[0:64, 0:W4].rearrange("p (a c) -> p a c", c=G)
        k3 = kT[0:64, 0:W4].rearrange("p (a c) -> p a c", c=G)
        for h4 in range(4):
            pb = self.proj_fm(l, 'dq%d' % h4, 64, G)
            S.op('act', lambda h, h4=h4: h.activation(out=q3[:, h4, :], in_=pb[0:64, 0:G], func=AF.Copy, scale=0.125), reads=[pb], writes=[qT])
            pb2 = self.proj_fm(l, 'dk%d' % h4, 64, G)
            S.op('dve', lambda h, h4=h4: h.tensor_copy(out=k3[:, h4, :], in_=pb2[0:64, 0:G]), reads=[pb2], writes=[kT])
        pg = self.proj_fm(l, 'dg', 16, G)
        dgT = self.sm()
        dg_sb = self.scratch()
        S.op('dve', lambda h: h.tensor_copy(out=dg_sb[0:16, 0:G], in_=pg[0:16, 0:G]), reads=[pg], writes=[dg_sb])
        spT = self.wsp()
        sp3 = spT[0:64, 0:W4].rearrange("p (a c) -> p a c", c=G)
        nb = self.sm()
        S.op('dve', lambda h: h.tensor_scalar(out=nb[0:64, 0:4], in0=self.gla_b[:, :], scalar1=-1.0, scalar2=None, op0=ALU.mult), reads=[self.gla_b], writes=[nb])
        for h4 in range(4):
            pl = self.pbank()
            S.op('pe', lambda h, h4=h4: h.matmul(pl[0:64, 0:G], lhsT=self.gla_w[:, h4 * 64:(h4 + 1) * 64], rhs=dg_sb[0:16, 0:G], start=True, stop=True),
                 reads=[self.gla_w, dg_sb], writes=[pl])
            S.op('act', lambda h, h4=h4: h.activation(out=sp3[:, h4, :], in_=pl[0:64, 0:G], func=AF.Exp, scale=-1.0, bias=nb[0:64, h4:h4 + 1]),
                 reads=[pl, nb], writes=[spT])
        S.op('act', lambda h: h.activation(out=spT[0:64, 0:W4], in_=spT[0:64, 0:W4], func=AF.Ln, bias=1.0), reads=[spT], writes=[spT])
        csT = self.wsp()
        S.op('dve', lambda h: h.tensor_tensor_scan(out=csT[0:64, 0:W4], data0=self.resetm[:, 0:W4], data1=spT[0:64, 0:W4], initial=0.0, op0=ALU.mult, op1=ALU.add),
             reads=[self.resetm, spT], writes=[csT])
        eq = self.wsp()
        ek = self.wsp()
        S.op('act', lambda h: h.activation(out=eq[0:64, 0:W4], in_=csT[0:64, 0:W4], func=AF.Exp, scale=-1.0 / 16.0), reads=[csT], writes=[eq])
        S.op('act', lambda h: h.activation(out=ek[0:64, 0:W4], in_=csT[0:64, 0:W4], func=AF.Exp, scale=1.0 / 16.0), reads=[csT], writes=[ek])
        S.op('dve', lambda h: h.tensor_tensor(out=qT[0:64, 0:W4], in0=qT[0:64, 0:W4], in1=eq[0:64, 0:W4], op=ALU.mult), reads=[qT, eq], writes=[qT])
        S.op('dve', lambda h: h.tensor_tensor(out=kT[0:64, 0:W4], in0=kT[0:64, 0:W4], in1=ek[0:64, 0:W4], op=ALU.mult), reads=[kT, ek], writes=[kT])
        eq3 = eq[0:64, 0:W4].rearrange("p (a c) -> p a c", c=G)
        oT = self.wsp()
        o3 = oT[:, 0:W4].rearrange("p (a c) -> p a c", c=G)
        for ci in range(nC):
            c0 = ci * C
            pv = self.pbank()
            for c4 in range(4):
                w, xTw = self.load_wx(l, 'dv%d' % c4)
                for k in range(8):
                    S.op('pe', lambda h, k=k, c4=c4, w=w: h.matmul(pv[0:C, c4 * 128:(c4 + 1) * 128], lhsT=xTw[:, k, c0:c0 + C], rhs=w[:, k, :],
                                                               start=(k == 0), stop=(k == 7)), reads=[w, xTw], writes=[pv])
            vt = self.scratch()
            S.op('act', lambda h: h.activation(out=vt[0:C, 0:512], in_=pv[0:C, 0:512], func=AF.Copy), reads=[pv], writes=[vt])
            pk = self.pbank()
            for h4 in range(4):
                S.op('pe', lambda h, h4=h4: h.transpose(pk[0:C, h4 * 64:(h4 + 1) * 64], k3[:, h4, c0:c0 + C], self.ident[0:64, 0:64]),
                     reads=[kT, cst], writes=[pk])
            kt = self.scratch()
            S.op('dve', lambda h: h.tensor_copy(out=kt[0:C, 0:256], in_=pk[0:C, 0:256]), reads=[pk], writes=[kt])
            pa = self.pbank()
            for h4 in range(4):
                S.op('pe', lambda h, h4=h4: h.matmul(pa[0:C, h4 * C:(h4 + 1) * C], lhsT=k3[:, h4, c0:c0 + C], rhs=q3[:, h4, c0:c0 + C], start=True, stop=True),
                     reads=[kT, qT], writes=[pa])
            at = self.scratch()
            pa3 = pa[0:C, 0:4 * C].rearrange("p (a c) -> p a c", c=C)
            at3 = at[0:C, 0:4 * C].rearrange("p (a c) -> p a c", c=C)
            msk = cst[0:C, 128:128 + C].unsqueeze(1).to_broadcast([C, 4, C])
            S.op('dve', lambda h: h.tensor_tensor(out=at3, in0=pa3, in1=msk, op=ALU.mult), reads=[pa, cst], writes=[at])
            po = self.pbank()
            for h4 in range(4):
                S.op('pe', lambda h, h4=h4: h.matmul(po[:, h4 * C:(h4 + 1) * C], lhsT=vt[0:C, h4 * 128:(h4 + 1) * 128], rhs=at3[:, h4, :], start=True, stop=False),
                     reads=[vt, at], writes=[po])
                S.op('pe', lambda h, h4=h4: h.matmul(po[:, h4 * C:(h4 + 1) * C], lhsT=self.gla_S[:, h4, :], rhs=q3[:, h4, c0:c0 + C], start=False, stop=True),
                     reads=[self.gla_S, qT], writes=[po])
            S.op('act', lambda h: h.activation(out=o3[:, :, c0:c0 + C], in_=po[:, 0:4 * C].rearrange("p (a c) -> p a c", c=C), func=AF.Copy), reads=[po], writes=[oT])
            pu = self.pbank()
            for h4 in range(4):
                S.op('pe', lambda h, h4=h4: h.matmul(pu[0:64, h4 * 128:(h4 + 1) * 128], lhsT=kt[0:C, h4 * 64:(h4 + 1) * 64], rhs=vt[0:C, h4 * 128:(h4 + 1) * 128], start=True, stop=True),
                     reads=[kt, vt], writes=[pu])
            S.op('dve', lambda h: h.tensor_tensor(out=self.gla_S[:, :, :], in0=self.gla_S[:, :, :], in1=pu[0:64, 0:512].rearrange("p (a c) -> p a c", c=128), op=ALU.add),
                 reads=[self.gla_S, pu], writes=[self.gla_S])
            for h4 in range(4):
                S.op('dve', lambda h, h4=h4: h.tensor_scalar(out=self.gla_S[:, h4, :], in0=self.gla_S[:, h4, :], scalar1=eq3[:, h4, c0 + C - 1:c0 + C], scalar2=None, op0=ALU.mult),
                     reads=[self.gla_S, eq], writes=[self.gla_S])
        if last:
            S.dma('pool', self.O['gla' + grp][l].rearrange("h k v -> k h v"), self.gla_S[:, :, :], reads=[self.gla_S], writes=[self.dk('ogla' + grp)])
        sq = spT
        S.op('act', lambda h: h.activation(out=sq[:, 0:W4], in_=oT[:, 0:W4], func=AF.Square), reads=[oT], writes=[sq])
        rst = csT
        for b0 in range(0, W4, 512):
            bw = min(512, W4 - b0)
            pm = self.pbank()
            S.op('pe', lambda h: h.matmul(pm[:, 0:bw], lhsT=cst[:, 384:512], rhs=sq[:, b0:b0 + bw], start=True, stop=True), reads=[cst, sq], writes=[pm])
            S.op('act', lambda h: h.activation(out=rst[:, b0:b0 + bw], in_=pm[:, 0:bw], func=AF.Sqrt, bias=self.epsb[:, 1:2]), reads=[pm, self.epsb], writes=[rst])
        S.op('dve', lambda h: h.reciprocal(out=rst[:, 0:W4], in_=rst[:, 0:W4]), reads=[rst], writes=[rst])
        S.op('dve', lambda h: h.scalar_tensor_tensor(out=oT[:, 0:W4], in0=oT[:, 0:W4], scalar=self.gla_n[:, 0:1], in1=rst[:, 0:W4], op0=ALU.mult, op1=ALU.mult),
             reads=[oT, rst, self.gla_n], writes=[oT])
        for c4 in range(4):
            pgt = self.proj_fm(l, 'dgt%d' % c4, 128, G)
            sg = self.scratch()
            S.op('act', lambda h: h.activation(out=sg[:, 0:G], in_=pgt[:, 0:G], func=AF.Silu), reads=[pgt], writes=[sg])
            S.op('dve', lambda h, c4=c4: h.tensor_tensor(out=self.brT[:, 12 + c4, 0:G], in0=o3[:, c4, :], in1=sg[:, 0:G], op=ALU.mult), reads=[oT, sg], writes=[self.brT])

    def out_stage(self, l, grp, g0, G, TP, xin, xin_tk, yout, yout_tk):
        S = self.S
        I = self.I
        mix = self.av('mixT', 4 * self.G)
        mix3 = mix.h.bitcast(BF16)[:, 0:8 * G].rearrange("p (a c) -> p a c", c=G)
        for dc in range(8):
            for b in range(4):
                pp = self.pbank()
                if b == 1:
                    wb = self.wbrB
                    S.dma('sp', wb[:, :, :], self.wbrb[l, b, dc].rearrange("(p h) f -> p (h f)", h=2).rearrange("p (a c) -> p a c", c=128), reads=[self.dk(('wbrb', l, b, dc))], writes=[wb])
                    for hh in range(8):
                        S.op('pe', lambda h, hh=hh: h.matmul(pp[:, 0:G], lhsT=wb[:, hh, :], rhs=self.brB[:, hh, 0:G], start=(hh == 0), stop=(hh == 7)),
                             reads=[wb, self.brB], writes=[pp])
                else:
                    wb = self.wbr
                    S.dma('sp', wb[:, :, :], self.wbrb[l, b, dc].rearrange("p (a c) -> p a c", c=128), reads=[self.dk(('wbrb', l, b, dc))], writes=[wb])
                    for kc in range(4):
                        S.op('pe', lambda h, kc=kc: h.matmul(pp[:, 0:G], lhsT=wb[:, kc, :], rhs=self.brT[:, 4 * b + kc, 0:G], start=(kc == 0), stop=(kc == 3)),
                             reads=[wb, self.brT], writes=[pp])
                pm = self.proj_fm(l, 'mg%d_%d' % (b, dc), 128, G)
                sg = self.scratch()
                S.op('act', lambda h: h.activation(out=sg[:, 0:G], in_=pm[:, 0:G], func=AF.Sigmoid), reads=[pm], writes=[sg])
                if b == 0:
                    S.op('dve', lambda h, dc=dc: h.tensor_tensor(out=mix3[:, dc, 0:G], in0=pp[:, 0:G], in1=sg[:, 0:G], op=ALU.mult), reads=[pp, sg], writes=[mix])
                else:
                    tmp = self.scratch()
                    S.op('dve', lambda h: h.tensor_tensor(out=tmp[:, 0:G], in0=pp[:, 0:G], in1=sg[:, 0:G], op=ALU.mult), reads=[pp, sg], writes=[tmp])
                    S.op('dve', lambda h, dc=dc: h.tensor_tensor(out=mix3[:, dc, 0:G], in0=mix3[:, dc, 0:G], in1=tmp[:, 0:G], op=ALU.add), reads=[mix, tmp], writes=[mix])
        for ti in range(G // TP):
            t0 = ti * TP
            xt = self.av('xtm', D)
            S.dma('sp', xt[0:TP, :], xin[g0 + t0: g0 + t0 + TP, :], reads=[xin_tk], writes=[xt])
            z = self.av('zt', D)
            for qt in range(8):
                S.dma('sp', self.wout[:, :, :], self.woutb[l, qt].rearrange("p (a c) -> p a c", c=128), reads=[self.dk(('woutb', l, qt))], writes=[self.wout])
                pz = self.pbank()
                for k in range(8):
                    S.op('pe', lambda h, k=k: h.matmul(pz[0:TP, 0:128], lhsT=mix3[:, k, t0:t0 + TP], rhs=self.wout[:, k, :], start=(k == 0), stop=(k == 7)),
                         reads=[mix, self.wout], writes=[pz])
                S.op('dve', lambda h, qt=qt: h.scalar_tensor_tensor(out=z[0:TP, qt * 128:(qt + 1) * 128], in0=xt[0:TP, qt * 128:(qt + 1) * 128], scalar=float(DN_ALPHA),
                                                                     in1=pz[0:TP, 0:128], op0=ALU.mult, op1=ALU.add), reads=[xt, pz], writes=[z])
            st = self.sm()
            for half in range(2):
                S.op('dve', lambda h, half=half: h.bn_stats(out=st[0:TP, half * 6:(half + 1) * 6], in_=z[0:TP, half * 512:(half + 1) * 512]), reads=[z], writes=[st])
            mv = self.sm()
            S.op('dve', lambda h: h.bn_aggr(out=mv[0:TP, 0:2], in_=st[0:TP, 0:12]), reads=[st], writes=[mv])
            S.op('act', lambda h: h.activation(out=mv[0:TP, 2:3], in_=mv[0:TP, 1:2], func=AF.Sqrt, bias=self.epsb[0:TP, 0:1]), reads=[mv, self.epsb], writes=[mv])
            S.op('dve', lambda h: h.reciprocal(out=mv[0:TP, 3:4], in_=mv[0:TP, 2:3]), reads=[mv], writes=[mv])
            S.op('dve', lambda h: h.tensor_scalar(out=z[0:TP, :], in0=z[0:TP, :], scalar1=mv[0:TP, 0:1], scalar2=mv[0:TP, 3:4], op0=ALU.subtract, op1=ALU.mult),
                 reads=[z, mv], writes=[z])
            S.op('dve', lambda h: h.tensor_tensor(out=z[0:TP, :], in0=z[0:TP, :], in1=self.lng[0:TP, :], op=ALU.mult), reads=[z, self.lng], writes=[z])
            S.op('dve', lambda h: h.tensor_tensor(out=z[0:TP, :], in0=z[0:TP, :], in1=self.lnb[0:TP, :], op=ALU.add), reads=[z, self.lnb], writes=[z])
            S.dma('pool', yout[g0 + t0: g0 + t0 + TP, :], z[0:TP, :], reads=[z], writes=[yout_tk])


def _prep_consts():
    c = np.zeros((128, 1024), np.float32)
    c[:, 0:128] = np.eye(128, dtype=np.float32)
    p = np.arange(128)[:, None]
    f = np.arange(128)[None, :]
    c[:, 128:256] = (f >= p).astype(np.float32)
    c[:, 256:384] = 1.0
    c[:, 384:512] = 1.0 / 128.0
    c[:, 512:768] = np.arange(1, 257, dtype=np.float32)[None, :]
    c[:, 768:896] = (p > f).astype(np.float32)
    c[0:64, 896] = 1.0
    c[64:128, 928] = 1.0
    return c


_CACHE = {}


def kernel(**inp):
    T = inp['x_prompt'].shape[1]
    if T not in _CACHE:
        b = Builder(T)
        _CACHE[T] = b.build()
    nc = _CACHE[T]
    f = lambda a: np.ascontiguousarray(np.asarray(a, dtype=np.float32))
    w_in = f(inp['w_in'])
    win = np.zeros((DEPTH, NCH, 128, 8, 128), np.float32)
    for ci, (_, cols) in enumerate(CHUNKS):
        blk = w_in[:, :, cols]
        blk = blk.reshape(DEPTH, 8, 128, len(cols)).transpose(0, 2, 1, 3)
        win[:, ci, :, :, :len(cols)] = blk
    cst = _prep_consts()
    glab = f(inp['gla_b_g']).reshape(DEPTH, 4, 64).transpose(0, 2, 1).copy()
    glan = f(inp['gla_norm']).reshape(DEPTH, 128, 1)
    def st_layout(a):
        return np.ascontiguousarray(a.reshape(DEPTH, 16, 2, 64).transpose(0, 2, 3, 1)).reshape(DEPTH, 128, 16)
    are, aim = st_layout(f(inp['s5_a_re'])), st_layout(f(inp['s5_a_im']))
    ldt = np.ascontiguousarray(np.broadcast_to(f(inp['s5_log_dt']).reshape(DEPTH, 16, 2, 1), (DEPTH, 16, 2, 64)).transpose(0, 2, 3, 1)).reshape(DEPTH, 128, 16)
    s5p = np.ascontiguousarray(np.stack([are, aim, ldt], axis=2))
    bre, bim = f(inp['s5_b_re']), f(inp['s5_b_im'])
    cre, cim = f(inp['s5_c_re']), f(inp['s5_c_im'])
    s5b = np.zeros((DEPTH, 2, 2, 128, 4, 128), np.float32)
    s5c = np.zeros((DEPTH, 2, 128, 16, 128), np.float32)
    for c in range(4):
        for s4 in range(4):
            for g2 in range(2):
                g = 8 * c + 2 * s4 + g2
                for ri, (bb, cc) in enumerate(((bre, cre), (bim, cim))):
                    s5b[:, 0 if s4 < 3 else 1, ri, 32 * s4 + 16 * g2: 32 * s4 + 16 * g2 + 16, c, 64 * g2: 64 * g2 + 64] = bb[:, g].transpose(0, 2, 1)
                    s5c[:, ri, 64 * g2: 64 * g2 + 64, 4 * c + s4, (2 * s4 + g2) * 16:(2 * s4 + g2) * 16 + 16] = cc[:, g].transpose(0, 2, 1)
    s5d = np.ascontiguousarray(f(inp['s5_d']).reshape(DEPTH, 4, 128).transpose(0, 2, 1))
    h0r, h0i = st_layout(f(inp['state_s5_re']).transpose(1, 0, 2, 3).reshape(NCORE * DEPTH, 32, 64).reshape(NCORE, DEPTH, 32, 64)[0]) if False else (None, None)
    gcw = np.ascontiguousarray(f(inp['gdn_conv']).reshape(DEPTH, 4, 12, 128).transpose(0, 3, 2, 1))
    gpar = np.ascontiguousarray(np.broadcast_to(np.concatenate([f(inp['gdn_a_log']), f(inp['gdn_dt_bias'])], axis=1)[:, None, :], (DEPTH, 128, 8)))
    gnorm = f(inp['gdn_norm']).reshape(DEPTH, 128, 1)
    rr = 127 - np.arange(384)
    oh = (_t5_bucket(rr)[None, :] == np.arange(32)[:, None]).astype(np.float32)
    common = dict(win=win, relb=f(inp['rel_bias']), oh=oh, gcw=gcw, gpar=gpar, gnorm=gnorm, s5p=s5p, s5b=s5b, s5c=s5c, s5d=s5d, wglu=f(inp['s5_w_glu']), wbr=f(inp['w_branch']), wout=f(inp['w_out']), lng=f(inp['ln_g']), lnb=f(inp['ln_b']), cst=cst,
                  glaw=f(inp['gla_w_g2']), glab=glab, glan=glan)
    xp = f(inp['x_prompt'])
    xs = f(inp['x_sample'])
    in_maps = []
    for c in range(NCORE):
        m = dict(common)
        m['xp'] = xp[c % 2]
        m['xs'] = xs[c]
        m['sgla'] = f(inp['state_gla'])[:, c]
        m['ck'] = f(inp['cache_k'])[:, c].reshape(DEPTH, PAST, 128)
        m['cv'] = f(inp['cache_v'])[:, c].reshape(DEPTH, PAST, 128)
        cki = f(inp['cache_kidx'])[:, c]
        m['cki'] = np.ascontiguousarray(np.concatenate([cki, cki], axis=-1))
        m['sgdn'] = f(inp['state_gdn'])[:, c]
        m['sconv'] = np.ascontiguousarray(f(inp['state_gdn_conv'])[:, c].reshape(DEPTH, 3, 12, 128).transpose(0, 3, 2, 1))
        m['s5h0'] = np.ascontiguousarray(np.stack([st_layout(f(inp['state_s5_re'])[:, c]), st_layout(f(inp['state_s5_im'])[:, c])], axis=2))
        in_maps.append(m)
    res = run_bass_kernel_spmd(nc, in_maps, core_ids=list(range(NCORE))).results
    P = lambda name: np.stack([res[b][name] for b in range(2)], axis=0)
    Sm = lambda name: np.stack([res[b][name] for b in range(NCORE)], axis=0)
    yp = P('yp')
    ys = Sm('ys')

    def kvfix(a, n):
        return np.ascontiguousarray(a.transpose(1, 0, 2, 3)).reshape(DEPTH, n, a.shape[2], 2, 64)

    def st(a):
        return np.ascontiguousarray(np.moveaxis(a, 0, 1))
    zeros = lambda *s: np.zeros(s, np.float32)

    def s5fix(a, ri):
        x = a[:, :, :, ri, :].reshape(a.shape[0], DEPTH, 2, 64, 16).transpose(1, 0, 4, 2, 3)
        return np.ascontiguousarray(x).reshape(DEPTH, a.shape[0], 32, 64)
    outs = [yp, ys]
    for g, getter, n, tt in (('p', P, 2, T), ('s', Sm, NCORE, DEC_SEQ)):
        outs += [kvfix(getter('k' + g), n), kvfix(getter('v' + g), n), st(getter('ki' + g)),
                 s5fix(getter('os5' + g), 0), s5fix(getter('os5' + g), 1), st(getter('ogdn' + g)),
                 st(getter('conv' + g)), st(getter('gla' + g))]
    return tuple(outs)
```

```python
import math
from contextlib import ExitStack
import numpy as np
import concourse.bass as bass
import concourse.mybir as mybir
from concourse.bass_utils import run_bass_kernel_spmd

F32 = mybir.dt.float32
BF16 = mybir.dt.bfloat16
U8 = mybir.dt.uint8
ALU = mybir.AluOpType
AF = mybir.ActivationFunctionType
AX = mybir.AxisListType

D = 1024
SEQ = 8192
DEPTH = 2
DEC_SEQ = 16
PAST = 1024
NCORE = 8
TOPK_MAX = 256
LN_EPS = 1e-5
RMS_EPS = 1e-6
DN_ALPHA = (2 * DEPTH) ** 0.25

_LAY = (('a_u', 512), ('a_gate', 512), ('b_q', 512), ('b_k', 128), ('b_v', 128), ('b_qi', 256), ('b_ki', 64),
        ('b_wi', 4), ('b_gate', 512), ('c_qkv', 1536), ('c_beta', 4), ('c_a', 4), ('c_gate', 512),
        ('d_q', 256), ('d_k', 256), ('d_v', 512), ('d_g', 16), ('d_gate', 512), ('merge', 4096))
OFF = {}
_o = 0
for _n, _w in _LAY:
    OFF[_n] = _o
    _o += _w
IN_WIDTH = _o


def _chunks():
    ch = []
    r = lambda n, a, b: list(range(OFF[n] + a, OFF[n] + b))
    ch.append(('kvA', r('b_k', 0, 128)))
    ch.append(('kvB', r('b_v', 0, 128)))
    ch.append(('kvC', r('b_ki', 0, 64) + r('b_ki', 0, 64)))
    ch.append(('kvD', r('b_wi', 0, 4) + r('c_beta', 0, 4) + r('c_a', 0, 4)))
    for h in range(4):
        ch.append(('qq%d' % h, r('b_q', 64 * h, 64 * h + 64) + r('b_q', 64 * (h + 4), 64 * (h + 4) + 64)))
    for h in range(4):
        ch.append(('qi%d' % h, r('b_qi', 64 * h, 64 * h + 64) + r('b_qi', 64 * h, 64 * h + 64)))
    for c in range(4):
        ch.append(('au%d' % c, r('a_u', 128 * c, 128 * c + 128)))
    for c in range(12):
        ch.append(('cq%d' % c, r('c_qkv', 128 * c, 128 * c + 128)))
    for h in range(4):
        ch.append(('dq%d' % h, r('d_q', 64 * h, 64 * h + 64)))
    for h in range(4):
        ch.append(('dk%d' % h, r('d_k', 64 * h, 64 * h + 64)))
    for c in range(4):
        ch.append(('dv%d' % c, r('d_v', 128 * c, 128 * c + 128)))
    ch.append(('dg', r('d_g', 0, 16)))
    for c in range(4):
        ch.append(('ag%d' % c, r('a_gate', 128 * c, 128 * c + 128)))
    for h in range(8):
        ch.append(('bg%d' % h, r('b_gate', 64 * h, 64 * h + 64)))
    for c in range(4):
        ch.append(('cg%d' % c, r('c_gate', 128 * c, 128 * c + 128)))
    for c in range(4):
        ch.append(('dgt%d' % c, r('d_gate', 128 * c, 128 * c + 128)))
    for b in range(4):
        for c in range(8):
            ch.append(('mg%d_%d' % (b, c), r('merge', 1024 * b + 128 * c, 1024 * b + 128 * c + 128)))
    return ch


CHUNKS = _chunks()
CIDX = {n: i for i, (n, _) in enumerate(CHUNKS)}
NCH = len(CHUNKS)


def _t5_bucket(rel):
    nb = 16
    max_exact = 8
    ret = np.where(rel > 0, nb, 0)
    dist = np.abs(rel)
    distf = np.maximum(dist, 1).astype(np.float32)
    large = max_exact + (np.log(distf / max_exact) / math.log(128 / max_exact) * (nb - max_exact)).astype(np.int32)
    large = np.minimum(large, nb - 1)
    return ret + np.where(dist < max_exact, dist, large)


class Tk:
    __slots__ = ('h', 'w', 'r')

    def __init__(self, h=None):
        self.h = h
        self.w = None
        self.r = []

    def __getitem__(self, idx):
        return self.h[idx]


class Sched:
    def __init__(self, nc, stack):
        self.nc = nc
        self.stack = stack
        self.eng = {}
        for n, h in [('pe', nc.tensor), ('dve', nc.vector), ('act', nc.scalar), ('pool', nc.gpsimd), ('sp', nc.sync)]:
            sem = stack.enter_context(nc.semaphore('s_' + n))
            self.eng[n] = dict(h=h, sem=sem, cnt=0, known={})
        self.dma_sems = {}
        for q in ('sp', 'pool', 'act'):
            self.dma_sems[q] = [[stack.enter_context(nc.semaphore('d%s%d' % (q, i))), 0] for i in range(12)]
        self.dma_rr = {'sp': 0, 'pool': 0, 'act': 0}
        self.nid = 0
        self.n_ins = 0

    def sb(self, shape, dt=F32):
        self.nid += 1
        return Tk(self.stack.enter_context(self.nc.sbuf_tensor('t%d' % self.nid, shape, dt)))

    def ps(self, shape, dt=F32):
        self.nid += 1
        return Tk(self.stack.enter_context(self.nc.psum_tensor('p%d' % self.nid, shape, dt)))

    def _wait(self, e, sem, val):
        E = self.eng[e]
        k = id(sem)
        if E['known'].get(k, 0) >= val:
            return
        E['h'].wait_ge(sem, val)
        E['known'][k] = val

    def _deps(self, e, reads, writes):
        E = self.eng[e]
        own = E['sem']
        pe = (e == 'pe')
        for t in reads:
            if t.w is not None and not (pe and t.w[0] is own):
                self._wait(e, *t.w)
        for t in writes:
            if t.w is not None and not (pe and t.w[0] is own):
                self._wait(e, *t.w)
            for (s, v) in t.r:
                if not (pe and s is own):
                    self._wait(e, s, v)

    def _mark(self, tok, reads, writes):
        for t in writes:
            t.w = tok
            t.r = []
        for t in reads:
            if t not in writes:
                if len(t.r) > 6:
                    d = {}
                    for (s, v) in t.r:
                        d[id(s)] = (s, max(v, d.get(id(s), (s, 0))[1]))
                    t.r = list(d.values())
                t.r.append(tok)

    def op(self, e, fn, reads=(), writes=()):
        E = self.eng[e]
        self._deps(e, reads, writes)
        ins = fn(E['h'])
        E['cnt'] += 1
        ins.then_inc(E['sem'], 1)
        self._mark((E['sem'], E['cnt']), reads, writes)
        self.n_ins += 1
        return ins

    def dma(self, e, out, in_, reads=(), writes=(), **kw):
        E = self.eng[e]
        slots = self.dma_sems[e]
        slot = slots[self.dma_rr[e]]
        self.dma_rr[e] = (self.dma_rr[e] + 1) % len(slots)
        if slot[1] > 0:
            self._wait(e, slot[0], slot[1])
        self._deps(e, reads, writes)
        ins = E['h'].dma_start(out=out, in_=in_, **kw)
        slot[1] += 16
        ins.then_inc(slot[0], 16)
        self._mark((slot[0], slot[1]), reads, writes)
        self.n_ins += 1
        return ins

    def finish(self, tiles):
        for t in tiles:
            if t.w is not None:
                self._wait('sp', *t.w)


class Builder:
    def __init__(self, T):
        self.T = T
        self.G = min(128, T)
        self.nc = bass.Bass("TRN2", target_bir_lowering=False)

    def dram_in(self, name, shape, dt=F32):
        return self.nc.dram_tensor(name, list(shape), dt, kind="ExternalInput").ap()

    def dram_out(self, name, shape, dt=F32):
        return self.nc.dram_tensor(name, list(shape), dt, kind="ExternalOutput").ap()

    def build(self):
        nc = self.nc
        T = self.T
        I = {}
        O = {}
        I['xp'] = self.dram_in('xp', [T, D])
        I['xs'] = self.dram_in('xs', [DEC_SEQ, D])
        I['win'] = self.dram_in('win', [DEPTH, NCH, 128, 8, 128])
        I['wbr'] = self.dram_in('wbr', [DEPTH, 4, 512, D])
        I['wout'] = self.dram_in('wout', [DEPTH, D, D])
        I['lng'] = self.dram_in('lng', [DEPTH, D])
        I['lnb'] = self.dram_in('lnb', [DEPTH, D])
        I['cst'] = self.dram_in('cst', [128, 1024])
        I['glaw'] = self.dram_in('glaw', [DEPTH, 16, 256])
        I['glab'] = self.dram_in('glab', [DEPTH, 64, 4])
        I['glan'] = self.dram_in('glan', [DEPTH, 128, 1])
        I['sgla'] = self.dram_in('sgla', [DEPTH, 4, 64, 128])
        I['relb'] = self.dram_in('relb', [32, 8])
        I['oh'] = self.dram_in('oh', [32, 384])
        I['ck'] = self.dram_in('ck', [DEPTH, PAST, 128])
        I['cv'] = self.dram_in('cv', [DEPTH, PAST, 128])
        I['cki'] = self.dram_in('cki', [DEPTH, PAST, 128])
        I['gcw'] = self.dram_in('gcw', [DEPTH, 128, 12, 4])
        I['gpar'] = self.dram_in('gpar', [DEPTH, 128, 8])
        I['gnorm'] = self.dram_in('gnorm', [DEPTH, 128, 1])
        I['sgdn'] = self.dram_in('sgdn', [DEPTH, 4, 128, 128])
        I['sconv'] = self.dram_in('sconv', [DEPTH, 128, 12, 3])
        I['s5p'] = self.dram_in('s5p', [DEPTH, 128, 3, 16])
        I['s5b'] = self.dram_in('s5b', [DEPTH, 2, 2, 128, 4, 128])
        I['s5c'] = self.dram_in('s5c', [DEPTH, 2, 128, 16, 128])
        I['s5d'] = self.dram_in('s5d', [DEPTH, 128, 4])
        I['s5h0'] = self.dram_in('s5h0', [DEPTH, 128, 2, 16])
        I['wglu'] = self.dram_in('wglu', [DEPTH, 512, 1024])
        O['yp'] = self.dram_out('yp', [T, D])
        O['ys'] = self.dram_out('ys', [DEC_SEQ, D])
        for g, tt in (('p', T), ('s', DEC_SEQ)):
            O['k' + g] = self.dram_out('k' + g, [DEPTH, tt, 128])
            O['v' + g] = self.dram_out('v' + g, [DEPTH, tt, 128])
            O['ki' + g] = self.dram_out('ki' + g, [DEPTH, tt, 64])
            O['gla' + g] = self.dram_out('gla' + g, [DEPTH, 4, 64, 128])
            O['conv' + g] = self.dram_out('conv' + g, [DEPTH, 3, 1536])
            O['gdn' + g] = self.dram_out('ogdn' + g, [DEPTH, 4, 128, 128])
            O['s5' + g] = self.dram_out('os5' + g, [DEPTH, 128, 2, 16])
        self.I, self.O = I, O
        self.y0p = nc.dram_tensor('y0p', [T, D], F32).ap()
        self.fd = nc.dram_tensor('fd', [8, 384], F32).ap()
        self.winb = nc.dram_tensor('winb', [DEPTH, NCH, 128, 8 * 128], BF16).ap()
        self.wbrb = nc.dram_tensor('wbrb', [DEPTH, 4, 8, 128, 4 * 128], BF16).ap()
        self.woutb = nc.dram_tensor('woutb', [DEPTH, 8, 128, 8 * 128], BF16).ap()
        self.cfd = nc.dram_tensor('cfd', [8, 1], F32).ap()
        self.y0s = nc.dram_tensor('y0s', [DEC_SEQ, D], F32).ap()
        with ExitStack() as st:
            S = Sched(nc, st)
            self.S = S
            self.dtk = {}
            self.setup()
            self.precast()
            for l in range(DEPTH):
                self.layer_setup(l)
                self.run_seq(l, 'p')
                self.run_seq(l, 's')
            S.finish(list(self.dtk.values()))
        return nc

    def dk(self, name):
        if name not in self.dtk:
            self.dtk[name] = Tk(None)
        return self.dtk[name]

    def setup(self):
        S = self.S
        G = self.G
        self.cst = S.sb([128, 1024])
        S.dma('sp', self.cst[:], self.I['cst'][:, :], writes=[self.cst])
        self.ident = self.cst
        self.pbanks = [S.ps([128, 512]) for _ in range(6)]
        self.pacc = [S.ps([128, 512]) for _ in range(2)]
        self.pb_i = 0
        self.xT = S.sb([128, 8, G])
        self.xTb = S.sb([128, 8, G], BF16)
        self.wch = [S.sb([128, 8, 128]) for _ in range(2)]
        self.wch_i = 0
        self.wchb = [S.sb([128, 8, 128], BF16) for _ in range(4)]
        self.wchb_i = 0
        self.brT = S.sb([128, 16, G], BF16)
        self.lng = S.sb([128, D])
        self.lnb = S.sb([128, D])
        self.wouts = [S.sb([128, 8, 128], BF16) for _ in range(1)]
        self.wbrs = [S.sb([128, 4, 128], BF16) for _ in range(2)]
        self.wbrBs = [S.sb([64, 8, 128], BF16) for _ in range(2)]
        self.wrr = 0
        self.brB = S.sb([64, 8, G], BF16)
        self.ARENA = 10240
        self.arena = S.sb([128, self.ARENA])
        self.fence_t = S.sb([1, 4])
        self.phase_views = {}
        self.phase_off = {}
        self.cur_phase = None
        self.fence_tok = None
        self.scr_i = 0
        self.ws_i = 0
        self.small = [S.sb([128, 16]) for _ in range(8)]
        self.sm3 = S.sb([128, 3, 16])
        self.small_i = 0
        self.epsb = S.sb([128, 2])
        S.op('dve', lambda h: h.memset(self.epsb[:, 0:1], LN_EPS), writes=[self.epsb])
        S.op('dve', lambda h: h.memset(self.epsb[:, 1:2], RMS_EPS), writes=[self.epsb])
        self.LC = min(128, G)
        self.s5_cos = S.sb([128, 16, self.LC])
        self.s5_sin = S.sb([128, 16, self.LC])
        self.s5_cr = S.sb([128, 16, 128])
        self.s5_ci = S.sb([128, 16, 128])
        self.s5_b = S.sb([128, 4, 4, 128])
        self.s5_q = S.sb([128, 12, 16])
        self.s5_d = S.sb([128, 4])
        self.s5_carry = S.sb([128, 2, 16])
        self.wglu = S.sb([128, 4, 128])
        T = self.T
        self.NK = max(T, PAST + 128)
        self.NH = 4096 if T == 8192 else self.NK
        self.NB = self.NK // 128
        self.kT = S.sb([128, self.NK], BF16)
        self.kiT = S.sb([128, self.NH])
        self.vA = S.sb([128, self.NB, 130], BF16)
        self.bias8 = S.sb([128, 4, 512])
        self.qT = S.sb([128, 4, G], BF16)
        self.qiT = S.sb([128, 4, G])
        self.wiT = S.sb([128, 4])
        self.offrow = S.sb([33, 4 * G], BF16)
        self.kmax2 = S.sb([33, 2])
        self.cf = S.sb([33, 4])
        self.bs = S.sb([128, 16])
        self.Et = [S.sb([128, 4 * G], BF16) for _ in range(2)]
        self.Pt = [S.sb([128, 4 * G], BF16) for _ in range(2)]
        self.selT = [S.sb([128, G], BF16) for _ in range(2)]
        self.onesb = S.sb([33, 128], BF16)
        self.phase('setup')
        self.dsa_setup()
        self.gdn_S = S.sb([128, 4, 128])
        self.gdn_cw = S.sb([128, 12, 4])
        self.gdn_par = S.sb([128, 8])
        self.gdn_n = S.sb([128, 1])
        self.gdn_tail = S.sb([128, 12, 3])
        self.gla_S = S.sb([64, 4, 128])
        self.gla_w = S.sb([16, 256])
        self.gla_b = S.sb([64, 4])
        self.gla_n = S.sb([128, 1])
        self.resetm = S.sb([64, 4 * G])

    def pbank(self):
        t = self.pbanks[self.pb_i]
        self.pb_i = (self.pb_i + 1) % len(self.pbanks)
        return t

    def phase(self, name):
        S = self.S
        if name == self.cur_phase:
            return
        prev = list(self.phase_views.get(self.cur_phase, {}).values()) if self.cur_phase else []
        nxt = list(self.phase_views.get(name, {}).values())
        ft = self.fence_t
        S.op('dve', lambda h: h.memset(ft[0:1, 0:1], 0.0), reads=[], writes=[ft] + prev + nxt)
        self.fence_tok = ft.w
        self.cur_phase = name
        self.phase_views.setdefault(name, {})
        self.phase_off.setdefault(name, 0)
        self.scr_i = 0
        self.ws_i = 0

    def av(self, key, ncols):
        pv = self.phase_views[self.cur_phase]
        if key not in pv:
            off = self.phase_off[self.cur_phase]
            assert off + ncols <= self.ARENA, (self.cur_phase, key, off, ncols)
            t = Tk(self.arena.h[:, off:off + ncols])
            t.w = self.fence_tok
            pv[key] = t
            self.phase_off[self.cur_phase] = off + ncols
        return pv[key]

    def scratch(self):
        t = self.av(('scr', self.scr_i % 8), 512)
        self.scr_i += 1
        return t

    def wsp(self, reset=False):
        if reset:
            self.ws_i = 0
        t = self.av(('ws', self.ws_i), 512)
        self.ws_i += 1
        return t

    def sm(self):
        t = self.small[self.small_i]
        self.small_i = (self.small_i + 1) % len(self.small)
        return t

    def layer_setup(self, l):
        S = self.S
        I = self.I
        S.dma('sp', self.lng[:], I['lng'][l:l + 1, :].partition_broadcast(128), writes=[self.lng])
        S.dma('sp', self.lnb[:], I['lnb'][l:l + 1, :].partition_broadcast(128), writes=[self.lnb])
        S.dma('sp', self.gla_w[:], I['glaw'][l], writes=[self.gla_w])
        S.dma('sp', self.gla_b[:], I['glab'][l], writes=[self.gla_b])
        S.dma('sp', self.gla_n[:], I['glan'][l], writes=[self.gla_n])
        self.phase('setup')
        self.s5_setup(l)
        self.gdn_setup(l)

    FP32_CHUNKS = ('kvC', 'kvD', 'qi0', 'qi1', 'qi2', 'qi3')

    def precast(self):
        S = self.S
        I = self.I
        st32 = self.wch[0]
        n = 0

        def piece(src_ap, rows, cols3, dst_ap, key):
            nonlocal n
            a, c = cols3
            w16 = self.wchb[n % len(self.wchb)]
            S.dma('sp', st32[0:rows, 0:a, 0:c], src_ap, writes=[st32])
            eng = 'act' if n % 2 else 'dve'
            if eng == 'act':
                S.op('act', lambda h: h.activation(out=w16[0:rows, 0:a, 0:c], in_=st32[0:rows, 0:a, 0:c], func=AF.Copy), reads=[st32], writes=[w16])
            else:
                S.op('dve', lambda h: h.tensor_copy(out=w16[0:rows, 0:a, 0:c], in_=st32[0:rows, 0:a, 0:c]), reads=[st32], writes=[w16])
            S.dma('sp', dst_ap, w16[0:rows, 0:a, 0:c], reads=[w16], writes=[self.dk(key)])
            n += 1
        for l in range(DEPTH):
            for name, _ in CHUNKS:
                if name in self.FP32_CHUNKS:
                    continue
                ci = CIDX[name]
                piece(I['win'][l, ci], 128, (8, 128), self.winb[l, ci].rearrange("p (a c) -> p a c", c=128), ('winb', l, ci))
            for b in range(4):
                for dc in range(8):
                    if b == 1:
                        piece(I['wbr'][l, b, :, dc * 128:(dc + 1) * 128].rearrange("(a p) c -> p a c", p=64), 64, (8, 128),
                              self.wbrb[l, b, dc].rearrange("(p h) f -> p (h f)", h=2).rearrange("p (a c) -> p a c", c=128), ('wbrb', l, b, dc))
                    else:
                        piece(I['wbr'][l, b, :, dc * 128:(dc + 1) * 128].rearrange("(a p) c -> p a c", p=128), 128, (4, 128),
                              self.wbrb[l, b, dc].rearrange("p (a c) -> p a c", c=128), ('wbrb', l, b, dc))
            for qt in range(8):
                piece(I['wout'][l, :, qt * 128:(qt + 1) * 128].rearrange("(a p) c -> p a c", p=128), 128, (8, 128),
                      self.woutb[l, qt].rearrange("p (a c) -> p a c", c=128), ('woutb', l, qt))

    def load_w(self, l, name):
        S = self.S
        w = self.wch[self.wch_i]
        self.wch_i = (self.wch_i + 1) % len(self.wch)
        S.dma('sp', w[:], self.I['win'][l, CIDX[name]], writes=[w])
        return w

    BF16_PREFIX = ('zzz',)

    def load_wx(self, l, name):
        S = self.S
        if name in self.FP32_CHUNKS:
            return self.load_w(l, name), self.xT
        w = self.wchb[self.wchb_i]
        self.wchb_i = (self.wchb_i + 1) % len(self.wchb)
        S.dma('sp', w[:], self.winb[l, CIDX[name]].rearrange("p (a c) -> p a c", c=128), reads=[self.dk(('winb', l, CIDX[name]))], writes=[w])
        return w, self.xTb

    def proj_fm(self, l, name, ncols, gw):
        S = self.S
        w, xT = self.load_wx(l, name)
        pb = self.pbank()
        for k in range(8):
            mm = 128 if xT is self.xTb else ncols
            S.op('pe', lambda h, k=k: h.matmul(pb[0:mm, 0:gw], lhsT=w[:, k, 0:mm], rhs=xT[:, k, 0:gw],
                                               start=(k == 0), stop=(k == 7)), reads=[w, xT], writes=[pb])
        return pb

    def proj_tm(self, l, name, ncols, t0, tw):
        S = self.S
        w, xT = self.load_wx(l, name)
        pb = self.pbank()
        for k in range(8):
            S.op('pe', lambda h, k=k: h.matmul(pb[0:tw, 0:ncols], lhsT=xT[:, k, t0:t0 + tw], rhs=w[:, k, 0:ncols],
                                               start=(k == 0), stop=(k == 7)), reads=[w, xT], writes=[pb])
        return pb

    def run_seq(self, l, grp):
        S = self.S
        I, O = self.I, self.O
        T = self.T if grp == 'p' else DEC_SEQ
        G = min(self.G, T)
        TP = min(128, T)
        C = min(64, T)
        nG = T // G
        if l == 0:
            xin = I['xp'] if grp == 'p' else I['xs']
            xin_tk = self.dk('in')
        else:
            xin = self.y0p if grp == 'p' else self.y0s
            xin_tk = self.dk('y0' + grp)
        yout = (O['yp'] if grp == 'p' else O['ys']) if l == DEPTH - 1 else (self.y0p if grp == 'p' else self.y0s)
        yout_tk = self.dk('yout' + grp) if l == DEPTH - 1 else self.dk('y0' + grp)

        if grp == 'p':
            S.op('dve', lambda h: h.memset(self.gla_S[:], 0.0), writes=[self.gla_S])
        else:
            S.dma('sp', self.gla_S[:], I['sgla'][l].rearrange("h k v -> k h v"), writes=[self.gla_S])
        self.s5_init(l, grp)
        self.gdn_init(l, grp)
        self.dsa_init(l, grp)
        S.op('dve', lambda h: h.memset(self.resetm[:], 1.0), writes=[self.resetm])
        rm3 = self.resetm[:, 0:4 * G].rearrange("p (a c) -> p a c", c=C)
        S.op('dve', lambda h: h.memset(rm3[:, :, 0:1], 0.0), writes=[self.resetm])

        for gi in range(nG):
            g0 = gi * G
            self.phase('kv')
            xts = []
            for ti in range(G // TP):
                xt = self.av('xtm', D)
                xts.append(xt)
                S.dma('sp', xt[0:TP, :], xin[g0 + ti * TP: g0 + (ti + 1) * TP, :], reads=[xin_tk], writes=[xt])
                for half in range(2):
                    pb = self.pbank()
                    for kk in range(4):
                        k = half * 4 + kk
                        S.op('pe', lambda h, k=k, kk=kk: h.transpose(pb[:, kk * 128: kk * 128 + TP], xt[0:TP, k * 128:(k + 1) * 128], self.ident[0:TP, 0:TP]),
                             reads=[xt, self.cst], writes=[pb])
                    dst = self.xT[:, half * 4:(half + 1) * 4, ti * TP:(ti + 1) * TP]
                    src = pb[:, :].rearrange("p (a c) -> p a c", c=128)[:, :, 0:TP]
                    dstb = self.xTb[:, half * 4:(half + 1) * 4, ti * TP:(ti + 1) * TP]
                    S.op('dve', lambda h: h.tensor_copy(out=dst, in_=src), reads=[pb], writes=[self.xT])
                    S.op('act', lambda h: h.activation(out=dstb, in_=dst, func=AF.Copy), reads=[self.xT], writes=[self.xTb])
            self.branch_zero(G)
            self.phase('kv')
            self.kv_part(l, grp, g0, G, TP)
            self.phase('s5')
            self.s5_part(l, grp, g0, G, last=(gi == nG - 1))
            self.phase('gla')
            self.gla_part(l, grp, g0, G, C, last=(gi == nG - 1))
            self.phase('gdn')
            self.gdn_part(l, grp, g0, G, C, last=(gi == nG - 1))
            self.phase('dsa')
            self.dsa_part(l, grp, g0, G, TP)
            self.phase('out')
            self.out_stage(l, grp, g0, G, TP, xin, xin_tk, yout, yout_tk)

    def branch_zero(self, G):
        S = self.S
        S.op('dve', lambda h: h.memset(self.brT[:, :, 0:G], 0.0), writes=[self.brT])
        S.op('dve', lambda h: h.memset(self.brB[:, :, 0:G], 0.0), writes=[self.brB])

    def kv_part(self, l, grp, g0, G, TP):
        S = self.S
        O = self.O
        kbase = 0 if grp == 'p' else PAST
        for ti in range(G // TP):
            t0 = ti * TP
            tiles = []
            for nm, ncols, oname, key in (('kvA', 128, 'k', 'k_tm'), ('kvB', 128, 'v', 'v_tm'), ('kvC', 128, 'ki', 'c_tm')):
                pb = self.proj_tm(l, nm, ncols, t0, TP)
                sc = self.av(key, 128)
                S.op('act', lambda h: h.activation(out=sc[0:TP, 0:ncols], in_=pb[0:TP, 0:ncols], func=AF.Copy), reads=[pb], writes=[sc])
                oc = 64 if oname == 'ki' else 128
                S.dma('pool', O[oname + grp][l, g0 + t0: g0 + t0 + TP, :], sc[0:TP, 0:oc], reads=[sc], writes=[self.dk('o' + oname + grp)])
                tiles.append(sc)
            self.add_keys(tiles[0], tiles[1], tiles[2], TP, kbase + g0 + t0)
            pb = self.proj_tm(l, 'kvD', 12, t0, TP)
            S.op('dve', lambda h: h.tensor_copy(out=self.wiT[0:TP, 0:4], in_=pb[0:TP, 0:4]), reads=[pb], writes=[self.wiT])
        T = self.T if grp == 'p' else DEC_SEQ
        if g0 + G == T:
            for c in range(12):
                w, xTw = self.load_wx(l, 'cq%d' % c)
                pb = self.pbank()
                for k in range(8):
                    S.op('pe', lambda h, k=k: h.matmul(pb[0:3, 0:128], lhsT=xTw[:, k, G - 3:G], rhs=w[:, k, :],
                                                       start=(k == 0), stop=(k == 7)), reads=[w, xTw], writes=[pb])
                sc = self.scratch()
                S.op('act', lambda h: h.activation(out=sc[0:3, 0:128], in_=pb[0:3, 0:128], func=AF.Copy), reads=[pb], writes=[sc])
                S.dma('pool', O['conv' + grp][l, :, c * 128:(c + 1) * 128], sc[0:3, 0:128], reads=[sc], writes=[self.dk('oconv' + grp)])


    def range_reduce(self, x, n):
        S = self.S
        TWO_PI = 2.0 * math.pi
        ki = self.s5_ki
        kf = self.s5_kf
        S.op('dve', lambda h: h.tensor_scalar(out=kf[:, 0:n], in0=x, scalar1=1.0 / TWO_PI, scalar2=None, op0=ALU.mult), reads=[self.s5_ang], writes=[self.s5_kfT])
        S.op('dve', lambda h: h.tensor_copy(out=ki[:, 0:n], in_=kf[:, 0:n]), reads=[self.s5_kfT], writes=[self.s5_kiT])
        S.op('dve', lambda h: h.tensor_copy(out=kf[:, 0:n], in_=ki[:, 0:n]), reads=[self.s5_kiT], writes=[self.s5_kfT])
        S.op('dve', lambda h: h.scalar_tensor_tensor(out=x, in0=kf[:, 0:n], scalar=-TWO_PI, in1=x, op0=ALU.mult, op1=ALU.add), reads=[self.s5_kfT, self.s5_ang], writes=[self.s5_ang])
        S.op('dve', lambda h: h.tensor_scalar(out=kf[:, 0:n], in0=x, scalar1=math.pi, scalar2=-TWO_PI, op0=ALU.is_gt, op1=ALU.mult), reads=[self.s5_ang], writes=[self.s5_kfT])
        S.op('dve', lambda h: h.tensor_tensor(out=x, in0=x, in1=kf[:, 0:n], op=ALU.add), reads=[self.s5_kfT, self.s5_ang], writes=[self.s5_ang])
        S.op('dve', lambda h: h.tensor_scalar(out=kf[:, 0:n], in0=x, scalar1=-math.pi, scalar2=TWO_PI, op0=ALU.is_lt, op1=ALU.mult), reads=[self.s5_ang], writes=[self.s5_kfT])
        S.op('dve', lambda h: h.tensor_tensor(out=x, in0=x, in1=kf[:, 0:n], op=ALU.add), reads=[self.s5_kfT, self.s5_ang], writes=[self.s5_ang])

    def s5_setup(self, l):
        S = self.S
        I = self.I
        LC = self.LC
        q = self.s5_q
        if not hasattr(self, 's5_ang'):
            self.s5_ang = S.sb([128, 128])
            self.s5_kfT = S.sb([128, 128])
            self.s5_kiT = S.sb([128, 128], mybir.dt.int32)
            self.s5_kf = self.s5_kfT
            self.s5_ki = self.s5_kiT
        ang = self.s5_ang
        raw = self.sm3
        S.dma('sp', raw[:, :, :], I['s5p'][l], writes=[raw])
        S.dma('sp', self.s5_d[:, :], I['s5d'][l], writes=[self.s5_d])
        S.dma('sp', self.s5_b[:, :, :, :], I['s5b'][l].rearrange("v r p c f -> p (v r) c f"), writes=[self.s5_b])
        Q = lambda i: q[:, i, :]
        rq = [q]
        S.op('dve', lambda h: h.tensor_scalar(out=Q(4), in0=raw[:, 0, :], scalar1=-1e-4, scalar2=None, op0=ALU.min), reads=[raw], writes=rq)
        S.op('dve', lambda h: h.tensor_copy(out=Q(5), in_=raw[:, 1, :]), reads=[raw], writes=rq)
        S.op('act', lambda h: h.activation(out=Q(6), in_=raw[:, 2, :], func=AF.Exp), reads=[raw], writes=rq)
        S.op('dve', lambda h: h.tensor_tensor(out=Q(7), in0=Q(4), in1=Q(6), op=ALU.mult), reads=rq, writes=rq)
        S.op('act', lambda h: h.activation(out=Q(0), in_=Q(7), func=AF.Exp), reads=rq, writes=rq)
        S.op('dve', lambda h: h.tensor_tensor(out=Q(1), in0=Q(5), in1=Q(6), op=ALU.mult), reads=rq, writes=rq)
        S.op('dve', lambda h: h.tensor_copy(out=ang[:, 0:16], in_=Q(1)), reads=rq, writes=[ang])
        self.range_reduce(ang[:, 0:16], 16)
        S.op('dve', lambda h: h.tensor_copy(out=Q(1), in_=ang[:, 0:16]), reads=[ang], writes=rq)
        S.op('act', lambda h: h.activation(out=Q(8), in_=ang[:, 0:16], func=AF.Sin), reads=[ang], writes=rq)
        S.op('dve', lambda h: h.tensor_scalar(out=ang[:, 0:16], in0=ang[:, 0:16], scalar1=math.pi / 2, scalar2=None, op0=ALU.add), reads=[ang], writes=[ang])
        self.range_reduce(ang[:, 0:16], 16)
        S.op('act', lambda h: h.activation(out=Q(9), in_=ang[:, 0:16], func=AF.Sin), reads=[ang], writes=rq)
        S.op('dve', lambda h: h.tensor_tensor(out=Q(9), in0=Q(9), in1=Q(0), op=ALU.mult), reads=rq, writes=rq)
        S.op('dve', lambda h: h.tensor_tensor(out=Q(8), in0=Q(8), in1=Q(0), op=ALU.mult), reads=rq, writes=rq)
        S.op('dve', lambda h: h.tensor_scalar(out=Q(9), in0=Q(9), scalar1=-1.0, scalar2=None, op0=ALU.add), reads=rq, writes=rq)
        S.op('dve', lambda h: h.tensor_tensor(out=Q(10), in0=Q(4), in1=Q(4), op=ALU.mult), reads=rq, writes=rq)
        S.op('dve', lambda h: h.tensor_tensor(out=Q(11), in0=Q(5), in1=Q(5), op=ALU.mult), reads=rq, writes=rq)
        S.op('dve', lambda h: h.tensor_tensor(out=Q(10), in0=Q(10), in1=Q(11), op=ALU.add), reads=rq, writes=rq)
        S.op('dve', lambda h: h.reciprocal(out=Q(10), in_=Q(10)), reads=rq, writes=rq)
        S.op('dve', lambda h: h.tensor_tensor(out=Q(2), in0=Q(9), in1=Q(4), op=ALU.mult), reads=rq, writes=rq)
        S.op('dve', lambda h: h.tensor_tensor(out=Q(11), in0=Q(8), in1=Q(5), op=ALU.mult), reads=rq, writes=rq)
        S.op('dve', lambda h: h.tensor_tensor(out=Q(2), in0=Q(2), in1=Q(11), op=ALU.add), reads=rq, writes=rq)
        S.op('dve', lambda h: h.tensor_tensor(out=Q(2), in0=Q(2), in1=Q(10), op=ALU.mult), reads=rq, writes=rq)
        S.op('dve', lambda h: h.tensor_tensor(out=Q(3), in0=Q(8), in1=Q(4), op=ALU.mult), reads=rq, writes=rq)
        S.op('dve', lambda h: h.tensor_tensor(out=Q(11), in0=Q(9), in1=Q(5), op=ALU.mult), reads=rq, writes=rq)
        S.op('dve', lambda h: h.tensor_tensor(out=Q(3), in0=Q(3), in1=Q(11), op=ALU.subtract), reads=rq, writes=rq)
        S.op('dve', lambda h: h.tensor_tensor(out=Q(3), in0=Q(3), in1=Q(10), op=ALU.mult), reads=rq, writes=rq)
        for st_ in range(16):
            for which, tab in ((0, self.s5_sin), (1, self.s5_cos)):
                S.op('dve', lambda h, st_=st_, which=which: h.tensor_scalar(out=ang[:, 0:LC], in0=self.cst[:, 512:512 + LC], scalar1=q[:, 1, st_:st_ + 1],
                                                                         scalar2=(math.pi / 2 if which else 0.0), op0=ALU.mult, op1=ALU.add), reads=[self.cst, q], writes=[ang])
                self.range_reduce(ang[:, 0:LC], LC)
                S.op('act', lambda h, st_=st_, tab=tab: h.activation(out=tab[:, st_, :], in_=ang[:, 0:LC], func=AF.Sin), reads=[ang], writes=[tab])
        for st_ in range(16):
            c_r = self.scratch()
            c_i = self.scratch()
            S.dma('sp', c_r[:, 0:128], I['s5c'][l, 0, :, st_, :], writes=[c_r])
            S.dma('sp', c_i[:, 0:128], I['s5c'][l, 1, :, st_, :], writes=[c_i])
            t1 = self.scratch()
            S.op('dve', lambda h, st_=st_: h.tensor_scalar(out=t1[:, 0:128], in0=c_i[:, 0:128], scalar1=q[:, 3, st_:st_ + 1], scalar2=None, op0=ALU.mult), reads=[c_i, q], writes=[t1])
            S.op('dve', lambda h, st_=st_: h.scalar_tensor_tensor(out=self.s5_cr[:, st_, :], in0=c_r[:, 0:128], scalar=q[:, 2, st_:st_ + 1], in1=t1[:, 0:128], op0=ALU.mult, op1=ALU.subtract),
                 reads=[c_r, q, t1], writes=[self.s5_cr])
            S.op('dve', lambda h, st_=st_: h.tensor_scalar(out=t1[:, 0:128], in0=c_i[:, 0:128], scalar1=q[:, 2, st_:st_ + 1], scalar2=-1.0, op0=ALU.mult, op1=ALU.mult), reads=[c_i, q], writes=[t1])
            S.op('dve', lambda h, st_=st_: h.tensor_scalar(out=c_r[:, 0:128], in0=c_r[:, 0:128], scalar1=q[:, 3, st_:st_ + 1], scalar2=None, op0=ALU.mult), reads=[c_r, q], writes=[c_r])
            S.op('dve', lambda h, st_=st_: h.tensor_tensor(out=self.s5_ci[:, st_, :], in0=t1[:, 0:128], in1=c_r[:, 0:128], op=ALU.subtract), reads=[t1, c_r], writes=[self.s5_ci])

    def s5_init(self, l, grp):
        S = self.S
        q = self.s5_q
        car = self.s5_carry
        if grp == 'p':
            S.op('dve', lambda h: h.memset(car[:, :, :], 0.0), writes=[car])
            return
        h0 = self.sm3
        S.dma('sp', h0[:, 0:2, :], self.I['s5h0'][l], writes=[h0])
        m = self.sm()
        n2 = self.sm()
        t = self.sm()
        S.op('dve', lambda h: h.tensor_tensor(out=m[:, 0:16], in0=q[:, 2, :], in1=q[:, 2, :], op=ALU.mult), reads=[q], writes=[m])
        S.op('dve', lambda h: h.tensor_tensor(out=n2[:, 0:16], in0=q[:, 3, :], in1=q[:, 3, :], op=ALU.mult), reads=[q], writes=[n2])
        S.op('dve', lambda h: h.tensor_tensor(out=m[:, 0:16], in0=m[:, 0:16], in1=n2[:, 0:16], op=ALU.add), reads=[m, n2], writes=[m])
        S.op('dve', lambda h: h.reciprocal(out=m[:, 0:16], in_=m[:, 0:16]), reads=[m], writes=[m])
        S.op('dve', lambda h: h.tensor_tensor(out=t[:, 0:16], in0=h0[:, 0, :], in1=q[:, 2, :], op=ALU.mult), reads=[h0, q], writes=[t])
        S.op('dve', lambda h: h.tensor_tensor(out=n2[:, 0:16], in0=h0[:, 1, :], in1=q[:, 3, :], op=ALU.mult), reads=[h0, q], writes=[n2])
        S.op('dve', lambda h: h.tensor_tensor(out=t[:, 0:16], in0=t[:, 0:16], in1=n2[:, 0:16], op=ALU.add), reads=[t, n2], writes=[t])
        S.op('dve', lambda h: h.tensor_tensor(out=car[:, 0, :], in0=t[:, 0:16], in1=m[:, 0:16], op=ALU.mult), reads=[t, m], writes=[car])
        S.op('dve', lambda h: h.tensor_tensor(out=t[:, 0:16], in0=h0[:, 1, :], in1=q[:, 2, :], op=ALU.mult), reads=[h0, q], writes=[t])
        S.op('dve', lambda h: h.tensor_tensor(out=n2[:, 0:16], in0=h0[:, 0, :], in1=q[:, 3, :], op=ALU.mult), reads=[h0, q], writes=[n2])
        S.op('dve', lambda h: h.tensor_tensor(out=t[:, 0:16], in0=t[:, 0:16], in1=n2[:, 0:16], op=ALU.subtract), reads=[t, n2], writes=[t])
        S.op('dve', lambda h: h.tensor_tensor(out=car[:, 1, :], in0=t[:, 0:16], in1=m[:, 0:16], op=ALU.mult), reads=[t, m], writes=[car])

    def s5_final(self, l, grp):
        S = self.S
        q = self.s5_q
        car = self.s5_carry
        o = self.sm3
        t = self.sm()
        S.op('dve', lambda h: h.tensor_tensor(out=o[:, 0, :], in0=car[:, 0, :], in1=q[:, 2, :], op=ALU.mult), reads=[car, q], writes=[o])
        S.op('dve', lambda h: h.tensor_tensor(out=t[:, 0:16], in0=car[:, 1, :], in1=q[:, 3, :], op=ALU.mult), reads=[car, q], writes=[t])
        S.op('dve', lambda h: h.tensor_tensor(out=o[:, 0, :], in0=o[:, 0, :], in1=t[:, 0:16], op=ALU.subtract), reads=[o, t], writes=[o])
        S.op('dve', lambda h: h.tensor_tensor(out=o[:, 1, :], in0=car[:, 0, :], in1=q[:, 3, :], op=ALU.mult), reads=[car, q], writes=[o])
        S.op('dve', lambda h: h.tensor_tensor(out=t[:, 0:16], in0=car[:, 1, :], in1=q[:, 2, :], op=ALU.mult), reads=[car, q], writes=[t])
        S.op('dve', lambda h: h.tensor_tensor(out=o[:, 1, :], in0=o[:, 1, :], in1=t[:, 0:16], op=ALU.add), reads=[o, t], writes=[o])
        S.dma('pool', self.O['s5' + grp][l], o[:, 0:2, :], reads=[o], writes=[self.dk('os5' + grp)])

    def s5_part(self, l, grp, g0, G, last):
        S = self.S
        I = self.I
        LC = min(self.LC, G)
        ug = self.av('ug', 4 * G)
        ug3 = ug.h[:, 0:4 * G].rearrange("p (a c) -> p a c", c=G)
        q = self.s5_q
        car = self.s5_carry
        for c in range(4):
            pb = self.proj_fm(l, 'au%d' % c, 128, G)
            S.op('act' if c % 2 else 'dve', (lambda h, c=c: h.activation(out=ug3[:, c, 0:G], in_=pb[:, 0:G], func=AF.Copy)) if c % 2 else (lambda h, c=c: h.tensor_copy(out=ug3[:, c, 0:G], in_=pb[:, 0:G])),
                 reads=[pb], writes=[ug])
        yT = self.wsp(True)
        y3 = yT[:, 0:4 * G].rearrange("p (a c) -> p a c", c=G)
        for s0 in range(0, G, LC):
            for c in range(4):
                py = self.pacc[c % 2]
                for s4 in range(4):
                    st_ = 4 * c + s4
                    brow = slice(32 * s4, 32 * s4 + 32) if s4 < 3 else slice(64, 128)
                    bvar = 0 if s4 < 3 else 2
                    pr = self.pbank()
                    pi = self.pbank()
                    S.op('pe', lambda h: h.matmul(pr[:, 0:LC], lhsT=self.s5_b[brow, bvar + 0, c, :], rhs=ug3[brow, c, s0:s0 + LC], start=True, stop=True),
                         reads=[self.s5_b, ug], writes=[pr])
                    S.op('pe', lambda h: h.matmul(pi[:, 0:LC], lhsT=self.s5_b[brow, bvar + 1, c, :], rhs=ug3[brow, c, s0:s0 + LC], start=True, stop=True),
                         reads=[self.s5_b, ug], writes=[pi])
                    cs = self.s5_cos[:, st_, 0:LC]
                    sn = self.s5_sin[:, st_, 0:LC]
                    a = self.scratch(); b = self.scratch(); xr = self.scratch(); xi = self.scratch()
                    rt = [self.s5_cos, self.s5_sin]
                    S.op('dve', lambda h: h.tensor_tensor(out=a[:, 0:LC], in0=pr[:, 0:LC], in1=cs, op=ALU.mult), reads=[pr] + rt, writes=[a])
                    S.op('dve', lambda h: h.tensor_tensor(out=b[:, 0:LC], in0=pi[:, 0:LC], in1=sn, op=ALU.mult), reads=[pi] + rt, writes=[b])
                    S.op('pool', lambda h: h.tensor_tensor(out=xr[:, 0:LC], in0=a[:, 0:LC], in1=b[:, 0:LC], op=ALU.add), reads=[a, b], writes=[xr])
                    a2 = self.scratch(); b2 = self.scratch()
                    S.op('dve', lambda h: h.tensor_tensor(out=a2[:, 0:LC], in0=pi[:, 0:LC], in1=cs, op=ALU.mult), reads=[pi] + rt, writes=[a2])
                    S.op('dve', lambda h: h.tensor_tensor(out=b2[:, 0:LC], in0=pr[:, 0:LC], in1=sn, op=ALU.mult), reads=[pr] + rt, writes=[b2])
                    S.op('pool', lambda h: h.tensor_tensor(out=xi[:, 0:LC], in0=a2[:, 0:LC], in1=b2[:, 0:LC], op=ALU.subtract), reads=[a2, b2], writes=[xi])
                    rho = q[:, 0, st_:st_ + 1].to_broadcast([128, LC])
                    S.op('dve', lambda h: h.tensor_tensor_scan(out=xr[:, 0:LC], data0=rho, data1=xr[:, 0:LC], initial=car[:, 0, st_:st_ + 1], op0=ALU.mult, op1=ALU.add),
                         reads=[q, xr, car], writes=[xr])
                    S.op('dve', lambda h: h.tensor_tensor_scan(out=xi[:, 0:LC], data0=rho, data1=xi[:, 0:LC], initial=car[:, 1, st_:st_ + 1], op0=ALU.mult, op1=ALU.add),
                         reads=[q, xi, car], writes=[xi])
                    S.op('pool', lambda h: h.tensor_tensor(out=a[:, 0:LC], in0=xr[:, 0:LC], in1=cs, op=ALU.mult), reads=[xr] + rt, writes=[a])
                    S.op('pool', lambda h: h.tensor_tensor(out=b[:, 0:LC], in0=xi[:, 0:LC], in1=sn, op=ALU.mult), reads=[xi] + rt, writes=[b])
                    S.op('dve', lambda h: h.tensor_tensor(out=a[:, 0:LC], in0=a[:, 0:LC], in1=b[:, 0:LC], op=ALU.subtract), reads=[a, b], writes=[a])
                    S.op('pool', lambda h: h.tensor_tensor(out=a2[:, 0:LC], in0=xr[:, 0:LC], in1=sn, op=ALU.mult), reads=[xr] + rt, writes=[a2])
                    S.op('pool', lambda h: h.tensor_tensor(out=b2[:, 0:LC], in0=xi[:, 0:LC], in1=cs, op=ALU.mult), reads=[xi] + rt, writes=[b2])
                    S.op('dve', lambda h: h.tensor_tensor(out=a2[:, 0:LC], in0=a2[:, 0:LC], in1=b2[:, 0:LC], op=ALU.add), reads=[a2, b2], writes=[a2])
                    S.op('dve', lambda h: h.tensor_copy(out=car[:, 0, st_:st_ + 1], in_=a[:, LC - 1:LC]), reads=[a], writes=[car])
                    S.op('dve', lambda h: h.tensor_copy(out=car[:, 1, st_:st_ + 1], in_=a2[:, LC - 1:LC]), reads=[a2], writes=[car])
                    S.op('pe', lambda h: h.matmul(py[:, 0:LC], lhsT=self.s5_cr[:, st_, :], rhs=a[:, 0:LC], start=(s4 == 0), stop=False), reads=[self.s5_cr, a], writes=[py])
                    S.op('pe', lambda h: h.matmul(py[:, 0:LC], lhsT=self.s5_ci[:, st_, :], rhs=a2[:, 0:LC], start=False, stop=(s4 == 3)), reads=[self.s5_ci, a2], writes=[py])
                S.op('dve', lambda h: h.scalar_tensor_tensor(out=y3[:, c, s0:s0 + LC], in0=ug3[:, c, s0:s0 + LC], scalar=self.s5_d[:, c:c + 1], in1=py[:, 0:LC], op0=ALU.mult, op1=ALU.add),
                     reads=[ug, self.s5_d, py], writes=[yT])
        if last:
            self.s5_final(l, grp)
        S.op('act', lambda h: h.activation(out=ug3[:, :, 0:G], in_=y3, func=AF.Gelu), reads=[yT], writes=[ug])
        for c in range(4):
            sgs = []
            for oc in (c + 4, c):
                S.dma('sp', self.wglu[:, :, :], I['wglu'][l, :, oc * 128:(oc + 1) * 128].rearrange("(a p) f -> p a f", p=128), writes=[self.wglu])
                pg = self.pbank()
                for kc in range(4):
                    S.op('pe', lambda h, kc=kc: h.matmul(pg[:, 0:G], lhsT=self.wglu[:, kc, :], rhs=ug3[:, kc, 0:G], start=(kc == 0), stop=(kc == 3)), reads=[self.wglu, ug], writes=[pg])
                sgs.append(pg)
            sg = self.scratch()
            S.op('act', lambda h: h.activation(out=sg[:, 0:G], in_=sgs[0][:, 0:G], func=AF.Sigmoid), reads=[sgs[0]], writes=[sg])
            va = self.scratch()
            S.op('dve', lambda h: h.tensor_tensor(out=va[:, 0:G], in0=sgs[1][:, 0:G], in1=sg[:, 0:G], op=ALU.mult), reads=[sgs[1], sg], writes=[va])
            pgt = self.proj_fm(l, 'ag%d' % c, 128, G)
            sg2 = self.scratch()
            S.op('act', lambda h: h.activation(out=sg2[:, 0:G], in_=pgt[:, 0:G], func=AF.Silu), reads=[pgt], writes=[sg2])
            S.op('dve', lambda h, c=c: h.tensor_tensor(out=self.brT[:, c, 0:G], in0=va[:, 0:G], in1=sg2[:, 0:G], op=ALU.mult), reads=[va, sg2], writes=[self.brT])


    def dsa_setup(self):
        S = self.S
        I = self.I
        cst = self.cst
        rb = self.sm()
        oh = self.av('oh', 384)
        S.dma('sp', rb[0:32, 0:8], I['relb'][:, :], writes=[rb])
        S.dma('sp', oh[0:32, 0:384], I['oh'][:, :], writes=[oh])
        pf = self.pbank()
        S.op('pe', lambda h: h.matmul(pf[0:8, 0:384], lhsT=rb[0:32, 0:8], rhs=oh[0:32, 0:384], start=True, stop=True), reads=[rb, oh], writes=[pf])
        f8 = self.av('f8', 384)
        cm = self.sm()
        S.op('dve', lambda h: h.tensor_copy(out=cm[0:8, 0:1], in_=pf[0:8, 382:383]), reads=[pf], writes=[cm])
        S.op('dve', lambda h: h.tensor_scalar(out=f8[0:8, 0:384], in0=pf[0:8, 0:384], scalar1=cm[0:8, 0:1], scalar2=8.0, op0=ALU.subtract, op1=ALU.mult), reads=[pf, cm], writes=[f8])
        S.op('dve', lambda h: h.tensor_reduce(out=cm[0:8, 1:2], in_=f8[0:8, 0:383], axis=AX.X, op=ALU.max, negate=True), reads=[f8], writes=[cm])
        fdk = self.dk('fd')
        S.dma('sp', self.fd[:, :], f8[0:8, 0:384], reads=[f8], writes=[fdk])
        S.dma('sp', self.cfd[:, :], cm[0:8, 1:2], reads=[cm], writes=[fdk])
        S.op('dve', lambda h: h.memset(self.cf[:, :], 0.0), writes=[self.cf])
        S.dma('sp', self.cf[0:1, 0:4], self.cfd[0:4, :].rearrange("a b -> b a"), reads=[fdk], writes=[self.cf])
        S.dma('sp', self.cf[32:33, 0:4], self.cfd[4:8, :].rearrange("a b -> b a"), reads=[fdk], writes=[self.cf])
        for lrow in range(128):
            for g in range(2):
                src = bass.AP(tensor=self.fd.tensor, offset=4 * g * 384 + 127 - lrow, ap=[[0, 1], [128, 2], [384, 4], [1, 128]])
                dst = self.bias8[lrow:lrow + 1, 2 * g:2 * g + 2, :].rearrange("p k (a c) -> p k a c", c=128)
                S.dma('sp' if g else 'pool', dst, src, reads=[fdk], writes=[self.bias8])
        S.op('dve', lambda h: h.memset(self.onesb[:, :], 1.0), writes=[self.onesb])

    def add_keys(self, k_tm, v_tm, c_tm, n, key0):
        S = self.S
        cst = self.cst
        blk = key0 // 128
        pk = self.pbank()
        S.op('pe', lambda h: h.transpose(pk[:, 0:n], k_tm[0:n, 0:128], self.ident[0:n, 0:n]), reads=[k_tm, cst], writes=[pk])
        S.op('act', lambda h: h.activation(out=self.kT[:, key0:key0 + n], in_=pk[:, 0:n], func=AF.Copy), reads=[pk], writes=[self.kT])
        sq = self.scratch()
        S.op('act', lambda h: h.activation(out=sq[:, 0:n], in_=pk[:, 0:n], func=AF.Square), reads=[pk], writes=[sq])
        pn = self.pbank()
        S.op('pe', lambda h: h.matmul(pn[0:33, 0:n], lhsT=cst[:, 896:929], rhs=sq[:, 0:n], start=True, stop=True), reads=[cst, sq], writes=[pn])
        S.op('dve', lambda h: h.tensor_reduce(out=self.kmax2[:, 1:2], in_=pn[0:33, 0:n], axis=AX.X, op=ALU.max), reads=[pn], writes=[self.kmax2])
        S.op('dve', lambda h: h.tensor_tensor(out=self.kmax2[:, 0:1], in0=self.kmax2[:, 0:1], in1=self.kmax2[:, 1:2], op=ALU.max), reads=[self.kmax2], writes=[self.kmax2])
        va = self.vA[0:n, blk, :].rearrange("p (g c) -> p g c", c=65)[:, :, 0:64]
        S.op('dve', lambda h: h.tensor_copy(out=va, in_=v_tm[0:n, 0:128].rearrange("p (g c) -> p g c", c=64)), reads=[v_tm], writes=[self.vA])
        pc = self.pbank()
        S.op('pe', lambda h: h.transpose(pc[:, 0:n], c_tm[0:n, 0:128], self.ident[0:n, 0:n]), reads=[c_tm, cst], writes=[pc])
        if key0 < self.NH:
            S.op('dve', lambda h: h.tensor_copy(out=self.kiT[0:64, key0:key0 + n], in_=pc[0:64, 0:n]), reads=[pc], writes=[self.kiT])
        else:
            S.op('dve', lambda h: h.tensor_copy(out=self.kiT[64:128, key0 - self.NH:key0 - self.NH + n], in_=pc[64:128, 0:n]), reads=[pc], writes=[self.kiT])

    def dsa_init(self, l, grp):
        S = self.S
        I = self.I
        S.op('dve', lambda h: h.memset(self.vA[:, :, :], 1.0), writes=[self.vA])
        S.op('dve', lambda h: h.memset(self.kmax2[:, :], 0.0), writes=[self.kmax2])
        if grp == 's':
            self.phase('kv')
            for b in range(PAST // 128):
                k_tm = self.av('k_tm', 128); v_tm = self.av('v_tm', 128); c_tm = self.av('c_tm', 128)
                S.dma('sp', k_tm[:, 0:128], I['ck'][l, b * 128:(b + 1) * 128, :], writes=[k_tm])
                S.dma('sp', v_tm[:, 0:128], I['cv'][l, b * 128:(b + 1) * 128, :], writes=[v_tm])
                S.dma('sp', c_tm[:, 0:128], I['cki'][l, b * 128:(b + 1) * 128, :], writes=[c_tm])
                self.add_keys(k_tm, v_tm, c_tm, 128, b * 128)

    def dsa_part(self, l, grp, g0, G, TP):
        S = self.S
        cst = self.cst
        kbase = 0 if grp == 'p' else PAST
        Qa = kbase + g0
        L = Qa + TP
        NKtot = (self.T if grp == 'p' else PAST + DEC_SEQ)
        KTOP = float(min(TOPK_MAX, NKtot // 4))
        W = 4 * TP
        sc = self.av('sc', 8192)
        junk = self.av('junk', 2048)
        junk8 = junk.h.bitcast(U8)
        offf_v = junk.h[0:33, 0:512]
        osb_v = junk.h[0:65, 512:1024]
        onn_v = junk.h[0:64, 1024:1536]
        qsq_v = junk.h[:, 1536:1536 + 4 * G].rearrange("p (a c) -> p a c", c=G)
        bs = self.bs
        for hh in range(4):
            pb = self.proj_fm(l, 'qq%d' % hh, 128, TP)
            S.op('act', lambda h, hh=hh: h.activation(out=self.qT[:, hh, 0:TP], in_=pb[:, 0:TP], func=AF.Copy), reads=[pb], writes=[self.qT])
            S.op('act', lambda h, hh=hh: h.activation(out=qsq_v[:, hh, 0:TP], in_=pb[:, 0:TP], func=AF.Square), reads=[pb], writes=[junk])
        for hh in range(4):
            pb = self.proj_fm(l, 'qi%d' % hh, 128, TP)
            S.op('dve', lambda h, hh=hh: h.tensor_copy(out=self.qiT[:, hh, 0:TP], in_=pb[:, 0:TP]), reads=[pb], writes=[self.qiT])
        pq = self.pbank()
        S.op('pe', lambda h: h.matmul(pq[0:33, 0:W], lhsT=cst[:, 896:929], rhs=qsq_v[:, :, 0:TP], start=True, stop=True), reads=[cst, junk], writes=[pq])
        S.op('act', lambda h: h.activation(out=offf_v[:, 0:W], in_=pq[0:33, 0:W], func=AF.Sqrt, scale=self.kmax2[:, 0:1]), reads=[pq, self.kmax2], writes=[junk])
        S.op('dve', lambda h: h.tensor_tensor(out=self.offrow[:, 0:W].rearrange("p (a c) -> p a c", c=TP), in0=self.cf[:, :].unsqueeze(2).to_broadcast([33, 4, TP]),
                                               in1=offf_v[:, 0:W].rearrange("p (a c) -> p a c", c=TP), op=ALU.subtract), reads=[self.cf, junk], writes=[self.offrow])
        for b0 in range(0, L, 512):
            bw = min(512, L - b0)
            if b0 < self.NH:
                rows = slice(0, 64); c0 = b0
            else:
                rows = slice(64, 128); c0 = b0 - self.NH
            for hh in range(4):
                ps = self.pbank()
                S.op('pe', lambda h, hh=hh: h.matmul(ps[0:TP, 0:bw], lhsT=self.qiT[rows, hh, 0:TP], rhs=self.kiT[rows, c0:c0 + bw], start=True, stop=True), reads=[self.qiT, self.kiT], writes=[ps])
                if hh == 0:
                    S.op('dve', lambda h: h.tensor_scalar(out=sc[0:TP, b0:b0 + bw], in0=ps[0:TP, 0:bw], scalar1=0.0, scalar2=self.wiT[0:TP, 0:1], op0=ALU.max, op1=ALU.mult), reads=[ps, self.wiT], writes=[sc])
                else:
                    tmp = junk.h[:, 512 * (hh % 2):512 * (hh % 2) + 512]
                    S.op('act', lambda h: h.activation(out=tmp[0:TP, 0:bw], in_=ps[0:TP, 0:bw], func=AF.Relu), reads=[ps], writes=[junk])
                    S.op('dve', lambda h, hh=hh: h.scalar_tensor_tensor(out=sc[0:TP, b0:b0 + bw], in0=tmp[0:TP, 0:bw], scalar=self.wiT[0:TP, hh:hh + 1], in1=sc[0:TP, b0:b0 + bw], op0=ALU.mult, op1=ALU.add),
                         reads=[junk, self.wiT, sc], writes=[sc])
        S.op('dve', lambda h: h.tensor_reduce(out=bs[0:TP, 0:1], in_=sc[0:TP, 0:L], axis=AX.X, op=ALU.max, apply_absolute_value=True), reads=[sc], writes=[bs])
        if grp == 'p' and TP == 128:
            S.op('dve', lambda h: h.memset(sc[0:64, L - 64:L], -1e30), writes=[sc])
        S.op('dve', lambda h: h.tensor_scalar(out=bs[0:TP, 2:3], in0=bs[0:TP, 0:1], scalar1=1.0, scalar2=None, op0=ALU.add), reads=[bs], writes=[bs])
        S.op('dve', lambda h: h.tensor_scalar(out=bs[0:TP, 1:2], in0=bs[0:TP, 2:3], scalar1=-1.0, scalar2=None, op0=ALU.mult), reads=[bs], writes=[bs])
        for it in range(20):
            S.op('dve', lambda h: h.tensor_scalar(out=bs[0:TP, 3:4], in0=bs[0:TP, 1:2], scalar1=bs[0:TP, 2:3], scalar2=0.5, op0=ALU.add, op1=ALU.mult), reads=[bs], writes=[bs])
            S.op('dve', lambda h: h.tensor_scalar(out=junk8[0:TP, 0:L], in0=sc[0:TP, 0:L], scalar1=bs[0:TP, 3:4], scalar2=None, op0=ALU.is_ge, op1=ALU.add, accum_out=bs[0:TP, 4:5]),
                 reads=[sc, bs], writes=[junk, bs])
            S.op('dve', lambda h: h.tensor_scalar(out=bs[0:TP, 5:6], in0=bs[0:TP, 4:5], scalar1=KTOP, scalar2=None, op0=ALU.is_ge), reads=[bs], writes=[bs])
            S.op('dve', lambda h: h.tensor_tensor(out=bs[0:TP, 6:7], in0=bs[0:TP, 3:4], in1=bs[0:TP, 1:2], op=ALU.subtract), reads=[bs], writes=[bs])
            S.op('dve', lambda h: h.tensor_tensor(out=bs[0:TP, 7:8], in0=bs[0:TP, 2:3], in1=bs[0:TP, 3:4], op=ALU.subtract), reads=[bs], writes=[bs])
            S.op('dve', lambda h: h.scalar_tensor_tensor(out=bs[0:TP, 1:2], in0=bs[0:TP, 6:7], scalar=bs[0:TP, 5:6], in1=bs[0:TP, 1:2], op0=ALU.mult, op1=ALU.add), reads=[bs], writes=[bs])
            S.op('dve', lambda h: h.scalar_tensor_tensor(out=bs[0:TP, 2:3], in0=bs[0:TP, 7:8], scalar=bs[0:TP, 5:6], in1=bs[0:TP, 3:4], op0=ALU.mult, op1=ALU.add), reads=[bs], writes=[bs])
        S.op('dve', lambda h: h.tensor_scalar(out=sc[0:TP, 0:L], in0=sc[0:TP, 0:L], scalar1=bs[0:TP, 1:2], scalar2=None, op0=ALU.is_ge), reads=[sc, bs], writes=[sc])
        nblk = (L + 127) // 128
        OT = self.pacc
        for kb in range(nblk):
            bw = min(128, L - kb * 128)
            k0 = kb * 128
            pst = self.pbank()
            S.op('pe', lambda h: h.transpose(pst[0:bw, 0:TP], sc[0:TP, k0:k0 + bw], self.ident[0:TP, 0:TP]), reads=[sc, cst], writes=[pst])
            sT = self.selT[kb % 2]
            S.op('act', lambda h: h.activation(out=sT[0:bw, 0:TP], in_=pst[0:bw, 0:TP], func=AF.Copy), reads=[pst], writes=[sT])
            kind = 0 if k0 == Qa else (1 if k0 == Qa - 128 else None)
            for g in range(2):
                ps = self.pbank()
                S.op('pe', lambda h, g=g: h.matmul(ps[0:bw, 0:W], lhsT=self.kT[64 * g:64 * g + 64, k0:k0 + bw], rhs=self.qT[64 * g:64 * g + 64, :, 0:TP], start=True, stop=False),
                     reads=[self.kT, self.qT], writes=[ps])
                S.op('pe', lambda h, g=g: h.matmul(ps[0:bw, 0:W], lhsT=self.onesb[32 * g:32 * g + 1, 0:bw], rhs=self.offrow[32 * g:32 * g + 1, 0:W], start=False, stop=True),
                     reads=[self.onesb, self.offrow], writes=[ps])
                Et = self.Et[g]
                Pt = self.Pt[g]
                if kind is not None:
                    tmp = junk.h[:, 1536:2048]
                    bt = self.bias8[0:bw, 2 * g + kind, :].rearrange("p (a c) -> p a c", c=128)[:, :, 0:TP]
                    S.op('dve', lambda h: h.tensor_tensor(out=tmp[0:bw, 0:W].rearrange("p (a c) -> p a c", c=TP), in0=ps[0:bw, 0:W].rearrange("p (a c) -> p a c", c=TP), in1=bt, op=ALU.add),
                         reads=[ps, self.bias8], writes=[junk])
                    S.op('act', lambda h: h.activation(out=Et[0:bw, 0:W], in_=tmp[0:bw, 0:W], func=AF.Exp, scale=0.125), reads=[junk], writes=[Et])
                else:
                    S.op('act', lambda h: h.activation(out=Et[0:bw, 0:W], in_=ps[0:bw, 0:W], func=AF.Exp, scale=0.125), reads=[ps], writes=[Et])
                S.op('dve', lambda h: h.tensor_tensor(out=Pt[0:bw, 0:W].rearrange("p (a c) -> p a c", c=TP), in0=Et[0:bw, 0:W].rearrange("p (a c) -> p a c", c=TP),
                                                       in1=sT[0:bw, 0:TP].unsqueeze(1).to_broadcast([bw, 4, TP]), op=ALU.mult), reads=[Et, sT], writes=[Pt])
                S.op('pe', lambda h, g=g: h.matmul(OT[g][0:65, 0:W], lhsT=self.vA[0:bw, kb, 65 * g:65 * g + 65], rhs=Pt[0:bw, 0:W], start=(kb == 0), stop=(kb == nblk - 1)),
                     reads=[self.vA, Pt], writes=[OT[g]])
        for g in range(2):
            osb = osb_v
            S.op('act', lambda h: h.activation(out=osb[0:65, 0:W], in_=OT[g][0:65, 0:W], func=AF.Copy), reads=[OT[g]], writes=[junk])
            S.op('dve', lambda h: h.reciprocal(out=osb[64:65, 0:W], in_=osb[64:65, 0:W]), reads=[junk], writes=[junk])
            pbc = self.pbank()
            S.op('pe', lambda h: h.matmul(pbc[0:64, 0:W], lhsT=cst[64:65, 256:320], rhs=osb[64:65, 0:W], start=True, stop=True), reads=[cst, junk], writes=[pbc])
            S.op('dve', lambda h: h.tensor_tensor(out=onn_v[:, 0:W], in0=osb[0:64, 0:W], in1=pbc[0:64, 0:W], op=ALU.mult), reads=[junk, pbc], writes=[junk])
            for hh in range(4):
                hd = 4 * g + hh
                pg = self.proj_fm(l, 'bg%d' % hd, 64, TP)
                sg = junk.h[:, 1536 + 128 * (hh % 2):1536 + 128 * (hh % 2) + 128]
                S.op('act', lambda h: h.activation(out=sg[0:64, 0:TP], in_=pg[0:64, 0:TP], func=AF.Silu), reads=[pg], writes=[junk])
                S.op('dve', lambda h, hh=hh, hd=hd: h.tensor_tensor(out=self.brB[:, hd, 0:TP], in0=onn_v[:, hh * TP:(hh + 1) * TP], in1=sg[0:64, 0:TP], op=ALU.mult), reads=[junk], writes=[self.brB])

    def gdn_setup(self, l):
        S = self.S
        I = self.I
        S.dma('sp', self.gdn_cw[:, :, :], I['gcw'][l], writes=[self.gdn_cw])
        S.dma('sp', self.gdn_par[:, :], I['gpar'][l], writes=[self.gdn_par])
        S.dma('sp', self.gdn_n[:, :], I['gnorm'][l], writes=[self.gdn_n])
        S.op('act', lambda h: h.activation(out=self.gdn_par[:, 0:4], in_=self.gdn_par[:, 0:4], func=AF.Exp), reads=[self.gdn_par], writes=[self.gdn_par])
        S.op('dve', lambda h: h.tensor_scalar(out=self.gdn_par[:, 0:4], in0=self.gdn_par[:, 0:4], scalar1=-1.0, scalar2=None, op0=ALU.mult), reads=[self.gdn_par], writes=[self.gdn_par])

    def gdn_init(self, l, grp):
        S = self.S
        if grp == 'p':
            S.op('dve', lambda h: h.memset(self.gdn_S[:, :, :], 0.0), writes=[self.gdn_S])
            S.op('dve', lambda h: h.memset(self.gdn_tail[:, :, :], 0.0), writes=[self.gdn_tail])
        else:
            S.dma('sp', self.gdn_S[:, :, :], self.I['sgdn'][l].rearrange("h k v -> k h v"), writes=[self.gdn_S])
            S.dma('sp', self.gdn_tail[:, :, :], self.I['sconv'][l], writes=[self.gdn_tail])

    def gdn_part(self, l, grp, g0, G, C, last):
        S = self.S
        cst = self.cst
        nC = G // C
        GE = G + 3
        xext = self.av('xext', 12 * GE)
        x3 = xext[:, 0:12 * GE].rearrange("p (a c) -> p a c", c=GE)
        qkv = self.av('qkv', 12 * G)
        q3 = qkv[:, 0:12 * G].rearrange("p (a c) -> p a c", c=G)
        oT = self.av('oT', 4 * G)
        o3 = oT[:, 0:4 * G].rearrange("p (a c) -> p a c", c=G)
        T1 = self.av('T1', 1024)
        T2 = self.av('T2', 1024)
        S.op('dve', lambda h: h.tensor_copy(out=x3[:, :, 0:3], in_=self.gdn_tail[:, :, :]), reads=[self.gdn_tail], writes=[xext])
        for c in range(12):
            pb = self.proj_fm(l, 'cq%d' % c, 128, G)
            S.op('act' if c % 2 else 'dve', (lambda h, c=c: h.activation(out=x3[:, c, 3:3 + G], in_=pb[:, 0:G], func=AF.Copy)) if c % 2 else (lambda h, c=c: h.tensor_copy(out=x3[:, c, 3:3 + G], in_=pb[:, 0:G])),
                 reads=[pb], writes=[xext])
        S.op('dve', lambda h: h.tensor_copy(out=self.gdn_tail[:, :, :], in_=x3[:, :, G:G + 3]), reads=[xext], writes=[self.gdn_tail])
        for c in range(12):
            S.op('dve', lambda h, c=c: h.tensor_scalar(out=q3[:, c, :], in0=x3[:, c, 0:G], scalar1=self.gdn_cw[:, c, 0:1], scalar2=None, op0=ALU.mult), reads=[xext, self.gdn_cw], writes=[qkv])
            for i in range(1, 4):
                S.op('dve', lambda h, c=c, i=i: h.scalar_tensor_tensor(out=q3[:, c, :], in0=x3[:, c, i:i + G], scalar=self.gdn_cw[:, c, i:i + 1], in1=q3[:, c, :], op0=ALU.mult, op1=ALU.add),
                     reads=[xext, self.gdn_cw, qkv], writes=[qkv])
        S.op('act', lambda h: h.activation(out=qkv[:, 0:12 * G], in_=qkv[:, 0:12 * G], func=AF.Silu), reads=[qkv], writes=[qkv])
        W8 = 8 * G
        S.op('act', lambda h: h.activation(out=T1[:, 0:W8], in_=qkv[:, 0:W8], func=AF.Square), reads=[qkv], writes=[T1])
        for b0 in range(0, W8, 512):
            bw = min(512, W8 - b0)
            pm = self.pbank()
            S.op('pe', lambda h: h.matmul(pm[:, 0:bw], lhsT=cst[:, 256:384], rhs=T1[:, b0:b0 + bw], start=True, stop=True), reads=[cst, T1], writes=[pm])
            S.op('act', lambda h: h.activation(out=T2[:, b0:b0 + bw], in_=pm[:, 0:bw], func=AF.Sqrt, bias=self.epsb[:, 1:2]), reads=[pm, self.epsb], writes=[T2])
        S.op('dve', lambda h: h.reciprocal(out=T2[:, 0:W8], in_=T2[:, 0:W8]), reads=[T2], writes=[T2])
        S.op('dve', lambda h: h.scalar_tensor_tensor(out=qkv[:, 0:4 * G], in0=qkv[:, 0:4 * G], scalar=float(128 ** -0.5), in1=T2[:, 0:4 * G], op0=ALU.mult, op1=ALU.mult), reads=[qkv, T2], writes=[qkv])
        S.op('dve', lambda h: h.tensor_tensor(out=qkv[:, 4 * G:8 * G], in0=qkv[:, 4 * G:8 * G], in1=T2[:, 4 * G:8 * G], op=ALU.mult), reads=[qkv, T2], writes=[qkv])
        wC = self.load_w(l, 'kvD')
        CC = 4 * C
        mU = cst[0:C, 128:128 + C].unsqueeze(1).to_broadcast([C, 4, C])
        mLs = cst[0:C, 768:768 + C].unsqueeze(1).to_broadcast([C, 4, C])
        K = {64: 5, 32: 4, 16: 3}[C]
        V = lambda key: self.av(key, 256)
        for ci in range(nC):
            c0 = ci * C
            pb = self.pbank()
            for k in range(8):
                S.op('pe', lambda h, k=k: h.matmul(pb[0:C, 0:12], lhsT=self.xT[:, k, c0:c0 + C], rhs=wC[:, k, 0:12], start=(k == 0), stop=(k == 7)), reads=[self.xT, wC], writes=[pb])
            bg = self.av('bg', 32)
            S.op('act', lambda h: h.activation(out=bg[0:C, 0:4], in_=pb[0:C, 4:8], func=AF.Sigmoid), reads=[pb], writes=[bg])
            S.op('dve', lambda h: h.tensor_tensor(out=bg[0:C, 4:8], in0=pb[0:C, 8:12], in1=self.gdn_par[0:C, 4:8], op=ALU.add), reads=[pb, self.gdn_par], writes=[bg])
            S.op('act', lambda h: h.activation(out=bg[0:C, 4:8], in_=bg[0:C, 4:8], func=AF.Exp), reads=[bg], writes=[bg])
            S.op('act', lambda h: h.activation(out=bg[0:C, 4:8], in_=bg[0:C, 4:8], func=AF.Ln, bias=1.0), reads=[bg], writes=[bg])
            S.op('dve', lambda h: h.tensor_tensor(out=bg[0:C, 8:12], in0=bg[0:C, 4:8], in1=self.gdn_par[0:C, 0:4], op=ALU.mult), reads=[bg, self.gdn_par], writes=[bg])
            pcol = self.pbank()
            S.op('pe', lambda h: h.matmul(pcol[0:C, 0:4], lhsT=cst[0:C, 128:128 + C], rhs=bg[0:C, 8:12], start=True, stop=True), reads=[cst, bg], writes=[pcol])
            R = V('R')
            R3 = R[0:C, 0:CC].rearrange("p (a c) -> p a c", c=C)
            for h4 in range(4):
                S.op('dve', lambda h, h4=h4: h.tensor_scalar(out=R3[:, h4, :], in0=cst[0:C, 128:128 + C], scalar1=bg[0:C, 8 + h4:9 + h4], scalar2=None, op0=ALU.mult), reads=[cst, bg], writes=[R])
            prow = self.pbank()
            S.op('pe', lambda h: h.matmul(prow[:, 0:CC], lhsT=cst[0:C, 256:384], rhs=R[0:C, 0:CC], start=True, stop=True), reads=[cst, R], writes=[prow])
            egrow = V('egrow')
            S.op('act', lambda h: h.activation(out=egrow[:, 0:CC], in_=prow[:, 0:CC], func=AF.Exp), reads=[prow], writes=[egrow])
            eg3 = egrow[:, 0:CC].rearrange("p (a c) -> p a c", c=C)
            S.op('dve', lambda h: h.tensor_copy(out=bg[0:C, 12:16], in_=pcol[0:C, 0:4]), reads=[pcol], writes=[bg])
            S.op('act', lambda h: h.activation(out=bg[0:C, 16:20], in_=pcol[0:C, 0:4], func=AF.Exp), reads=[pcol], writes=[bg])
            S.op('dve', lambda h: h.tensor_tensor(out=bg[0:C, 20:24], in0=bg[0:C, 16:20], in1=bg[0:C, 0:4], op=ALU.mult), reads=[bg], writes=[bg])
            Dm = V('Dm')
            D3 = Dm[0:C, 0:CC].rearrange("p (a c) -> p a c", c=C)
            p3 = prow[0:C, 0:CC].rearrange("p (a c) -> p a c", c=C)
            for h4 in range(4):
                S.op('dve', lambda h, h4=h4: h.tensor_scalar(out=D3[:, h4, :], in0=p3[:, h4, :], scalar1=bg[0:C, 12 + h4:13 + h4], scalar2=-1.0, op0=ALU.subtract, op1=ALU.mult), reads=[prow, bg], writes=[Dm])
            Elo = V('Elo')
            Eup = V('Eup')
            S.op('dve', lambda h: h.tensor_scalar(out=Elo[0:C, 0:CC], in0=Dm[0:C, 0:CC], scalar1=0.0, scalar2=None, op0=ALU.min), reads=[Dm], writes=[Elo])
            S.op('act', lambda h: h.activation(out=Elo[0:C, 0:CC], in_=Elo[0:C, 0:CC], func=AF.Exp), reads=[Elo], writes=[Elo])
            S.op('dve', lambda h: h.tensor_scalar(out=Eup[0:C, 0:CC], in0=Dm[0:C, 0:CC], scalar1=-1.0, scalar2=0.0, op0=ALU.mult, op1=ALU.min), reads=[Dm], writes=[Eup])
            S.op('act', lambda h: h.activation(out=Eup[0:C, 0:CC], in_=Eup[0:C, 0:CC], func=AF.Exp), reads=[Eup], writes=[Eup])
            El3 = Elo[0:C, 0:CC].rearrange("p (a c) -> p a c", c=C)
            Eu3 = Eup[0:C, 0:CC].rearrange("p (a c) -> p a c", c=C)
            S.op('dve', lambda h: h.tensor_copy(out=bg[0:C, 24:28], in_=Eu3[:, :, C - 1]), reads=[Eup], writes=[bg])
            S.op('dve', lambda h: h.tensor_tensor(out=El3, in0=El3, in1=mLs, op=ALU.mult), reads=[Elo, cst], writes=[Elo])
            S.op('dve', lambda h: h.tensor_tensor(out=Eu3, in0=Eu3, in1=mU, op=ALU.mult), reads=[Eup, cst], writes=[Eup])
            for h4 in range(4):
                S.op('dve', lambda h, h4=h4: h.tensor_scalar(out=El3[:, h4, :], in0=El3[:, h4, :], scalar1=bg[0:C, h4:h4 + 1], scalar2=None, op0=ALU.mult), reads=[Elo, bg], writes=[Elo])
            pkk = self.pbank()
            pmt = self.pbank()
            for h4 in range(4):
                S.op('pe', lambda h, h4=h4: h.matmul(pkk[0:C, h4 * C:(h4 + 1) * C], lhsT=q3[:, 4 + h4, c0:c0 + C], rhs=q3[:, 4 + h4, c0:c0 + C], start=True, stop=True), reads=[qkv], writes=[pkk])
                S.op('pe', lambda h, h4=h4: h.matmul(pmt[0:C, h4 * C:(h4 + 1) * C], lhsT=q3[:, 4 + h4, c0:c0 + C], rhs=q3[:, h4, c0:c0 + C], start=True, stop=True), reads=[qkv], writes=[pmt])
            P = [V('P0')]
            AT = V('AT')
            MT = V('MT')
            S.op('dve', lambda h: h.tensor_tensor(out=P[0][0:C, 0:CC], in0=pkk[0:C, 0:CC], in1=Elo[0:C, 0:CC], op=ALU.mult), reads=[pkk, Elo], writes=[P[0]])
            S.op('dve', lambda h: h.tensor_tensor(out=MT[0:C, 0:CC], in0=pmt[0:C, 0:CC], in1=Eup[0:C, 0:CC], op=ALU.mult), reads=[pmt, Eup], writes=[MT])
            pat = self.pbank()
            for h4 in range(4):
                S.op('pe', lambda h, h4=h4: h.transpose(pat[0:C, h4 * C:(h4 + 1) * C], P[0][0:C, h4 * C:(h4 + 1) * C], self.ident[0:C, 0:C]), reads=[P[0], cst], writes=[pat])
            S.op('act', lambda h: h.activation(out=AT[0:C, 0:CC], in_=pat[0:C, 0:CC], func=AF.Copy), reads=[pat], writes=[AT])
            X = T1
            X3 = X[0:C, 0:1024].rearrange("p (a c) -> p a c", c=256)
            Kdec = T2
            Kd3 = Kdec[0:C, 0:512].rearrange("p (a c) -> p a c", c=128)
            vn = T2
            vn3 = vn[0:C, 512:1024].rearrange("p (a c) -> p a c", c=128)
            pkt = self.pbank()
            pvt = self.pbank()
            for h4 in range(4):
                S.op('pe', lambda h, h4=h4: h.transpose(pkt[0:C, h4 * 128:(h4 + 1) * 128], q3[:, 4 + h4, c0:c0 + C], self.ident[:, 0:128]), reads=[qkv, cst], writes=[pkt])
                S.op('pe', lambda h, h4=h4: h.transpose(pvt[0:C, h4 * 128:(h4 + 1) * 128], q3[:, 8 + h4, c0:c0 + C], self.ident[:, 0:128]), reads=[qkv, cst], writes=[pvt])
            for h4 in range(4):
                S.op('dve', lambda h, h4=h4: h.tensor_scalar(out=X3[:, h4, 0:128], in0=pkt[0:C, h4 * 128:(h4 + 1) * 128], scalar1=bg[0:C, 20 + h4:21 + h4], scalar2=None, op0=ALU.mult), reads=[pkt, bg], writes=[X])
                S.op('dve', lambda h, h4=h4: h.tensor_scalar(out=X3[:, h4, 128:256], in0=pvt[0:C, h4 * 128:(h4 + 1) * 128], scalar1=bg[0:C, h4:h4 + 1], scalar2=None, op0=ALU.mult), reads=[pvt, bg], writes=[X])
                S.op('dve', lambda h, h4=h4: h.tensor_scalar(out=Kd3[:, h4, :], in0=pkt[0:C, h4 * 128:(h4 + 1) * 128], scalar1=bg[0:C, 24 + h4:25 + h4], scalar2=None, op0=ALU.mult), reads=[pkt, bg], writes=[Kdec])
            PTs = [AT]
            Ps = [P[0]]
            for k in range(1, K + 1):
                pk_ = self.pbank()
                pt_ = self.pbank()
                Pp, PTp = Ps[-1], PTs[-1]
                for h4 in range(4):
                    sl = slice(h4 * C, (h4 + 1) * C)
                    if k < K:
                        S.op('pe', lambda h, sl=sl: h.matmul(pk_[0:C, sl], lhsT=PTp[0:C, sl], rhs=Pp[0:C, sl], start=True, stop=True), reads=[PTp, Pp], writes=[pk_])
                    S.op('pe', lambda h, sl=sl: h.matmul(pt_[0:C, sl], lhsT=Pp[0:C, sl], rhs=PTp[0:C, sl], start=True, stop=True), reads=[PTp, Pp], writes=[pt_])
                nPT = V('PTk%d' % k)
                S.op('act', lambda h: h.activation(out=nPT[0:C, 0:CC], in_=pt_[0:C, 0:CC], func=AF.Copy), reads=[pt_], writes=[nPT])
                PTs.append(nPT)
                if k < K:
                    nP = V('Pk%d' % k)
                    S.op('dve', lambda h: h.tensor_copy(out=nP[0:C, 0:CC], in_=pk_[0:C, 0:CC]), reads=[pk_], writes=[nP])
                    Ps.append(nP)
            for k in range(K, -1, -1):
                PTk = PTs[k]
                for half in range(2):
                    px = self.pbank()
                    for hh in range(2):
                        h4 = 2 * half + hh
                        S.op('pe', lambda h, h4=h4, hh=hh: h.matmul(px[0:C, hh * 256:(hh + 1) * 256], lhsT=PTk[0:C, h4 * C:(h4 + 1) * C], rhs=X3[:, h4, :], start=True, stop=True), reads=[PTk, X], writes=[px])
                    xs = X[0:C, half * 512:(half + 1) * 512]
                    S.op('dve', lambda h: h.tensor_tensor(out=xs, in0=xs, in1=px[0:C, 0:512], op=(ALU.subtract if k == 0 else ALU.add)), reads=[X, px], writes=[X])
            pwt = self.pbank()
            for h4 in range(4):
                S.op('pe', lambda h, h4=h4: h.transpose(pwt[:, h4 * C:(h4 + 1) * C], X3[:, h4, 0:128], self.ident[0:C, 0:C]), reads=[X, cst], writes=[pwt])
            WT = Dm
            S.op('act', lambda h: h.activation(out=WT[:, 0:CC], in_=pwt[:, 0:CC], func=AF.Copy), reads=[pwt], writes=[WT])
            pws = self.pbank()
            for h4 in range(4):
                S.op('pe', lambda h, h4=h4: h.matmul(pws[0:C, h4 * 128:(h4 + 1) * 128], lhsT=WT[:, h4 * C:(h4 + 1) * C], rhs=self.gdn_S[:, h4, :], start=True, stop=True), reads=[WT, self.gdn_S], writes=[pws])
            S.op('dve', lambda h: h.tensor_tensor(out=vn3, in0=X3[:, :, 128:256], in1=pws[0:C, 0:512].rearrange("p (a c) -> p a c", c=128), op=ALU.subtract), reads=[X, pws], writes=[vn])
            QeT = R
            Qe3 = QeT[:, 0:CC].rearrange("p (a c) -> p a c", c=C)
            S.op('dve', lambda h: h.tensor_tensor(out=Qe3, in0=q3[:, 0:4, c0:c0 + C], in1=eg3, op=ALU.mult), reads=[qkv, egrow], writes=[QeT])
            po = self.pbank()
            for h4 in range(4):
                sl = slice(h4 * C, (h4 + 1) * C)
                S.op('pe', lambda h, h4=h4, sl=sl: h.matmul(po[:, sl], lhsT=self.gdn_S[:, h4, :], rhs=QeT[:, sl], start=True, stop=False), reads=[self.gdn_S, QeT], writes=[po])
                S.op('pe', lambda h, h4=h4, sl=sl: h.matmul(po[:, sl], lhsT=vn3[:, h4, :], rhs=MT[0:C, sl], start=False, stop=True), reads=[vn, MT], writes=[po])
            S.op('act', lambda h: h.activation(out=o3[:, :, c0:c0 + C], in_=po[:, 0:CC].rearrange("p (a c) -> p a c", c=C), func=AF.Copy), reads=[po], writes=[oT])
            pu = self.pbank()
            for h4 in range(4):
                S.op('pe', lambda h, h4=h4: h.matmul(pu[:, h4 * 128:(h4 + 1) * 128], lhsT=Kd3[:, h4, :], rhs=vn3[:, h4, :], start=True, stop=True), reads=[Kdec, vn], writes=[pu])
            for h4 in range(4):
                S.op('dve', lambda h, h4=h4: h.scalar_tensor_tensor(out=self.gdn_S[:, h4, :], in0=self.gdn_S[:, h4, :], scalar=eg3[:, h4, C - 1:C], in1=pu[:, h4 * 128:(h4 + 1) * 128], op0=ALU.mult, op1=ALU.add),
                     reads=[self.gdn_S, egrow, pu], writes=[self.gdn_S])
        if last:
            S.dma('pool', self.O['gdn' + grp][l].rearrange("h k v -> k h v"), self.gdn_S[:, :, :], reads=[self.gdn_S], writes=[self.dk('ogdn' + grp)])
        W4 = 4 * G
        S.op('act', lambda h: h.activation(out=T1[:, 0:W4], in_=oT[:, 0:W4], func=AF.Square), reads=[oT], writes=[T1])
        pm = self.pbank()
        S.op('pe', lambda h: h.matmul(pm[:, 0:W4], lhsT=cst[:, 384:512], rhs=T1[:, 0:W4], start=True, stop=True), reads=[cst, T1], writes=[pm])
        S.op('act', lambda h: h.activation(out=T2[:, 0:W4], in_=pm[:, 0:W4], func=AF.Sqrt, bias=self.epsb[:, 1:2]), reads=[pm, self.epsb], writes=[T2])
        S.op('dve', lambda h: h.reciprocal(out=T2[:, 0:W4], in_=T2[:, 0:W4]), reads=[T2], writes=[T2])
        S.op('dve', lambda h: h.scalar_tensor_tensor(out=oT[:, 0:W4], in0=oT[:, 0:W4], scalar=self.gdn_n[:, 0:1], in1=T2[:, 0:W4], op0=ALU.mult, op1=ALU.mult), reads=[oT, T2, self.gdn_n], writes=[oT])
        for c4 in range(4):
            pgt = self.proj_fm(l, 'cg%d' % c4, 128, G)
            sg = V('R') if c4 % 2 else V('Dm')
            S.op('act', lambda h: h.activation(out=sg[:, 0:G], in_=pgt[:, 0:G], func=AF.Silu), reads=[pgt], writes=[sg])
            S.op('dve', lambda h, c4=c4: h.tensor_tensor(out=self.brT[:, 8 + c4, 0:G], in0=o3[:, c4, :], in1=sg[:, 0:G], op=ALU.mult), reads=[oT, sg], writes=[self.brT])

    def gla_part(self, l, grp, g0, G, C, last):
        S = self.S
        cst = self.cst
        nC = G // C
        qT = self.wsp(True)
        kT = self.wsp()
        W4 = 4 * G
        q3 = qT[0:64, 0:W4].rearrange("p (a c) -> p a c", c=G)
        k3 = kT[0:64, 0:W4].rearrange("p (a c) -> p a c", c=G)
        for h4 in range(4):
            pb = self.proj_fm(l, 'dq%d' % h4, 64, G)
            S.op('act', lambda h, h4=h4: h.activation(out=q3[:, h4, :], in_=pb[0:64, 0:G], func=AF.Copy, scale=0.125), reads=[pb], writes=[qT])
            pb2 = self.proj_fm(l, 'dk%d' % h4, 64, G)
            S.op('dve', lambda h, h4=h4: h.tensor_copy(out=k3[:, h4, :], in_=pb2[0:64, 0:G]), reads=[pb2], writes=[kT])
        pg = self.proj_fm(l, 'dg', 16, G)
        dgT = self.sm()
        dg_sb = self.scratch()
        S.op('dve', lambda h: h.tensor_copy(out=dg_sb[0:16, 0:G], in_=pg[0:16, 0:G]), reads=[pg], writes=[dg_sb])
        spT = self.wsp()
        sp3 = spT[0:64, 0:W4].rearrange("p (a c) -> p a c", c=G)
        nb = self.sm()
        S.op('dve', lambda h: h.tensor_scalar(out=nb[0:64, 0:4], in0=self.gla_b[:, :], scalar1=-1.0, scalar2=None, op0=ALU.mult), reads=[self.gla_b], writes=[nb])
        for h4 in range(4):
            pl = self.pbank()
            S.op('pe', lambda h, h4=h4: h.matmul(pl[0:64, 0:G], lhsT=self.gla_w[:, h4 * 64:(h4 + 1) * 64], rhs=dg_sb[0:16, 0:G], start=True, stop=True),
                 reads=[self.gla_w, dg_sb], writes=[pl])
            S.op('act', lambda h, h4=h4: h.activation(out=sp3[:, h4, :], in_=pl[0:64, 0:G], func=AF.Exp, scale=-1.0, bias=nb[0:64, h4:h4 + 1]),
                 reads=[pl, nb], writes=[spT])
        S.op('act', lambda h: h.activation(out=spT[0:64, 0:W4], in_=spT[0:64, 0:W4], func=AF.Ln, bias=1.0), reads=[spT], writes=[spT])
        csT = self.wsp()
        S.op('dve', lambda h: h.tensor_tensor_scan(out=csT[0:64, 0:W4], data0=self.resetm[:, 0:W4], data1=spT[0:64, 0:W4], initial=0.0, op0=ALU.mult, op1=ALU.add),
             reads=[self.resetm, spT], writes=[csT])
        eq = self.wsp()
        ek = self.wsp()
        S.op('act', lambda h: h.activation(out=eq[0:64, 0:W4], in_=csT[0:64, 0:W4], func=AF.Exp, scale=-1.0 / 16.0), reads=[csT], writes=[eq])
        S.op('act', lambda h: h.activation(out=ek[0:64, 0:W4], in_=csT[0:64, 0:W4], func=AF.Exp, scale=1.0 / 16.0), reads=[csT], writes=[ek])
        S.op('dve', lambda h: h.tensor_tensor(out=qT[0:64, 0:W4], in0=qT[0:64, 0:W4], in1=eq[0:64, 0:W4], op=ALU.mult), reads=[qT, eq], writes=[qT])
        S.op('dve', lambda h: h.tensor_tensor(out=kT[0:64, 0:W4], in0=kT[0:64, 0:W4], in1=ek[0:64, 0:W4], op=ALU.mult), reads=[kT, ek], writes=[kT])
        eq3 = eq[0:64, 0:W4].rearrange("p (a c) -> p a c", c=G)
        oT = self.wsp()
        o3 = oT[:, 0:W4].rearrange("p (a c) -> p a c", c=G)
        for ci in range(nC):
            c0 = ci * C
            pv = self.pbank()
            for c4 in range(4):
                w, xTw = self.load_wx(l, 'dv%d' % c4)
                for k in range(8):
                    S.op('pe', lambda h, k=k, c4=c4, w=w: h.matmul(pv[0:C, c4 * 128:(c4 + 1) * 128], lhsT=xTw[:, k, c0:c0 + C], rhs=w[:, k, :],
                                                               start=(k == 0), stop=(k == 7)), reads=[w, xTw], writes=[pv])
            vt = self.scratch()
            S.op('act', lambda h: h.activation(out=vt[0:C, 0:512], in_=pv[0:C, 0:512], func=AF.Copy), reads=[pv], writes=[vt])
            pk = self.pbank()
            for h4 in range(4):
                S.op('pe', lambda h, h4=h4: h.transpose(pk[0:C, h4 * 64:(h4 + 1) * 64], k3[:, h4, c0:c0 + C], self.ident[0:64, 0:64]),
                     reads=[kT, cst], writes=[pk])
            kt = self.scratch()
            S.op('dve', lambda h: h.tensor_copy(out=kt[0:C, 0:256], in_=pk[0:C, 0:256]), reads=[pk], writes=[kt])
            pa = self.pbank()
            for h4 in range(4):
                S.op('pe', lambda h, h4=h4: h.matmul(pa[0:C, h4 * C:(h4 + 1) * C], lhsT=k3[:, h4, c0:c0 + C], rhs=q3[:, h4, c0:c0 + C], start=True, stop=True),
                     reads=[kT, qT], writes=[pa])
            at = self.scratch()
            pa3 = pa[0:C, 0:4 * C].rearrange("p (a c) -> p a c", c=C)
            at3 = at[0:C, 0:4 * C].rearrange("p (a c) -> p a c", c=C)
            msk = cst[0:C, 128:128 + C].unsqueeze(1).to_broadcast([C, 4, C])
            S.op('dve', lambda h: h.tensor_tensor(out=at3, in0=pa3, in1=msk, op=ALU.mult), reads=[pa, cst], writes=[at])
            po = self.pbank()
            for h4 in range(4):
                S.op('pe', lambda h, h4=h4: h.matmul(po[:, h4 * C:(h4 + 1) * C], lhsT=vt[0:C, h4 * 128:(h4 + 1) * 128], rhs=at3[:, h4, :], start=True, stop=False),
                     reads=[vt, at], writes=[po])
                S.op('pe', lambda h, h4=h4: h.matmul(po[:, h4 * C:(h4 + 1) * C], lhsT=self.gla_S[:, h4, :], rhs=q3[:, h4, c0:c0 + C], start=False, stop=True),
                     reads=[self.gla_S, qT], writes=[po])
            S.op('act', lambda h: h.activation(out=o3[:, :, c0:c0 + C], in_=po[:, 0:4 * C].rearrange("p (a c) -> p a c", c=C), func=AF.Copy), reads=[po], writes=[oT])
            pu = self.pbank()
            for h4 in range(4):
                S.op('pe', lambda h, h4=h4: h.matmul(pu[0:64, h4 * 128:(h4 + 1) * 128], lhsT=kt[0:C, h4 * 64:(h4 + 1) * 64], rhs=vt[0:C, h4 * 128:(h4 + 1) * 128], start=True, stop=True),
                     reads=[kt, vt], writes=[pu])
            S.op('dve', lambda h: h.tensor_tensor(out=self.gla_S[:, :, :], in0=self.gla_S[:, :, :], in1=pu[0:64, 0:512].rearrange("p (a c) -> p a c", c=128), op=ALU.add),
                 reads=[self.gla_S, pu], writes=[self.gla_S])
            for h4 in range(4):
                S.op('dve', lambda h, h4=h4: h.tensor_scalar(out=self.gla_S[:, h4, :], in0=self.gla_S[:, h4, :], scalar1=eq3[:, h4, c0 + C - 1:c0 + C], scalar2=None, op0=ALU.mult),
                     reads=[self.gla_S, eq], writes=[self.gla_S])
        if last:
            S.dma('pool', self.O['gla' + grp][l].rearrange("h k v -> k h v"), self.gla_S[:, :, :], reads=[self.gla_S], writes=[self.dk('ogla' + grp)])
        sq = spT
        S.op('act', lambda h: h.activation(out=sq[:, 0:W4], in_=oT[:, 0:W4], func=AF.Square), reads=[oT], writes=[sq])
        rst = csT
        for b0 in range(0, W4, 512):
            bw = min(512, W4 - b0)
            pm = self.pbank()
            S.op('pe', lambda h: h.matmul(pm[:, 0:bw], lhsT=cst[:, 384:512], rhs=sq[:, b0:b0 + bw], start=True, stop=True), reads=[cst, sq], writes=[pm])
            S.op('act', lambda h: h.activation(out=rst[:, b0:b0 + bw], in_=pm[:, 0:bw], func=AF.Sqrt, bias=self.epsb[:, 1:2]), reads=[pm, self.epsb], writes=[rst])
        S.op('dve', lambda h: h.reciprocal(out=rst[:, 0:W4], in_=rst[:, 0:W4]), reads=[rst], writes=[rst])
        S.op('dve', lambda h: h.scalar_tensor_tensor(out=oT[:, 0:W4], in0=oT[:, 0:W4], scalar=self.gla_n[:, 0:1], in1=rst[:, 0:W4], op0=ALU.mult, op1=ALU.mult),
             reads=[oT, rst, self.gla_n], writes=[oT])
        for c4 in range(4):
            pgt = self.proj_fm(l, 'dgt%d' % c4, 128, G)
            sg = self.scratch()
            S.op('act', lambda h: h.activation(out=sg[:, 0:G], in_=pgt[:, 0:G], func=AF.Silu), reads=[pgt], writes=[sg])
            S.op('dve', lambda h, c4=c4: h.tensor_tensor(out=self.brT[:, 12 + c4, 0:G], in0=o3[:, c4, :], in1=sg[:, 0:G], op=ALU.mult), reads=[oT, sg], writes=[self.brT])

    def out_stage(self, l, grp, g0, G, TP, xin, xin_tk, yout, yout_tk):
        S = self.S
        I = self.I
        mix = self.av('mixT', 4 * self.G)
        mix3 = mix.h.bitcast(BF16)[:, 0:8 * G].rearrange("p (a c) -> p a c", c=G)
        for dc in range(8):
            for b in range(4):
                pp = self.pbank()
                if b == 1:
                    self.wrr += 1
                    wb = self.wbrBs[self.wrr % 2]
                    S.dma('sp', wb[:, :, :], self.wbrb[l, b, dc].rearrange("(p h) f -> p (h f)", h=2).rearrange("p (a c) -> p a c", c=128), reads=[self.dk(('wbrb', l, b, dc))], writes=[wb])
                    for hh in range(8):
                        S.op('pe', lambda h, hh=hh: h.matmul(pp[:, 0:G], lhsT=wb[:, hh, :], rhs=self.brB[:, hh, 0:G], start=(hh == 0), stop=(hh == 7)),
                             reads=[wb, self.brB], writes=[pp])
                else:
                    self.wrr += 1
                    wb = self.wbrs[self.wrr % 2]
                    S.dma('sp', wb[:, :, :], self.wbrb[l, b, dc].rearrange("p (a c) -> p a c", c=128), reads=[self.dk(('wbrb', l, b, dc))], writes=[wb])
                    for kc in range(4):
                        S.op('pe', lambda h, kc=kc: h.matmul(pp[:, 0:G], lhsT=wb[:, kc, :], rhs=self.brT[:, 4 * b + kc, 0:G], start=(kc == 0), stop=(kc == 3)),
                             reads=[wb, self.brT], writes=[pp])
                pm = self.proj_fm(l, 'mg%d_%d' % (b, dc), 128, G)
                sg = self.scratch()
                S.op('act', lambda h: h.activation(out=sg[:, 0:G], in_=pm[:, 0:G], func=AF.Sigmoid), reads=[pm], writes=[sg])
                if b == 0:
                    S.op('dve', lambda h, dc=dc: h.tensor_tensor(out=mix3[:, dc, 0:G], in0=pp[:, 0:G], in1=sg[:, 0:G], op=ALU.mult), reads=[pp, sg], writes=[mix])
                else:
                    tmp = self.scratch()
                    S.op('dve', lambda h: h.tensor_tensor(out=tmp[:, 0:G], in0=pp[:, 0:G], in1=sg[:, 0:G], op=ALU.mult), reads=[pp, sg], writes=[tmp])
                    S.op('dve', lambda h, dc=dc: h.tensor_tensor(out=mix3[:, dc, 0:G], in0=mix3[:, dc, 0:G], in1=tmp[:, 0:G], op=ALU.add), reads=[mix, tmp], writes=[mix])
        for ti in range(G // TP):
            t0 = ti * TP
            xt = self.av('xtm', D)
            S.dma('sp', xt[0:TP, :], xin[g0 + t0: g0 + t0 + TP, :], reads=[xin_tk], writes=[xt])
            z = self.av('zt', D)
            for qt in range(8):
                wo = self.wouts[0]
                S.dma('sp', wo[:, :, :], self.woutb[l, qt].rearrange("p (a c) -> p a c", c=128), reads=[self.dk(('woutb', l, qt))], writes=[wo])
                pz = self.pbank()
                for k in range(8):
                    S.op('pe', lambda h, k=k: h.matmul(pz[0:TP, 0:128], lhsT=mix3[:, k, t0:t0 + TP], rhs=wo[:, k, :], start=(k == 0), stop=(k == 7)),
                         reads=[mix, wo], writes=[pz])
                S.op('dve', lambda h, qt=qt: h.scalar_tensor_tensor(out=z[0:TP, qt * 128:(qt + 1) * 128], in0=xt[0:TP, qt * 128:(qt + 1) * 128], scalar=float(DN_ALPHA),
                                                                     in1=pz[0:TP, 0:128], op0=ALU.mult, op1=ALU.add), reads=[xt, pz], writes=[z])
            st = self.sm()
            for half in range(2):
                S.op('dve', lambda h, half=half: h.bn_stats(out=st[0:TP, half * 6:(half + 1) * 6], in_=z[0:TP, half * 512:(half + 1) * 512]), reads=[z], writes=[st])
            mv = self.sm()
            S.op('dve', lambda h: h.bn_aggr(out=mv[0:TP, 0:2], in_=st[0:TP, 0:12]), reads=[st], writes=[mv])
            S.op('act', lambda h: h.activation(out=mv[0:TP, 2:3], in_=mv[0:TP, 1:2], func=AF.Sqrt, bias=self.epsb[0:TP, 0:1]), reads=[mv, self.epsb], writes=[mv])
            S.op('dve', lambda h: h.reciprocal(out=mv[0:TP, 3:4], in_=mv[0:TP, 2:3]), reads=[mv], writes=[mv])
            S.op('dve', lambda h: h.tensor_scalar(out=z[0:TP, :], in0=z[0:TP, :], scalar1=mv[0:TP, 0:1], scalar2=mv[0:TP, 3:4], op0=ALU.subtract, op1=ALU.mult),
                 reads=[z, mv], writes=[z])
            S.op('dve', lambda h: h.tensor_tensor(out=z[0:TP, :], in0=z[0:TP, :], in1=self.lng[0:TP, :], op=ALU.mult), reads=[z, self.lng], writes=[z])
            S.op('dve', lambda h: h.tensor_tensor(out=z[0:TP, :], in0=z[0:TP, :], in1=self.lnb[0:TP, :], op=ALU.add), reads=[z, self.lnb], writes=[z])
            S.dma('pool', yout[g0 + t0: g0 + t0 + TP, :], z[0:TP, :], reads=[z], writes=[yout_tk])


def _prep_consts():
    c = np.zeros((128, 1024), np.float32)
    c[:, 0:128] = np.eye(128, dtype=np.float32)
    p = np.arange(128)[:, None]
    f = np.arange(128)[None, :]
    c[:, 128:256] = (f >= p).astype(np.float32)
    c[:, 256:384] = 1.0
    c[:, 384:512] = 1.0 / 128.0
    c[:, 512:768] = np.arange(1, 257, dtype=np.float32)[None, :]
    c[:, 768:896] = (p > f).astype(np.float32)
    c[0:64, 896] = 1.0
    c[64:128, 928] = 1.0
    return c


_CACHE = {}


def kernel(**inp):
    T = inp['x_prompt'].shape[1]
    if T not in _CACHE:
        b = Builder(T)
        _CACHE[T] = b.build()
    nc = _CACHE[T]
    f = lambda a: np.ascontiguousarray(np.asarray(a, dtype=np.float32))
    w_in = f(inp['w_in'])
    win = np.zeros((DEPTH, NCH, 128, 8, 128), np.float32)
    for ci, (_, cols) in enumerate(CHUNKS):
        blk = w_in[:, :, cols]
        blk = blk.reshape(DEPTH, 8, 128, len(cols)).transpose(0, 2, 1, 3)
        win[:, ci, :, :, :len(cols)] = blk
    cst = _prep_consts()
    glab = f(inp['gla_b_g']).reshape(DEPTH, 4, 64).transpose(0, 2, 1).copy()
    glan = f(inp['gla_norm']).reshape(DEPTH, 128, 1)
    def st_layout(a):
        return np.ascontiguousarray(a.reshape(DEPTH, 16, 2, 64).transpose(0, 2, 3, 1)).reshape(DEPTH, 128, 16)
    are, aim = st_layout(f(inp['s5_a_re'])), st_layout(f(inp['s5_a_im']))
    ldt = np.ascontiguousarray(np.broadcast_to(f(inp['s5_log_dt']).reshape(DEPTH, 16, 2, 1), (DEPTH, 16, 2, 64)).transpose(0, 2, 3, 1)).reshape(DEPTH, 128, 16)
    s5p = np.ascontiguousarray(np.stack([are, aim, ldt], axis=2))
    bre, bim = f(inp['s5_b_re']), f(inp['s5_b_im'])
    cre, cim = f(inp['s5_c_re']), f(inp['s5_c_im'])
    s5b = np.zeros((DEPTH, 2, 2, 128, 4, 128), np.float32)
    s5c = np.zeros((DEPTH, 2, 128, 16, 128), np.float32)
    for c in range(4):
        for s4 in range(4):
            for g2 in range(2):
                g = 8 * c + 2 * s4 + g2
                for ri, (bb, cc) in enumerate(((bre, cre), (bim, cim))):
                    s5b[:, 0 if s4 < 3 else 1, ri, 32 * s4 + 16 * g2: 32 * s4 + 16 * g2 + 16, c, 64 * g2: 64 * g2 + 64] = bb[:, g].transpose(0, 2, 1)
                    s5c[:, ri, 64 * g2: 64 * g2 + 64, 4 * c + s4, (2 * s4 + g2) * 16:(2 * s4 + g2) * 16 + 16] = cc[:, g].transpose(0, 2, 1)
    s5d = np.ascontiguousarray(f(inp['s5_d']).reshape(DEPTH, 4, 128).transpose(0, 2, 1))
    h0r, h0i = st_layout(f(inp['state_s5_re']).transpose(1, 0, 2, 3).reshape(NCORE * DEPTH, 32, 64).reshape(NCORE, DEPTH, 32, 64)[0]) if False else (None, None)
    gcw = np.ascontiguousarray(f(inp['gdn_conv']).reshape(DEPTH, 4, 12, 128).transpose(0, 3, 2, 1))
    gpar = np.ascontiguousarray(np.broadcast_to(np.concatenate([f(inp['gdn_a_log']), f(inp['gdn_dt_bias'])], axis=1)[:, None, :], (DEPTH, 128, 8)))
    gnorm = f(inp['gdn_norm']).reshape(DEPTH, 128, 1)
    rr = 127 - np.arange(384)
    oh = (_t5_bucket(rr)[None, :] == np.arange(32)[:, None]).astype(np.float32)
    common = dict(win=win, relb=f(inp['rel_bias']), oh=oh, gcw=gcw, gpar=gpar, gnorm=gnorm, s5p=s5p, s5b=s5b, s5c=s5c, s5d=s5d, wglu=f(inp['s5_w_glu']), wbr=f(inp['w_branch']), wout=f(inp['w_out']), lng=f(inp['ln_g']), lnb=f(inp['ln_b']), cst=cst,
                  glaw=f(inp['gla_w_g2']), glab=glab, glan=glan)
    xp = f(inp['x_prompt'])
    xs = f(inp['x_sample'])
    in_maps = []
    for c in range(NCORE):
        m = dict(common)
        m['xp'] = xp[c % 2]
        m['xs'] = xs[c]
        m['sgla'] = f(inp['state_gla'])[:, c]
        m['ck'] = f(inp['cache_k'])[:, c].reshape(DEPTH, PAST, 128)
        m['cv'] = f(inp['cache_v'])[:, c].reshape(DEPTH, PAST, 128)
        cki = f(inp['cache_kidx'])[:, c]
        m['cki'] = np.ascontiguousarray(np.concatenate([cki, cki], axis=-1))
        m['sgdn'] = f(inp['state_gdn'])[:, c]
        m['sconv'] = np.ascontiguousarray(f(inp['state_gdn_conv'])[:, c].reshape(DEPTH, 3, 12, 128).transpose(0, 3, 2, 1))
        m['s5h0'] = np.ascontiguousarray(np.stack([st_layout(f(inp['state_s5_re'])[:, c]), st_layout(f(inp['state_s5_im'])[:, c])], axis=2))
        in_maps.append(m)
    res = run_bass_kernel_spmd(nc, in_maps, core_ids=list(range(NCORE))).results
    P = lambda name: np.stack([res[b][name] for b in range(2)], axis=0)
    Sm = lambda name: np.stack([res[b][name] for b in range(NCORE)], axis=0)
    yp = P('yp')
    ys = Sm('ys')

    def kvfix(a, n):
        return np.ascontiguousarray(a.transpose(1, 0, 2, 3)).reshape(DEPTH, n, a.shape[2], 2, 64)

    def st(a):
        return np.ascontiguousarray(np.moveaxis(a, 0, 1))
    zeros = lambda *s: np.zeros(s, np.float32)

    def s5fix(a, ri):
        x = a[:, :, :, ri, :].reshape(a.shape[0], DEPTH, 2, 64, 16).transpose(1, 0, 4, 2, 3)
        return np.ascontiguousarray(x).reshape(DEPTH, a.shape[0], 32, 64)
    outs = [yp, ys]
    for g, getter, n, tt in (('p', P, 2, T), ('s', Sm, NCORE, DEC_SEQ)):
        outs += [kvfix(getter('k' + g), n), kvfix(getter('v' + g), n), st(getter('ki' + g)),
                 s5fix(getter('os5' + g), 0), s5fix(getter('os5' + g), 1), st(getter('ogdn' + g)),
                 st(getter('conv' + g)), st(getter('gla' + g))]
    return tuple(outs)
```

```python
import math
from contextlib import ExitStack
import numpy as np
import concourse.bass as bass
import concourse.mybir as mybir
from concourse.bass_utils import run_bass_kernel_spmd

F32 = mybir.dt.float32
BF16 = mybir.dt.bfloat16
U8 = mybir.dt.uint8
ALU = mybir.AluOpType
AF = mybir.ActivationFunctionType
AX = mybir.AxisListType

D = 1024
SEQ = 8192
DEPTH = 2
DEC_SEQ = 16
PAST = 1024
NCORE = 8
TOPK_MAX = 256
LN_EPS = 1e-5
RMS_EPS = 1e-6
DN_ALPHA = (2 * DEPTH) ** 0.25

_LAY = (('a_u', 512), ('a_gate', 512), ('b_q', 512), ('b_k', 128), ('b_v', 128), ('b_qi', 256), ('b_ki', 64),
        ('b_wi', 4), ('b_gate', 512), ('c_qkv', 1536), ('c_beta', 4), ('c_a', 4), ('c_gate', 512),
        ('d_q', 256), ('d_k', 256), ('d_v', 512), ('d_g', 16), ('d_gate', 512), ('merge', 4096))
OFF = {}
_o = 0
for _n, _w in _LAY:
    OFF[_n] = _o
    _o += _w
IN_WIDTH = _o


def _chunks():
    ch = []
    r = lambda n, a, b: list(range(OFF[n] + a, OFF[n] + b))
    ch.append(('kvA', r('b_k', 0, 128)))
    ch.append(('kvB', r('b_v', 0, 128)))
    ch.append(('kvC', r('b_ki', 0, 64) + r('b_ki', 0, 64)))
    ch.append(('kvD', r('b_wi', 0, 4) + r('c_beta', 0, 4) + r('c_a', 0, 4)))
    for h in range(4):
        ch.append(('qq%d' % h, r('b_q', 64 * h, 64 * h + 64) + r('b_q', 64 * (h + 4), 64 * (h + 4) + 64)))
    for h in range(4):
        ch.append(('qi%d' % h, r('b_qi', 64 * h, 64 * h + 64) + r('b_qi', 64 * h, 64 * h + 64)))
    for c in range(4):
        ch.append(('au%d' % c, r('a_u', 128 * c, 128 * c + 128)))
    for c in range(12):
        ch.append(('cq%d' % c, r('c_qkv', 128 * c, 128 * c + 128)))
    for h in range(4):
        ch.append(('dq%d' % h, r('d_q', 64 * h, 64 * h + 64)))
    for h in range(4):
        ch.append(('dk%d' % h, r('d_k', 64 * h, 64 * h + 64)))
    for c in range(4):
        ch.append(('dv%d' % c, r('d_v', 128 * c, 128 * c + 128)))
    ch.append(('dg', r('d_g', 0, 16)))
    for c in range(4):
        ch.append(('ag%d' % c, r('a_gate', 128 * c, 128 * c + 128)))
    for h in range(8):
        ch.append(('bg%d' % h, r('b_gate', 64 * h, 64 * h + 64)))
    for c in range(4):
        ch.append(('cg%d' % c, r('c_gate', 128 * c, 128 * c + 128)))
    for c in range(4):
        ch.append(('dgt%d' % c, r('d_gate', 128 * c, 128 * c + 128)))
    for b in range(4):
        for c in range(8):
            ch.append(('mg%d_%d' % (b, c), r('merge', 1024 * b + 128 * c, 1024 * b + 128 * c + 128)))
    return ch


CHUNKS = _chunks()
CIDX = {n: i for i, (n, _) in enumerate(CHUNKS)}
NCH = len(CHUNKS)


def _t5_bucket(rel):
    nb = 16
    max_exact = 8
    ret = np.where(rel > 0, nb, 0)
    dist = np.abs(rel)
    distf = np.maximum(dist, 1).astype(np.float32)
    large = max_exact + (np.log(distf / max_exact) / math.log(128 / max_exact) * (nb - max_exact)).astype(np.int32)
    large = np.minimum(large, nb - 1)
    return ret + np.where(dist < max_exact, dist, large)


class Tk:
    __slots__ = ('h', 'w', 'r')

    def __init__(self, h=None):
        self.h = h
        self.w = None
        self.r = []

    def __getitem__(self, idx):
        return self.h[idx]


class Sched:
    def __init__(self, nc, stack):
        self.nc = nc
        self.stack = stack
        self.eng = {}
        for n, h in [('pe', nc.tensor), ('dve', nc.vector), ('act', nc.scalar), ('pool', nc.gpsimd), ('sp', nc.sync)]:
            sem = stack.enter_context(nc.semaphore('s_' + n))
            self.eng[n] = dict(h=h, sem=sem, cnt=0, known={})
        self.dma_sems = {}
        for q in ('sp', 'pool', 'act'):
            self.dma_sems[q] = [[stack.enter_context(nc.semaphore('d%s%d' % (q, i))), 0] for i in range(12)]
        self.dma_rr = {'sp': 0, 'pool': 0, 'act': 0}
        self.nid = 0
        self.n_ins = 0

    def sb(self, shape, dt=F32):
        self.nid += 1
        return Tk(self.stack.enter_context(self.nc.sbuf_tensor('t%d' % self.nid, shape, dt)))

    def ps(self, shape, dt=F32):
        self.nid += 1
        return Tk(self.stack.enter_context(self.nc.psum_tensor('p%d' % self.nid, shape, dt)))

    def _wait(self, e, sem, val):
        E = self.eng[e]
        k = id(sem)
        if E['known'].get(k, 0) >= val:
            return
        E['h'].wait_ge(sem, val)
        E['known'][k] = val

    def _deps(self, e, reads, writes):
        E = self.eng[e]
        own = E['sem']
        pe = (e == 'pe')
        for t in reads:
            if t.w is not None and not (pe and t.w[0] is own):
                self._wait(e, *t.w)
        for t in writes:
            if t.w is not None and not (pe and t.w[0] is own):
                self._wait(e, *t.w)
            for (s, v) in t.r:
                if not (pe and s is own):
                    self._wait(e, s, v)

    def _mark(self, tok, reads, writes):
        for t in writes:
            t.w = tok
            t.r = []
        for t in reads:
            if t not in writes:
                if len(t.r) > 6:
                    d = {}
                    for (s, v) in t.r:
                        d[id(s)] = (s, max(v, d.get(id(s), (s, 0))[1]))
                    t.r = list(d.values())
                t.r.append(tok)

    def op(self, e, fn, reads=(), writes=()):
        E = self.eng[e]
        self._deps(e, reads, writes)
        ins = fn(E['h'])
        E['cnt'] += 1
        ins.then_inc(E['sem'], 1)
        self._mark((E['sem'], E['cnt']), reads, writes)
        self.n_ins += 1
        return ins

    def dma(self, e, out, in_, reads=(), writes=(), **kw):
        E = self.eng[e]
        slots = self.dma_sems[e]
        slot = slots[self.dma_rr[e]]
        self.dma_rr[e] = (self.dma_rr[e] + 1) % len(slots)
        if slot[1] > 0:
            self._wait(e, slot[0], slot[1])
        self._deps(e, reads, writes)
        ins = E['h'].dma_start(out=out, in_=in_, **kw)
        slot[1] += 16
        ins.then_inc(slot[0], 16)
        self._mark((slot[0], slot[1]), reads, writes)
        self.n_ins += 1
        return ins

    def finish(self, tiles):
        for t in tiles:
            if t.w is not None:
                self._wait('sp', *t.w)


class Builder:
    def __init__(self, T):
        self.T = T
        self.G = min(128, T)
        self.nc = bass.Bass("TRN2", target_bir_lowering=False)

    def dram_in(self, name, shape, dt=F32):
        return self.nc.dram_tensor(name, list(shape), dt, kind="ExternalInput").ap()

    def dram_out(self, name, shape, dt=F32):
        return self.nc.dram_tensor(name, list(shape), dt, kind="ExternalOutput").ap()

    def build(self):
        nc = self.nc
        T = self.T
        I = {}
        O = {}
        I['xp'] = self.dram_in('xp', [T, D])
        I['xs'] = self.dram_in('xs', [DEC_SEQ, D])
        I['win'] = self.dram_in('win', [DEPTH, NCH, 128, 8, 128])
        I['wbr'] = self.dram_in('wbr', [DEPTH, 4, 512, D])
        I['wout'] = self.dram_in('wout', [DEPTH, D, D])
        I['lng'] = self.dram_in('lng', [DEPTH, D])
        I['lnb'] = self.dram_in('lnb', [DEPTH, D])
        I['cst'] = self.dram_in('cst', [128, 1024])
        I['glaw'] = self.dram_in('glaw', [DEPTH, 16, 256])
        I['glab'] = self.dram_in('glab', [DEPTH, 64, 4])
        I['glan'] = self.dram_in('glan', [DEPTH, 128, 1])
        I['sgla'] = self.dram_in('sgla', [DEPTH, 4, 64, 128])
        I['relb'] = self.dram_in('relb', [32, 8])
        I['oh'] = self.dram_in('oh', [32, 384])
        I['ck'] = self.dram_in('ck', [DEPTH, PAST, 128])
        I['cv'] = self.dram_in('cv', [DEPTH, PAST, 128])
        I['cki'] = self.dram_in('cki', [DEPTH, PAST, 128])
        I['gcw'] = self.dram_in('gcw', [DEPTH, 128, 12, 4])
        I['gpar'] = self.dram_in('gpar', [DEPTH, 128, 8])
        I['gnorm'] = self.dram_in('gnorm', [DEPTH, 128, 1])
        I['sgdn'] = self.dram_in('sgdn', [DEPTH, 4, 128, 128])
        I['sconv'] = self.dram_in('sconv', [DEPTH, 128, 12, 3])
        I['s5p'] = self.dram_in('s5p', [DEPTH, 128, 3, 16])
        I['s5b'] = self.dram_in('s5b', [DEPTH, 2, 2, 128, 4, 128])
        I['s5c'] = self.dram_in('s5c', [DEPTH, 2, 128, 16, 128])
        I['s5d'] = self.dram_in('s5d', [DEPTH, 128, 4])
        I['s5h0'] = self.dram_in('s5h0', [DEPTH, 128, 2, 16])
        I['wglu'] = self.dram_in('wglu', [DEPTH, 512, 1024])
        O['yp'] = self.dram_out('yp', [T, D])
        O['ys'] = self.dram_out('ys', [DEC_SEQ, D])
        for g, tt in (('p', T), ('s', DEC_SEQ)):
            O['k' + g] = self.dram_out('k' + g, [DEPTH, tt, 128])
            O['v' + g] = self.dram_out('v' + g, [DEPTH, tt, 128])
            O['ki' + g] = self.dram_out('ki' + g, [DEPTH, tt, 64])
            O['gla' + g] = self.dram_out('gla' + g, [DEPTH, 4, 64, 128])
            O['conv' + g] = self.dram_out('conv' + g, [DEPTH, 3, 1536])
            O['gdn' + g] = self.dram_out('ogdn' + g, [DEPTH, 4, 128, 128])
            O['s5' + g] = self.dram_out('os5' + g, [DEPTH, 128, 2, 16])
        self.I, self.O = I, O
        self.y0p = nc.dram_tensor('y0p', [T, D], F32).ap()
        self.fd = nc.dram_tensor('fd', [8, 384], F32).ap()
        self.winb = nc.dram_tensor('winb', [DEPTH, NCH, 128, 8 * 128], BF16).ap()
        self.wbrb = nc.dram_tensor('wbrb', [DEPTH, 4, 8, 128, 4 * 128], BF16).ap()
        self.woutb = nc.dram_tensor('woutb', [DEPTH, 8, 128, 8 * 128], BF16).ap()
        self.cfd = nc.dram_tensor('cfd', [8, 1], F32).ap()
        self.y0s = nc.dram_tensor('y0s', [DEC_SEQ, D], F32).ap()
        with ExitStack() as st:
            S = Sched(nc, st)
            self.S = S
            self.dtk = {}
            self.setup()
            self.precast()
            for l in range(DEPTH):
                self.layer_setup(l)
                self.run_seq(l, 'p')
                self.run_seq(l, 's')
            S.finish(list(self.dtk.values()))
        return nc

    def dk(self, name):
        if name not in self.dtk:
            self.dtk[name] = Tk(None)
        return self.dtk[name]

    def setup(self):
        S = self.S
        G = self.G
        self.cst = S.sb([128, 1024])
        S.dma('sp', self.cst[:], self.I['cst'][:, :], writes=[self.cst])
        self.ident = self.cst
        self.pbanks = [S.ps([128, 512]) for _ in range(6)]
        self.pacc = [S.ps([128, 512]) for _ in range(2)]
        self.pb_i = 0
        self.xT = S.sb([128, 8, G])
        self.xTb = S.sb([128, 8, G], BF16)
        self.wch = [S.sb([128, 8, 128]) for _ in range(2)]
        self.wch_i = 0
        self.wchb = [S.sb([128, 8, 128], BF16) for _ in range(4)]
        self.wchb_i = 0
        self.brT = S.sb([128, 16, G], BF16)
        self.lng = S.sb([128, D])
        self.lnb = S.sb([128, D])
        self.wouts = [S.sb([128, 8, 128], BF16) for _ in range(1)]
        self.wbrs = [S.sb([128, 4, 128], BF16) for _ in range(2)]
        self.wbrBs = [S.sb([64, 8, 128], BF16) for _ in range(2)]
        self.wrr = 0
        self.brB = S.sb([64, 8, G], BF16)
        self.ARENA = 10240
        self.arena = S.sb([128, self.ARENA])
        self.fence_t = S.sb([1, 4])
        self.phase_views = {}
        self.phase_off = {}
        self.cur_phase = None
        self.fence_tok = None
        self.scr_i = 0
        self.ws_i = 0
        self.small = [S.sb([128, 16]) for _ in range(8)]
        self.sm3 = S.sb([128, 3, 16])
        self.small_i = 0
        self.epsb = S.sb([128, 2])
        S.op('dve', lambda h: h.memset(self.epsb[:, 0:1], LN_EPS), writes=[self.epsb])
        S.op('dve', lambda h: h.memset(self.epsb[:, 1:2], RMS_EPS), writes=[self.epsb])
        self.LC = min(128, G)
        self.s5_cos = S.sb([128, 16, self.LC])
        self.s5_sin = S.sb([128, 16, self.LC])
        self.s5_cr = S.sb([128, 16, 128])
        self.s5_ci = S.sb([128, 16, 128])
        self.s5_b = S.sb([128, 4, 4, 128])
        self.s5_q = S.sb([128, 12, 16])
        self.s5_d = S.sb([128, 4])
        self.s5_carry = S.sb([128, 2, 16])
        self.wglu = S.sb([128, 4, 128])
        T = self.T
        self.NK = max(T, PAST + 128)
        self.NH = 4096 if T == 8192 else self.NK
        self.NB = self.NK // 128
        self.kT = S.sb([128, self.NK], BF16)
        self.kiT = S.sb([128, self.NH])
        self.vA = S.sb([128, self.NB, 130], BF16)
        self.bias8 = S.sb([128, 4, 512])
        self.qT = S.sb([128, 4, G], BF16)
        self.qiT = S.sb([128, 4, G])
        self.wiT = S.sb([128, 4])
        self.offrow = S.sb([33, 4 * G], BF16)
        self.kmax2 = S.sb([33, 2])
        self.cf = S.sb([33, 4])
        self.bs = S.sb([128, 16])
        self.Et = [S.sb([128, 4 * G], BF16) for _ in range(2)]
        self.Pt = [S.sb([128, 4 * G], BF16) for _ in range(2)]
        self.selT = [S.sb([128, G], BF16) for _ in range(2)]
        self.onesb = S.sb([33, 128], BF16)
        self.phase('setup')
        self.dsa_setup()
        self.gdn_S = S.sb([128, 4, 128])
        self.gdn_cw = S.sb([128, 12, 4])
        self.gdn_par = S.sb([128, 8])
        self.gdn_n = S.sb([128, 1])
        self.gdn_tail = S.sb([128, 12, 3])
        self.gla_S = S.sb([64, 4, 128])
        self.gla_w = S.sb([16, 256])
        self.gla_b = S.sb([64, 4])
        self.gla_n = S.sb([128, 1])
        self.resetm = S.sb([64, 4 * G])

    def pbank(self):
        t = self.pbanks[self.pb_i]
        self.pb_i = (self.pb_i + 1) % len(self.pbanks)
        return t

    def phase(self, name):
        S = self.S
        if name == self.cur_phase:
            return
        prev = list(self.phase_views.get(self.cur_phase, {}).values()) if self.cur_phase else []
        nxt = list(self.phase_views.get(name, {}).values())
        ft = self.fence_t
        S.op('dve', lambda h: h.memset(ft[0:1, 0:1], 0.0), reads=[], writes=[ft] + prev + nxt)
        self.fence_tok = ft.w
        self.cur_phase = name
        self.phase_views.setdefault(name, {})
        self.phase_off.setdefault(name, 0)
        self.scr_i = 0
        self.ws_i = 0

    def av(self, key, ncols):
        pv = self.phase_views[self.cur_phase]
        if key not in pv:
            off = self.phase_off[self.cur_phase]
            assert off + ncols <= self.ARENA, (self.cur_phase, key, off, ncols)
            t = Tk(self.arena.h[:, off:off + ncols])
            t.w = self.fence_tok
            pv[key] = t
            self.phase_off[self.cur_phase] = off + ncols
        return pv[key]

    def scratch(self):
        t = self.av(('scr', self.scr_i % 8), 512)
        self.scr_i += 1
        return t

    def wsp(self, reset=False):
        if reset:
            self.ws_i = 0
        t = self.av(('ws', self.ws_i), 512)
        self.ws_i += 1
        return t

    def sm(self):
        t = self.small[self.small_i]
        self.small_i = (self.small_i + 1) % len(self.small)
        return t

    def layer_setup(self, l):
        S = self.S
        I = self.I
        S.dma('sp', self.lng[:], I['lng'][l:l + 1, :].partition_broadcast(128), writes=[self.lng])
        S.dma('sp', self.lnb[:], I['lnb'][l:l + 1, :].partition_broadcast(128), writes=[self.lnb])
        S.dma('sp', self.gla_w[:], I['glaw'][l], writes=[self.gla_w])
        S.dma('sp', self.gla_b[:], I['glab'][l], writes=[self.gla_b])
        S.dma('sp', self.gla_n[:], I['glan'][l], writes=[self.gla_n])
        self.phase('setup')
        self.s5_setup(l)
        self.gdn_setup(l)

    FP32_CHUNKS = ('kvC', 'kvD', 'qi0', 'qi1', 'qi2', 'qi3')

    def precast(self):
        S = self.S
        I = self.I
        st32 = self.wch[0]
        n = 0

        def piece(src_ap, rows, cols3, dst_ap, key):
            nonlocal n
            a, c = cols3
            w16 = self.wchb[n % len(self.wchb)]
            S.dma('sp', st32[0:rows, 0:a, 0:c], src_ap, writes=[st32])
            eng = 'act' if n % 2 else 'dve'
            if eng == 'act':
                S.op('act', lambda h: h.activation(out=w16[0:rows, 0:a, 0:c], in_=st32[0:rows, 0:a, 0:c], func=AF.Copy), reads=[st32], writes=[w16])
            else:
                S.op('dve', lambda h: h.tensor_copy(out=w16[0:rows, 0:a, 0:c], in_=st32[0:rows, 0:a, 0:c]), reads=[st32], writes=[w16])
            S.dma('sp', dst_ap, w16[0:rows, 0:a, 0:c], reads=[w16], writes=[self.dk(key)])
            n += 1
        for l in range(DEPTH):
            for name, _ in CHUNKS:
                if name in self.FP32_CHUNKS:
                    continue
                ci = CIDX[name]
                piece(I['win'][l, ci], 128, (8, 128), self.winb[l, ci].rearrange("p (a c) -> p a c", c=128), ('winb', l, ci))
            for b in range(4):
                for dc in range(8):
                    if b == 1:
                        piece(I['wbr'][l, b, :, dc * 128:(dc + 1) * 128].rearrange("(a p) c -> p a c", p=64), 64, (8, 128),
                              self.wbrb[l, b, dc].rearrange("(p h) f -> p (h f)", h=2).rearrange("p (a c) -> p a c", c=128), ('wbrb', l, b, dc))
                    else:
                        piece(I['wbr'][l, b, :, dc * 128:(dc + 1) * 128].rearrange("(a p) c -> p a c", p=128), 128, (4, 128),
                              self.wbrb[l, b, dc].rearrange("p (a c) -> p a c", c=128), ('wbrb', l, b, dc))
            for qt in range(8):
                piece(I['wout'][l, :, qt * 128:(qt + 1) * 128].rearrange("(a p) c -> p a c", p=128), 128, (8, 128),
                      self.woutb[l, qt].rearrange("p (a c) -> p a c", c=128), ('woutb', l, qt))

    def load_w(self, l, name):
        S = self.S
        w = self.wch[self.wch_i]
        self.wch_i = (self.wch_i + 1) % len(self.wch)
        S.dma('sp', w[:], self.I['win'][l, CIDX[name]], writes=[w])
        return w

    BF16_PREFIX = ('zzz',)

    def load_wx(self, l, name):
        S = self.S
        if name in self.FP32_CHUNKS:
            return self.load_w(l, name), self.xT
        w = self.wchb[self.wchb_i]
        self.wchb_i = (self.wchb_i + 1) % len(self.wchb)
        S.dma('sp', w[:], self.winb[l, CIDX[name]].rearrange("p (a c) -> p a c", c=128), reads=[self.dk(('winb', l, CIDX[name]))], writes=[w])
        return w, self.xTb

    def proj_fm(self, l, name, ncols, gw):
        S = self.S
        w, xT = self.load_wx(l, name)
        pb = self.pbank()
        for k in range(8):
            mm = 128 if xT is self.xTb else ncols
            S.op('pe', lambda h, k=k: h.matmul(pb[0:mm, 0:gw], lhsT=w[:, k, 0:mm], rhs=xT[:, k, 0:gw],
                                               start=(k == 0), stop=(k == 7)), reads=[w, xT], writes=[pb])
        return pb

    def proj_tm(self, l, name, ncols, t0, tw):
        S = self.S
        w, xT = self.load_wx(l, name)
        pb = self.pbank()
        for k in range(8):
            S.op('pe', lambda h, k=k: h.matmul(pb[0:tw, 0:ncols], lhsT=xT[:, k, t0:t0 + tw], rhs=w[:, k, 0:ncols],
                                               start=(k == 0), stop=(k == 7)), reads=[w, xT], writes=[pb])
        return pb

    def run_seq(self, l, grp):
        S = self.S
        I, O = self.I, self.O
        T = self.T if grp == 'p' else DEC_SEQ
        G = min(self.G, T)
        TP = min(128, T)
        C = min(64, T)
        nG = T // G
        if l == 0:
            xin = I['xp'] if grp == 'p' else I['xs']
            xin_tk = self.dk('in')
        else:
            xin = self.y0p if grp == 'p' else self.y0s
            xin_tk = self.dk('y0' + grp)
        yout = (O['yp'] if grp == 'p' else O['ys']) if l == DEPTH - 1 else (self.y0p if grp == 'p' else self.y0s)
        yout_tk = self.dk('yout' + grp) if l == DEPTH - 1 else self.dk('y0' + grp)

        if grp == 'p':
            S.op('dve', lambda h: h.memset(self.gla_S[:], 0.0), writes=[self.gla_S])
        else:
            S.dma('sp', self.gla_S[:], I['sgla'][l].rearrange("h k v -> k h v"), writes=[self.gla_S])
        self.s5_init(l, grp)
        self.gdn_init(l, grp)
        self.dsa_init(l, grp)
        S.op('dve', lambda h: h.memset(self.resetm[:], 1.0), writes=[self.resetm])
        rm3 = self.resetm[:, 0:4 * G].rearrange("p (a c) -> p a c", c=C)
        S.op('dve', lambda h: h.memset(rm3[:, :, 0:1], 0.0), writes=[self.resetm])

        for gi in range(nG):
            g0 = gi * G
            self.phase('kv')
            xts = []
            for ti in range(G // TP):
                xt = self.av('xtm', D)
                xts.append(xt)
                S.dma('sp', xt[0:TP, :], xin[g0 + ti * TP: g0 + (ti + 1) * TP, :], reads=[xin_tk], writes=[xt])
                for half in range(2):
                    pb = self.pbank()
                    for kk in range(4):
                        k = half * 4 + kk
                        S.op('pe', lambda h, k=k, kk=kk: h.transpose(pb[:, kk * 128: kk * 128 + TP], xt[0:TP, k * 128:(k + 1) * 128], self.ident[0:TP, 0:TP]),
                             reads=[xt, self.cst], writes=[pb])
                    dst = self.xT[:, half * 4:(half + 1) * 4, ti * TP:(ti + 1) * TP]
                    src = pb[:, :].rearrange("p (a c) -> p a c", c=128)[:, :, 0:TP]
                    dstb = self.xTb[:, half * 4:(half + 1) * 4, ti * TP:(ti + 1) * TP]
                    S.op('dve', lambda h: h.tensor_copy(out=dst, in_=src), reads=[pb], writes=[self.xT])
                    S.op('act', lambda h: h.activation(out=dstb, in_=dst, func=AF.Copy), reads=[self.xT], writes=[self.xTb])
            self.phase('kv')
            self.kv_part(l, grp, g0, G, TP)
            self.phase('s5')
            self.s5_part(l, grp, g0, G, last=(gi == nG - 1))
            self.phase('gla')
            self.gla_part(l, grp, g0, G, C, last=(gi == nG - 1))
            self.phase('gdn')
            self.gdn_part(l, grp, g0, G, C, last=(gi == nG - 1))
            self.phase('dsa')
            self.dsa_part(l, grp, g0, G, TP)
            self.phase('out')
            self.out_stage(l, grp, g0, G, TP, xin, xin_tk, yout, yout_tk)

    def branch_zero(self, G):
        S = self.S
        S.op('dve', lambda h: h.memset(self.brT[:, :, 0:G], 0.0), writes=[self.brT])
        S.op('dve', lambda h: h.memset(self.brB[:, :, 0:G], 0.0), writes=[self.brB])

    def kv_part(self, l, grp, g0, G, TP):
        S = self.S
        O = self.O
        kbase = 0 if grp == 'p' else PAST
        for ti in range(G // TP):
            t0 = ti * TP
            tiles = []
            for nm, ncols, oname, key in (('kvA', 128, 'k', 'k_tm'), ('kvB', 128, 'v', 'v_tm'), ('kvC', 128, 'ki', 'c_tm')):
                pb = self.proj_tm(l, nm, ncols, t0, TP)
                sc = self.av(key, 128)
                S.op('act', lambda h: h.activation(out=sc[0:TP, 0:ncols], in_=pb[0:TP, 0:ncols], func=AF.Copy), reads=[pb], writes=[sc])
                oc = 64 if oname == 'ki' else 128
                S.dma('pool', O[oname + grp][l, g0 + t0: g0 + t0 + TP, :], sc[0:TP, 0:oc], reads=[sc], writes=[self.dk('o' + oname + grp)])
                tiles.append(sc)
            self.add_keys(tiles[0], tiles[1], tiles[2], TP, kbase + g0 + t0)
            pb = self.proj_tm(l, 'kvD', 12, t0, TP)
            S.op('dve', lambda h: h.tensor_copy(out=self.wiT[0:TP, 0:4], in_=pb[0:TP, 0:4]), reads=[pb], writes=[self.wiT])
        T = self.T if grp == 'p' else DEC_SEQ
        if g0 + G == T:
            for c in range(12):
                w, xTw = self.load_wx(l, 'cq%d' % c)
                pb = self.pbank()
                for k in range(8):
                    S.op('pe', lambda h, k=k: h.matmul(pb[0:3, 0:128], lhsT=xTw[:, k, G - 3:G], rhs=w[:, k, :],
                                                       start=(k == 0), stop=(k == 7)), reads=[w, xTw], writes=[pb])
                sc = self.scratch()
                S.op('act', lambda h: h.activation(out=sc[0:3, 0:128], in_=pb[0:3, 0:128], func=AF.Copy), reads=[pb], writes=[sc])
                S.dma('pool', O['conv' + grp][l, :, c * 128:(c + 1) * 128], sc[0:3, 0:128], reads=[sc], writes=[self.dk('oconv' + grp)])


    def range_reduce(self, x, n):
        S = self.S
        TWO_PI = 2.0 * math.pi
        ki = self.s5_ki
        kf = self.s5_kf
        S.op('dve', lambda h: h.tensor_scalar(out=kf[:, 0:n], in0=x, scalar1=1.0 / TWO_PI, scalar2=None, op0=ALU.mult), reads=[self.s5_ang], writes=[self.s5_kfT])
        S.op('dve', lambda h: h.tensor_copy(out=ki[:, 0:n], in_=kf[:, 0:n]), reads=[self.s5_kfT], writes=[self.s5_kiT])
        S.op('dve', lambda h: h.tensor_copy(out=kf[:, 0:n], in_=ki[:, 0:n]), reads=[self.s5_kiT], writes=[self.s5_kfT])
        S.op('dve', lambda h: h.scalar_tensor_tensor(out=x, in0=kf[:, 0:n], scalar=-TWO_PI, in1=x, op0=ALU.mult, op1=ALU.add), reads=[self.s5_kfT, self.s5_ang], writes=[self.s5_ang])
        S.op('dve', lambda h: h.tensor_scalar(out=kf[:, 0:n], in0=x, scalar1=math.pi, scalar2=-TWO_PI, op0=ALU.is_gt, op1=ALU.mult), reads=[self.s5_ang], writes=[self.s5_kfT])
        S.op('dve', lambda h: h.tensor_tensor(out=x, in0=x, in1=kf[:, 0:n], op=ALU.add), reads=[self.s5_kfT, self.s5_ang], writes=[self.s5_ang])
        S.op('dve', lambda h: h.tensor_scalar(out=kf[:, 0:n], in0=x, scalar1=-math.pi, scalar2=TWO_PI, op0=ALU.is_lt, op1=ALU.mult), reads=[self.s5_ang], writes=[self.s5_kfT])
        S.op('dve', lambda h: h.tensor_tensor(out=x, in0=x, in1=kf[:, 0:n], op=ALU.add), reads=[self.s5_kfT, self.s5_ang], writes=[self.s5_ang])

    def s5_setup(self, l):
        S = self.S
        I = self.I
        LC = self.LC
        q = self.s5_q
        if not hasattr(self, 's5_ang'):
            self.s5_ang = S.sb([128, 128])
            self.s5_kfT = S.sb([128, 128])
            self.s5_kiT = S.sb([128, 128], mybir.dt.int32)
            self.s5_kf = self.s5_kfT
            self.s5_ki = self.s5_kiT
        ang = self.s5_ang
        raw = self.sm3
        S.dma('sp', raw[:, :, :], I['s5p'][l], writes=[raw])
        S.dma('sp', self.s5_d[:, :], I['s5d'][l], writes=[self.s5_d])
        S.dma('sp', self.s5_b[:, :, :, :], I['s5b'][l].rearrange("v r p c f -> p (v r) c f"), writes=[self.s5_b])
        Q = lambda i: q[:, i, :]
        rq = [q]
        S.op('dve', lambda h: h.tensor_scalar(out=Q(4), in0=raw[:, 0, :], scalar1=-1e-4, scalar2=None, op0=ALU.min), reads=[raw], writes=rq)
        S.op('dve', lambda h: h.tensor_copy(out=Q(5), in_=raw[:, 1, :]), reads=[raw], writes=rq)
        S.op('act', lambda h: h.activation(out=Q(6), in_=raw[:, 2, :], func=AF.Exp), reads=[raw], writes=rq)
        S.op('dve', lambda h: h.tensor_tensor(out=Q(7), in0=Q(4), in1=Q(6), op=ALU.mult), reads=rq, writes=rq)
        S.op('act', lambda h: h.activation(out=Q(0), in_=Q(7), func=AF.Exp), reads=rq, writes=rq)
        S.op('dve', lambda h: h.tensor_tensor(out=Q(1), in0=Q(5), in1=Q(6), op=ALU.mult), reads=rq, writes=rq)
        S.op('dve', lambda h: h.tensor_copy(out=ang[:, 0:16], in_=Q(1)), reads=rq, writes=[ang])
        self.range_reduce(ang[:, 0:16], 16)
        S.op('dve', lambda h: h.tensor_copy(out=Q(1), in_=ang[:, 0:16]), reads=[ang], writes=rq)
        S.op('act', lambda h: h.activation(out=Q(8), in_=ang[:, 0:16], func=AF.Sin), reads=[ang], writes=rq)
        S.op('dve', lambda h: h.tensor_scalar(out=ang[:, 0:16], in0=ang[:, 0:16], scalar1=math.pi / 2, scalar2=None, op0=ALU.add), reads=[ang], writes=[ang])
        self.range_reduce(ang[:, 0:16], 16)
        S.op('act', lambda h: h.activation(out=Q(9), in_=ang[:, 0:16], func=AF.Sin), reads=[ang], writes=rq)
        S.op('dve', lambda h: h.tensor_tensor(out=Q(9), in0=Q(9), in1=Q(0), op=ALU.mult), reads=rq, writes=rq)
        S.op('dve', lambda h: h.tensor_tensor(out=Q(8), in0=Q(8), in1=Q(0), op=ALU.mult), reads=rq, writes=rq)
        S.op('dve', lambda h: h.tensor_scalar(out=Q(9), in0=Q(9), scalar1=-1.0, scalar2=None, op0=ALU.add), reads=rq, writes=rq)
        S.op('dve', lambda h: h.tensor_tensor(out=Q(10), in0=Q(4), in1=Q(4), op=ALU.mult), reads=rq, writes=rq)
        S.op('dve', lambda h: h.tensor_tensor(out=Q(11), in0=Q(5), in1=Q(5), op=ALU.mult), reads=rq, writes=rq)
        S.op('dve', lambda h: h.tensor_tensor(out=Q(10), in0=Q(10), in1=Q(11), op=ALU.add), reads=rq, writes=rq)
        S.op('dve', lambda h: h.reciprocal(out=Q(10), in_=Q(10)), reads=rq, writes=rq)
        S.op('dve', lambda h: h.tensor_tensor(out=Q(2), in0=Q(9), in1=Q(4), op=ALU.mult), reads=rq, writes=rq)
        S.op('dve', lambda h: h.tensor_tensor(out=Q(11), in0=Q(8), in1=Q(5), op=ALU.mult), reads=rq, writes=rq)
        S.op('dve', lambda h: h.tensor_tensor(out=Q(2), in0=Q(2), in1=Q(11), op=ALU.add), reads=rq, writes=rq)
        S.op('dve', lambda h: h.tensor_tensor(out=Q(2), in0=Q(2), in1=Q(10), op=ALU.mult), reads=rq, writes=rq)
        S.op('dve', lambda h: h.tensor_tensor(out=Q(3), in0=Q(8), in1=Q(4), op=ALU.mult), reads=rq, writes=rq)
        S.op('dve', lambda h: h.tensor_tensor(out=Q(11), in0=Q(9), in1=Q(5), op=ALU.mult), reads=rq, writes=rq)
        S.op('dve', lambda h: h.tensor_tensor(out=Q(3), in0=Q(3), in1=Q(11), op=ALU.subtract), reads=rq, writes=rq)
        S.op('dve', lambda h: h.tensor_tensor(out=Q(3), in0=Q(3), in1=Q(10), op=ALU.mult), reads=rq, writes=rq)
        for st_ in range(16):
            for which, tab in ((0, self.s5_sin), (1, self.s5_cos)):
                S.op('dve', lambda h, st_=st_, which=which: h.tensor_scalar(out=ang[:, 0:LC], in0=self.cst[:, 512:512 + LC], scalar1=q[:, 1, st_:st_ + 1],
                                                                         scalar2=(math.pi / 2 if which else 0.0), op0=ALU.mult, op1=ALU.add), reads=[self.cst, q], writes=[ang])
                self.range_reduce(ang[:, 0:LC], LC)
                S.op('act', lambda h, st_=st_, tab=tab: h.activation(out=tab[:, st_, :], in_=ang[:, 0:LC], func=AF.Sin), reads=[ang], writes=[tab])
        for st_ in range(16):
            c_r = self.scratch()
            c_i = self.scratch()
            S.dma('sp', c_r[:, 0:128], I['s5c'][l, 0, :, st_, :], writes=[c_r])
            S.dma('sp', c_i[:, 0:128], I['s5c'][l, 1, :, st_, :], writes=[c_i])
            t1 = self.scratch()
            S.op('dve', lambda h, st_=st_: h.tensor_scalar(out=t1[:, 0:128], in0=c_i[:, 0:128], scalar1=q[:, 3, st_:st_ + 1], scalar2=None, op0=ALU.mult), reads=[c_i, q], writes=[t1])
            S.op('dve', lambda h, st_=st_: h.scalar_tensor_tensor(out=self.s5_cr[:, st_, :], in0=c_r[:, 0:128], scalar=q[:, 2, st_:st_ + 1], in1=t1[:, 0:128], op0=ALU.mult, op1=ALU.subtract),
                 reads=[c_r, q, t1], writes=[self.s5_cr])
            S.op('dve', lambda h, st_=st_: h.tensor_scalar(out=t1[:, 0:128], in0=c_i[:, 0:128], scalar1=q[:, 2, st_:st_ + 1], scalar2=-1.0, op0=ALU.mult, op1=ALU.mult), reads=[c_i, q], writes=[t1])
            S.op('dve', lambda h, st_=st_: h.tensor_scalar(out=c_r[:, 0:128], in0=c_r[:, 0:128], scalar1=q[:, 3, st_:st_ + 1], scalar2=None, op0=ALU.mult), reads=[c_r, q], writes=[c_r])
            S.op('dve', lambda h, st_=st_: h.tensor_tensor(out=self.s5_ci[:, st_, :], in0=t1[:, 0:128], in1=c_r[:, 0:128], op=ALU.subtract), reads=[t1, c_r], writes=[self.s5_ci])

    def s5_init(self, l, grp):
        S = self.S
        q = self.s5_q
        car = self.s5_carry
        if grp == 'p':
            S.op('dve', lambda h: h.memset(car[:, :, :], 0.0), writes=[car])
            return
        h0 = self.sm3
        S.dma('sp', h0[:, 0:2, :], self.I['s5h0'][l], writes=[h0])
        m = self.sm()
        n2 = self.sm()
        t = self.sm()
        S.op('dve', lambda h: h.tensor_tensor(out=m[:, 0:16], in0=q[:, 2, :], in1=q[:, 2, :], op=ALU.mult), reads=[q], writes=[m])
        S.op('dve', lambda h: h.tensor_tensor(out=n2[:, 0:16], in0=q[:, 3, :], in1=q[:, 3, :], op=ALU.mult), reads=[q], writes=[n2])
        S.op('dve', lambda h: h.tensor_tensor(out=m[:, 0:16], in0=m[:, 0:16], in1=n2[:, 0:16], op=ALU.add), reads=[m, n2], writes=[m])
        S.op('dve', lambda h: h.reciprocal(out=m[:, 0:16], in_=m[:, 0:16]), reads=[m], writes=[m])
        S.op('dve', lambda h: h.tensor_tensor(out=t[:, 0:16], in0=h0[:, 0, :], in1=q[:, 2, :], op=ALU.mult), reads=[h0, q], writes=[t])
        S.op('dve', lambda h: h.tensor_tensor(out=n2[:, 0:16], in0=h0[:, 1, :], in1=q[:, 3, :], op=ALU.mult), reads=[h0, q], writes=[n2])
        S.op('dve', lambda h: h.tensor_tensor(out=t[:, 0:16], in0=t[:, 0:16], in1=n2[:, 0:16], op=ALU.add), reads=[t, n2], writes=[t])
        S.op('dve', lambda h: h.tensor_tensor(out=car[:, 0, :], in0=t[:, 0:16], in1=m[:, 0:16], op=ALU.mult), reads=[t, m], writes=[car])
        S.op('dve', lambda h: h.tensor_tensor(out=t[:, 0:16], in0=h0[:, 1, :], in1=q[:, 2, :], op=ALU.mult), reads=[h0, q], writes=[t])
        S.op('dve', lambda h: h.tensor_tensor(out=n2[:, 0:16], in0=h0[:, 0, :], in1=q[:, 3, :], op=ALU.mult), reads=[h0, q], writes=[n2])
        S.op('dve', lambda h: h.tensor_tensor(out=t[:, 0:16], in0=t[:, 0:16], in1=n2[:, 0:16], op=ALU.subtract), reads=[t, n2], writes=[t])
        S.op('dve', lambda h: h.tensor_tensor(out=car[:, 1, :], in0=t[:, 0:16], in1=m[:, 0:16], op=ALU.mult), reads=[t, m], writes=[car])

    def s5_final(self, l, grp):
        S = self.S
        q = self.s5_q
        car = self.s5_carry
        o = self.sm3
        t = self.sm()
        S.op('dve', lambda h: h.tensor_tensor(out=o[:, 0, :], in0=car[:, 0, :], in1=q[:, 2, :], op=ALU.mult), reads=[car, q], writes=[o])
        S.op('dve', lambda h: h.tensor_tensor(out=t[:, 0:16], in0=car[:, 1, :], in1=q[:, 3, :], op=ALU.mult), reads=[car, q], writes=[t])
        S.op('dve', lambda h: h.tensor_tensor(out=o[:, 0, :], in0=o[:, 0, :], in1=t[:, 0:16], op=ALU.subtract), reads=[o, t], writes=[o])
        S.op('dve', lambda h: h.tensor_tensor(out=o[:, 1, :], in0=car[:, 0, :], in1=q[:, 3, :], op=ALU.mult), reads=[car, q], writes=[o])
        S.op('dve', lambda h: h.tensor_tensor(out=t[:, 0:16], in0=car[:, 1, :], in1=q[:, 2, :], op=ALU.mult), reads=[car, q], writes=[t])
        S.op('dve', lambda h: h.tensor_tensor(out=o[:, 1, :], in0=o[:, 1, :], in1=t[:, 0:16], op=ALU.add), reads=[o, t], writes=[o])
        S.dma('pool', self.O['s5' + grp][l], o[:, 0:2, :], reads=[o], writes=[self.dk('os5' + grp)])

    def s5_part(self, l, grp, g0, G, last):
        S = self.S
        I = self.I
        LC = min(self.LC, G)
        ug = self.av('ug', 4 * G)
        ug3 = ug.h[:, 0:4 * G].rearrange("p (a c) -> p a c", c=G)
        q = self.s5_q
        car = self.s5_carry
        for c in range(4):
            pb = self.proj_fm(l, 'au%d' % c, 128, G)
            S.op('act' if c % 2 else 'dve', (lambda h, c=c: h.activation(out=ug3[:, c, 0:G], in_=pb[:, 0:G], func=AF.Copy)) if c % 2 else (lambda h, c=c: h.tensor_copy(out=ug3[:, c, 0:G], in_=pb[:, 0:G])),
                 reads=[pb], writes=[ug])
        yT = self.wsp(True)
        y3 = yT[:, 0:4 * G].rearrange("p (a c) -> p a c", c=G)
        rt = [self.s5_cos, self.s5_sin]
        nscr = [0]

        def s5scr():
            t = self.av(('s5scr', nscr[0] % 12), 128)
            nscr[0] += 1
            return t
        for s0 in range(0, G, LC):
            for c in range(4):
                py = self.pacc[c % 2]
                for pair in range(2):
                    tl = []
                    for s4 in (2 * pair, 2 * pair + 1):
                        st_ = 4 * c + s4
                        brow = slice(32 * s4, 32 * s4 + 32) if s4 < 3 else slice(64, 128)
                        bvar = 0 if s4 < 3 else 2
                        pr = self.pbank()
                        pi = self.pbank()
                        S.op('pe', lambda h: h.matmul(pr[:, 0:LC], lhsT=self.s5_b[brow, bvar + 0, c, :], rhs=ug3[brow, c, s0:s0 + LC], start=True, stop=True),
                             reads=[self.s5_b, ug], writes=[pr])
                        S.op('pe', lambda h: h.matmul(pi[:, 0:LC], lhsT=self.s5_b[brow, bvar + 1, c, :], rhs=ug3[brow, c, s0:s0 + LC], start=True, stop=True),
                             reads=[self.s5_b, ug], writes=[pi])
                        tl.append(dict(s4=s4, st=st_, pr=pr, pi=pi, cs=self.s5_cos[:, st_, 0:LC], sn=self.s5_sin[:, st_, 0:LC],
                                       a=s5scr(), b=s5scr(), xr=s5scr(), xi=s5scr(), a2=s5scr(), b2=s5scr()))
                    for t in tl:
                        pr, pi, cs, sn, a_, b_, xr, xi, a2, b2 = t['pr'], t['pi'], t['cs'], t['sn'], t['a'], t['b'], t['xr'], t['xi'], t['a2'], t['b2']
                        S.op('dve', lambda h: h.tensor_tensor(out=a_[:, 0:LC], in0=pr[:, 0:LC], in1=cs, op=ALU.mult), reads=[pr] + rt, writes=[a_])
                        S.op('dve', lambda h: h.tensor_tensor(out=b_[:, 0:LC], in0=pi[:, 0:LC], in1=sn, op=ALU.mult), reads=[pi] + rt, writes=[b_])
                        S.op('pool', lambda h: h.tensor_tensor(out=xr[:, 0:LC], in0=a_[:, 0:LC], in1=b_[:, 0:LC], op=ALU.add), reads=[a_, b_], writes=[xr])
                        S.op('dve', lambda h: h.tensor_tensor(out=a2[:, 0:LC], in0=pi[:, 0:LC], in1=cs, op=ALU.mult), reads=[pi] + rt, writes=[a2])
                        S.op('dve', lambda h: h.tensor_tensor(out=b2[:, 0:LC], in0=pr[:, 0:LC], in1=sn, op=ALU.mult), reads=[pr] + rt, writes=[b2])
                        S.op('pool', lambda h: h.tensor_tensor(out=xi[:, 0:LC], in0=a2[:, 0:LC], in1=b2[:, 0:LC], op=ALU.subtract), reads=[a2, b2], writes=[xi])
                    for t in tl:
                        st_, xr, xi = t['st'], t['xr'], t['xi']
                        rho = q[:, 0, st_:st_ + 1].to_broadcast([128, LC])
                        S.op('dve', lambda h: h.tensor_tensor_scan(out=xr[:, 0:LC], data0=rho, data1=xr[:, 0:LC], initial=car[:, 0, st_:st_ + 1], op0=ALU.mult, op1=ALU.add),
                             reads=[q, xr, car], writes=[xr])
                        S.op('dve', lambda h: h.tensor_tensor_scan(out=xi[:, 0:LC], data0=rho, data1=xi[:, 0:LC], initial=car[:, 1, st_:st_ + 1], op0=ALU.mult, op1=ALU.add),
                             reads=[q, xi, car], writes=[xi])
                    for t in tl:
                        cs, sn, a_, b_, xr, xi, a2, b2 = t['cs'], t['sn'], t['a'], t['b'], t['xr'], t['xi'], t['a2'], t['b2']
                        S.op('pool', lambda h: h.tensor_tensor(out=a_[:, 0:LC], in0=xr[:, 0:LC], in1=cs, op=ALU.mult), reads=[xr] + rt, writes=[a_])
                        S.op('pool', lambda h: h.tensor_tensor(out=b_[:, 0:LC], in0=xi[:, 0:LC], in1=sn, op=ALU.mult), reads=[xi] + rt, writes=[b_])
                        S.op('pool', lambda h: h.tensor_tensor(out=a2[:, 0:LC], in0=xr[:, 0:LC], in1=sn, op=ALU.mult), reads=[xr] + rt, writes=[a2])
                        S.op('pool', lambda h: h.tensor_tensor(out=b2[:, 0:LC], in0=xi[:, 0:LC], in1=cs, op=ALU.mult), reads=[xi] + rt, writes=[b2])
                    for t in tl:
                        a_, b_, a2, b2 = t['a'], t['b'], t['a2'], t['b2']
                        S.op('dve', lambda h: h.tensor_tensor(out=a_[:, 0:LC], in0=a_[:, 0:LC], in1=b_[:, 0:LC], op=ALU.subtract), reads=[a_, b_], writes=[a_])
                        S.op('dve', lambda h: h.tensor_tensor(out=a2[:, 0:LC], in0=a2[:, 0:LC], in1=b2[:, 0:LC], op=ALU.add), reads=[a2, b2], writes=[a2])
                    for t in tl:
                        st_, a_, a2 = t['st'], t['a'], t['a2']
                        S.op('dve', lambda h: h.tensor_copy(out=car[:, 0, st_:st_ + 1], in_=a_[:, LC - 1:LC]), reads=[a_], writes=[car])
                        S.op('dve', lambda h: h.tensor_copy(out=car[:, 1, st_:st_ + 1], in_=a2[:, LC - 1:LC]), reads=[a2], writes=[car])
                    for t in tl:
                        s4, st_, a_, a2 = t['s4'], t['st'], t['a'], t['a2']
                        S.op('pe', lambda h: h.matmul(py[:, 0:LC], lhsT=self.s5_cr[:, st_, :], rhs=a_[:, 0:LC], start=(s4 == 0), stop=False), reads=[self.s5_cr, a_], writes=[py])
                        S.op('pe', lambda h: h.matmul(py[:, 0:LC], lhsT=self.s5_ci[:, st_, :], rhs=a2[:, 0:LC], start=False, stop=(s4 == 3)), reads=[self.s5_ci, a2], writes=[py])
                S.op('dve', lambda h: h.scalar_tensor_tensor(out=y3[:, c, s0:s0 + LC], in0=ug3[:, c, s0:s0 + LC], scalar=self.s5_d[:, c:c + 1], in1=py[:, 0:LC], op0=ALU.mult, op1=ALU.add),
                     reads=[ug, self.s5_d, py], writes=[yT])
        if last:
            self.s5_final(l, grp)
        S.op('act', lambda h: h.activation(out=ug3[:, :, 0:G], in_=y3, func=AF.Gelu), reads=[yT], writes=[ug])
        for c in range(4):
            sgs = []
            for oc in (c + 4, c):
                S.dma('sp', self.wglu[:, :, :], I['wglu'][l, :, oc * 128:(oc + 1) * 128].rearrange("(a p) f -> p a f", p=128), writes=[self.wglu])
                pg = self.pbank()
                for kc in range(4):
                    S.op('pe', lambda h, kc=kc: h.matmul(pg[:, 0:G], lhsT=self.wglu[:, kc, :], rhs=ug3[:, kc, 0:G], start=(kc == 0), stop=(kc == 3)), reads=[self.wglu, ug], writes=[pg])
                sgs.append(pg)
            sg = self.scratch()
            S.op('act', lambda h: h.activation(out=sg[:, 0:G], in_=sgs[0][:, 0:G], func=AF.Sigmoid), reads=[sgs[0]], writes=[sg])
            va = self.scratch()
            S.op('dve', lambda h: h.tensor_tensor(out=va[:, 0:G], in0=sgs[1][:, 0:G], in1=sg[:, 0:G], op=ALU.mult), reads=[sgs[1], sg], writes=[va])
            pgt = self.proj_fm(l, 'ag%d' % c, 128, G)
            sg2 = self.scratch()
            S.op('act', lambda h: h.activation(out=sg2[:, 0:G], in_=pgt[:, 0:G], func=AF.Silu), reads=[pgt], writes=[sg2])
            S.op('dve', lambda h, c=c: h.tensor_tensor(out=self.brT[:, c, 0:G], in0=va[:, 0:G], in1=sg2[:, 0:G], op=ALU.mult), reads=[va, sg2], writes=[self.brT])


    def dsa_setup(self):
        S = self.S
        I = self.I
        cst = self.cst
        rb = self.sm()
        oh = self.av('oh', 384)
        S.dma('sp', rb[0:32, 0:8], I['relb'][:, :], writes=[rb])
        S.dma('sp', oh[0:32, 0:384], I['oh'][:, :], writes=[oh])
        pf = self.pbank()
        S.op('pe', lambda h: h.matmul(pf[0:8, 0:384], lhsT=rb[0:32, 0:8], rhs=oh[0:32, 0:384], start=True, stop=True), reads=[rb, oh], writes=[pf])
        f8 = self.av('f8', 384)
        cm = self.sm()
        S.op('dve', lambda h: h.tensor_copy(out=cm[0:8, 0:1], in_=pf[0:8, 382:383]), reads=[pf], writes=[cm])
        S.op('dve', lambda h: h.tensor_scalar(out=f8[0:8, 0:384], in0=pf[0:8, 0:384], scalar1=cm[0:8, 0:1], scalar2=8.0, op0=ALU.subtract, op1=ALU.mult), reads=[pf, cm], writes=[f8])
        S.op('dve', lambda h: h.tensor_reduce(out=cm[0:8, 1:2], in_=f8[0:8, 0:383], axis=AX.X, op=ALU.max, negate=True), reads=[f8], writes=[cm])
        fdk = self.dk('fd')
        S.dma('sp', self.fd[:, :], f8[0:8, 0:384], reads=[f8], writes=[fdk])
        S.dma('sp', self.cfd[:, :], cm[0:8, 1:2], reads=[cm], writes=[fdk])
        S.op('dve', lambda h: h.memset(self.cf[:, :], 0.0), writes=[self.cf])
        S.dma('sp', self.cf[0:1, 0:4], self.cfd[0:4, :].rearrange("a b -> b a"), reads=[fdk], writes=[self.cf])
        S.dma('sp', self.cf[32:33, 0:4], self.cfd[4:8, :].rearrange("a b -> b a"), reads=[fdk], writes=[self.cf])
        for lrow in range(128):
            for g in range(2):
                src = bass.AP(tensor=self.fd.tensor, offset=4 * g * 384 + 127 - lrow, ap=[[0, 1], [128, 2], [384, 4], [1, 128]])
                dst = self.bias8[lrow:lrow + 1, 2 * g:2 * g + 2, :].rearrange("p k (a c) -> p k a c", c=128)
                S.dma('sp' if g else 'pool', dst, src, reads=[fdk], writes=[self.bias8])
        S.op('dve', lambda h: h.memset(self.onesb[:, :], 1.0), writes=[self.onesb])

    def add_keys(self, k_tm, v_tm, c_tm, n, key0):
        S = self.S
        cst = self.cst
        blk = key0 // 128
        pk = self.pbank()
        S.op('pe', lambda h: h.transpose(pk[:, 0:n], k_tm[0:n, 0:128], self.ident[0:n, 0:n]), reads=[k_tm, cst], writes=[pk])
        S.op('act', lambda h: h.activation(out=self.kT[:, key0:key0 + n], in_=pk[:, 0:n], func=AF.Copy), reads=[pk], writes=[self.kT])
        sq = self.scratch()
        S.op('act', lambda h: h.activation(out=sq[:, 0:n], in_=pk[:, 0:n], func=AF.Square), reads=[pk], writes=[sq])
        pn = self.pbank()
        S.op('pe', lambda h: h.matmul(pn[0:33, 0:n], lhsT=cst[:, 896:929], rhs=sq[:, 0:n], start=True, stop=True), reads=[cst, sq], writes=[pn])
        S.op('dve', lambda h: h.tensor_reduce(out=self.kmax2[:, 1:2], in_=pn[0:33, 0:n], axis=AX.X, op=ALU.max), reads=[pn], writes=[self.kmax2])
        S.op('dve', lambda h: h.tensor_tensor(out=self.kmax2[:, 0:1], in0=self.kmax2[:, 0:1], in1=self.kmax2[:, 1:2], op=ALU.max), reads=[self.kmax2], writes=[self.kmax2])
        va = self.vA[0:n, blk, :].rearrange("p (g c) -> p g c", c=65)[:, :, 0:64]
        S.op('dve', lambda h: h.tensor_copy(out=va, in_=v_tm[0:n, 0:128].rearrange("p (g c) -> p g c", c=64)), reads=[v_tm], writes=[self.vA])
        pc = self.pbank()
        S.op('pe', lambda h: h.transpose(pc[:, 0:n], c_tm[0:n, 0:128], self.ident[0:n, 0:n]), reads=[c_tm, cst], writes=[pc])
        if key0 < self.NH:
            S.op('dve', lambda h: h.tensor_copy(out=self.kiT[0:64, key0:key0 + n], in_=pc[0:64, 0:n]), reads=[pc], writes=[self.kiT])
        else:
            S.op('dve', lambda h: h.tensor_copy(out=self.kiT[64:128, key0 - self.NH:key0 - self.NH + n], in_=pc[64:128, 0:n]), reads=[pc], writes=[self.kiT])

    def dsa_init(self, l, grp):
        S = self.S
        I = self.I
        S.op('dve', lambda h: h.memset(self.vA[:, :, :], 1.0), writes=[self.vA])
        S.op('dve', lambda h: h.memset(self.kmax2[:, :], 0.0), writes=[self.kmax2])
        if grp == 's':
            self.phase('kv')
            for b in range(PAST // 128):
                k_tm = self.av('k_tm', 128); v_tm = self.av('v_tm', 128); c_tm = self.av('c_tm', 128)
                S.dma('sp', k_tm[:, 0:128], I['ck'][l, b * 128:(b + 1) * 128, :], writes=[k_tm])
                S.dma('sp', v_tm[:, 0:128], I['cv'][l, b * 128:(b + 1) * 128, :], writes=[v_tm])
                S.dma('sp', c_tm[:, 0:128], I['cki'][l, b * 128:(b + 1) * 128, :], writes=[c_tm])
                self.add_keys(k_tm, v_tm, c_tm, 128, b * 128)

    def dsa_part(self, l, grp, g0, G, TP):
        S = self.S
        cst = self.cst
        kbase = 0 if grp == 'p' else PAST
        Qa = kbase + g0
        L = Qa + TP
        NKtot = (self.T if grp == 'p' else PAST + DEC_SEQ)
        KTOP = float(min(TOPK_MAX, NKtot // 4))
        W = 4 * TP
        sc = self.av('sc', 8192)
        junk = self.av('junk', 2048)
        junk8 = junk.h.bitcast(U8)
        offf_v = junk.h[0:33, 0:512]
        osb_v = junk.h[0:65, 512:1024]
        onn_v = junk.h[0:64, 1024:1536]
        qsq_v = junk.h[:, 1536:1536 + 4 * G].rearrange("p (a c) -> p a c", c=G)
        bs = self.bs
        for hh in range(4):
            pb = self.proj_fm(l, 'qq%d' % hh, 128, TP)
            S.op('act', lambda h, hh=hh: h.activation(out=self.qT[:, hh, 0:TP], in_=pb[:, 0:TP], func=AF.Copy), reads=[pb], writes=[self.qT])
            S.op('act', lambda h, hh=hh: h.activation(out=qsq_v[:, hh, 0:TP], in_=pb[:, 0:TP], func=AF.Square), reads=[pb], writes=[junk])
        for hh in range(4):
            pb = self.proj_fm(l, 'qi%d' % hh, 128, TP)
            S.op('dve', lambda h, hh=hh: h.tensor_copy(out=self.qiT[:, hh, 0:TP], in_=pb[:, 0:TP]), reads=[pb], writes=[self.qiT])
        pq = self.pbank()
        S.op('pe', lambda h: h.matmul(pq[0:33, 0:W], lhsT=cst[:, 896:929], rhs=qsq_v[:, :, 0:TP], start=True, stop=True), reads=[cst, junk], writes=[pq])
        S.op('act', lambda h: h.activation(out=offf_v[:, 0:W], in_=pq[0:33, 0:W], func=AF.Sqrt, scale=self.kmax2[:, 0:1]), reads=[pq, self.kmax2], writes=[junk])
        S.op('dve', lambda h: h.tensor_tensor(out=self.offrow[:, 0:W].rearrange("p (a c) -> p a c", c=TP), in0=self.cf[:, :].unsqueeze(2).to_broadcast([33, 4, TP]),
                                               in1=offf_v[:, 0:W].rearrange("p (a c) -> p a c", c=TP), op=ALU.subtract), reads=[self.cf, junk], writes=[self.offrow])
        for b0 in range(0, L, 512):
            bw = min(512, L - b0)
            if b0 < self.NH:
                rows = slice(0, 64); c0 = b0
            else:
                rows = slice(64, 128); c0 = b0 - self.NH
            for hh in range(4):
                ps = self.pbank()
                S.op('pe', lambda h, hh=hh: h.matmul(ps[0:TP, 0:bw], lhsT=self.qiT[rows, hh, 0:TP], rhs=self.kiT[rows, c0:c0 + bw], start=True, stop=True), reads=[self.qiT, self.kiT], writes=[ps])
                if hh == 0:
                    S.op('dve', lambda h: h.tensor_scalar(out=sc[0:TP, b0:b0 + bw], in0=ps[0:TP, 0:bw], scalar1=0.0, scalar2=self.wiT[0:TP, 0:1], op0=ALU.max, op1=ALU.mult), reads=[ps, self.wiT], writes=[sc])
                else:
                    tmp = junk.h[:, 512 * (hh % 2):512 * (hh % 2) + 512]
                    S.op('act', lambda h: h.activation(out=tmp[0:TP, 0:bw], in_=ps[0:TP, 0:bw], func=AF.Relu), reads=[ps], writes=[junk])
                    S.op('dve', lambda h, hh=hh: h.scalar_tensor_tensor(out=sc[0:TP, b0:b0 + bw], in0=tmp[0:TP, 0:bw], scalar=self.wiT[0:TP, hh:hh + 1], in1=sc[0:TP, b0:b0 + bw], op0=ALU.mult, op1=ALU.add),
                         reads=[junk, self.wiT, sc], writes=[sc])
        S.op('dve', lambda h: h.tensor_reduce(out=bs[0:TP, 0:1], in_=sc[0:TP, 0:L], axis=AX.X, op=ALU.max, apply_absolute_value=True), reads=[sc], writes=[bs])
        if grp == 'p' and TP == 128:
            S.op('dve', lambda h: h.memset(sc[0:64, L - 64:L], -1e30), writes=[sc])
        S.op('dve', lambda h: h.tensor_scalar(out=bs[0:TP, 1:2], in0=bs[0:TP, 0:1], scalar1=1.0, scalar2=-1.0, op0=ALU.add, op1=ALU.mult), reads=[bs], writes=[bs])
        S.op('dve', lambda h: h.tensor_scalar(out=bs[0:TP, 2:3], in0=bs[0:TP, 0:1], scalar1=1.0, scalar2=2.0, op0=ALU.add, op1=ALU.mult), reads=[bs], writes=[bs])
        for it in range(20):
            hw = float(2.0 ** -(it + 1))
            S.op('dve', lambda h: h.scalar_tensor_tensor(out=bs[0:TP, 3:4], in0=bs[0:TP, 2:3], scalar=hw, in1=bs[0:TP, 1:2], op0=ALU.mult, op1=ALU.add), reads=[bs], writes=[bs])
            S.op('dve', lambda h: h.tensor_scalar(out=junk8[0:TP, 0:L], in0=sc[0:TP, 0:L], scalar1=bs[0:TP, 3:4], scalar2=None, op0=ALU.is_ge, op1=ALU.add, accum_out=bs[0:TP, 4:5]),
                 reads=[sc, bs], writes=[junk, bs])
            S.op('dve', lambda h: h.tensor_scalar(out=bs[0:TP, 5:6], in0=bs[0:TP, 4:5], scalar1=KTOP, scalar2=hw, op0=ALU.is_ge, op1=ALU.mult), reads=[bs], writes=[bs])
            S.op('dve', lambda h: h.scalar_tensor_tensor(out=bs[0:TP, 1:2], in0=bs[0:TP, 5:6], scalar=bs[0:TP, 2:3], in1=bs[0:TP, 1:2], op0=ALU.mult, op1=ALU.add), reads=[bs], writes=[bs])
        S.op('dve', lambda h: h.tensor_scalar(out=sc[0:TP, 0:L], in0=sc[0:TP, 0:L], scalar1=bs[0:TP, 1:2], scalar2=None, op0=ALU.is_ge), reads=[sc, bs], writes=[sc])
        nblk = (L + 127) // 128
        OT = self.pacc
        for kb in range(nblk):
            bw = min(128, L - kb * 128)
            k0 = kb * 128
            pst = self.pbank()
            S.op('pe', lambda h: h.transpose(pst[0:bw, 0:TP], sc[0:TP, k0:k0 + bw], self.ident[0:TP, 0:TP]), reads=[sc, cst], writes=[pst])
            sT = self.selT[kb % 2]
            S.op('act', lambda h: h.activation(out=sT[0:bw, 0:TP], in_=pst[0:bw, 0:TP], func=AF.Copy), reads=[pst], writes=[sT])
            kind = 0 if k0 == Qa else (1 if k0 == Qa - 128 else None)
            for g in range(2):
                ps = self.pbank()
                S.op('pe', lambda h, g=g: h.matmul(ps[0:bw, 0:W], lhsT=self.kT[64 * g:64 * g + 64, k0:k0 + bw], rhs=self.qT[64 * g:64 * g + 64, :, 0:TP], start=True, stop=False),
                     reads=[self.kT, self.qT], writes=[ps])
                S.op('pe', lambda h, g=g: h.matmul(ps[0:bw, 0:W], lhsT=self.onesb[32 * g:32 * g + 1, 0:bw], rhs=self.offrow[32 * g:32 * g + 1, 0:W], start=False, stop=True),
                     reads=[self.onesb, self.offrow], writes=[ps])
                Et = self.Et[g]
                Pt = self.Pt[g]
                if kind is not None:
                    tmp = junk.h[:, 1536:2048]
                    bt = self.bias8[0:bw, 2 * g + kind, :].rearrange("p (a c) -> p a c", c=128)[:, :, 0:TP]
                    S.op('dve', lambda h: h.tensor_tensor(out=tmp[0:bw, 0:W].rearrange("p (a c) -> p a c", c=TP), in0=ps[0:bw, 0:W].rearrange("p (a c) -> p a c", c=TP), in1=bt, op=ALU.add),
                         reads=[ps, self.bias8], writes=[junk])
                    S.op('act', lambda h: h.activation(out=Et[0:bw, 0:W], in_=tmp[0:bw, 0:W], func=AF.Exp, scale=0.125), reads=[junk], writes=[Et])
                else:
                    S.op('act', lambda h: h.activation(out=Et[0:bw, 0:W], in_=ps[0:bw, 0:W], func=AF.Exp, scale=0.125), reads=[ps], writes=[Et])
                S.op('dve', lambda h: h.tensor_tensor(out=Pt[0:bw, 0:W].rearrange("p (a c) -> p a c", c=TP), in0=Et[0:bw, 0:W].rearrange("p (a c) -> p a c", c=TP),
                                                       in1=sT[0:bw, 0:TP].unsqueeze(1).to_broadcast([bw, 4, TP]), op=ALU.mult), reads=[Et, sT], writes=[Pt])
                S.op('pe', lambda h, g=g: h.matmul(OT[g][0:65, 0:W], lhsT=self.vA[0:bw, kb, 65 * g:65 * g + 65], rhs=Pt[0:bw, 0:W], start=(kb == 0), stop=(kb == nblk - 1)),
                     reads=[self.vA, Pt], writes=[OT[g]])
        for g in range(2):
            osb = osb_v
            S.op('act', lambda h: h.activation(out=osb[0:65, 0:W], in_=OT[g][0:65, 0:W], func=AF.Copy), reads=[OT[g]], writes=[junk])
            S.op('dve', lambda h: h.reciprocal(out=osb[64:65, 0:W], in_=osb[64:65, 0:W]), reads=[junk], writes=[junk])
            pbc = self.pbank()
            S.op('pe', lambda h: h.matmul(pbc[0:64, 0:W], lhsT=cst[64:65, 256:320], rhs=osb[64:65, 0:W], start=True, stop=True), reads=[cst, junk], writes=[pbc])
            S.op('dve', lambda h: h.tensor_tensor(out=onn_v[:, 0:W], in0=osb[0:64, 0:W], in1=pbc[0:64, 0:W], op=ALU.mult), reads=[junk, pbc], writes=[junk])
            for hh in range(4):
                hd = 4 * g + hh
                pg = self.proj_fm(l, 'bg%d' % hd, 64, TP)
                sg = junk.h[:, 1536 + 128 * (hh % 2):1536 + 128 * (hh % 2) + 128]
                S.op('act', lambda h: h.activation(out=sg[0:64, 0:TP], in_=pg[0:64, 0:TP], func=AF.Silu), reads=[pg], writes=[junk])
                S.op('dve', lambda h, hh=hh, hd=hd: h.tensor_tensor(out=self.brB[:, hd, 0:TP], in0=onn_v[:, hh * TP:(hh + 1) * TP], in1=sg[0:64, 0:TP], op=ALU.mult), reads=[junk], writes=[self.brB])

    def gdn_setup(self, l):
        S = self.S
        I = self.I
        S.dma('sp', self.gdn_cw[:, :, :], I['gcw'][l], writes=[self.gdn_cw])
        S.dma('sp', self.gdn_par[:, :], I['gpar'][l], writes=[self.gdn_par])
        S.dma('sp', self.gdn_n[:, :], I['gnorm'][l], writes=[self.gdn_n])
        S.op('act', lambda h: h.activation(out=self.gdn_par[:, 0:4], in_=self.gdn_par[:, 0:4], func=AF.Exp), reads=[self.gdn_par], writes=[self.gdn_par])
        S.op('dve', lambda h: h.tensor_scalar(out=self.gdn_par[:, 0:4], in0=self.gdn_par[:, 0:4], scalar1=-1.0, scalar2=None, op0=ALU.mult), reads=[self.gdn_par], writes=[self.gdn_par])

    def gdn_init(self, l, grp):
        S = self.S
        if grp == 'p':
            S.op('dve', lambda h: h.memset(self.gdn_S[:, :, :], 0.0), writes=[self.gdn_S])
            S.op('dve', lambda h: h.memset(self.gdn_tail[:, :, :], 0.0), writes=[self.gdn_tail])
        else:
            S.dma('sp', self.gdn_S[:, :, :], self.I['sgdn'][l].rearrange("h k v -> k h v"), writes=[self.gdn_S])
            S.dma('sp', self.gdn_tail[:, :, :], self.I['sconv'][l], writes=[self.gdn_tail])

    def gdn_part(self, l, grp, g0, G, C, last):
        S = self.S
        cst = self.cst
        nC = G // C
        GE = G + 3
        xext = self.av('xext', 12 * GE)
        x3 = xext[:, 0:12 * GE].rearrange("p (a c) -> p a c", c=GE)
        qkv = self.av('qkv', 12 * G)
        q3 = qkv[:, 0:12 * G].rearrange("p (a c) -> p a c", c=G)
        oT = self.av('oT', 4 * G)
        o3 = oT[:, 0:4 * G].rearrange("p (a c) -> p a c", c=G)
        T1 = self.av('T1', 1024)
        T2 = self.av('T2', 1024)
        S.op('dve', lambda h: h.tensor_copy(out=x3[:, :, 0:3], in_=self.gdn_tail[:, :, :]), reads=[self.gdn_tail], writes=[xext])
        for c in range(12):
            pb = self.proj_fm(l, 'cq%d' % c, 128, G)
            S.op('act' if c % 2 else 'dve', (lambda h, c=c: h.activation(out=x3[:, c, 3:3 + G], in_=pb[:, 0:G], func=AF.Copy)) if c % 2 else (lambda h, c=c: h.tensor_copy(out=x3[:, c, 3:3 + G], in_=pb[:, 0:G])),
                 reads=[pb], writes=[xext])
        S.op('dve', lambda h: h.tensor_copy(out=self.gdn_tail[:, :, :], in_=x3[:, :, G:G + 3]), reads=[xext], writes=[self.gdn_tail])
        for c in range(12):
            S.op('dve', lambda h, c=c: h.tensor_scalar(out=q3[:, c, :], in0=x3[:, c, 0:G], scalar1=self.gdn_cw[:, c, 0:1], scalar2=None, op0=ALU.mult), reads=[xext, self.gdn_cw], writes=[qkv])
            for i in range(1, 4):
                S.op('dve', lambda h, c=c, i=i: h.scalar_tensor_tensor(out=q3[:, c, :], in0=x3[:, c, i:i + G], scalar=self.gdn_cw[:, c, i:i + 1], in1=q3[:, c, :], op0=ALU.mult, op1=ALU.add),
                     reads=[xext, self.gdn_cw, qkv], writes=[qkv])
        S.op('act', lambda h: h.activation(out=qkv[:, 0:12 * G], in_=qkv[:, 0:12 * G], func=AF.Silu), reads=[qkv], writes=[qkv])
        W8 = 8 * G
        S.op('act', lambda h: h.activation(out=T1[:, 0:W8], in_=qkv[:, 0:W8], func=AF.Square), reads=[qkv], writes=[T1])
        for b0 in range(0, W8, 512):
            bw = min(512, W8 - b0)
            pm = self.pbank()
            S.op('pe', lambda h: h.matmul(pm[:, 0:bw], lhsT=cst[:, 256:384], rhs=T1[:, b0:b0 + bw], start=True, stop=True), reads=[cst, T1], writes=[pm])
            S.op('act', lambda h: h.activation(out=T2[:, b0:b0 + bw], in_=pm[:, 0:bw], func=AF.Sqrt, bias=self.epsb[:, 1:2]), reads=[pm, self.epsb], writes=[T2])
        S.op('dve', lambda h: h.reciprocal(out=T2[:, 0:W8], in_=T2[:, 0:W8]), reads=[T2], writes=[T2])
        S.op('dve', lambda h: h.scalar_tensor_tensor(out=qkv[:, 0:4 * G], in0=qkv[:, 0:4 * G], scalar=float(128 ** -0.5), in1=T2[:, 0:4 * G], op0=ALU.mult, op1=ALU.mult), reads=[qkv, T2], writes=[qkv])
        S.op('dve', lambda h: h.tensor_tensor(out=qkv[:, 4 * G:8 * G], in0=qkv[:, 4 * G:8 * G], in1=T2[:, 4 * G:8 * G], op=ALU.mult), reads=[qkv, T2], writes=[qkv])
        wC = self.load_w(l, 'kvD')
        CC = 4 * C
        mU = cst[0:C, 128:128 + C].unsqueeze(1).to_broadcast([C, 4, C])
        mLs = cst[0:C, 768:768 + C].unsqueeze(1).to_broadcast([C, 4, C])
        K = {64: 5, 32: 4, 16: 3}[C]
        V = lambda key: self.av(key, 256)
        for ci in range(nC):
            c0 = ci * C
            pb = self.pbank()
            for k in range(8):
                S.op('pe', lambda h, k=k: h.matmul(pb[0:C, 0:12], lhsT=self.xT[:, k, c0:c0 + C], rhs=wC[:, k, 0:12], start=(k == 0), stop=(k == 7)), reads=[self.xT, wC], writes=[pb])
            bg = self.av('bg', 32)
            S.op('act', lambda h: h.activation(out=bg[0:C, 0:4], in_=pb[0:C, 4:8], func=AF.Sigmoid), reads=[pb], writes=[bg])
            S.op('dve', lambda h: h.tensor_tensor(out=bg[0:C, 4:8], in0=pb[0:C, 8:12], in1=self.gdn_par[0:C, 4:8], op=ALU.add), reads=[pb, self.gdn_par], writes=[bg])
            S.op('act', lambda h: h.activation(out=bg[0:C, 4:8], in_=bg[0:C, 4:8], func=AF.Exp), reads=[bg], writes=[bg])
            S.op('act', lambda h: h.activation(out=bg[0:C, 4:8], in_=bg[0:C, 4:8], func=AF.Ln, bias=1.0), reads=[bg], writes=[bg])
            S.op('dve', lambda h: h.tensor_tensor(out=bg[0:C, 8:12], in0=bg[0:C, 4:8], in1=self.gdn_par[0:C, 0:4], op=ALU.mult), reads=[bg, self.gdn_par], writes=[bg])
            pcol = self.pbank()
            S.op('pe', lambda h: h.matmul(pcol[0:C, 0:4], lhsT=cst[0:C, 128:128 + C], rhs=bg[0:C, 8:12], start=True, stop=True), reads=[cst, bg], writes=[pcol])
            R = V('R')
            R3 = R[0:C, 0:CC].rearrange("p (a c) -> p a c", c=C)
            for h4 in range(4):
                S.op('dve', lambda h, h4=h4: h.tensor_scalar(out=R3[:, h4, :], in0=cst[0:C, 128:128 + C], scalar1=bg[0:C, 8 + h4:9 + h4], scalar2=None, op0=ALU.mult), reads=[cst, bg], writes=[R])
            prow = self.pbank()
            S.op('pe', lambda h: h.matmul(prow[:, 0:CC], lhsT=cst[0:C, 256:384], rhs=R[0:C, 0:CC], start=True, stop=True), reads=[cst, R], writes=[prow])
            egrow = V('egrow')
            S.op('act', lambda h: h.activation(out=egrow[:, 0:CC], in_=prow[:, 0:CC], func=AF.Exp), reads=[prow], writes=[egrow])
            eg3 = egrow[:, 0:CC].rearrange("p (a c) -> p a c", c=C)
            S.op('dve', lambda h: h.tensor_copy(out=bg[0:C, 12:16], in_=pcol[0:C, 0:4]), reads=[pcol], writes=[bg])
            S.op('act', lambda h: h.activation(out=bg[0:C, 16:20], in_=pcol[0:C, 0:4], func=AF.Exp), reads=[pcol], writes=[bg])
            S.op('dve', lambda h: h.tensor_tensor(out=bg[0:C, 20:24], in0=bg[0:C, 16:20], in1=bg[0:C, 0:4], op=ALU.mult), reads=[bg], writes=[bg])
            Dm = V('Dm')
            D3 = Dm[0:C, 0:CC].rearrange("p (a c) -> p a c", c=C)
            p3 = prow[0:C, 0:CC].rearrange("p (a c) -> p a c", c=C)
            for h4 in range(4):
                S.op('dve', lambda h, h4=h4: h.tensor_scalar(out=D3[:, h4, :], in0=p3[:, h4, :], scalar1=bg[0:C, 12 + h4:13 + h4], scalar2=-1.0, op0=ALU.subtract, op1=ALU.mult), reads=[prow, bg], writes=[Dm])
            Elo = V('Elo')
            Eup = V('Eup')
            S.op('dve', lambda h: h.tensor_scalar(out=Elo[0:C, 0:CC], in0=Dm[0:C, 0:CC], scalar1=0.0, scalar2=None, op0=ALU.min), reads=[Dm], writes=[Elo])
            S.op('act', lambda h: h.activation(out=Elo[0:C, 0:CC], in_=Elo[0:C, 0:CC], func=AF.Exp), reads=[Elo], writes=[Elo])
            S.op('dve', lambda h: h.tensor_scalar(out=Eup[0:C, 0:CC], in0=Dm[0:C, 0:CC], scalar1=-1.0, scalar2=0.0, op0=ALU.mult, op1=ALU.min), reads=[Dm], writes=[Eup])
            S.op('act', lambda h: h.activation(out=Eup[0:C, 0:CC], in_=Eup[0:C, 0:CC], func=AF.Exp), reads=[Eup], writes=[Eup])
            El3 = Elo[0:C, 0:CC].rearrange("p (a c) -> p a c", c=C)
            Eu3 = Eup[0:C, 0:CC].rearrange("p (a c) -> p a c", c=C)
            S.op('dve', lambda h: h.tensor_copy(out=bg[0:C, 24:28], in_=Eu3[:, :, C - 1]), reads=[Eup], writes=[bg])
            S.op('dve', lambda h: h.tensor_tensor(out=El3, in0=El3, in1=mLs, op=ALU.mult), reads=[Elo, cst], writes=[Elo])
            S.op('dve', lambda h: h.tensor_tensor(out=Eu3, in0=Eu3, in1=mU, op=ALU.mult), reads=[Eup, cst], writes=[Eup])
            for h4 in range(4):
                S.op('dve', lambda h, h4=h4: h.tensor_scalar(out=El3[:, h4, :], in0=El3[:, h4, :], scalar1=bg[0:C, h4:h4 + 1], scalar2=None, op0=ALU.mult), reads=[Elo, bg], writes=[Elo])
            pkk = self.pbank()
            pmt = self.pbank()
            for h4 in range(4):
                S.op('pe', lambda h, h4=h4: h.matmul(pkk[0:C, h4 * C:(h4 + 1) * C], lhsT=q3[:, 4 + h4, c0:c0 + C], rhs=q3[:, 4 + h4, c0:c0 + C], start=True, stop=True), reads=[qkv], writes=[pkk])
                S.op('pe', lambda h, h4=h4: h.matmul(pmt[0:C, h4 * C:(h4 + 1) * C], lhsT=q3[:, 4 + h4, c0:c0 + C], rhs=q3[:, h4, c0:c0 + C], start=True, stop=True), reads=[qkv], writes=[pmt])
            P = [V('P0')]
            AT = V('AT')
            MT = V('MT')
            S.op('dve', lambda h: h.tensor_tensor(out=P[0][0:C, 0:CC], in0=pkk[0:C, 0:CC], in1=Elo[0:C, 0:CC], op=ALU.mult), reads=[pkk, Elo], writes=[P[0]])
            S.op('dve', lambda h: h.tensor_tensor(out=MT[0:C, 0:CC], in0=pmt[0:C, 0:CC], in1=Eup[0:C, 0:CC], op=ALU.mult), reads=[pmt, Eup], writes=[MT])
            pat = self.pbank()
            for h4 in range(4):
                S.op('pe', lambda h, h4=h4: h.transpose(pat[0:C, h4 * C:(h4 + 1) * C], P[0][0:C, h4 * C:(h4 + 1) * C], self.ident[0:C, 0:C]), reads=[P[0], cst], writes=[pat])
            S.op('act', lambda h: h.activation(out=AT[0:C, 0:CC], in_=pat[0:C, 0:CC], func=AF.Copy), reads=[pat], writes=[AT])
            X = T1
            X3 = X[0:C, 0:1024].rearrange("p (a c) -> p a c", c=256)
            Kdec = T2
            Kd3 = Kdec[0:C, 0:512].rearrange("p (a c) -> p a c", c=128)
            vn = T2
            vn3 = vn[0:C, 512:1024].rearrange("p (a c) -> p a c", c=128)
            pkt = self.pbank()
            pvt = self.pbank()
            for h4 in range(4):
                S.op('pe', lambda h, h4=h4: h.transpose(pkt[0:C, h4 * 128:(h4 + 1) * 128], q3[:, 4 + h4, c0:c0 + C], self.ident[:, 0:128]), reads=[qkv, cst], writes=[pkt])
                S.op('pe', lambda h, h4=h4: h.transpose(pvt[0:C, h4 * 128:(h4 + 1) * 128], q3[:, 8 + h4, c0:c0 + C], self.ident[:, 0:128]), reads=[qkv, cst], writes=[pvt])
            for h4 in range(4):
                S.op('dve', lambda h, h4=h4: h.tensor_scalar(out=X3[:, h4, 0:128], in0=pkt[0:C, h4 * 128:(h4 + 1) * 128], scalar1=bg[0:C, 20 + h4:21 + h4], scalar2=None, op0=ALU.mult), reads=[pkt, bg], writes=[X])
                S.op('dve', lambda h, h4=h4: h.tensor_scalar(out=X3[:, h4, 128:256], in0=pvt[0:C, h4 * 128:(h4 + 1) * 128], scalar1=bg[0:C, h4:h4 + 1], scalar2=None, op0=ALU.mult), reads=[pvt, bg], writes=[X])
                S.op('dve', lambda h, h4=h4: h.tensor_scalar(out=Kd3[:, h4, :], in0=pkt[0:C, h4 * 128:(h4 + 1) * 128], scalar1=bg[0:C, 24 + h4:25 + h4], scalar2=None, op0=ALU.mult), reads=[pkt, bg], writes=[Kdec])
            PTs = [AT]
            Ps = [P[0]]
            for k in range(1, K + 1):
                pk_ = self.pbank()
                pt_ = self.pbank()
                Pp, PTp = Ps[-1], PTs[-1]
                for h4 in range(4):
                    sl = slice(h4 * C, (h4 + 1) * C)
                    if k < K:
                        S.op('pe', lambda h, sl=sl: h.matmul(pk_[0:C, sl], lhsT=PTp[0:C, sl], rhs=Pp[0:C, sl], start=True, stop=True), reads=[PTp, Pp], writes=[pk_])
                    S.op('pe', lambda h, sl=sl: h.matmul(pt_[0:C, sl], lhsT=Pp[0:C, sl], rhs=PTp[0:C, sl], start=True, stop=True), reads=[PTp, Pp], writes=[pt_])
                nPT = V('PTk%d' % k)
                S.op('act', lambda h: h.activation(out=nPT[0:C, 0:CC], in_=pt_[0:C, 0:CC], func=AF.Copy), reads=[pt_], writes=[nPT])
                PTs.append(nPT)
                if k < K:
                    nP = V('Pk%d' % k)
                    S.op('dve', lambda h: h.tensor_copy(out=nP[0:C, 0:CC], in_=pk_[0:C, 0:CC]), reads=[pk_], writes=[nP])
                    Ps.append(nP)
            for k in range(K, -1, -1):
                PTk = PTs[k]
                for half in range(2):
                    px = self.pbank()
                    for hh in range(2):
                        h4 = 2 * half + hh
                        S.op('pe', lambda h, h4=h4, hh=hh: h.matmul(px[0:C, hh * 256:(hh + 1) * 256], lhsT=PTk[0:C, h4 * C:(h4 + 1) * C], rhs=X3[:, h4, :], start=True, stop=True), reads=[PTk, X], writes=[px])
                    xs = X[0:C, half * 512:(half + 1) * 512]
                    S.op('dve', lambda h: h.tensor_tensor(out=xs, in0=xs, in1=px[0:C, 0:512], op=(ALU.subtract if k == 0 else ALU.add)), reads=[X, px], writes=[X])
            pwt = self.pbank()
            for h4 in range(4):
                S.op('pe', lambda h, h4=h4: h.transpose(pwt[:, h4 * C:(h4 + 1) * C], X3[:, h4, 0:128], self.ident[0:C, 0:C]), reads=[X, cst], writes=[pwt])
            WT = Dm
            S.op('act', lambda h: h.activation(out=WT[:, 0:CC], in_=pwt[:, 0:CC], func=AF.Copy), reads=[pwt], writes=[WT])
            pws = self.pbank()
            for h4 in range(4):
                S.op('pe', lambda h, h4=h4: h.matmul(pws[0:C, h4 * 128:(h4 + 1) * 128], lhsT=WT[:, h4 * C:(h4 + 1) * C], rhs=self.gdn_S[:, h4, :], start=True, stop=True), reads=[WT, self.gdn_S], writes=[pws])
            S.op('dve', lambda h: h.tensor_tensor(out=vn3, in0=X3[:, :, 128:256], in1=pws[0:C, 0:512].rearrange("p (a c) -> p a c", c=128), op=ALU.subtract), reads=[X, pws], writes=[vn])
            QeT = R
            Qe3 = QeT[:, 0:CC].rearrange("p (a c) -> p a c", c=C)
            S.op('dve', lambda h: h.tensor_tensor(out=Qe3, in0=q3[:, 0:4, c0:c0 + C], in1=eg3, op=ALU.mult), reads=[qkv, egrow], writes=[QeT])
            po = self.pbank()
            for h4 in range(4):
                sl = slice(h4 * C, (h4 + 1) * C)
                S.op('pe', lambda h, h4=h4, sl=sl: h.matmul(po[:, sl], lhsT=self.gdn_S[:, h4, :], rhs=QeT[:, sl], start=True, stop=False), reads=[self.gdn_S, QeT], writes=[po])
                S.op('pe', lambda h, h4=h4, sl=sl: h.matmul(po[:, sl], lhsT=vn3[:, h4, :], rhs=MT[0:C, sl], start=False, stop=True), reads=[vn, MT], writes=[po])
            S.op('act', lambda h: h.activation(out=o3[:, :, c0:c0 + C], in_=po[:, 0:CC].rearrange("p (a c) -> p a c", c=C), func=AF.Copy), reads=[po], writes=[oT])
            pu = self.pbank()
            for h4 in range(4):
                S.op('pe', lambda h, h4=h4: h.matmul(pu[:, h4 * 128:(h4 + 1) * 128], lhsT=Kd3[:, h4, :], rhs=vn3[:, h4, :], start=True, stop=True), reads=[Kdec, vn], writes=[pu])
            for h4 in range(4):
                S.op('dve', lambda h, h4=h4: h.scalar_tensor_tensor(out=self.gdn_S[:, h4, :], in0=self.gdn_S[:, h4, :], scalar=eg3[:, h4, C - 1:C], in1=pu[:, h4 * 128:(h4 + 1) * 128], op0=ALU.mult, op1=ALU.add),
                     reads=[self.gdn_S, egrow, pu], writes=[self.gdn_S])
        if last:
            S.dma('pool', self.O['gdn' + grp][l].rearrange("h k v -> k h v"), self.gdn_S[:, :, :], reads=[self.gdn_S], writes=[self.dk('ogdn' + grp)])
        W4 = 4 * G
        S.op('act', lambda h: h.activation(out=T1[:, 0:W4], in_=oT[:, 0:W4], func=AF.Square), reads=[oT], writes=[T1])
        pm = self.pbank()
        S.op('pe', lambda h: h.matmul(pm[:, 0:W4], lhsT=cst[:, 384:512], rhs=T1[:, 0:W4], start=True, stop=True), reads=[cst, T1], writes=[pm])
        S.op('act', lambda h: h.activation(out=T2[:, 0:W4], in_=pm[:, 0:W4], func=AF.Sqrt, bias=self.epsb[:, 1:2]), reads=[pm, self.epsb], writes=[T2])
        S.op('dve', lambda h: h.reciprocal(out=T2[:, 0:W4], in_=T2[:, 0:W4]), reads=[T2], writes=[T2])
        S.op('dve', lambda h: h.scalar_tensor_tensor(out=oT[:, 0:W4], in0=oT[:, 0:W4], scalar=self.gdn_n[:, 0:1], in1=T2[:, 0:W4], op0=ALU.mult, op1=ALU.mult), reads=[oT, T2, self.gdn_n], writes=[oT])
        for c4 in range(4):
            pgt = self.proj_fm(l, 'cg%d' % c4, 128, G)
            sg = V('R') if c4 % 2 else V('Dm')
            S.op('act', lambda h: h.activation(out=sg[:, 0:G], in_=pgt[:, 0:G], func=AF.Silu), reads=[pgt], writes=[sg])
            S.op('dve', lambda h, c4=c4: h.tensor_tensor(out=self.brT[:, 8 + c4, 0:G], in0=o3[:, c4, :], in1=sg[:, 0:G], op=ALU.mult), reads=[oT, sg], writes=[self.brT])

    def gla_part(self, l, grp, g0, G, C, last):
        S = self.S
        cst = self.cst
        nC = G // C
        qT = self.wsp(True)
        kT = self.wsp()
        W4 = 4 * G
        q3 = qT[0:64, 0:W4].rearrange("p (a c) -> p a c", c=G)
        k3 = kT[0:64, 0:W4].rearrange("p (a c) -> p a c", c=G)
        for h4 in range(4):
            pb = self.proj_fm(l, 'dq%d' % h4, 64, G)
            S.op('act', lambda h, h4=h4: h.activation(out=q3[:, h4, :], in_=pb[0:64, 0:G], func=AF.Copy, scale=0.125), reads=[pb], writes=[qT])
            pb2 = self.proj_fm(l, 'dk%d' % h4, 64, G)
            S.op('dve', lambda h, h4=h4: h.tensor_copy(out=k3[:, h4, :], in_=pb2[0:64, 0:G]), reads=[pb2], writes=[kT])
        pg = self.proj_fm(l, 'dg', 16, G)
        dgT = self.sm()
        dg_sb = self.scratch()
        S.op('dve', lambda h: h.tensor_copy(out=dg_sb[0:16, 0:G], in_=pg[0:16, 0:G]), reads=[pg], writes=[dg_sb])
        spT = self.wsp()
        sp3 = spT[0:64, 0:W4].rearrange("p (a c) -> p a c", c=G)
        nb = self.sm()
        S.op('dve', lambda h: h.tensor_scalar(out=nb[0:64, 0:4], in0=self.gla_b[:, :], scalar1=-1.0, scalar2=None, op0=ALU.mult), reads=[self.gla_b], writes=[nb])
        for h4 in range(4):
            pl = self.pbank()
            S.op('pe', lambda h, h4=h4: h.matmul(pl[0:64, 0:G], lhsT=self.gla_w[:, h4 * 64:(h4 + 1) * 64], rhs=dg_sb[0:16, 0:G], start=True, stop=True),
                 reads=[self.gla_w, dg_sb], writes=[pl])
            S.op('act', lambda h, h4=h4: h.activation(out=sp3[:, h4, :], in_=pl[0:64, 0:G], func=AF.Exp, scale=-1.0, bias=nb[0:64, h4:h4 + 1]),
                 reads=[pl, nb], writes=[spT])
        S.op('act', lambda h: h.activation(out=spT[0:64, 0:W4], in_=spT[0:64, 0:W4], func=AF.Ln, bias=1.0), reads=[spT], writes=[spT])
        csT = self.wsp()
        S.op('dve', lambda h: h.tensor_tensor_scan(out=csT[0:64, 0:W4], data0=self.resetm[:, 0:W4], data1=spT[0:64, 0:W4], initial=0.0, op0=ALU.mult, op1=ALU.add),
             reads=[self.resetm, spT], writes=[csT])
        eq = self.wsp()
        ek = self.wsp()
        S.op('act', lambda h: h.activation(out=eq[0:64, 0:W4], in_=csT[0:64, 0:W4], func=AF.Exp, scale=-1.0 / 16.0), reads=[csT], writes=[eq])
        S.op('act', lambda h: h.activation(out=ek[0:64, 0:W4], in_=csT[0:64, 0:W4], func=AF.Exp, scale=1.0 / 16.0), reads=[csT], writes=[ek])
        S.op('dve', lambda h: h.tensor_tensor(out=qT[0:64, 0:W4], in0=qT[0:64, 0:W4], in1=eq[0:64, 0:W4], op=ALU.mult), reads=[qT, eq], writes=[qT])
        S.op('dve', lambda h: h.tensor_tensor(out=kT[0:64, 0:W4], in0=kT[0:64, 0:W4], in1=ek[0:64, 0:W4], op=ALU.mult), reads=[kT, ek], writes=[kT])
        eq3 = eq[0:64, 0:W4].rearrange("p (a c) -> p a c", c=G)
        oT = self.wsp()
        o3 = oT[:, 0:W4].rearrange("p (a c) -> p a c", c=G)
        for ci in range(nC):
            c0 = ci * C
            pv = self.pbank()
            for c4 in range(4):
                w, xTw = self.load_wx(l, 'dv%d' % c4)
                for k in range(8):
                    S.op('pe', lambda h, k=k, c4=c4, w=w: h.matmul(pv[0:C, c4 * 128:(c4 + 1) * 128], lhsT=xTw[:, k, c0:c0 + C], rhs=w[:, k, :],
                                                               start=(k == 0), stop=(k == 7)), reads=[w, xTw], writes=[pv])
            vt = self.scratch()
            S.op('act', lambda h: h.activation(out=vt[0:C, 0:512], in_=pv[0:C, 0:512], func=AF.Copy), reads=[pv], writes=[vt])
            pk = self.pbank()
            for h4 in range(4):
                S.op('pe', lambda h, h4=h4: h.transpose(pk[0:C, h4 * 64:(h4 + 1) * 64], k3[:, h4, c0:c0 + C], self.ident[0:64, 0:64]),
                     reads=[kT, cst], writes=[pk])
            kt = self.scratch()
            S.op('dve', lambda h: h.tensor_copy(out=kt[0:C, 0:256], in_=pk[0:C, 0:256]), reads=[pk], writes=[kt])
            pa = self.pbank()
            for h4 in range(4):
                S.op('pe', lambda h, h4=h4: h.matmul(pa[0:C, h4 * C:(h4 + 1) * C], lhsT=k3[:, h4, c0:c0 + C], rhs=q3[:, h4, c0:c0 + C], start=True, stop=True),
                     reads=[kT, qT], writes=[pa])
            at = self.scratch()
            pa3 = pa[0:C, 0:4 * C].rearrange("p (a c) -> p a c", c=C)
            at3 = at[0:C, 0:4 * C].rearrange("p (a c) -> p a c", c=C)
            msk = cst[0:C, 128:128 + C].unsqueeze(1).to_broadcast([C, 4, C])
            S.op('dve', lambda h: h.tensor_tensor(out=at3, in0=pa3, in1=msk, op=ALU.mult), reads=[pa, cst], writes=[at])
            po = self.pbank()
            for h4 in range(4):
                S.op('pe', lambda h, h4=h4: h.matmul(po[:, h4 * C:(h4 + 1) * C], lhsT=vt[0:C, h4 * 128:(h4 + 1) * 128], rhs=at3[:, h4, :], start=True, stop=False),
                     reads=[vt, at], writes=[po])
                S.op('pe', lambda h, h4=h4: h.matmul(po[:, h4 * C:(h4 + 1) * C], lhsT=self.gla_S[:, h4, :], rhs=q3[:, h4, c0:c0 + C], start=False, stop=True),
                     reads=[self.gla_S, qT], writes=[po])
            S.op('act', lambda h: h.activation(out=o3[:, :, c0:c0 + C], in_=po[:, 0:4 * C].rearrange("p (a c) -> p a c", c=C), func=AF.Copy), reads=[po], writes=[oT])
            pu = self.pbank()
            for h4 in range(4):
                S.op('pe', lambda h, h4=h4: h.matmul(pu[0:64, h4 * 128:(h4 + 1) * 128], lhsT=kt[0:C, h4 * 64:(h4 + 1) * 64], rhs=vt[0:C, h4 * 128:(h4 + 1) * 128], start=True, stop=True),
                     reads=[kt, vt], writes=[pu])
            S.op('dve', lambda h: h.tensor_tensor(out=self.gla_S[:, :, :], in0=self.gla_S[:, :, :], in1=pu[0:64, 0:512].rearrange("p (a c) -> p a c", c=128), op=ALU.add),
                 reads=[self.gla_S, pu], writes=[self.gla_S])
            for h4 in range(4):
                S.op('dve', lambda h, h4=h4: h.tensor_scalar(out=self.gla_S[:, h4, :], in0=self.gla_S[:, h4, :], scalar1=eq3[:, h4, c0 + C - 1:c0 + C], scalar2=None, op0=ALU.mult),
                     reads=[self.gla_S, eq], writes=[self.gla_S])
        if last:
            S.dma('pool', self.O['gla' + grp][l].rearrange("h k v -> k h v"), self.gla_S[:, :, :], reads=[self.gla_S], writes=[self.dk('ogla' + grp)])
        sq = spT
        S.op('act', lambda h: h.activation(out=sq[:, 0:W4], in_=oT[:, 0:W4], func=AF.Square), reads=[oT], writes=[sq])
        rst = csT
        for b0 in range(0, W4, 512):
            bw = min(512, W4 - b0)
            pm = self.pbank()
            S.op('pe', lambda h: h.matmul(pm[:, 0:bw], lhsT=cst[:, 384:512], rhs=sq[:, b0:b0 + bw], start=True, stop=True), reads=[cst, sq], writes=[pm])
            S.op('act', lambda h: h.activation(out=rst[:, b0:b0 + bw], in_=pm[:, 0:bw], func=AF.Sqrt, bias=self.epsb[:, 1:2]), reads=[pm, self.epsb], writes=[rst])
        S.op('dve', lambda h: h.reciprocal(out=rst[:, 0:W4], in_=rst[:, 0:W4]), reads=[rst], writes=[rst])
        S.op('dve', lambda h: h.scalar_tensor_tensor(out=oT[:, 0:W4], in0=oT[:, 0:W4], scalar=self.gla_n[:, 0:1], in1=rst[:, 0:W4], op0=ALU.mult, op1=ALU.mult),
             reads=[oT, rst, self.gla_n], writes=[oT])
        for c4 in range(4):
            pgt = self.proj_fm(l, 'dgt%d' % c4, 128, G)
            sg = self.scratch()
            S.op('act', lambda h: h.activation(out=sg[:, 0:G], in_=pgt[:, 0:G], func=AF.Silu), reads=[pgt], writes=[sg])
            S.op('dve', lambda h, c4=c4: h.tensor_tensor(out=self.brT[:, 12 + c4, 0:G], in0=o3[:, c4, :], in1=sg[:, 0:G], op=ALU.mult), reads=[oT, sg], writes=[self.brT])

    def out_stage(self, l, grp, g0, G, TP, xin, xin_tk, yout, yout_tk):
        S = self.S
        I = self.I
        mix = self.av('mixT', 4 * self.G)
        mix3 = mix.h.bitcast(BF16)[:, 0:8 * G].rearrange("p (a c) -> p a c", c=G)
        for dc in range(8):
            for b in range(4):
                pp = self.pbank()
                if b == 1:
                    self.wrr += 1
                    wb = self.wbrBs[self.wrr % 2]
                    S.dma('sp', wb[:, :, :], self.wbrb[l, b, dc].rearrange("(p h) f -> p (h f)", h=2).rearrange("p (a c) -> p a c", c=128), reads=[self.dk(('wbrb', l, b, dc))], writes=[wb])
                    for hh in range(8):
                        S.op('pe', lambda h, hh=hh: h.matmul(pp[:, 0:G], lhsT=wb[:, hh, :], rhs=self.brB[:, hh, 0:G], start=(hh == 0), stop=(hh == 7)),
                             reads=[wb, self.brB], writes=[pp])
                else:
                    self.wrr += 1
                    wb = self.wbrs[self.wrr % 2]
                    S.dma('sp', wb[:, :, :], self.wbrb[l, b, dc].rearrange("p (a c) -> p a c", c=128), reads=[self.dk(('wbrb', l, b, dc))], writes=[wb])
                    for kc in range(4):
                        S.op('pe', lambda h, kc=kc: h.matmul(pp[:, 0:G], lhsT=wb[:, kc, :], rhs=self.brT[:, 4 * b + kc, 0:G], start=(kc == 0), stop=(kc == 3)),
                             reads=[wb, self.brT], writes=[pp])
                pm = self.proj_fm(l, 'mg%d_%d' % (b, dc), 128, G)
                sg = self.scratch()
                S.op('act', lambda h: h.activation(out=sg[:, 0:G], in_=pm[:, 0:G], func=AF.Sigmoid), reads=[pm], writes=[sg])
                if b == 0:
                    S.op('dve', lambda h, dc=dc: h.tensor_tensor(out=mix3[:, dc, 0:G], in0=pp[:, 0:G], in1=sg[:, 0:G], op=ALU.mult), reads=[pp, sg], writes=[mix])
                else:
                    tmp = self.scratch()
                    S.op('dve', lambda h: h.tensor_tensor(out=tmp[:, 0:G], in0=pp[:, 0:G], in1=sg[:, 0:G], op=ALU.mult), reads=[pp, sg], writes=[tmp])
                    S.op('dve', lambda h, dc=dc: h.tensor_tensor(out=mix3[:, dc, 0:G], in0=mix3[:, dc, 0:G], in1=tmp[:, 0:G], op=ALU.add), reads=[mix, tmp], writes=[mix])
        for ti in range(G // TP):
            t0 = ti * TP
            xt = self.av('xtm', D)
            S.dma('sp', xt[0:TP, :], xin[g0 + t0: g0 + t0 + TP, :], reads=[xin_tk], writes=[xt])
            z = self.av('zt', D)
            for qt in range(8):
                wo = self.wouts[0]
                S.dma('sp', wo[:, :, :], self.woutb[l, qt].rearrange("p (a c) -> p a c", c=128), reads=[self.dk(('woutb', l, qt))], writes=[wo])
                pz = self.pbank()
                for k in range(8):
                    S.op('pe', lambda h, k=k: h.matmul(pz[0:TP, 0:128], lhsT=mix3[:, k, t0:t0 + TP], rhs=wo[:, k, :], start=(k == 0), stop=(k == 7)),
                         reads=[mix, wo], writes=[pz])
                S.op('dve', lambda h, qt=qt: h.scalar_tensor_tensor(out=z[0:TP, qt * 128:(qt + 1) * 128], in0=xt[0:TP, qt * 128:(qt + 1) * 128], scalar=float(DN_ALPHA),
                                                                     in1=pz[0:TP, 0:128], op0=ALU.mult, op1=ALU.add), reads=[xt, pz], writes=[z])
            st = self.sm()
            for half in range(2):
                S.op('dve', lambda h, half=half: h.bn_stats(out=st[0:TP, half * 6:(half + 1) * 6], in_=z[0:TP, half * 512:(half + 1) * 512]), reads=[z], writes=[st])
            mv = self.sm()
            S.op('dve', lambda h: h.bn_aggr(out=mv[0:TP, 0:2], in_=st[0:TP, 0:12]), reads=[st], writes=[mv])
            S.op('act', lambda h: h.activation(out=mv[0:TP, 2:3], in_=mv[0:TP, 1:2], func=AF.Sqrt, bias=self.epsb[0:TP, 0:1]), reads=[mv, self.epsb], writes=[mv])
            S.op('dve', lambda h: h.reciprocal(out=mv[0:TP, 3:4], in_=mv[0:TP, 2:3]), reads=[mv], writes=[mv])
            S.op('dve', lambda h: h.tensor_scalar(out=z[0:TP, :], in0=z[0:TP, :], scalar1=mv[0:TP, 0:1], scalar2=mv[0:TP, 3:4], op0=ALU.subtract, op1=ALU.mult),
                 reads=[z, mv], writes=[z])
            S.op('dve', lambda h: h.tensor_tensor(out=z[0:TP, :], in0=z[0:TP, :], in1=self.lng[0:TP, :], op=ALU.mult), reads=[z, self.lng], writes=[z])
            S.op('dve', lambda h: h.tensor_tensor(out=z[0:TP, :], in0=z[0:TP, :], in1=self.lnb[0:TP, :], op=ALU.add), reads=[z, self.lnb], writes=[z])
            S.dma('pool', yout[g0 + t0: g0 + t0 + TP, :], z[0:TP, :], reads=[z], writes=[yout_tk])


def _prep_consts():
    c = np.zeros((128, 1024), np.float32)
    c[:, 0:128] = np.eye(128, dtype=np.float32)
    p = np.arange(128)[:, None]
    f = np.arange(128)[None, :]
    c[:, 128:256] = (f >= p).astype(np.float32)
    c[:, 256:384] = 1.0
    c[:, 384:512] = 1.0 / 128.0
    c[:, 512:768] = np.arange(1, 257, dtype=np.float32)[None, :]
    c[:, 768:896] = (p > f).astype(np.float32)
    c[0:64, 896] = 1.0
    c[64:128, 928] = 1.0
    return c


_CACHE = {}


def kernel(**inp):
    T = inp['x_prompt'].shape[1]
    if T not in _CACHE:
        b = Builder(T)
        _CACHE[T] = b.build()
    nc = _CACHE[T]
    f = lambda a: np.ascontiguousarray(np.asarray(a, dtype=np.float32))
    w_in = f(inp['w_in'])
    win = np.zeros((DEPTH, NCH, 128, 8, 128), np.float32)
    for ci, (_, cols) in enumerate(CHUNKS):
        blk = w_in[:, :, cols]
        blk = blk.reshape(DEPTH, 8, 128, len(cols)).transpose(0, 2, 1, 3)
        win[:, ci, :, :, :len(cols)] = blk
    cst = _prep_consts()
    glab = f(inp['gla_b_g']).reshape(DEPTH, 4, 64).transpose(0, 2, 1).copy()
    glan = f(inp['gla_norm']).reshape(DEPTH, 128, 1)
    def st_layout(a):
        return np.ascontiguousarray(a.reshape(DEPTH, 16, 2, 64).transpose(0, 2, 3, 1)).reshape(DEPTH, 128, 16)
    are, aim = st_layout(f(inp['s5_a_re'])), st_layout(f(inp['s5_a_im']))
    ldt = np.ascontiguousarray(np.broadcast_to(f(inp['s5_log_dt']).reshape(DEPTH, 16, 2, 1), (DEPTH, 16, 2, 64)).transpose(0, 2, 3, 1)).reshape(DEPTH, 128, 16)
    s5p = np.ascontiguousarray(np.stack([are, aim, ldt], axis=2))
    bre, bim = f(inp['s5_b_re']), f(inp['s5_b_im'])
    cre, cim = f(inp['s5_c_re']), f(inp['s5_c_im'])
    s5b = np.zeros((DEPTH, 2, 2, 128, 4, 128), np.float32)
    s5c = np.zeros((DEPTH, 2, 128, 16, 128), np.float32)
    for c in range(4):
        for s4 in range(4):
            for g2 in range(2):
                g = 8 * c + 2 * s4 + g2
                for ri, (bb, cc) in enumerate(((bre, cre), (bim, cim))):
                    s5b[:, 0 if s4 < 3 else 1, ri, 32 * s4 + 16 * g2: 32 * s4 + 16 * g2 + 16, c, 64 * g2: 64 * g2 + 64] = bb[:, g].transpose(0, 2, 1)
                    s5c[:, ri, 64 * g2: 64 * g2 + 64, 4 * c + s4, (2 * s4 + g2) * 16:(2 * s4 + g2) * 16 + 16] = cc[:, g].transpose(0, 2, 1)
    s5d = np.ascontiguousarray(f(inp['s5_d']).reshape(DEPTH, 4, 128).transpose(0, 2, 1))
    h0r, h0i = st_layout(f(inp['state_s5_re']).transpose(1, 0, 2, 3).reshape(NCORE * DEPTH, 32, 64).reshape(NCORE, DEPTH, 32, 64)[0]) if False else (None, None)
    gcw = np.ascontiguousarray(f(inp['gdn_conv']).reshape(DEPTH, 4, 12, 128).transpose(0, 3, 2, 1))
    gpar = np.ascontiguousarray(np.broadcast_to(np.concatenate([f(inp['gdn_a_log']), f(inp['gdn_dt_bias'])], axis=1)[:, None, :], (DEPTH, 128, 8)))
    gnorm = f(inp['gdn_norm']).reshape(DEPTH, 128, 1)
    rr = 127 - np.arange(384)
    oh = (_t5_bucket(rr)[None, :] == np.arange(32)[:, None]).astype(np.float32)
    common = dict(win=win, relb=f(inp['rel_bias']), oh=oh, gcw=gcw, gpar=gpar, gnorm=gnorm, s5p=s5p, s5b=s5b, s5c=s5c, s5d=s5d, wglu=f(inp['s5_w_glu']), wbr=f(inp['w_branch']), wout=f(inp['w_out']), lng=f(inp['ln_g']), lnb=f(inp['ln_b']), cst=cst,
                  glaw=f(inp['gla_w_g2']), glab=glab, glan=glan)
    xp = f(inp['x_prompt'])
    xs = f(inp['x_sample'])
    in_maps = []
    for c in range(NCORE):
        m = dict(common)
        m['xp'] = xp[c % 2]
        m['xs'] = xs[c]
        m['sgla'] = f(inp['state_gla'])[:, c]
        m['ck'] = f(inp['cache_k'])[:, c].reshape(DEPTH, PAST, 128)
        m['cv'] = f(inp['cache_v'])[:, c].reshape(DEPTH, PAST, 128)
        cki = f(inp['cache_kidx'])[:, c]
        m['cki'] = np.ascontiguousarray(np.concatenate([cki, cki], axis=-1))
        m['sgdn'] = f(inp['state_gdn'])[:, c]
        m['sconv'] = np.ascontiguousarray(f(inp['state_gdn_conv'])[:, c].reshape(DEPTH, 3, 12, 128).transpose(0, 3, 2, 1))
        m['s5h0'] = np.ascontiguousarray(np.stack([st_layout(f(inp['state_s5_re'])[:, c]), st_layout(f(inp['state_s5_im'])[:, c])], axis=2))
        in_maps.append(m)
    res = run_bass_kernel_spmd(nc, in_maps, core_ids=list(range(NCORE))).results
    P = lambda name: np.stack([res[b][name] for b in range(2)], axis=0)
    Sm = lambda name: np.stack([res[b][name] for b in range(NCORE)], axis=0)
    yp = P('yp')
    ys = Sm('ys')

    def kvfix(a, n):
        return np.ascontiguousarray(a.transpose(1, 0, 2, 3)).reshape(DEPTH, n, a.shape[2], 2, 64)

    def st(a):
        return np.ascontiguousarray(np.moveaxis(a, 0, 1))
    zeros = lambda *s: np.zeros(s, np.float32)

    def s5fix(a, ri):
        x = a[:, :, :, ri, :].reshape(a.shape[0], DEPTH, 2, 64, 16).transpose(1, 0, 4, 2, 3)
        return np.ascontiguousarray(x).reshape(DEPTH, a.shape[0], 32, 64)
    outs = [yp, ys]
    for g, getter, n, tt in (('p', P, 2, T), ('s', Sm, NCORE, DEC_SEQ)):
        outs += [kvfix(getter('k' + g), n), kvfix(getter('v' + g), n), st(getter('ki' + g)),
                 s5fix(getter('os5' + g), 0), s5fix(getter('os5' + g), 1), st(getter('ogdn' + g)),
                 st(getter('conv' + g)), st(getter('gla' + g))]
    return tuple(outs)
```

```python
import math
from contextlib import ExitStack
import numpy as np
import concourse.bass as bass
import concourse.mybir as mybir
from concourse.bass_utils import run_bass_kernel_spmd

F32 = mybir.dt.float32
BF16 = mybir.dt.bfloat16
U8 = mybir.dt.uint8
ALU = mybir.AluOpType
AF = mybir.ActivationFunctionType
AX = mybir.AxisListType

D = 1024
SEQ = 8192
DEPTH = 2
DEC_SEQ = 16
PAST = 1024
NCORE = 8
TOPK_MAX = 256
LN_EPS = 1e-5
RMS_EPS = 1e-6
DN_ALPHA = (2 * DEPTH) ** 0.25

_LAY = (('a_u', 512), ('a_gate', 512), ('b_q', 512), ('b_k', 128), ('b_v', 128), ('b_qi', 256), ('b_ki', 64),
        ('b_wi', 4), ('b_gate', 512), ('c_qkv', 1536), ('c_beta', 4), ('c_a', 4), ('c_gate', 512),
        ('d_q', 256), ('d_k', 256), ('d_v', 512), ('d_g', 16), ('d_gate', 512), ('merge', 4096))
OFF = {}
_o = 0
for _n, _w in _LAY:
    OFF[_n] = _o
    _o += _w
IN_WIDTH = _o


def _chunks():
    ch = []
    r = lambda n, a, b: list(range(OFF[n] + a, OFF[n] + b))
    ch.append(('kvA', r('b_k', 0, 128)))
    ch.append(('kvB', r('b_v', 0, 128)))
    ch.append(('kvC', r('b_ki', 0, 64) + r('b_ki', 0, 64)))
    ch.append(('kvD', r('b_wi', 0, 4) + r('c_beta', 0, 4) + r('c_a', 0, 4)))
    for h in range(4):
        ch.append(('qq%d' % h, r('b_q', 64 * h, 64 * h + 64) + r('b_q', 64 * (h + 4), 64 * (h + 4) + 64)))
    for h in range(4):
        ch.append(('qi%d' % h, r('b_qi', 64 * h, 64 * h + 64) + r('b_qi', 64 * h, 64 * h + 64)))
    for c in range(4):
        ch.append(('au%d' % c, r('a_u', 128 * c, 128 * c + 128)))
    for c in range(12):
        ch.append(('cq%d' % c, r('c_qkv', 128 * c, 128 * c + 128)))
    for h in range(4):
        ch.append(('dq%d' % h, r('d_q', 64 * h, 64 * h + 64)))
    for h in range(4):
        ch.append(('dk%d' % h, r('d_k', 64 * h, 64 * h + 64)))
    for c in range(4):
        ch.append(('dv%d' % c, r('d_v', 128 * c, 128 * c + 128)))
    ch.append(('dg', r('d_g', 0, 16)))
    for c in range(4):
        ch.append(('ag%d' % c, r('a_gate', 128 * c, 128 * c + 128)))
    for h in range(8):
        ch.append(('bg%d' % h, r('b_gate', 64 * h, 64 * h + 64)))
    for c in range(4):
        ch.append(('cg%d' % c, r('c_gate', 128 * c, 128 * c + 128)))
    for c in range(4):
        ch.append(('dgt%d' % c, r('d_gate', 128 * c, 128 * c + 128)))
    for b in range(4):
        for c in range(8):
            ch.append(('mg%d_%d' % (b, c), r('merge', 1024 * b + 128 * c, 1024 * b + 128 * c + 128)))
    return ch


CHUNKS = _chunks()
CIDX = {n: i for i, (n, _) in enumerate(CHUNKS)}
NCH = len(CHUNKS)


def _t5_bucket(rel):
    nb = 16
    max_exact = 8
    ret = np.where(rel > 0, nb, 0)
    dist = np.abs(rel)
    distf = np.maximum(dist, 1).astype(np.float32)
    large = max_exact + (np.log(distf / max_exact) / math.log(128 / max_exact) * (nb - max_exact)).astype(np.int32)
    large = np.minimum(large, nb - 1)
    return ret + np.where(dist < max_exact, dist, large)


class Tk:
    __slots__ = ('h', 'w', 'r')

    def __init__(self, h=None):
        self.h = h
        self.w = None
        self.r = []

    def __getitem__(self, idx):
        return self.h[idx]


class Sched:
    def __init__(self, nc, stack):
        self.nc = nc
        self.stack = stack
        self.eng = {}
        for n, h in [('pe', nc.tensor), ('dve', nc.vector), ('act', nc.scalar), ('pool', nc.gpsimd), ('sp', nc.sync)]:
            sem = stack.enter_context(nc.semaphore('s_' + n))
            self.eng[n] = dict(h=h, sem=sem, cnt=0, known={})
        self.dma_sems = {}
        for q in ('sp', 'pool', 'act'):
            self.dma_sems[q] = [[stack.enter_context(nc.semaphore('d%s%d' % (q, i))), 0] for i in range(12)]
        self.dma_rr = {'sp': 0, 'pool': 0, 'act': 0}
        self.nid = 0
        self.n_ins = 0

    def sb(self, shape, dt=F32):
        self.nid += 1
        return Tk(self.stack.enter_context(self.nc.sbuf_tensor('t%d' % self.nid, shape, dt)))

    def ps(self, shape, dt=F32):
        self.nid += 1
        return Tk(self.stack.enter_context(self.nc.psum_tensor('p%d' % self.nid, shape, dt)))

    def _wait(self, e, sem, val):
        E = self.eng[e]
        k = id(sem)
        if E['known'].get(k, 0) >= val:
            return
        E['h'].wait_ge(sem, val)
        E['known'][k] = val

    def _deps(self, e, reads, writes):
        E = self.eng[e]
        own = E['sem']
        pe = (e == 'pe')
        for t in reads:
            if t.w is not None and not (pe and t.w[0] is own):
                self._wait(e, *t.w)
        for t in writes:
            if t.w is not None and not (pe and t.w[0] is own):
                self._wait(e, *t.w)
            for (s, v) in t.r:
                if not (pe and s is own):
                    self._wait(e, s, v)

    def _mark(self, tok, reads, writes):
        for t in writes:
            t.w = tok
            t.r = []
        for t in reads:
            if t not in writes:
                if len(t.r) > 6:
                    d = {}
                    for (s, v) in t.r:
                        d[id(s)] = (s, max(v, d.get(id(s), (s, 0))[1]))
                    t.r = list(d.values())
                t.r.append(tok)

    def op(self, e, fn, reads=(), writes=()):
        E = self.eng[e]
        self._deps(e, reads, writes)
        ins = fn(E['h'])
        E['cnt'] += 1
        ins.then_inc(E['sem'], 1)
        self._mark((E['sem'], E['cnt']), reads, writes)
        self.n_ins += 1
        return ins

    def dma(self, e, out, in_, reads=(), writes=(), **kw):
        E = self.eng[e]
        slots = self.dma_sems[e]
        slot = slots[self.dma_rr[e]]
        self.dma_rr[e] = (self.dma_rr[e] + 1) % len(slots)
        if slot[1] > 0:
            self._wait(e, slot[0], slot[1])
        self._deps(e, reads, writes)
        ins = E['h'].dma_start(out=out, in_=in_, **kw)
        slot[1] += 16
        ins.then_inc(slot[0], 16)
        self._mark((slot[0], slot[1]), reads, writes)
        self.n_ins += 1
        return ins

    def finish(self, tiles):
        for t in tiles:
            if t.w is not None:
                self._wait('sp', *t.w)


class Builder:
    def __init__(self, T):
        self.T = T
        self.G = min(128, T)
        self.nc = bass.Bass("TRN2", target_bir_lowering=False)

    def dram_in(self, name, shape, dt=F32):
        return self.nc.dram_tensor(name, list(shape), dt, kind="ExternalInput").ap()

    def dram_out(self, name, shape, dt=F32):
        return self.nc.dram_tensor(name, list(shape), dt, kind="ExternalOutput").ap()

    def build(self):
        nc = self.nc
        T = self.T
        I = {}
        O = {}
        I['xp'] = self.dram_in('xp', [T, D])
        I['xs'] = self.dram_in('xs', [DEC_SEQ, D])
        I['win'] = self.dram_in('win', [DEPTH, NCH, 128, 8, 128])
        I['wbr'] = self.dram_in('wbr', [DEPTH, 4, 512, D])
        I['wout'] = self.dram_in('wout', [DEPTH, D, D])
        I['lng'] = self.dram_in('lng', [DEPTH, D])
        I['lnb'] = self.dram_in('lnb', [DEPTH, D])
        I['cst'] = self.dram_in('cst', [128, 1024])
        I['glaw'] = self.dram_in('glaw', [DEPTH, 16, 256])
        I['glab'] = self.dram_in('glab', [DEPTH, 64, 4])
        I['glan'] = self.dram_in('glan', [DEPTH, 128, 1])
        I['sgla'] = self.dram_in('sgla', [DEPTH, 4, 64, 128])
        I['relb'] = self.dram_in('relb', [32, 8])
        I['oh'] = self.dram_in('oh', [32, 384])
        I['ck'] = self.dram_in('ck', [DEPTH, PAST, 128])
        I['cv'] = self.dram_in('cv', [DEPTH, PAST, 128])
        I['cki'] = self.dram_in('cki', [DEPTH, PAST, 128])
        I['gcw'] = self.dram_in('gcw', [DEPTH, 128, 12, 4])
        I['gpar'] = self.dram_in('gpar', [DEPTH, 128, 8])
        I['gnorm'] = self.dram_in('gnorm', [DEPTH, 128, 1])
        I['sgdn'] = self.dram_in('sgdn', [DEPTH, 4, 128, 128])
        I['sconv'] = self.dram_in('sconv', [DEPTH, 128, 12, 3])
        I['s5p'] = self.dram_in('s5p', [DEPTH, 128, 3, 16])
        I['s5b'] = self.dram_in('s5b', [DEPTH, 2, 2, 128, 4, 128])
        I['s5c'] = self.dram_in('s5c', [DEPTH, 2, 128, 16, 128])
        I['s5d'] = self.dram_in('s5d', [DEPTH, 128, 4])
        I['s5h0'] = self.dram_in('s5h0', [DEPTH, 128, 2, 16])
        I['wglu'] = self.dram_in('wglu', [DEPTH, 512, 1024])
        O['yp'] = self.dram_out('yp', [T, D])
        O['ys'] = self.dram_out('ys', [DEC_SEQ, D])
        for g, tt in (('p', T), ('s', DEC_SEQ)):
            O['k' + g] = self.dram_out('k' + g, [DEPTH, tt, 128])
            O['v' + g] = self.dram_out('v' + g, [DEPTH, tt, 128])
            O['ki' + g] = self.dram_out('ki' + g, [DEPTH, tt, 64])
            O['gla' + g] = self.dram_out('gla' + g, [DEPTH, 4, 64, 128])
            O['conv' + g] = self.dram_out('conv' + g, [DEPTH, 3, 1536])
            O['gdn' + g] = self.dram_out('ogdn' + g, [DEPTH, 4, 128, 128])
            O['s5' + g] = self.dram_out('os5' + g, [DEPTH, 128, 2, 16])
        self.I, self.O = I, O
        self.y0p = nc.dram_tensor('y0p', [T, D], F32).ap()
        self.fd = nc.dram_tensor('fd', [8, 384], F32).ap()
        self.winb = nc.dram_tensor('winb', [DEPTH, NCH, 128, 8 * 128], BF16).ap()
        self.wbrb = nc.dram_tensor('wbrb', [DEPTH, 4, 8, 128, 4 * 128], BF16).ap()
        self.woutb = nc.dram_tensor('woutb', [DEPTH, 8, 128, 8 * 128], BF16).ap()
        self.cfd = nc.dram_tensor('cfd', [8, 1], F32).ap()
        self.y0s = nc.dram_tensor('y0s', [DEC_SEQ, D], F32).ap()
        with ExitStack() as st:
            S = Sched(nc, st)
            self.S = S
            self.dtk = {}
            self.setup()
            self.precast()
            for l in range(DEPTH):
                self.layer_setup(l)
                self.run_seq(l, 'p')
                self.run_seq(l, 's')
            S.finish(list(self.dtk.values()))
        return nc

    def dk(self, name):
        if name not in self.dtk:
            self.dtk[name] = Tk(None)
        return self.dtk[name]

    def setup(self):
        S = self.S
        G = self.G
        self.cst = S.sb([128, 1024])
        S.dma('sp', self.cst[:], self.I['cst'][:, :], writes=[self.cst])
        self.ident = self.cst
        self.pbanks = [S.ps([128, 512]) for _ in range(6)]
        self.pacc = [S.ps([128, 512]) for _ in range(2)]
        self.pb_i = 0
        self.xT = S.sb([128, 8, G])
        self.xTb = S.sb([128, 8, G], BF16)
        self.wch = [S.sb([128, 8, 128]) for _ in range(2)]
        self.wch_i = 0
        self.wchb = [S.sb([128, 8, 128], BF16) for _ in range(4)]
        self.wchb_i = 0
        self.brT = S.sb([128, 16, G], BF16)
        self.lng = S.sb([128, D])
        self.lnb = S.sb([128, D])
        self.wouts = [S.sb([128, 8, 128], BF16) for _ in range(1)]
        self.wbrs = [S.sb([128, 4, 128], BF16) for _ in range(2)]
        self.wbrBs = [S.sb([64, 8, 128], BF16) for _ in range(2)]
        self.wrr = 0
        self.brB = S.sb([64, 8, G], BF16)
        self.ARENA = 10240
        self.arena = S.sb([128, self.ARENA])
        self.fence_t = S.sb([1, 4])
        self.phase_views = {}
        self.phase_off = {}
        self.cur_phase = None
        self.fence_tok = None
        self.scr_i = 0
        self.ws_i = 0
        self.small = [S.sb([128, 16]) for _ in range(8)]
        self.sm3 = S.sb([128, 3, 16])
        self.small_i = 0
        self.epsb = S.sb([128, 2])
        S.op('dve', lambda h: h.memset(self.epsb[:, 0:1], LN_EPS), writes=[self.epsb])
        S.op('dve', lambda h: h.memset(self.epsb[:, 1:2], RMS_EPS), writes=[self.epsb])
        self.LC = min(128, G)
        self.s5_cos = S.sb([128, 16, self.LC])
        self.s5_sin = S.sb([128, 16, self.LC])
        self.s5_cr = S.sb([128, 16, 128])
        self.s5_ci = S.sb([128, 16, 128])
        self.s5_b = S.sb([128, 4, 4, 128])
        self.s5_q = S.sb([128, 12, 16])
        self.s5_d = S.sb([128, 4])
        self.s5_carry = S.sb([128, 2, 16])
        self.wglu = S.sb([128, 4, 128])
        T = self.T
        self.NK = max(T, PAST + 128)
        self.NH = 4096 if T == 8192 else self.NK
        self.NB = self.NK // 128
        self.kT = S.sb([128, self.NK], BF16)
        self.kiT = S.sb([128, self.NH])
        self.vA = S.sb([128, self.NB, 130], BF16)
        self.bias8 = S.sb([128, 4, 512])
        self.qT = S.sb([128, 4, G], BF16)
        self.qiT = S.sb([128, 4, G])
        self.wiT = S.sb([128, 4])
        self.offrow = S.sb([33, 4 * G], BF16)
        self.kmax2 = S.sb([33, 2])
        self.cf = S.sb([33, 4])
        self.bs = S.sb([128, 16])
        self.Et = [S.sb([128, 4 * G], BF16) for _ in range(2)]
        self.Pt2 = [[S.sb([128, 4 * G], BF16) for _ in range(2)] for _ in range(2)]
        self.selT = [S.sb([128, G], BF16) for _ in range(2)]
        self.onesb = S.sb([33, 128], BF16)
        self.phase('setup')
        self.dsa_setup()
        self.gdn_S = S.sb([128, 4, 128])
        self.gdn_cw = S.sb([128, 12, 4])
        self.gdn_par = S.sb([128, 8])
        self.gdn_n = S.sb([128, 1])
        self.gdn_tail = S.sb([128, 12, 3])
        self.gla_S = S.sb([64, 4, 128])
        self.gla_w = S.sb([16, 256])
        self.gla_b = S.sb([64, 4])
        self.gla_n = S.sb([128, 1])
        self.resetm = S.sb([64, 4 * G])

    def pbank(self):
        t = self.pbanks[self.pb_i]
        self.pb_i = (self.pb_i + 1) % len(self.pbanks)
        return t

    def phase(self, name):
        S = self.S
        if name == self.cur_phase:
            return
        prev = list(self.phase_views.get(self.cur_phase, {}).values()) if self.cur_phase else []
        nxt = list(self.phase_views.get(name, {}).values())
        ft = self.fence_t
        S.op('dve', lambda h: h.memset(ft[0:1, 0:1], 0.0), reads=[], writes=[ft] + prev + nxt)
        self.fence_tok = ft.w
        self.cur_phase = name
        self.phase_views.setdefault(name, {})
        self.phase_off.setdefault(name, 0)
        self.scr_i = 0
        self.ws_i = 0

    def av(self, key, ncols):
        pv = self.phase_views[self.cur_phase]
        if key not in pv:
            off = self.phase_off[self.cur_phase]
            assert off + ncols <= self.ARENA, (self.cur_phase, key, off, ncols)
            t = Tk(self.arena.h[:, off:off + ncols])
            t.w = self.fence_tok
            pv[key] = t
            self.phase_off[self.cur_phase] = off + ncols
        return pv[key]

    def scratch(self):
        t = self.av(('scr', self.scr_i % 8), 512)
        self.scr_i += 1
        return t

    def wsp(self, reset=False):
        if reset:
            self.ws_i = 0
        t = self.av(('ws', self.ws_i), 512)
        self.ws_i += 1
        return t

    def sm(self):
        t = self.small[self.small_i]
        self.small_i = (self.small_i + 1) % len(self.small)
        return t

    def layer_setup(self, l):
        S = self.S
        I = self.I
        S.dma('sp', self.lng[:], I['lng'][l:l + 1, :].partition_broadcast(128), writes=[self.lng])
        S.dma('sp', self.lnb[:], I['lnb'][l:l + 1, :].partition_broadcast(128), writes=[self.lnb])
        S.dma('sp', self.gla_w[:], I['glaw'][l], writes=[self.gla_w])
        S.dma('sp', self.gla_b[:], I['glab'][l], writes=[self.gla_b])
        S.dma('sp', self.gla_n[:], I['glan'][l], writes=[self.gla_n])
        self.phase('setup')
        self.s5_setup(l)
        self.gdn_setup(l)

    FP32_CHUNKS = ('kvC', 'kvD', 'qi0', 'qi1', 'qi2', 'qi3')

    def precast(self):
        S = self.S
        I = self.I
        st32 = self.wch[0]
        n = 0

        def piece(src_ap, rows, cols3, dst_ap, key):
            nonlocal n
            a, c = cols3
            w16 = self.wchb[n % len(self.wchb)]
            S.dma('sp', st32[0:rows, 0:a, 0:c], src_ap, writes=[st32])
            eng = 'act' if n % 2 else 'dve'
            if eng == 'act':
                S.op('act', lambda h: h.activation(out=w16[0:rows, 0:a, 0:c], in_=st32[0:rows, 0:a, 0:c], func=AF.Copy), reads=[st32], writes=[w16])
            else:
                S.op('dve', lambda h: h.tensor_copy(out=w16[0:rows, 0:a, 0:c], in_=st32[0:rows, 0:a, 0:c]), reads=[st32], writes=[w16])
            S.dma('sp', dst_ap, w16[0:rows, 0:a, 0:c], reads=[w16], writes=[self.dk(key)])
            n += 1
        for l in range(DEPTH):
            for name, _ in CHUNKS:
                if name in self.FP32_CHUNKS:
                    continue
                ci = CIDX[name]
                piece(I['win'][l, ci], 128, (8, 128), self.winb[l, ci].rearrange("p (a c) -> p a c", c=128), ('winb', l, ci))
            for b in range(4):
                for dc in range(8):
                    if b == 1:
                        piece(I['wbr'][l, b, :, dc * 128:(dc + 1) * 128].rearrange("(a p) c -> p a c", p=64), 64, (8, 128),
                              self.wbrb[l, b, dc].rearrange("(p h) f -> p (h f)", h=2).rearrange("p (a c) -> p a c", c=128), ('wbrb', l, b, dc))
                    else:
                        piece(I['wbr'][l, b, :, dc * 128:(dc + 1) * 128].rearrange("(a p) c -> p a c", p=128), 128, (4, 128),
                              self.wbrb[l, b, dc].rearrange("p (a c) -> p a c", c=128), ('wbrb', l, b, dc))
            for qt in range(8):
                piece(I['wout'][l, :, qt * 128:(qt + 1) * 128].rearrange("(a p) c -> p a c", p=128), 128, (8, 128),
                      self.woutb[l, qt].rearrange("p (a c) -> p a c", c=128), ('woutb', l, qt))

    def load_w(self, l, name):
        S = self.S
        w = self.wch[self.wch_i]
        self.wch_i = (self.wch_i + 1) % len(self.wch)
        S.dma('sp', w[:], self.I['win'][l, CIDX[name]], writes=[w])
        return w

    BF16_PREFIX = ('zzz',)

    def load_wx(self, l, name):
        S = self.S
        if name in self.FP32_CHUNKS:
            return self.load_w(l, name), self.xT
        w = self.wchb[self.wchb_i]
        self.wchb_i = (self.wchb_i + 1) % len(self.wchb)
        S.dma('sp', w[:], self.winb[l, CIDX[name]].rearrange("p (a c) -> p a c", c=128), reads=[self.dk(('winb', l, CIDX[name]))], writes=[w])
        return w, self.xTb

    def proj_fm(self, l, name, ncols, gw):
        S = self.S
        w, xT = self.load_wx(l, name)
        pb = self.pbank()
        for k in range(8):
            mm = 128 if xT is self.xTb else ncols
            S.op('pe', lambda h, k=k: h.matmul(pb[0:mm, 0:gw], lhsT=w[:, k, 0:mm], rhs=xT[:, k, 0:gw],
                                               start=(k == 0), stop=(k == 7)), reads=[w, xT], writes=[pb])
        return pb

    def proj_tm(self, l, name, ncols, t0, tw):
        S = self.S
        w, xT = self.load_wx(l, name)
        pb = self.pbank()
        for k in range(8):
            S.op('pe', lambda h, k=k: h.matmul(pb[0:tw, 0:ncols], lhsT=xT[:, k, t0:t0 + tw], rhs=w[:, k, 0:ncols],
                                               start=(k == 0), stop=(k == 7)), reads=[w, xT], writes=[pb])
        return pb

    def run_seq(self, l, grp):
        S = self.S
        I, O = self.I, self.O
        T = self.T if grp == 'p' else DEC_SEQ
        G = min(self.G, T)
        TP = min(128, T)
        C = min(64, T)
        nG = T // G
        if l == 0:
            xin = I['xp'] if grp == 'p' else I['xs']
            xin_tk = self.dk('in')
        else:
            xin = self.y0p if grp == 'p' else self.y0s
            xin_tk = self.dk('y0' + grp)
        yout = (O['yp'] if grp == 'p' else O['ys']) if l == DEPTH - 1 else (self.y0p if grp == 'p' else self.y0s)
        yout_tk = self.dk('yout' + grp) if l == DEPTH - 1 else self.dk('y0' + grp)

        if grp == 'p':
            S.op('dve', lambda h: h.memset(self.gla_S[:], 0.0), writes=[self.gla_S])
        else:
            S.dma('sp', self.gla_S[:], I['sgla'][l].rearrange("h k v -> k h v"), writes=[self.gla_S])
        self.s5_init(l, grp)
        self.gdn_init(l, grp)
        self.dsa_init(l, grp)
        S.op('dve', lambda h: h.memset(self.resetm[:], 1.0), writes=[self.resetm])
        rm3 = self.resetm[:, 0:4 * G].rearrange("p (a c) -> p a c", c=C)
        S.op('dve', lambda h: h.memset(rm3[:, :, 0:1], 0.0), writes=[self.resetm])

        for gi in range(nG):
            g0 = gi * G
            self.phase('kv')
            xts = []
            for ti in range(G // TP):
                xt = self.av('xtm', D)
                xts.append(xt)
                S.dma('sp', xt[0:TP, :], xin[g0 + ti * TP: g0 + (ti + 1) * TP, :], reads=[xin_tk], writes=[xt])
                for half in range(2):
                    pb = self.pbank()
                    for kk in range(4):
                        k = half * 4 + kk
                        S.op('pe', lambda h, k=k, kk=kk: h.transpose(pb[:, kk * 128: kk * 128 + TP], xt[0:TP, k * 128:(k + 1) * 128], self.ident[0:TP, 0:TP]),
                             reads=[xt, self.cst], writes=[pb])
                    dst = self.xT[:, half * 4:(half + 1) * 4, ti * TP:(ti + 1) * TP]
                    src = pb[:, :].rearrange("p (a c) -> p a c", c=128)[:, :, 0:TP]
                    dstb = self.xTb[:, half * 4:(half + 1) * 4, ti * TP:(ti + 1) * TP]
                    S.op('dve', lambda h: h.tensor_copy(out=dst, in_=src), reads=[pb], writes=[self.xT])
                    S.op('act', lambda h: h.activation(out=dstb, in_=dst, func=AF.Copy), reads=[self.xT], writes=[self.xTb])
            self.phase('kv')
            self.kv_part(l, grp, g0, G, TP)
            self.phase('s5')
            self.s5_part(l, grp, g0, G, last=(gi == nG - 1))
            self.phase('gla')
            self.gla_part(l, grp, g0, G, C, last=(gi == nG - 1))
            self.phase('gdn')
            self.gdn_part(l, grp, g0, G, C, last=(gi == nG - 1))
            self.phase('dsa')
            self.dsa_part(l, grp, g0, G, TP)
            self.phase('out')
            self.out_stage(l, grp, g0, G, TP, xin, xin_tk, yout, yout_tk)

    def branch_zero(self, G):
        S = self.S
        S.op('dve', lambda h: h.memset(self.brT[:, :, 0:G], 0.0), writes=[self.brT])
        S.op('dve', lambda h: h.memset(self.brB[:, :, 0:G], 0.0), writes=[self.brB])

    def kv_part(self, l, grp, g0, G, TP):
        S = self.S
        O = self.O
        kbase = 0 if grp == 'p' else PAST
        for ti in range(G // TP):
            t0 = ti * TP
            tiles = []
            for nm, ncols, oname, key in (('kvA', 128, 'k', 'k_tm'), ('kvB', 128, 'v', 'v_tm'), ('kvC', 128, 'ki', 'c_tm')):
                pb = self.proj_tm(l, nm, ncols, t0, TP)
                sc = self.av(key, 128)
                S.op('act', lambda h: h.activation(out=sc[0:TP, 0:ncols], in_=pb[0:TP, 0:ncols], func=AF.Copy), reads=[pb], writes=[sc])
                oc = 64 if oname == 'ki' else 128
                S.dma('pool', O[oname + grp][l, g0 + t0: g0 + t0 + TP, :], sc[0:TP, 0:oc], reads=[sc], writes=[self.dk('o' + oname + grp)])
                tiles.append(sc)
            self.add_keys(tiles[0], tiles[1], tiles[2], TP, kbase + g0 + t0)
            pb = self.proj_tm(l, 'kvD', 12, t0, TP)
            S.op('dve', lambda h: h.tensor_copy(out=self.wiT[0:TP, 0:4], in_=pb[0:TP, 0:4]), reads=[pb], writes=[self.wiT])
        T = self.T if grp == 'p' else DEC_SEQ
        if g0 + G == T:
            for c in range(12):
                w, xTw = self.load_wx(l, 'cq%d' % c)
                pb = self.pbank()
                for k in range(8):
                    S.op('pe', lambda h, k=k: h.matmul(pb[0:3, 0:128], lhsT=xTw[:, k, G - 3:G], rhs=w[:, k, :],
                                                       start=(k == 0), stop=(k == 7)), reads=[w, xTw], writes=[pb])
                sc = self.scratch()
                S.op('act', lambda h: h.activation(out=sc[0:3, 0:128], in_=pb[0:3, 0:128], func=AF.Copy), reads=[pb], writes=[sc])
                S.dma('pool', O['conv' + grp][l, :, c * 128:(c + 1) * 128], sc[0:3, 0:128], reads=[sc], writes=[self.dk('oconv' + grp)])


    def range_reduce(self, x, n):
        S = self.S
        TWO_PI = 2.0 * math.pi
        ki = self.s5_ki
        kf = self.s5_kf
        S.op('dve', lambda h: h.tensor_scalar(out=kf[:, 0:n], in0=x, scalar1=1.0 / TWO_PI, scalar2=None, op0=ALU.mult), reads=[self.s5_ang], writes=[self.s5_kfT])
        S.op('dve', lambda h: h.tensor_copy(out=ki[:, 0:n], in_=kf[:, 0:n]), reads=[self.s5_kfT], writes=[self.s5_kiT])
        S.op('dve', lambda h: h.tensor_copy(out=kf[:, 0:n], in_=ki[:, 0:n]), reads=[self.s5_kiT], writes=[self.s5_kfT])
        S.op('dve', lambda h: h.scalar_tensor_tensor(out=x, in0=kf[:, 0:n], scalar=-TWO_PI, in1=x, op0=ALU.mult, op1=ALU.add), reads=[self.s5_kfT, self.s5_ang], writes=[self.s5_ang])
        S.op('dve', lambda h: h.tensor_scalar(out=kf[:, 0:n], in0=x, scalar1=math.pi, scalar2=-TWO_PI, op0=ALU.is_gt, op1=ALU.mult), reads=[self.s5_ang], writes=[self.s5_kfT])
        S.op('dve', lambda h: h.tensor_tensor(out=x, in0=x, in1=kf[:, 0:n], op=ALU.add), reads=[self.s5_kfT, self.s5_ang], writes=[self.s5_ang])
        S.op('dve', lambda h: h.tensor_scalar(out=kf[:, 0:n], in0=x, scalar1=-math.pi, scalar2=TWO_PI, op0=ALU.is_lt, op1=ALU.mult), reads=[self.s5_ang], writes=[self.s5_kfT])
        S.op('dve', lambda h: h.tensor_tensor(out=x, in0=x, in1=kf[:, 0:n], op=ALU.add), reads=[self.s5_kfT, self.s5_ang], writes=[self.s5_ang])

    def s5_setup(self, l):
        S = self.S
        I = self.I
        LC = self.LC
        q = self.s5_q
        if not hasattr(self, 's5_ang'):
            self.s5_ang = S.sb([128, 128])
            self.s5_kfT = S.sb([128, 128])
            self.s5_kiT = S.sb([128, 128], mybir.dt.int32)
            self.s5_kf = self.s5_kfT
            self.s5_ki = self.s5_kiT
        ang = self.s5_ang
        raw = self.sm3
        S.dma('sp', raw[:, :, :], I['s5p'][l], writes=[raw])
        S.dma('sp', self.s5_d[:, :], I['s5d'][l], writes=[self.s5_d])
        S.dma('sp', self.s5_b[:, :, :, :], I['s5b'][l].rearrange("v r p c f -> p (v r) c f"), writes=[self.s5_b])
        Q = lambda i: q[:, i, :]
        rq = [q]
        S.op('dve', lambda h: h.tensor_scalar(out=Q(4), in0=raw[:, 0, :], scalar1=-1e-4, scalar2=None, op0=ALU.min), reads=[raw], writes=rq)
        S.op('dve', lambda h: h.tensor_copy(out=Q(5), in_=raw[:, 1, :]), reads=[raw], writes=rq)
        S.op('act', lambda h: h.activation(out=Q(6), in_=raw[:, 2, :], func=AF.Exp), reads=[raw], writes=rq)
        S.op('dve', lambda h: h.tensor_tensor(out=Q(7), in0=Q(4), in1=Q(6), op=ALU.mult), reads=rq, writes=rq)
        S.op('act', lambda h: h.activation(out=Q(0), in_=Q(7), func=AF.Exp), reads=rq, writes=rq)
        S.op('dve', lambda h: h.tensor_tensor(out=Q(1), in0=Q(5), in1=Q(6), op=ALU.mult), reads=rq, writes=rq)
        S.op('dve', lambda h: h.tensor_copy(out=ang[:, 0:16], in_=Q(1)), reads=rq, writes=[ang])
        self.range_reduce(ang[:, 0:16], 16)
        S.op('dve', lambda h: h.tensor_copy(out=Q(1), in_=ang[:, 0:16]), reads=[ang], writes=rq)
        S.op('act', lambda h: h.activation(out=Q(8), in_=ang[:, 0:16], func=AF.Sin), reads=[ang], writes=rq)
        S.op('dve', lambda h: h.tensor_scalar(out=ang[:, 0:16], in0=ang[:, 0:16], scalar1=math.pi / 2, scalar2=None, op0=ALU.add), reads=[ang], writes=[ang])
        self.range_reduce(ang[:, 0:16], 16)
        S.op('act', lambda h: h.activation(out=Q(9), in_=ang[:, 0:16], func=AF.Sin), reads=[ang], writes=rq)
        S.op('dve', lambda h: h.tensor_tensor(out=Q(9), in0=Q(9), in1=Q(0), op=ALU.mult), reads=rq, writes=rq)
        S.op('dve', lambda h: h.tensor_tensor(out=Q(8), in0=Q(8), in1=Q(0), op=ALU.mult), reads=rq, writes=rq)
        S.op('dve', lambda h: h.tensor_scalar(out=Q(9), in0=Q(9), scalar1=-1.0, scalar2=None, op0=ALU.add), reads=rq, writes=rq)
        S.op('dve', lambda h: h.tensor_tensor(out=Q(10), in0=Q(4), in1=Q(4), op=ALU.mult), reads=rq, writes=rq)
        S.op('dve', lambda h: h.tensor_tensor(out=Q(11), in0=Q(5), in1=Q(5), op=ALU.mult), reads=rq, writes=rq)
        S.op('dve', lambda h: h.tensor_tensor(out=Q(10), in0=Q(10), in1=Q(11), op=ALU.add), reads=rq, writes=rq)
        S.op('dve', lambda h: h.reciprocal(out=Q(10), in_=Q(10)), reads=rq, writes=rq)
        S.op('dve', lambda h: h.tensor_tensor(out=Q(2), in0=Q(9), in1=Q(4), op=ALU.mult), reads=rq, writes=rq)
        S.op('dve', lambda h: h.tensor_tensor(out=Q(11), in0=Q(8), in1=Q(5), op=ALU.mult), reads=rq, writes=rq)
        S.op('dve', lambda h: h.tensor_tensor(out=Q(2), in0=Q(2), in1=Q(11), op=ALU.add), reads=rq, writes=rq)
        S.op('dve', lambda h: h.tensor_tensor(out=Q(2), in0=Q(2), in1=Q(10), op=ALU.mult), reads=rq, writes=rq)
        S.op('dve', lambda h: h.tensor_tensor(out=Q(3), in0=Q(8), in1=Q(4), op=ALU.mult), reads=rq, writes=rq)
        S.op('dve', lambda h: h.tensor_tensor(out=Q(11), in0=Q(9), in1=Q(5), op=ALU.mult), reads=rq, writes=rq)
        S.op('dve', lambda h: h.tensor_tensor(out=Q(3), in0=Q(3), in1=Q(11), op=ALU.subtract), reads=rq, writes=rq)
        S.op('dve', lambda h: h.tensor_tensor(out=Q(3), in0=Q(3), in1=Q(10), op=ALU.mult), reads=rq, writes=rq)
        for st_ in range(16):
            for which, tab in ((0, self.s5_sin), (1, self.s5_cos)):
                S.op('dve', lambda h, st_=st_, which=which: h.tensor_scalar(out=ang[:, 0:LC], in0=self.cst[:, 512:512 + LC], scalar1=q[:, 1, st_:st_ + 1],
                                                                         scalar2=(math.pi / 2 if which else 0.0), op0=ALU.mult, op1=ALU.add), reads=[self.cst, q], writes=[ang])
                self.range_reduce(ang[:, 0:LC], LC)
                S.op('act', lambda h, st_=st_, tab=tab: h.activation(out=tab[:, st_, :], in_=ang[:, 0:LC], func=AF.Sin), reads=[ang], writes=[tab])
        for st_ in range(16):
            c_r = self.scratch()
            c_i = self.scratch()
            S.dma('sp', c_r[:, 0:128], I['s5c'][l, 0, :, st_, :], writes=[c_r])
            S.dma('sp', c_i[:, 0:128], I['s5c'][l, 1, :, st_, :], writes=[c_i])
            t1 = self.scratch()
            S.op('dve', lambda h, st_=st_: h.tensor_scalar(out=t1[:, 0:128], in0=c_i[:, 0:128], scalar1=q[:, 3, st_:st_ + 1], scalar2=None, op0=ALU.mult), reads=[c_i, q], writes=[t1])
            S.op('dve', lambda h, st_=st_: h.scalar_tensor_tensor(out=self.s5_cr[:, st_, :], in0=c_r[:, 0:128], scalar=q[:, 2, st_:st_ + 1], in1=t1[:, 0:128], op0=ALU.mult, op1=ALU.subtract),
                 reads=[c_r, q, t1], writes=[self.s5_cr])
            S.op('dve', lambda h, st_=st_: h.tensor_scalar(out=t1[:, 0:128], in0=c_i[:, 0:128], scalar1=q[:, 2, st_:st_ + 1], scalar2=-1.0, op0=ALU.mult, op1=ALU.mult), reads=[c_i, q], writes=[t1])
            S.op('dve', lambda h, st_=st_: h.tensor_scalar(out=c_r[:, 0:128], in0=c_r[:, 0:128], scalar1=q[:, 3, st_:st_ + 1], scalar2=None, op0=ALU.mult), reads=[c_r, q], writes=[c_r])
            S.op('dve', lambda h, st_=st_: h.tensor_tensor(out=self.s5_ci[:, st_, :], in0=t1[:, 0:128], in1=c_r[:, 0:128], op=ALU.subtract), reads=[t1, c_r], writes=[self.s5_ci])

    def s5_init(self, l, grp):
        S = self.S
        q = self.s5_q
        car = self.s5_carry
        if grp == 'p':
            S.op('dve', lambda h: h.memset(car[:, :, :], 0.0), writes=[car])
            return
        h0 = self.sm3
        S.dma('sp', h0[:, 0:2, :], self.I['s5h0'][l], writes=[h0])
        m = self.sm()
        n2 = self.sm()
        t = self.sm()
        S.op('dve', lambda h: h.tensor_tensor(out=m[:, 0:16], in0=q[:, 2, :], in1=q[:, 2, :], op=ALU.mult), reads=[q], writes=[m])
        S.op('dve', lambda h: h.tensor_tensor(out=n2[:, 0:16], in0=q[:, 3, :], in1=q[:, 3, :], op=ALU.mult), reads=[q], writes=[n2])
        S.op('dve', lambda h: h.tensor_tensor(out=m[:, 0:16], in0=m[:, 0:16], in1=n2[:, 0:16], op=ALU.add), reads=[m, n2], writes=[m])
        S.op('dve', lambda h: h.reciprocal(out=m[:, 0:16], in_=m[:, 0:16]), reads=[m], writes=[m])
        S.op('dve', lambda h: h.tensor_tensor(out=t[:, 0:16], in0=h0[:, 0, :], in1=q[:, 2, :], op=ALU.mult), reads=[h0, q], writes=[t])
        S.op('dve', lambda h: h.tensor_tensor(out=n2[:, 0:16], in0=h0[:, 1, :], in1=q[:, 3, :], op=ALU.mult), reads=[h0, q], writes=[n2])
        S.op('dve', lambda h: h.tensor_tensor(out=t[:, 0:16], in0=t[:, 0:16], in1=n2[:, 0:16], op=ALU.add), reads=[t, n2], writes=[t])
        S.op('dve', lambda h: h.tensor_tensor(out=car[:, 0, :], in0=t[:, 0:16], in1=m[:, 0:16], op=ALU.mult), reads=[t, m], writes=[car])
        S.op('dve', lambda h: h.tensor_tensor(out=t[:, 0:16], in0=h0[:, 1, :], in1=q[:, 2, :], op=ALU.mult), reads=[h0, q], writes=[t])
        S.op('dve', lambda h: h.tensor_tensor(out=n2[:, 0:16], in0=h0[:, 0, :], in1=q[:, 3, :], op=ALU.mult), reads=[h0, q], writes=[n2])
        S.op('dve', lambda h: h.tensor_tensor(out=t[:, 0:16], in0=t[:, 0:16], in1=n2[:, 0:16], op=ALU.subtract), reads=[t, n2], writes=[t])
        S.op('dve', lambda h: h.tensor_tensor(out=car[:, 1, :], in0=t[:, 0:16], in1=m[:, 0:16], op=ALU.mult), reads=[t, m], writes=[car])

    def s5_final(self, l, grp):
        S = self.S
        q = self.s5_q
        car = self.s5_carry
        o = self.sm3
        t = self.sm()
        S.op('dve', lambda h: h.tensor_tensor(out=o[:, 0, :], in0=car[:, 0, :], in1=q[:, 2, :], op=ALU.mult), reads=[car, q], writes=[o])
        S.op('dve', lambda h: h.tensor_tensor(out=t[:, 0:16], in0=car[:, 1, :], in1=q[:, 3, :], op=ALU.mult), reads=[car, q], writes=[t])
        S.op('dve', lambda h: h.tensor_tensor(out=o[:, 0, :], in0=o[:, 0, :], in1=t[:, 0:16], op=ALU.subtract), reads=[o, t], writes=[o])
        S.op('dve', lambda h: h.tensor_tensor(out=o[:, 1, :], in0=car[:, 0, :], in1=q[:, 3, :], op=ALU.mult), reads=[car, q], writes=[o])
        S.op('dve', lambda h: h.tensor_tensor(out=t[:, 0:16], in0=car[:, 1, :], in1=q[:, 2, :], op=ALU.mult), reads=[car, q], writes=[t])
        S.op('dve', lambda h: h.tensor_tensor(out=o[:, 1, :], in0=o[:, 1, :], in1=t[:, 0:16], op=ALU.add), reads=[o, t], writes=[o])
        S.dma('pool', self.O['s5' + grp][l], o[:, 0:2, :], reads=[o], writes=[self.dk('os5' + grp)])

    def s5_part(self, l, grp, g0, G, last):
        S = self.S
        I = self.I
        LC = min(self.LC, G)
        ug = self.av('ug', 4 * G)
        ug3 = ug.h[:, 0:4 * G].rearrange("p (a c) -> p a c", c=G)
        q = self.s5_q
        car = self.s5_carry
        for c in range(4):
            pb = self.proj_fm(l, 'au%d' % c, 128, G)
            S.op('act' if c % 2 else 'dve', (lambda h, c=c: h.activation(out=ug3[:, c, 0:G], in_=pb[:, 0:G], func=AF.Copy)) if c % 2 else (lambda h, c=c: h.tensor_copy(out=ug3[:, c, 0:G], in_=pb[:, 0:G])),
                 reads=[pb], writes=[ug])
        yT = self.wsp(True)
        y3 = yT[:, 0:4 * G].rearrange("p (a c) -> p a c", c=G)
        rt = [self.s5_cos, self.s5_sin]
        nscr = [0]

        def s5scr():
            t = self.av(('s5scr', nscr[0] % 12), 128)
            nscr[0] += 1
            return t
        for s0 in range(0, G, LC):
            for c in range(4):
                py = self.pacc[c % 2]
                for pair in range(2):
                    tl = []
                    for s4 in (2 * pair, 2 * pair + 1):
                        st_ = 4 * c + s4
                        brow = slice(32 * s4, 32 * s4 + 32) if s4 < 3 else slice(64, 128)
                        bvar = 0 if s4 < 3 else 2
                        pr = self.pbank()
                        pi = self.pbank()
                        S.op('pe', lambda h: h.matmul(pr[:, 0:LC], lhsT=self.s5_b[brow, bvar + 0, c, :], rhs=ug3[brow, c, s0:s0 + LC], start=True, stop=True),
                             reads=[self.s5_b, ug], writes=[pr])
                        S.op('pe', lambda h: h.matmul(pi[:, 0:LC], lhsT=self.s5_b[brow, bvar + 1, c, :], rhs=ug3[brow, c, s0:s0 + LC], start=True, stop=True),
                             reads=[self.s5_b, ug], writes=[pi])
                        tl.append(dict(s4=s4, st=st_, pr=pr, pi=pi, cs=self.s5_cos[:, st_, 0:LC], sn=self.s5_sin[:, st_, 0:LC],
                                       a=s5scr(), b=s5scr(), xr=s5scr(), xi=s5scr(), a2=s5scr(), b2=s5scr()))
                    for t in tl:
                        pr, pi, cs, sn, a_, b_, xr, xi, a2, b2 = t['pr'], t['pi'], t['cs'], t['sn'], t['a'], t['b'], t['xr'], t['xi'], t['a2'], t['b2']
                        S.op('dve', lambda h: h.tensor_tensor(out=a_[:, 0:LC], in0=pr[:, 0:LC], in1=cs, op=ALU.mult), reads=[pr] + rt, writes=[a_])
                        S.op('dve', lambda h: h.tensor_tensor(out=b_[:, 0:LC], in0=pi[:, 0:LC], in1=sn, op=ALU.mult), reads=[pi] + rt, writes=[b_])
                        S.op('pool', lambda h: h.tensor_tensor(out=xr[:, 0:LC], in0=a_[:, 0:LC], in1=b_[:, 0:LC], op=ALU.add), reads=[a_, b_], writes=[xr])
                        S.op('dve', lambda h: h.tensor_tensor(out=a2[:, 0:LC], in0=pi[:, 0:LC], in1=cs, op=ALU.mult), reads=[pi] + rt, writes=[a2])
                        S.op('dve', lambda h: h.tensor_tensor(out=b2[:, 0:LC], in0=pr[:, 0:LC], in1=sn, op=ALU.mult), reads=[pr] + rt, writes=[b2])
                        S.op('pool', lambda h: h.tensor_tensor(out=xi[:, 0:LC], in0=a2[:, 0:LC], in1=b2[:, 0:LC], op=ALU.subtract), reads=[a2, b2], writes=[xi])
                    for t in tl:
                        st_, xr, xi = t['st'], t['xr'], t['xi']
                        rho = q[:, 0, st_:st_ + 1].to_broadcast([128, LC])
                        S.op('dve', lambda h: h.tensor_tensor_scan(out=xr[:, 0:LC], data0=rho, data1=xr[:, 0:LC], initial=car[:, 0, st_:st_ + 1], op0=ALU.mult, op1=ALU.add),
                             reads=[q, xr, car], writes=[xr])
                        S.op('dve', lambda h: h.tensor_tensor_scan(out=xi[:, 0:LC], data0=rho, data1=xi[:, 0:LC], initial=car[:, 1, st_:st_ + 1], op0=ALU.mult, op1=ALU.add),
                             reads=[q, xi, car], writes=[xi])
                    for t in tl:
                        cs, sn, a_, b_, xr, xi, a2, b2 = t['cs'], t['sn'], t['a'], t['b'], t['xr'], t['xi'], t['a2'], t['b2']
                        S.op('pool', lambda h: h.tensor_tensor(out=a_[:, 0:LC], in0=xr[:, 0:LC], in1=cs, op=ALU.mult), reads=[xr] + rt, writes=[a_])
                        S.op('pool', lambda h: h.tensor_tensor(out=b_[:, 0:LC], in0=xi[:, 0:LC], in1=sn, op=ALU.mult), reads=[xi] + rt, writes=[b_])
                        S.op('pool', lambda h: h.tensor_tensor(out=a2[:, 0:LC], in0=xr[:, 0:LC], in1=sn, op=ALU.mult), reads=[xr] + rt, writes=[a2])
                        S.op('pool', lambda h: h.tensor_tensor(out=b2[:, 0:LC], in0=xi[:, 0:LC], in1=cs, op=ALU.mult), reads=[xi] + rt, writes=[b2])
                    for t in tl:
                        a_, b_, a2, b2 = t['a'], t['b'], t['a2'], t['b2']
                        S.op('dve', lambda h: h.tensor_tensor(out=a_[:, 0:LC], in0=a_[:, 0:LC], in1=b_[:, 0:LC], op=ALU.subtract), reads=[a_, b_], writes=[a_])
                        S.op('dve', lambda h: h.tensor_tensor(out=a2[:, 0:LC], in0=a2[:, 0:LC], in1=b2[:, 0:LC], op=ALU.add), reads=[a2, b2], writes=[a2])
                    for t in tl:
                        st_, a_, a2 = t['st'], t['a'], t['a2']
                        S.op('dve', lambda h: h.tensor_copy(out=car[:, 0, st_:st_ + 1], in_=a_[:, LC - 1:LC]), reads=[a_], writes=[car])
                        S.op('dve', lambda h: h.tensor_copy(out=car[:, 1, st_:st_ + 1], in_=a2[:, LC - 1:LC]), reads=[a2], writes=[car])
                    for t in tl:
                        s4, st_, a_, a2 = t['s4'], t['st'], t['a'], t['a2']
                        S.op('pe', lambda h: h.matmul(py[:, 0:LC], lhsT=self.s5_cr[:, st_, :], rhs=a_[:, 0:LC], start=(s4 == 0), stop=False), reads=[self.s5_cr, a_], writes=[py])
                        S.op('pe', lambda h: h.matmul(py[:, 0:LC], lhsT=self.s5_ci[:, st_, :], rhs=a2[:, 0:LC], start=False, stop=(s4 == 3)), reads=[self.s5_ci, a2], writes=[py])
                S.op('dve', lambda h: h.scalar_tensor_tensor(out=y3[:, c, s0:s0 + LC], in0=ug3[:, c, s0:s0 + LC], scalar=self.s5_d[:, c:c + 1], in1=py[:, 0:LC], op0=ALU.mult, op1=ALU.add),
                     reads=[ug, self.s5_d, py], writes=[yT])
        if last:
            self.s5_final(l, grp)
        S.op('act', lambda h: h.activation(out=ug3[:, :, 0:G], in_=y3, func=AF.Gelu), reads=[yT], writes=[ug])
        for c in range(4):
            sgs = []
            for oc in (c + 4, c):
                S.dma('sp', self.wglu[:, :, :], I['wglu'][l, :, oc * 128:(oc + 1) * 128].rearrange("(a p) f -> p a f", p=128), writes=[self.wglu])
                pg = self.pbank()
                for kc in range(4):
                    S.op('pe', lambda h, kc=kc: h.matmul(pg[:, 0:G], lhsT=self.wglu[:, kc, :], rhs=ug3[:, kc, 0:G], start=(kc == 0), stop=(kc == 3)), reads=[self.wglu, ug], writes=[pg])
                sgs.append(pg)
            sg = self.scratch()
            S.op('act', lambda h: h.activation(out=sg[:, 0:G], in_=sgs[0][:, 0:G], func=AF.Sigmoid), reads=[sgs[0]], writes=[sg])
            va = self.scratch()
            S.op('dve', lambda h: h.tensor_tensor(out=va[:, 0:G], in0=sgs[1][:, 0:G], in1=sg[:, 0:G], op=ALU.mult), reads=[sgs[1], sg], writes=[va])
            pgt = self.proj_fm(l, 'ag%d' % c, 128, G)
            sg2 = self.scratch()
            S.op('act', lambda h: h.activation(out=sg2[:, 0:G], in_=pgt[:, 0:G], func=AF.Silu), reads=[pgt], writes=[sg2])
            S.op('dve', lambda h, c=c: h.tensor_tensor(out=self.brT[:, c, 0:G], in0=va[:, 0:G], in1=sg2[:, 0:G], op=ALU.mult), reads=[va, sg2], writes=[self.brT])


    def dsa_setup(self):
        S = self.S
        I = self.I
        cst = self.cst
        rb = self.sm()
        oh = self.av('oh', 384)
        S.dma('sp', rb[0:32, 0:8], I['relb'][:, :], writes=[rb])
        S.dma('sp', oh[0:32, 0:384], I['oh'][:, :], writes=[oh])
        pf = self.pbank()
        S.op('pe', lambda h: h.matmul(pf[0:8, 0:384], lhsT=rb[0:32, 0:8], rhs=oh[0:32, 0:384], start=True, stop=True), reads=[rb, oh], writes=[pf])
        f8 = self.av('f8', 384)
        cm = self.sm()
        S.op('dve', lambda h: h.tensor_copy(out=cm[0:8, 0:1], in_=pf[0:8, 382:383]), reads=[pf], writes=[cm])
        S.op('dve', lambda h: h.tensor_scalar(out=f8[0:8, 0:384], in0=pf[0:8, 0:384], scalar1=cm[0:8, 0:1], scalar2=8.0, op0=ALU.subtract, op1=ALU.mult), reads=[pf, cm], writes=[f8])
        S.op('dve', lambda h: h.tensor_reduce(out=cm[0:8, 1:2], in_=f8[0:8, 0:383], axis=AX.X, op=ALU.max, negate=True), reads=[f8], writes=[cm])
        fdk = self.dk('fd')
        S.dma('sp', self.fd[:, :], f8[0:8, 0:384], reads=[f8], writes=[fdk])
        S.dma('sp', self.cfd[:, :], cm[0:8, 1:2], reads=[cm], writes=[fdk])
        S.op('dve', lambda h: h.memset(self.cf[:, :], 0.0), writes=[self.cf])
        S.dma('sp', self.cf[0:1, 0:4], self.cfd[0:4, :].rearrange("a b -> b a"), reads=[fdk], writes=[self.cf])
        S.dma('sp', self.cf[32:33, 0:4], self.cfd[4:8, :].rearrange("a b -> b a"), reads=[fdk], writes=[self.cf])
        for lrow in range(128):
            for g in range(2):
                src = bass.AP(tensor=self.fd.tensor, offset=4 * g * 384 + 127 - lrow, ap=[[0, 1], [128, 2], [384, 4], [1, 128]])
                dst = self.bias8[lrow:lrow + 1, 2 * g:2 * g + 2, :].rearrange("p k (a c) -> p k a c", c=128)
                S.dma('sp' if g else 'pool', dst, src, reads=[fdk], writes=[self.bias8])
        S.op('dve', lambda h: h.memset(self.onesb[:, :], 1.0), writes=[self.onesb])

    def add_keys(self, k_tm, v_tm, c_tm, n, key0):
        S = self.S
        cst = self.cst
        blk = key0 // 128
        pk = self.pbank()
        S.op('pe', lambda h: h.transpose(pk[:, 0:n], k_tm[0:n, 0:128], self.ident[0:n, 0:n]), reads=[k_tm, cst], writes=[pk])
        S.op('act', lambda h: h.activation(out=self.kT[:, key0:key0 + n], in_=pk[:, 0:n], func=AF.Copy), reads=[pk], writes=[self.kT])
        sq = self.scratch()
        S.op('act', lambda h: h.activation(out=sq[:, 0:n], in_=pk[:, 0:n], func=AF.Square), reads=[pk], writes=[sq])
        pn = self.pbank()
        S.op('pe', lambda h: h.matmul(pn[0:33, 0:n], lhsT=cst[:, 896:929], rhs=sq[:, 0:n], start=True, stop=True), reads=[cst, sq], writes=[pn])
        S.op('dve', lambda h: h.tensor_reduce(out=self.kmax2[:, 1:2], in_=pn[0:33, 0:n], axis=AX.X, op=ALU.max), reads=[pn], writes=[self.kmax2])
        S.op('dve', lambda h: h.tensor_tensor(out=self.kmax2[:, 0:1], in0=self.kmax2[:, 0:1], in1=self.kmax2[:, 1:2], op=ALU.max), reads=[self.kmax2], writes=[self.kmax2])
        va = self.vA[0:n, blk, :].rearrange("p (g c) -> p g c", c=65)[:, :, 0:64]
        S.op('dve', lambda h: h.tensor_copy(out=va, in_=v_tm[0:n, 0:128].rearrange("p (g c) -> p g c", c=64)), reads=[v_tm], writes=[self.vA])
        pc = self.pbank()
        S.op('pe', lambda h: h.transpose(pc[:, 0:n], c_tm[0:n, 0:128], self.ident[0:n, 0:n]), reads=[c_tm, cst], writes=[pc])
        if key0 < self.NH:
            S.op('dve', lambda h: h.tensor_copy(out=self.kiT[0:64, key0:key0 + n], in_=pc[0:64, 0:n]), reads=[pc], writes=[self.kiT])
        else:
            S.op('dve', lambda h: h.tensor_copy(out=self.kiT[64:128, key0 - self.NH:key0 - self.NH + n], in_=pc[64:128, 0:n]), reads=[pc], writes=[self.kiT])

    def dsa_init(self, l, grp):
        S = self.S
        I = self.I
        S.op('dve', lambda h: h.memset(self.vA[:, :, :], 1.0), writes=[self.vA])
        S.op('dve', lambda h: h.memset(self.kmax2[:, :], 0.0), writes=[self.kmax2])
        if grp == 's':
            self.phase('kv')
            for b in range(PAST // 128):
                k_tm = self.av('k_tm', 128); v_tm = self.av('v_tm', 128); c_tm = self.av('c_tm', 128)
                S.dma('sp', k_tm[:, 0:128], I['ck'][l, b * 128:(b + 1) * 128, :], writes=[k_tm])
                S.dma('sp', v_tm[:, 0:128], I['cv'][l, b * 128:(b + 1) * 128, :], writes=[v_tm])
                S.dma('sp', c_tm[:, 0:128], I['cki'][l, b * 128:(b + 1) * 128, :], writes=[c_tm])
                self.add_keys(k_tm, v_tm, c_tm, 128, b * 128)

    def dsa_part(self, l, grp, g0, G, TP):
        S = self.S
        cst = self.cst
        kbase = 0 if grp == 'p' else PAST
        Qa = kbase + g0
        L = Qa + TP
        NKtot = (self.T if grp == 'p' else PAST + DEC_SEQ)
        KTOP = float(min(TOPK_MAX, NKtot // 4))
        W = 4 * TP
        sc = self.av('sc', 8192)
        junk = self.av('junk', 2048)
        junk8 = junk.h.bitcast(U8)
        offf_v = junk.h[0:33, 0:512]
        osb_v = junk.h[0:65, 512:1024]
        onn_v = junk.h[0:64, 1024:1536]
        qsq_v = junk.h[:, 1536:1536 + 4 * G].rearrange("p (a c) -> p a c", c=G)
        bs = self.bs
        for hh in range(4):
            pb = self.proj_fm(l, 'qq%d' % hh, 128, TP)
            S.op('act', lambda h, hh=hh: h.activation(out=self.qT[:, hh, 0:TP], in_=pb[:, 0:TP], func=AF.Copy), reads=[pb], writes=[self.qT])
            S.op('act', lambda h, hh=hh: h.activation(out=qsq_v[:, hh, 0:TP], in_=pb[:, 0:TP], func=AF.Square), reads=[pb], writes=[junk])
        for hh in range(4):
            pb = self.proj_fm(l, 'qi%d' % hh, 128, TP)
            S.op('dve', lambda h, hh=hh: h.tensor_copy(out=self.qiT[:, hh, 0:TP], in_=pb[:, 0:TP]), reads=[pb], writes=[self.qiT])
        pq = self.pbank()
        S.op('pe', lambda h: h.matmul(pq[0:33, 0:W], lhsT=cst[:, 896:929], rhs=qsq_v[:, :, 0:TP], start=True, stop=True), reads=[cst, junk], writes=[pq])
        S.op('act', lambda h: h.activation(out=offf_v[:, 0:W], in_=pq[0:33, 0:W], func=AF.Sqrt, scale=self.kmax2[:, 0:1]), reads=[pq, self.kmax2], writes=[junk])
        S.op('dve', lambda h: h.tensor_tensor(out=self.offrow[:, 0:W].rearrange("p (a c) -> p a c", c=TP), in0=self.cf[:, :].unsqueeze(2).to_broadcast([33, 4, TP]),
                                               in1=offf_v[:, 0:W].rearrange("p (a c) -> p a c", c=TP), op=ALU.subtract), reads=[self.cf, junk], writes=[self.offrow])
        for b0 in range(0, L, 512):
            bw = min(512, L - b0)
            if b0 < self.NH:
                rows = slice(0, 64); c0 = b0
            else:
                rows = slice(64, 128); c0 = b0 - self.NH
            for hh in range(4):
                ps = self.pbank()
                S.op('pe', lambda h, hh=hh: h.matmul(ps[0:TP, 0:bw], lhsT=self.qiT[rows, hh, 0:TP], rhs=self.kiT[rows, c0:c0 + bw], start=True, stop=True), reads=[self.qiT, self.kiT], writes=[ps])
                if hh == 0:
                    S.op('dve', lambda h: h.tensor_scalar(out=sc[0:TP, b0:b0 + bw], in0=ps[0:TP, 0:bw], scalar1=0.0, scalar2=self.wiT[0:TP, 0:1], op0=ALU.max, op1=ALU.mult), reads=[ps, self.wiT], writes=[sc])
                else:
                    tmp = junk.h[:, 512 * (hh % 2):512 * (hh % 2) + 512]
                    S.op('act', lambda h: h.activation(out=tmp[0:TP, 0:bw], in_=ps[0:TP, 0:bw], func=AF.Relu), reads=[ps], writes=[junk])
                    S.op('dve', lambda h, hh=hh: h.scalar_tensor_tensor(out=sc[0:TP, b0:b0 + bw], in0=tmp[0:TP, 0:bw], scalar=self.wiT[0:TP, hh:hh + 1], in1=sc[0:TP, b0:b0 + bw], op0=ALU.mult, op1=ALU.add),
                         reads=[junk, self.wiT, sc], writes=[sc])
        S.op('dve', lambda h: h.tensor_reduce(out=bs[0:TP, 0:1], in_=sc[0:TP, 0:L], axis=AX.X, op=ALU.max, apply_absolute_value=True), reads=[sc], writes=[bs])
        if grp == 'p' and TP == 128:
            S.op('dve', lambda h: h.memset(sc[0:64, L - 64:L], -1e30), writes=[sc])
        S.op('dve', lambda h: h.tensor_scalar(out=bs[0:TP, 1:2], in0=bs[0:TP, 0:1], scalar1=1.0, scalar2=-1.0, op0=ALU.add, op1=ALU.mult), reads=[bs], writes=[bs])
        S.op('dve', lambda h: h.tensor_scalar(out=bs[0:TP, 2:3], in0=bs[0:TP, 0:1], scalar1=1.0, scalar2=2.0, op0=ALU.add, op1=ALU.mult), reads=[bs], writes=[bs])
        for it in range(20):
            hw = float(2.0 ** -(it + 1))
            S.op('dve', lambda h: h.scalar_tensor_tensor(out=bs[0:TP, 3:4], in0=bs[0:TP, 2:3], scalar=hw, in1=bs[0:TP, 1:2], op0=ALU.mult, op1=ALU.add), reads=[bs], writes=[bs])
            S.op('dve', lambda h: h.tensor_scalar(out=junk8[0:TP, 0:L], in0=sc[0:TP, 0:L], scalar1=bs[0:TP, 3:4], scalar2=None, op0=ALU.is_ge, op1=ALU.add, accum_out=bs[0:TP, 4:5]),
                 reads=[sc, bs], writes=[junk, bs])
            S.op('dve', lambda h: h.tensor_scalar(out=bs[0:TP, 5:6], in0=bs[0:TP, 4:5], scalar1=KTOP, scalar2=hw, op0=ALU.is_ge, op1=ALU.mult), reads=[bs], writes=[bs])
            S.op('dve', lambda h: h.scalar_tensor_tensor(out=bs[0:TP, 1:2], in0=bs[0:TP, 5:6], scalar=bs[0:TP, 2:3], in1=bs[0:TP, 1:2], op0=ALU.mult, op1=ALU.add), reads=[bs], writes=[bs])
        S.op('dve', lambda h: h.tensor_scalar(out=sc[0:TP, 0:L], in0=sc[0:TP, 0:L], scalar1=bs[0:TP, 1:2], scalar2=None, op0=ALU.is_ge), reads=[sc, bs], writes=[sc])
        nblk = (L + 127) // 128
        OT = self.pacc

        def stage_a(kb):
            bw = min(128, L - kb * 128)
            k0 = kb * 128
            pst = self.pbank()
            S.op('pe', lambda h: h.transpose(pst[0:bw, 0:TP], sc[0:TP, k0:k0 + bw], self.ident[0:TP, 0:TP]), reads=[sc, cst], writes=[pst])
            sT = self.selT[kb % 2]
            S.op('act', lambda h: h.activation(out=sT[0:bw, 0:TP], in_=pst[0:bw, 0:TP], func=AF.Copy), reads=[pst], writes=[sT])
            kind = 0 if k0 == Qa else (1 if k0 == Qa - 128 else None)
            pss = []
            for g in range(2):
                ps = self.pbank()
                S.op('pe', lambda h, g=g: h.matmul(ps[0:bw, 0:W], lhsT=self.kT[64 * g:64 * g + 64, k0:k0 + bw], rhs=self.qT[64 * g:64 * g + 64, :, 0:TP], start=True, stop=False),
                     reads=[self.kT, self.qT], writes=[ps])
                S.op('pe', lambda h, g=g: h.matmul(ps[0:bw, 0:W], lhsT=self.onesb[32 * g:32 * g + 1, 0:bw], rhs=self.offrow[32 * g:32 * g + 1, 0:W], start=False, stop=True),
                     reads=[self.onesb, self.offrow], writes=[ps])
                pss.append(ps)
            for g in range(2):
                ps = pss[g]
                Et = self.Et[g]
                if kind is not None:
                    tmp = junk.h[:, 1536:2048]
                    bt = self.bias8[0:bw, 2 * g + kind, :].rearrange("p (a c) -> p a c", c=128)[:, :, 0:TP]
                    S.op('dve', lambda h: h.tensor_tensor(out=tmp[0:bw, 0:W].rearrange("p (a c) -> p a c", c=TP), in0=ps[0:bw, 0:W].rearrange("p (a c) -> p a c", c=TP), in1=bt, op=ALU.add),
                         reads=[ps, self.bias8], writes=[junk])
                    S.op('act', lambda h: h.activation(out=Et[0:bw, 0:W], in_=tmp[0:bw, 0:W], func=AF.Exp, scale=0.125), reads=[junk], writes=[Et])
                else:
                    S.op('act', lambda h: h.activation(out=Et[0:bw, 0:W], in_=ps[0:bw, 0:W], func=AF.Exp, scale=0.125), reads=[ps], writes=[Et])
            for g in range(2):
                Et = self.Et[g]
                Pt = self.Pt2[kb % 2][g]
                S.op('dve', lambda h: h.tensor_tensor(out=Pt[0:bw, 0:W].rearrange("p (a c) -> p a c", c=TP), in0=Et[0:bw, 0:W].rearrange("p (a c) -> p a c", c=TP),
                                                       in1=sT[0:bw, 0:TP].unsqueeze(1).to_broadcast([bw, 4, TP]), op=ALU.mult), reads=[Et, sT], writes=[Pt])

        def stage_b(kb):
            bw = min(128, L - kb * 128)
            for g in range(2):
                Pt = self.Pt2[kb % 2][g]
                S.op('pe', lambda h, g=g: h.matmul(OT[g][0:65, 0:W], lhsT=self.vA[0:bw, kb, 65 * g:65 * g + 65], rhs=Pt[0:bw, 0:W], start=(kb == 0), stop=(kb == nblk - 1)),
                     reads=[self.vA, Pt], writes=[OT[g]])
        stage_a(0)
        for kb in range(1, nblk):
            stage_a(kb)
            stage_b(kb - 1)
        stage_b(nblk - 1)
        for g in range(2):
            osb = osb_v
            S.op('act', lambda h: h.activation(out=osb[0:65, 0:W], in_=OT[g][0:65, 0:W], func=AF.Copy), reads=[OT[g]], writes=[junk])
            S.op('dve', lambda h: h.reciprocal(out=osb[64:65, 0:W], in_=osb[64:65, 0:W]), reads=[junk], writes=[junk])
            pbc = self.pbank()
            S.op('pe', lambda h: h.matmul(pbc[0:64, 0:W], lhsT=cst[64:65, 256:320], rhs=osb[64:65, 0:W], start=True, stop=True), reads=[cst, junk], writes=[pbc])
            S.op('dve', lambda h: h.tensor_tensor(out=onn_v[:, 0:W], in0=osb[0:64, 0:W], in1=pbc[0:64, 0:W], op=ALU.mult), reads=[junk, pbc], writes=[junk])
            for hh in range(4):
                hd = 4 * g + hh
                pg = self.proj_fm(l, 'bg%d' % hd, 64, TP)
                sg = junk.h[:, 1536 + 128 * (hh % 2):1536 + 128 * (hh % 2) + 128]
                S.op('act', lambda h: h.activation(out=sg[0:64, 0:TP], in_=pg[0:64, 0:TP], func=AF.Silu), reads=[pg], writes=[junk])
                S.op('dve', lambda h, hh=hh, hd=hd: h.tensor_tensor(out=self.brB[:, hd, 0:TP], in0=onn_v[:, hh * TP:(hh + 1) * TP], in1=sg[0:64, 0:TP], op=ALU.mult), reads=[junk], writes=[self.brB])

    def gdn_setup(self, l):
        S = self.S
        I = self.I
        S.dma('sp', self.gdn_cw[:, :, :], I['gcw'][l], writes=[self.gdn_cw])
        S.dma('sp', self.gdn_par[:, :], I['gpar'][l], writes=[self.gdn_par])
        S.dma('sp', self.gdn_n[:, :], I['gnorm'][l], writes=[self.gdn_n])
        S.op('act', lambda h: h.activation(out=self.gdn_par[:, 0:4], in_=self.gdn_par[:, 0:4], func=AF.Exp), reads=[self.gdn_par], writes=[self.gdn_par])
        S.op('dve', lambda h: h.tensor_scalar(out=self.gdn_par[:, 0:4], in0=self.gdn_par[:, 0:4], scalar1=-1.0, scalar2=None, op0=ALU.mult), reads=[self.gdn_par], writes=[self.gdn_par])

    def gdn_init(self, l, grp):
        S = self.S
        if grp == 'p':
            S.op('dve', lambda h: h.memset(self.gdn_S[:, :, :], 0.0), writes=[self.gdn_S])
            S.op('dve', lambda h: h.memset(self.gdn_tail[:, :, :], 0.0), writes=[self.gdn_tail])
        else:
            S.dma('sp', self.gdn_S[:, :, :], self.I['sgdn'][l].rearrange("h k v -> k h v"), writes=[self.gdn_S])
            S.dma('sp', self.gdn_tail[:, :, :], self.I['sconv'][l], writes=[self.gdn_tail])

    def gdn_part(self, l, grp, g0, G, C, last):
        S = self.S
        cst = self.cst
        nC = G // C
        GE = G + 3
        xext = self.av('xext', 12 * GE)
        x3 = xext[:, 0:12 * GE].rearrange("p (a c) -> p a c", c=GE)
        qkv = self.av('qkv', 12 * G)
        q3 = qkv[:, 0:12 * G].rearrange("p (a c) -> p a c", c=G)
        oT = self.av('oT', 4 * G)
        o3 = oT[:, 0:4 * G].rearrange("p (a c) -> p a c", c=G)
        T1 = self.av('T1', 1024)
        T2 = self.av('T2', 1024)
        S.op('dve', lambda h: h.tensor_copy(out=x3[:, :, 0:3], in_=self.gdn_tail[:, :, :]), reads=[self.gdn_tail], writes=[xext])
        for c in range(12):
            pb = self.proj_fm(l, 'cq%d' % c, 128, G)
            S.op('act' if c % 2 else 'dve', (lambda h, c=c: h.activation(out=x3[:, c, 3:3 + G], in_=pb[:, 0:G], func=AF.Copy)) if c % 2 else (lambda h, c=c: h.tensor_copy(out=x3[:, c, 3:3 + G], in_=pb[:, 0:G])),
                 reads=[pb], writes=[xext])
        S.op('dve', lambda h: h.tensor_copy(out=self.gdn_tail[:, :, :], in_=x3[:, :, G:G + 3]), reads=[xext], writes=[self.gdn_tail])
        for c in range(12):
            S.op('dve', lambda h, c=c: h.tensor_scalar(out=q3[:, c, :], in0=x3[:, c, 0:G], scalar1=self.gdn_cw[:, c, 0:1], scalar2=None, op0=ALU.mult), reads=[xext, self.gdn_cw], writes=[qkv])
            for i in range(1, 4):
                S.op('dve', lambda h, c=c, i=i: h.scalar_tensor_tensor(out=q3[:, c, :], in0=x3[:, c, i:i + G], scalar=self.gdn_cw[:, c, i:i + 1], in1=q3[:, c, :], op0=ALU.mult, op1=ALU.add),
                     reads=[xext, self.gdn_cw, qkv], writes=[qkv])
        S.op('act', lambda h: h.activation(out=qkv[:, 0:12 * G], in_=qkv[:, 0:12 * G], func=AF.Silu), reads=[qkv], writes=[qkv])
        W8 = 8 * G
        S.op('act', lambda h: h.activation(out=T1[:, 0:W8], in_=qkv[:, 0:W8], func=AF.Square), reads=[qkv], writes=[T1])
        for b0 in range(0, W8, 512):
            bw = min(512, W8 - b0)
            pm = self.pbank()
            S.op('pe', lambda h: h.matmul(pm[:, 0:bw], lhsT=cst[:, 256:384], rhs=T1[:, b0:b0 + bw], start=True, stop=True), reads=[cst, T1], writes=[pm])
            S.op('act', lambda h: h.activation(out=T2[:, b0:b0 + bw], in_=pm[:, 0:bw], func=AF.Sqrt, bias=self.epsb[:, 1:2]), reads=[pm, self.epsb], writes=[T2])
        S.op('dve', lambda h: h.reciprocal(out=T2[:, 0:W8], in_=T2[:, 0:W8]), reads=[T2], writes=[T2])
        S.op('dve', lambda h: h.scalar_tensor_tensor(out=qkv[:, 0:4 * G], in0=qkv[:, 0:4 * G], scalar=float(128 ** -0.5), in1=T2[:, 0:4 * G], op0=ALU.mult, op1=ALU.mult), reads=[qkv, T2], writes=[qkv])
        S.op('dve', lambda h: h.tensor_tensor(out=qkv[:, 4 * G:8 * G], in0=qkv[:, 4 * G:8 * G], in1=T2[:, 4 * G:8 * G], op=ALU.mult), reads=[qkv, T2], writes=[qkv])
        wC = self.load_w(l, 'kvD')
        CC = 4 * C
        mU = cst[0:C, 128:128 + C].unsqueeze(1).to_broadcast([C, 4, C])
        mLs = cst[0:C, 768:768 + C].unsqueeze(1).to_broadcast([C, 4, C])
        K = {64: 5, 32: 4, 16: 3}[C]
        V = lambda key: self.av(key, 256)
        for ci in range(nC):
            c0 = ci * C
            pb = self.pbank()
            for k in range(8):
                S.op('pe', lambda h, k=k: h.matmul(pb[0:C, 0:12], lhsT=self.xT[:, k, c0:c0 + C], rhs=wC[:, k, 0:12], start=(k == 0), stop=(k == 7)), reads=[self.xT, wC], writes=[pb])
            bg = self.av('bg', 32)
            S.op('act', lambda h: h.activation(out=bg[0:C, 0:4], in_=pb[0:C, 4:8], func=AF.Sigmoid), reads=[pb], writes=[bg])
            S.op('dve', lambda h: h.tensor_tensor(out=bg[0:C, 4:8], in0=pb[0:C, 8:12], in1=self.gdn_par[0:C, 4:8], op=ALU.add), reads=[pb, self.gdn_par], writes=[bg])
            S.op('act', lambda h: h.activation(out=bg[0:C, 4:8], in_=bg[0:C, 4:8], func=AF.Exp), reads=[bg], writes=[bg])
            S.op('act', lambda h: h.activation(out=bg[0:C, 4:8], in_=bg[0:C, 4:8], func=AF.Ln, bias=1.0), reads=[bg], writes=[bg])
            S.op('dve', lambda h: h.tensor_tensor(out=bg[0:C, 8:12], in0=bg[0:C, 4:8], in1=self.gdn_par[0:C, 0:4], op=ALU.mult), reads=[bg, self.gdn_par], writes=[bg])
            pcol = self.pbank()
            S.op('pe', lambda h: h.matmul(pcol[0:C, 0:4], lhsT=cst[0:C, 128:128 + C], rhs=bg[0:C, 8:12], start=True, stop=True), reads=[cst, bg], writes=[pcol])
            R = V('R')
            R3 = R[0:C, 0:CC].rearrange("p (a c) -> p a c", c=C)
            for h4 in range(4):
                S.op('dve', lambda h, h4=h4: h.tensor_scalar(out=R3[:, h4, :], in0=cst[0:C, 128:128 + C], scalar1=bg[0:C, 8 + h4:9 + h4], scalar2=None, op0=ALU.mult), reads=[cst, bg], writes=[R])
            prow = self.pbank()
            S.op('pe', lambda h: h.matmul(prow[:, 0:CC], lhsT=cst[0:C, 256:384], rhs=R[0:C, 0:CC], start=True, stop=True), reads=[cst, R], writes=[prow])
            egrow = V('egrow')
            S.op('act', lambda h: h.activation(out=egrow[:, 0:CC], in_=prow[:, 0:CC], func=AF.Exp), reads=[prow], writes=[egrow])
            eg3 = egrow[:, 0:CC].rearrange("p (a c) -> p a c", c=C)
            S.op('dve', lambda h: h.tensor_copy(out=bg[0:C, 12:16], in_=pcol[0:C, 0:4]), reads=[pcol], writes=[bg])
            S.op('act', lambda h: h.activation(out=bg[0:C, 16:20], in_=pcol[0:C, 0:4], func=AF.Exp), reads=[pcol], writes=[bg])
            S.op('dve', lambda h: h.tensor_tensor(out=bg[0:C, 20:24], in0=bg[0:C, 16:20], in1=bg[0:C, 0:4], op=ALU.mult), reads=[bg], writes=[bg])
            Dm = V('Dm')
            D3 = Dm[0:C, 0:CC].rearrange("p (a c) -> p a c", c=C)
            p3 = prow[0:C, 0:CC].rearrange("p (a c) -> p a c", c=C)
            for h4 in range(4):
                S.op('dve', lambda h, h4=h4: h.tensor_scalar(out=D3[:, h4, :], in0=p3[:, h4, :], scalar1=bg[0:C, 12 + h4:13 + h4], scalar2=-1.0, op0=ALU.subtract, op1=ALU.mult), reads=[prow, bg], writes=[Dm])
            Elo = V('Elo')
            Eup = V('Eup')
            S.op('dve', lambda h: h.tensor_scalar(out=Elo[0:C, 0:CC], in0=Dm[0:C, 0:CC], scalar1=0.0, scalar2=None, op0=ALU.min), reads=[Dm], writes=[Elo])
            S.op('act', lambda h: h.activation(out=Elo[0:C, 0:CC], in_=Elo[0:C, 0:CC], func=AF.Exp), reads=[Elo], writes=[Elo])
            S.op('dve', lambda h: h.tensor_scalar(out=Eup[0:C, 0:CC], in0=Dm[0:C, 0:CC], scalar1=-1.0, scalar2=0.0, op0=ALU.mult, op1=ALU.min), reads=[Dm], writes=[Eup])
            S.op('act', lambda h: h.activation(out=Eup[0:C, 0:CC], in_=Eup[0:C, 0:CC], func=AF.Exp), reads=[Eup], writes=[Eup])
            El3 = Elo[0:C, 0:CC].rearrange("p (a c) -> p a c", c=C)
            Eu3 = Eup[0:C, 0:CC].rearrange("p (a c) -> p a c", c=C)
            S.op('dve', lambda h: h.tensor_copy(out=bg[0:C, 24:28], in_=Eu3[:, :, C - 1]), reads=[Eup], writes=[bg])
            S.op('dve', lambda h: h.tensor_tensor(out=El3, in0=El3, in1=mLs, op=ALU.mult), reads=[Elo, cst], writes=[Elo])
            S.op('dve', lambda h: h.tensor_tensor(out=Eu3, in0=Eu3, in1=mU, op=ALU.mult), reads=[Eup, cst], writes=[Eup])
            for h4 in range(4):
                S.op('dve', lambda h, h4=h4: h.tensor_scalar(out=El3[:, h4, :], in0=El3[:, h4, :], scalar1=bg[0:C, h4:h4 + 1], scalar2=None, op0=ALU.mult), reads=[Elo, bg], writes=[Elo])
            pkk = self.pbank()
            pmt = self.pbank()
            for h4 in range(4):
                S.op('pe', lambda h, h4=h4: h.matmul(pkk[0:C, h4 * C:(h4 + 1) * C], lhsT=q3[:, 4 + h4, c0:c0 + C], rhs=q3[:, 4 + h4, c0:c0 + C], start=True, stop=True), reads=[qkv], writes=[pkk])
                S.op('pe', lambda h, h4=h4: h.matmul(pmt[0:C, h4 * C:(h4 + 1) * C], lhsT=q3[:, 4 + h4, c0:c0 + C], rhs=q3[:, h4, c0:c0 + C], start=True, stop=True), reads=[qkv], writes=[pmt])
            P = [V('P0')]
            AT = V('AT')
            MT = V('MT')
            S.op('dve', lambda h: h.tensor_tensor(out=P[0][0:C, 0:CC], in0=pkk[0:C, 0:CC], in1=Elo[0:C, 0:CC], op=ALU.mult), reads=[pkk, Elo], writes=[P[0]])
            S.op('dve', lambda h: h.tensor_tensor(out=MT[0:C, 0:CC], in0=pmt[0:C, 0:CC], in1=Eup[0:C, 0:CC], op=ALU.mult), reads=[pmt, Eup], writes=[MT])
            pat = self.pbank()
            for h4 in range(4):
                S.op('pe', lambda h, h4=h4: h.transpose(pat[0:C, h4 * C:(h4 + 1) * C], P[0][0:C, h4 * C:(h4 + 1) * C], self.ident[0:C, 0:C]), reads=[P[0], cst], writes=[pat])
            S.op('act', lambda h: h.activation(out=AT[0:C, 0:CC], in_=pat[0:C, 0:CC], func=AF.Copy), reads=[pat], writes=[AT])
            X = T1
            X3 = X[0:C, 0:1024].rearrange("p (a c) -> p a c", c=256)
            Kdec = T2
            Kd3 = Kdec[0:C, 0:512].rearrange("p (a c) -> p a c", c=128)
            vn = T2
            vn3 = vn[0:C, 512:1024].rearrange("p (a c) -> p a c", c=128)
            pkt = self.pbank()
            pvt = self.pbank()
            for h4 in range(4):
                S.op('pe', lambda h, h4=h4: h.transpose(pkt[0:C, h4 * 128:(h4 + 1) * 128], q3[:, 4 + h4, c0:c0 + C], self.ident[:, 0:128]), reads=[qkv, cst], writes=[pkt])
                S.op('pe', lambda h, h4=h4: h.transpose(pvt[0:C, h4 * 128:(h4 + 1) * 128], q3[:, 8 + h4, c0:c0 + C], self.ident[:, 0:128]), reads=[qkv, cst], writes=[pvt])
            for h4 in range(4):
                S.op('dve', lambda h, h4=h4: h.tensor_scalar(out=X3[:, h4, 0:128], in0=pkt[0:C, h4 * 128:(h4 + 1) * 128], scalar1=bg[0:C, 20 + h4:21 + h4], scalar2=None, op0=ALU.mult), reads=[pkt, bg], writes=[X])
                S.op('dve', lambda h, h4=h4: h.tensor_scalar(out=X3[:, h4, 128:256], in0=pvt[0:C, h4 * 128:(h4 + 1) * 128], scalar1=bg[0:C, h4:h4 + 1], scalar2=None, op0=ALU.mult), reads=[pvt, bg], writes=[X])
                S.op('dve', lambda h, h4=h4: h.tensor_scalar(out=Kd3[:, h4, :], in0=pkt[0:C, h4 * 128:(h4 + 1) * 128], scalar1=bg[0:C, 24 + h4:25 + h4], scalar2=None, op0=ALU.mult), reads=[pkt, bg], writes=[Kdec])
            PTs = [AT]
            Ps = [P[0]]
            for k in range(1, K + 1):
                pk_ = self.pbank()
                pt_ = self.pbank()
                Pp, PTp = Ps[-1], PTs[-1]
                for h4 in range(4):
                    sl = slice(h4 * C, (h4 + 1) * C)
                    if k < K:
                        S.op('pe', lambda h, sl=sl: h.matmul(pk_[0:C, sl], lhsT=PTp[0:C, sl], rhs=Pp[0:C, sl], start=True, stop=True), reads=[PTp, Pp], writes=[pk_])
                    S.op('pe', lambda h, sl=sl: h.matmul(pt_[0:C, sl], lhsT=Pp[0:C, sl], rhs=PTp[0:C, sl], start=True, stop=True), reads=[PTp, Pp], writes=[pt_])
                nPT = V('PTk%d' % k)
                S.op('act', lambda h: h.activation(out=nPT[0:C, 0:CC], in_=pt_[0:C, 0:CC], func=AF.Copy), reads=[pt_], writes=[nPT])
                PTs.append(nPT)
                if k < K:
                    nP = V('Pk%d' % k)
                    S.op('dve', lambda h: h.tensor_copy(out=nP[0:C, 0:CC], in_=pk_[0:C, 0:CC]), reads=[pk_], writes=[nP])
                    Ps.append(nP)
            for k in range(K, -1, -1):
                PTk = PTs[k]
                for half in range(2):
                    px = self.pbank()
                    for hh in range(2):
                        h4 = 2 * half + hh
                        S.op('pe', lambda h, h4=h4, hh=hh: h.matmul(px[0:C, hh * 256:(hh + 1) * 256], lhsT=PTk[0:C, h4 * C:(h4 + 1) * C], rhs=X3[:, h4, :], start=True, stop=True), reads=[PTk, X], writes=[px])
                    xs = X[0:C, half * 512:(half + 1) * 512]
                    S.op('dve', lambda h: h.tensor_tensor(out=xs, in0=xs, in1=px[0:C, 0:512], op=(ALU.subtract if k == 0 else ALU.add)), reads=[X, px], writes=[X])
            pwt = self.pbank()
            for h4 in range(4):
                S.op('pe', lambda h, h4=h4: h.transpose(pwt[:, h4 * C:(h4 + 1) * C], X3[:, h4, 0:128], self.ident[0:C, 0:C]), reads=[X, cst], writes=[pwt])
            WT = Dm
            S.op('act', lambda h: h.activation(out=WT[:, 0:CC], in_=pwt[:, 0:CC], func=AF.Copy), reads=[pwt], writes=[WT])
            pws = self.pbank()
            for h4 in range(4):
                S.op('pe', lambda h, h4=h4: h.matmul(pws[0:C, h4 * 128:(h4 + 1) * 128], lhsT=WT[:, h4 * C:(h4 + 1) * C], rhs=self.gdn_S[:, h4, :], start=True, stop=True), reads=[WT, self.gdn_S], writes=[pws])
            S.op('dve', lambda h: h.tensor_tensor(out=vn3, in0=X3[:, :, 128:256], in1=pws[0:C, 0:512].rearrange("p (a c) -> p a c", c=128), op=ALU.subtract), reads=[X, pws], writes=[vn])
            QeT = R
            Qe3 = QeT[:, 0:CC].rearrange("p (a c) -> p a c", c=C)
            S.op('dve', lambda h: h.tensor_tensor(out=Qe3, in0=q3[:, 0:4, c0:c0 + C], in1=eg3, op=ALU.mult), reads=[qkv, egrow], writes=[QeT])
            po = self.pbank()
            for h4 in range(4):
                sl = slice(h4 * C, (h4 + 1) * C)
                S.op('pe', lambda h, h4=h4, sl=sl: h.matmul(po[:, sl], lhsT=self.gdn_S[:, h4, :], rhs=QeT[:, sl], start=True, stop=False), reads=[self.gdn_S, QeT], writes=[po])
                S.op('pe', lambda h, h4=h4, sl=sl: h.matmul(po[:, sl], lhsT=vn3[:, h4, :], rhs=MT[0:C, sl], start=False, stop=True), reads=[vn, MT], writes=[po])
            S.op('act', lambda h: h.activation(out=o3[:, :, c0:c0 + C], in_=po[:, 0:CC].rearrange("p (a c) -> p a c", c=C), func=AF.Copy), reads=[po], writes=[oT])
            pu = self.pbank()
            for h4 in range(4):
                S.op('pe', lambda h, h4=h4: h.matmul(pu[:, h4 * 128:(h4 + 1) * 128], lhsT=Kd3[:, h4, :], rhs=vn3[:, h4, :], start=True, stop=True), reads=[Kdec, vn], writes=[pu])
            for h4 in range(4):
                S.op('dve', lambda h, h4=h4: h.scalar_tensor_tensor(out=self.gdn_S[:, h4, :], in0=self.gdn_S[:, h4, :], scalar=eg3[:, h4, C - 1:C], in1=pu[:, h4 * 128:(h4 + 1) * 128], op0=ALU.mult, op1=ALU.add),
                     reads=[self.gdn_S, egrow, pu], writes=[self.gdn_S])
        if last:
            S.dma('pool', self.O['gdn' + grp][l].rearrange("h k v -> k h v"), self.gdn_S[:, :, :], reads=[self.gdn_S], writes=[self.dk('ogdn' + grp)])
        W4 = 4 * G
        S.op('act', lambda h: h.activation(out=T1[:, 0:W4], in_=oT[:, 0:W4], func=AF.Square), reads=[oT], writes=[T1])
        pm = self.pbank()
        S.op('pe', lambda h: h.matmul(pm[:, 0:W4], lhsT=cst[:, 384:512], rhs=T1[:, 0:W4], start=True, stop=True), reads=[cst, T1], writes=[pm])
        S.op('act', lambda h: h.activation(out=T2[:, 0:W4], in_=pm[:, 0:W4], func=AF.Sqrt, bias=self.epsb[:, 1:2]), reads=[pm, self.epsb], writes=[T2])
        S.op('dve', lambda h: h.reciprocal(out=T2[:, 0:W4], in_=T2[:, 0:W4]), reads=[T2], writes=[T2])
        S.op('dve', lambda h: h.scalar_tensor_tensor(out=oT[:, 0:W4], in0=oT[:, 0:W4], scalar=self.gdn_n[:, 0:1], in1=T2[:, 0:W4], op0=ALU.mult, op1=ALU.mult), reads=[oT, T2, self.gdn_n], writes=[oT])
        for c4 in range(4):
            pgt = self.proj_fm(l, 'cg%d' % c4, 128, G)
            sg = V('R') if c4 % 2 else V('Dm')
            S.op('act', lambda h: h.activation(out=sg[:, 0:G], in_=pgt[:, 0:G], func=AF.Silu), reads=[pgt], writes=[sg])
            S.op('dve', lambda h, c4=c4: h.tensor_tensor(out=self.brT[:, 8 + c4, 0:G], in0=o3[:, c4, :], in1=sg[:, 0:G], op=ALU.mult), reads=[oT, sg], writes=[self.brT])

    def gla_part(self, l, grp, g0, G, C, last):
        S = self.S
        cst = self.cst
        nC = G // C
        qT = self.wsp(True)
        kT = self.wsp()
        W4 = 4 * G
        q3 = qT[0:64, 0:W4].rearrange("p (a c) -> p a c", c=G)
        k3 = kT[0:64, 0:W4].rearrange("p (a c) -> p a c", c=G)
        for h4 in range(4):
            pb = self.proj_fm(l, 'dq%d' % h4, 64, G)
            S.op('act', lambda h, h4=h4: h.activation(out=q3[:, h4, :], in_=pb[0:64, 0:G], func=AF.Copy, scale=0.125), reads=[pb], writes=[qT])
            pb2 = self.proj_fm(l, 'dk%d' % h4, 64, G)
            S.op('dve', lambda h, h4=h4: h.tensor_copy(out=k3[:, h4, :], in_=pb2[0:64, 0:G]), reads=[pb2], writes=[kT])
        pg = self.proj_fm(l, 'dg', 16, G)
        dgT = self.sm()
        dg_sb = self.scratch()
        S.op('dve', lambda h: h.tensor_copy(out=dg_sb[0:16, 0:G], in_=pg[0:16, 0:G]), reads=[pg], writes=[dg_sb])
        spT = self.wsp()
        sp3 = spT[0:64, 0:W4].rearrange("p (a c) -> p a c", c=G)
        nb = self.sm()
        S.op('dve', lambda h: h.tensor_scalar(out=nb[0:64, 0:4], in0=self.gla_b[:, :], scalar1=-1.0, scalar2=None, op0=ALU.mult), reads=[self.gla_b], writes=[nb])
        for h4 in range(4):
            pl = self.pbank()
            S.op('pe', lambda h, h4=h4: h.matmul(pl[0:64, 0:G], lhsT=self.gla_w[:, h4 * 64:(h4 + 1) * 64], rhs=dg_sb[0:16, 0:G], start=True, stop=True),
                 reads=[self.gla_w, dg_sb], writes=[pl])
            S.op('act', lambda h, h4=h4: h.activation(out=sp3[:, h4, :], in_=pl[0:64, 0:G], func=AF.Exp, scale=-1.0, bias=nb[0:64, h4:h4 + 1]),
                 reads=[pl, nb], writes=[spT])
        S.op('act', lambda h: h.activation(out=spT[0:64, 0:W4], in_=spT[0:64, 0:W4], func=AF.Ln, bias=1.0), reads=[spT], writes=[spT])
        csT = self.wsp()
        S.op('dve', lambda h: h.tensor_tensor_scan(out=csT[0:64, 0:W4], data0=self.resetm[:, 0:W4], data1=spT[0:64, 0:W4], initial=0.0, op0=ALU.mult, op1=ALU.add),
             reads=[self.resetm, spT], writes=[csT])
        eq = self.wsp()
        ek = self.wsp()
        S.op('act', lambda h: h.activation(out=eq[0:64, 0:W4], in_=csT[0:64, 0:W4], func=AF.Exp, scale=-1.0 / 16.0), reads=[csT], writes=[eq])
        S.op('act', lambda h: h.activation(out=ek[0:64, 0:W4], in_=csT[0:64, 0:W4], func=AF.Exp, scale=1.0 / 16.0), reads=[csT], writes=[ek])
        S.op('dve', lambda h: h.tensor_tensor(out=qT[0:64, 0:W4], in0=qT[0:64, 0:W4], in1=eq[0:64, 0:W4], op=ALU.mult), reads=[qT, eq], writes=[qT])
        S.op('dve', lambda h: h.tensor_tensor(out=kT[0:64, 0:W4], in0=kT[0:64, 0:W4], in1=ek[0:64, 0:W4], op=ALU.mult), reads=[kT, ek], writes=[kT])
        eq3 = eq[0:64, 0:W4].rearrange("p (a c) -> p a c", c=G)
        oT = self.wsp()
        o3 = oT[:, 0:W4].rearrange("p (a c) -> p a c", c=G)
        for ci in range(nC):
            c0 = ci * C
            pv = self.pbank()
            for c4 in range(4):
                w, xTw = self.load_wx(l, 'dv%d' % c4)
                for k in range(8):
                    S.op('pe', lambda h, k=k, c4=c4, w=w: h.matmul(pv[0:C, c4 * 128:(c4 + 1) * 128], lhsT=xTw[:, k, c0:c0 + C], rhs=w[:, k, :],
                                                               start=(k == 0), stop=(k == 7)), reads=[w, xTw], writes=[pv])
            vt = self.scratch()
            S.op('act', lambda h: h.activation(out=vt[0:C, 0:512], in_=pv[0:C, 0:512], func=AF.Copy), reads=[pv], writes=[vt])
            pk = self.pbank()
            for h4 in range(4):
                S.op('pe', lambda h, h4=h4: h.transpose(pk[0:C, h4 * 64:(h4 + 1) * 64], k3[:, h4, c0:c0 + C], self.ident[0:64, 0:64]),
                     reads=[kT, cst], writes=[pk])
            kt = self.scratch()
            S.op('dve', lambda h: h.tensor_copy(out=kt[0:C, 0:256], in_=pk[0:C, 0:256]), reads=[pk], writes=[kt])
            pa = self.pbank()
            for h4 in range(4):
                S.op('pe', lambda h, h4=h4: h.matmul(pa[0:C, h4 * C:(h4 + 1) * C], lhsT=k3[:, h4, c0:c0 + C], rhs=q3[:, h4, c0:c0 + C], start=True, stop=True),
                     reads=[kT, qT], writes=[pa])
            at = self.scratch()
            pa3 = pa[0:C, 0:4 * C].rearrange("p (a c) -> p a c", c=C)
            at3 = at[0:C, 0:4 * C].rearrange("p (a c) -> p a c", c=C)
            msk = cst[0:C, 128:128 + C].unsqueeze(1).to_broadcast([C, 4, C])
            S.op('dve', lambda h: h.tensor_tensor(out=at3, in0=pa3, in1=msk, op=ALU.mult), reads=[pa, cst], writes=[at])
            po = self.pbank()
            for h4 in range(4):
                S.op('pe', lambda h, h4=h4: h.matmul(po[:, h4 * C:(h4 + 1) * C], lhsT=vt[0:C, h4 * 128:(h4 + 1) * 128], rhs=at3[:, h4, :], start=True, stop=False),
                     reads=[vt, at], writes=[po])
                S.op('pe', lambda h, h4=h4: h.matmul(po[:, h4 * C:(h4 + 1) * C], lhsT=self.gla_S[:, h4, :], rhs=q3[:, h4, c0:c0 + C], start=False, stop=True),
                     reads=[self.gla_S, qT], writes=[po])
            S.op('act', lambda h: h.activation(out=o3[:, :, c0:c0 + C], in_=po[:, 0:4 * C].rearrange("p (a c) -> p a c", c=C), func=AF.Copy), reads=[po], writes=[oT])
            pu = self.pbank()
            for h4 in range(4):
                S.op('pe', lambda h, h4=h4: h.matmul(pu[0:64, h4 * 128:(h4 + 1) * 128], lhsT=kt[0:C, h4 * 64:(h4 + 1) * 64], rhs=vt[0:C, h4 * 128:(h4 + 1) * 128], start=True, stop=True),
                     reads=[kt, vt], writes=[pu])
            S.op('dve', lambda h: h.tensor_tensor(out=self.gla_S[:, :, :], in0=self.gla_S[:, :, :], in1=pu[0:64, 0:512].rearrange("p (a c) -> p a c", c=128), op=ALU.add),
                 reads=[self.gla_S, pu], writes=[self.gla_S])
            for h4 in range(4):
                S.op('dve', lambda h, h4=h4: h.tensor_scalar(out=self.gla_S[:, h4, :], in0=self.gla_S[:, h4, :], scalar1=eq3[:, h4, c0 + C - 1:c0 + C], scalar2=None, op0=ALU.mult),
                     reads=[self.gla_S, eq], writes=[self.gla_S])
        if last:
            S.dma('pool', self.O['gla' + grp][l].rearrange("h k v -> k h v"), self.gla_S[:, :, :], reads=[self.gla_S], writes=[self.dk('ogla' + grp)])
        sq = spT
        S.op('act', lambda h: h.activation(out=sq[:, 0:W4], in_=oT[:, 0:W4], func=AF.Square), reads=[oT], writes=[sq])
        rst = csT
        for b0 in range(0, W4, 512):
            bw = min(512, W4 - b0)
            pm = self.pbank()
            S.op('pe', lambda h: h.matmul(pm[:, 0:bw], lhsT=cst[:, 384:512], rhs=sq[:, b0:b0 + bw], start=True, stop=True), reads=[cst, sq], writes=[pm])
            S.op('act', lambda h: h.activation(out=rst[:, b0:b0 + bw], in_=pm[:, 0:bw], func=AF.Sqrt, bias=self.epsb[:, 1:2]), reads=[pm, self.epsb], writes=[rst])
        S.op('dve', lambda h: h.reciprocal(out=rst[:, 0:W4], in_=rst[:, 0:W4]), reads=[rst], writes=[rst])
        S.op('dve', lambda h: h.scalar_tensor_tensor(out=oT[:, 0:W4], in0=oT[:, 0:W4], scalar=self.gla_n[:, 0:1], in1=rst[:, 0:W4], op0=ALU.mult, op1=ALU.mult),
             reads=[oT, rst, self.gla_n], writes=[oT])
        for c4 in range(4):
            pgt = self.proj_fm(l, 'dgt%d' % c4, 128, G)
            sg = self.scratch()
            S.op('act', lambda h: h.activation(out=sg[:, 0:G], in_=pgt[:, 0:G], func=AF.Silu), reads=[pgt], writes=[sg])
            S.op('dve', lambda h, c4=c4: h.tensor_tensor(out=self.brT[:, 12 + c4, 0:G], in0=o3[:, c4, :], in1=sg[:, 0:G], op=ALU.mult), reads=[oT, sg], writes=[self.brT])

    def out_stage(self, l, grp, g0, G, TP, xin, xin_tk, yout, yout_tk):
        S = self.S
        I = self.I
        mix = self.av('mixT', 4 * self.G)
        mix3 = mix.h.bitcast(BF16)[:, 0:8 * G].rearrange("p (a c) -> p a c", c=G)
        for dc in range(8):
            for b in range(4):
                pp = self.pbank()
                if b == 1:
                    self.wrr += 1
                    wb = self.wbrBs[self.wrr % 2]
                    S.dma('sp', wb[:, :, :], self.wbrb[l, b, dc].rearrange("(p h) f -> p (h f)", h=2).rearrange("p (a c) -> p a c", c=128), reads=[self.dk(('wbrb', l, b, dc))], writes=[wb])
                    for hh in range(8):
                        S.op('pe', lambda h, hh=hh: h.matmul(pp[:, 0:G], lhsT=wb[:, hh, :], rhs=self.brB[:, hh, 0:G], start=(hh == 0), stop=(hh == 7)),
                             reads=[wb, self.brB], writes=[pp])
                else:
                    self.wrr += 1
                    wb = self.wbrs[self.wrr % 2]
                    S.dma('sp', wb[:, :, :], self.wbrb[l, b, dc].rearrange("p (a c) -> p a c", c=128), reads=[self.dk(('wbrb', l, b, dc))], writes=[wb])
                    for kc in range(4):
                        S.op('pe', lambda h, kc=kc: h.matmul(pp[:, 0:G], lhsT=wb[:, kc, :], rhs=self.brT[:, 4 * b + kc, 0:G], start=(kc == 0), stop=(kc == 3)),
                             reads=[wb, self.brT], writes=[pp])
                pm = self.proj_fm(l, 'mg%d_%d' % (b, dc), 128, G)
                sg = self.scratch()
                S.op('act', lambda h: h.activation(out=sg[:, 0:G], in_=pm[:, 0:G], func=AF.Sigmoid), reads=[pm], writes=[sg])
                if b == 0:
                    S.op('dve', lambda h, dc=dc: h.tensor_tensor(out=mix3[:, dc, 0:G], in0=pp[:, 0:G], in1=sg[:, 0:G], op=ALU.mult), reads=[pp, sg], writes=[mix])
                else:
                    tmp = self.scratch()
                    S.op('dve', lambda h: h.tensor_tensor(out=tmp[:, 0:G], in0=pp[:, 0:G], in1=sg[:, 0:G], op=ALU.mult), reads=[pp, sg], writes=[tmp])
                    S.op('dve', lambda h, dc=dc: h.tensor_tensor(out=mix3[:, dc, 0:G], in0=mix3[:, dc, 0:G], in1=tmp[:, 0:G], op=ALU.add), reads=[mix, tmp], writes=[mix])
        for ti in range(G // TP):
            t0 = ti * TP
            xt = self.av('xtm', D)
            S.dma('sp', xt[0:TP, :], xin[g0 + t0: g0 + t0 + TP, :], reads=[xin_tk], writes=[xt])
            z = self.av('zt', D)
            for qt in range(8):
                wo = self.wouts[0]
                S.dma('sp', wo[:, :, :], self.woutb[l, qt].rearrange("p (a c) -> p a c", c=128), reads=[self.dk(('woutb', l, qt))], writes=[wo])
                pz = self.pbank()
                for k in range(8):
                    S.op('pe', lambda h, k=k: h.matmul(pz[0:TP, 0:128], lhsT=mix3[:, k, t0:t0 + TP], rhs=wo[:, k, :], start=(k == 0), stop=(k == 7)),
                         reads=[mix, wo], writes=[pz])
                S.op('dve', lambda h, qt=qt: h.scalar_tensor_tensor(out=z[0:TP, qt * 128:(qt + 1) * 128], in0=xt[0:TP, qt * 128:(qt + 1) * 128], scalar=float(DN_ALPHA),
                                                                     in1=pz[0:TP, 0:128], op0=ALU.mult, op1=ALU.add), reads=[xt, pz], writes=[z])
            st = self.sm()
            for half in range(2):
                S.op('dve', lambda h, half=half: h.bn_stats(out=st[0:TP, half * 6:(half + 1) * 6], in_=z[0:TP, half * 512:(half + 1) * 512]), reads=[z], writes=[st])
            mv = self.sm()
            S.op('dve', lambda h: h.bn_aggr(out=mv[0:TP, 0:2], in_=st[0:TP, 0:12]), reads=[st], writes=[mv])
            S.op('act', lambda h: h.activation(out=mv[0:TP, 2:3], in_=mv[0:TP, 1:2], func=AF.Sqrt, bias=self.epsb[0:TP, 0:1]), reads=[mv, self.epsb], writes=[mv])
            S.op('dve', lambda h: h.reciprocal(out=mv[0:TP, 3:4], in_=mv[0:TP, 2:3]), reads=[mv], writes=[mv])
            S.op('dve', lambda h: h.tensor_scalar(out=z[0:TP, :], in0=z[0:TP, :], scalar1=mv[0:TP, 0:1], scalar2=mv[0:TP, 3:4], op0=ALU.subtract, op1=ALU.mult),
                 reads=[z, mv], writes=[z])
            S.op('dve', lambda h: h.tensor_tensor(out=z[0:TP, :], in0=z[0:TP, :], in1=self.lng[0:TP, :], op=ALU.mult), reads=[z, self.lng], writes=[z])
            S.op('dve', lambda h: h.tensor_tensor(out=z[0:TP, :], in0=z[0:TP, :], in1=self.lnb[0:TP, :], op=ALU.add), reads=[z, self.lnb], writes=[z])
            S.dma('pool', yout[g0 + t0: g0 + t0 + TP, :], z[0:TP, :], reads=[z], writes=[yout_tk])


def _prep_consts():
    c = np.zeros((128, 1024), np.float32)
    c[:, 0:128] = np.eye(128, dtype=np.float32)
    p = np.arange(128)[:, None]
    f = np.arange(128)[None, :]
    c[:, 128:256] = (f >= p).astype(np.float32)
    c[:, 256:384] = 1.0
    c[:, 384:512] = 1.0 / 128.0
    c[:, 512:768] = np.arange(1, 257, dtype=np.float32)[None, :]
    c[:, 768:896] = (p > f).astype(np.float32)
    c[0:64, 896] = 1.0
    c[64:128, 928] = 1.0
    return c


_CACHE = {}


def kernel(**inp):
    T = inp['x_prompt'].shape[1]
    if T not in _CACHE:
        b = Builder(T)
        _CACHE[T] = b.build()
    nc = _CACHE[T]
    f = lambda a: np.ascontiguousarray(np.asarray(a, dtype=np.float32))
    w_in = f(inp['w_in'])
    win = np.zeros((DEPTH, NCH, 128, 8, 128), np.float32)
    for ci, (_, cols) in enumerate(CHUNKS):
        blk = w_in[:, :, cols]
        blk = blk.reshape(DEPTH, 8, 128, len(cols)).transpose(0, 2, 1, 3)
        win[:, ci, :, :, :len(cols)] = blk
    cst = _prep_consts()
    glab = f(inp['gla_b_g']).reshape(DEPTH, 4, 64).transpose(0, 2, 1).copy()
    glan = f(inp['gla_norm']).reshape(DEPTH, 128, 1)
    def st_layout(a):
        return np.ascontiguousarray(a.reshape(DEPTH, 16, 2, 64).transpose(0, 2, 3, 1)).reshape(DEPTH, 128, 16)
    are, aim = st_layout(f(inp['s5_a_re'])), st_layout(f(inp['s5_a_im']))
    ldt = np.ascontiguousarray(np.broadcast_to(f(inp['s5_log_dt']).reshape(DEPTH, 16, 2, 1), (DEPTH, 16, 2, 64)).transpose(0, 2, 3, 1)).reshape(DEPTH, 128, 16)
    s5p = np.ascontiguousarray(np.stack([are, aim, ldt], axis=2))
    bre, bim = f(inp['s5_b_re']), f(inp['s5_b_im'])
    cre, cim = f(inp['s5_c_re']), f(inp['s5_c_im'])
    s5b = np.zeros((DEPTH, 2, 2, 128, 4, 128), np.float32)
    s5c = np.zeros((DEPTH, 2, 128, 16, 128), np.float32)
    for c in range(4):
        for s4 in range(4):
            for g2 in range(2):
                g = 8 * c + 2 * s4 + g2
                for ri, (bb, cc) in enumerate(((bre, cre), (bim, cim))):
                    s5b[:, 0 if s4 < 3 else 1, ri, 32 * s4 + 16 * g2: 32 * s4 + 16 * g2 + 16, c, 64 * g2: 64 * g2 + 64] = bb[:, g].transpose(0, 2, 1)
                    s5c[:, ri, 64 * g2: 64 * g2 + 64, 4 * c + s4, (2 * s4 + g2) * 16:(2 * s4 + g2) * 16 + 16] = cc[:, g].transpose(0, 2, 1)
    s5d = np.ascontiguousarray(f(inp['s5_d']).reshape(DEPTH, 4, 128).transpose(0, 2, 1))
    h0r, h0i = st_layout(f(inp['state_s5_re']).transpose(1, 0, 2, 3).reshape(NCORE * DEPTH, 32, 64).reshape(NCORE, DEPTH, 32, 64)[0]) if False else (None, None)
    gcw = np.ascontiguousarray(f(inp['gdn_conv']).reshape(DEPTH, 4, 12, 128).transpose(0, 3, 2, 1))
    gpar = np.ascontiguousarray(np.broadcast_to(np.concatenate([f(inp['gdn_a_log']), f(inp['gdn_dt_bias'])], axis=1)[:, None, :], (DEPTH, 128, 8)))
    gnorm = f(inp['gdn_norm']).reshape(DEPTH, 128, 1)
    rr = 127 - np.arange(384)
    oh = (_t5_bucket(rr)[None, :] == np.arange(32)[:, None]).astype(np.float32)
    common = dict(win=win, relb=f(inp['rel_bias']), oh=oh, gcw=gcw, gpar=gpar, gnorm=gnorm, s5p=s5p, s5b=s5b, s5c=s5c, s5d=s5d, wglu=f(inp['s5_w_glu']), wbr=f(inp['w_branch']), wout=f(inp['w_out']), lng=f(inp['ln_g']), lnb=f(inp['ln_b']), cst=cst,
                  glaw=f(inp['gla_w_g2']), glab=glab, glan=glan)
    xp = f(inp['x_prompt'])
    xs = f(inp['x_sample'])
    in_maps = []
    for c in range(NCORE):
        m = dict(common)
        m['xp'] = xp[c % 2]
        m['xs'] = xs[c]
        m['sgla'] = f(inp['state_gla'])[:, c]
        m['ck'] = f(inp['cache_k'])[:, c].reshape(DEPTH, PAST, 128)
        m['cv'] = f(inp['cache_v'])[:, c].reshape(DEPTH, PAST, 128)
        cki = f(inp['cache_kidx'])[:, c]
        m['cki'] = np.ascontiguousarray(np.concatenate([cki, cki], axis=-1))
        m['sgdn'] = f(inp['state_gdn'])[:, c]
        m['sconv'] = np.ascontiguousarray(f(inp['state_gdn_conv'])[:, c].reshape(DEPTH, 3, 12, 128).transpose(0, 3, 2, 1))
        m['s5h0'] = np.ascontiguousarray(np.stack([st_layout(f(inp['state_s5_re'])[:, c]), st_layout(f(inp['state_s5_im'])[:, c])], axis=2))
        in_maps.append(m)
    res = run_bass_kernel_spmd(nc, in_maps, core_ids=list(range(NCORE))).results
    P = lambda name: np.stack([res[b][name] for b in range(2)], axis=0)
    Sm = lambda name: np.stack([res[b][name] for b in range(NCORE)], axis=0)
    yp = P('yp')
    ys = Sm('ys')

    def kvfix(a, n):
        return np.ascontiguousarray(a.transpose(1, 0, 2, 3)).reshape(DEPTH, n, a.shape[2], 2, 64)

    def st(a):
        return np.ascontiguousarray(np.moveaxis(a, 0, 1))
    zeros = lambda *s: np.zeros(s, np.float32)

    def s5fix(a, ri):
        x = a[:, :, :, ri, :].reshape(a.shape[0], DEPTH, 2, 64, 16).transpose(1, 0, 4, 2, 3)
        return np.ascontiguousarray(x).reshape(DEPTH, a.shape[0], 32, 64)
    outs = [yp, ys]
    for g, getter, n, tt in (('p', P, 2, T), ('s', Sm, NCORE, DEC_SEQ)):
        outs += [kvfix(getter('k' + g), n), kvfix(getter('v' + g), n), st(getter('ki' + g)),
                 s5fix(getter('os5' + g), 0), s5fix(getter('os5' + g), 1), st(getter('ogdn' + g)),
                 st(getter('conv' + g)), st(getter('gla' + g))]
    return tuple(outs)
```
